# Optimizing a Trainium2 kernel written in Bass

```python
import math
import jax, jax.numpy as jnp
from jax import lax
import numpy as np

D_MODEL = 1024
BATCH = 32
SEQ = 256
DEPTH = 4
DEC_BATCH = 8
DEC_SEQ = 4096
PAST_LEN = 512

GRID_W = 64
POS_BASE = 10000.0
W_A = 256
H_A = 4
DK = 64
DV = 64
CONV_W = 4
DN_CHUNK = 64
W_B = 256
G_B = 4
SGU_CHUNK = 128
W_C = 256
POOL_WINDOWS = (2, 4, 8, 16)
C_C = 64
W_D = 256
G_D = 4
C_D = 64
MIX_W = W_A + W_B + W_C + W_D
SPLIT_SIZES = (3 * W_A, W_A, 2 * H_A, 2 * H_A, 2 * W_B, W_B, W_C, W_C, W_D, W_D)
P_IN = 3 * W_A + W_A + 4 * H_A + 3 * W_B + 2 * W_C + 2 * W_D
EPS = 1e-6

kernel_name = "hybrid_delta_sgu_pool_fourier_dit_step"


def _split_points():
    return [int(i) for i in np.cumsum(SPLIT_SIZES)[:-1]]


def rmsnorm(x, g):
    xf = x.astype(jnp.float32)
    y = xf * lax.rsqrt(jnp.mean(xf * xf, axis=-1, keepdims=True) + EPS)
    return y.astype(x.dtype) * g


def l2norm(x):
    xf = x.astype(jnp.float32)
    return xf * lax.rsqrt(jnp.sum(xf * xf, axis=-1, keepdims=True) + EPS)


def grid_pos_embed(T, dtype):
    rows = T // GRID_W
    r = jnp.repeat(jnp.arange(rows, dtype=jnp.float32), GRID_W)
    col = jnp.tile(jnp.arange(GRID_W, dtype=jnp.float32), rows)
    n_freq = D_MODEL // 4
    freqs = jnp.power(POS_BASE, -jnp.arange(n_freq, dtype=jnp.float32) / n_freq)
    ar = r[:, None] * freqs[None]
    ac = col[:, None] * freqs[None]
    return jnp.concatenate([jnp.sin(ar), jnp.cos(ar), jnp.sin(ac), jnp.cos(ac)], axis=-1).astype(dtype)


def short_conv(x, w):
    C = x.shape[-1]
    pad = (CONV_W // 2, CONV_W - 1 - CONV_W // 2)
    return lax.conv_general_dilated(x, w[:, None, :].astype(x.dtype), window_strides=(1,), padding=[pad],
                                    dimension_numbers=("NWC", "WIO", "NWC"), feature_group_count=C)


def chunk_delta_rule(q, k, v, beta, g, S0):
    B, T, H, _ = q.shape
    N = T // DN_CHUNK
    f32 = jnp.float32

    def chunks(a):
        a = a.astype(f32).reshape((B, N, DN_CHUNK) + a.shape[2:])
        return jnp.moveaxis(a, (1, 3), (0, 2))

    qc, kc, vc, bc, gc = chunks(q), chunks(k), chunks(v), chunks(beta), chunks(g)
    gcum = jnp.cumsum(gc, axis=-1)
    idx = jnp.arange(DN_CHUNK)
    incl = idx[:, None] >= idx[None, :]
    strict = idx[:, None] > idx[None, :]
    diff = gcum[..., :, None] - gcum[..., None, :]
    decay_mat = jnp.where(incl, jnp.exp(jnp.where(incl, diff, 0.0)), 0.0)
    kb = kc * bc[..., None]
    A = jnp.where(strict, jnp.einsum("nbhid,nbhjd->nbhij", kb, kc) * decay_mat, 0.0)
    eye = jnp.eye(DN_CHUNK, dtype=f32)
    Tinv = lax.linalg.triangular_solve(eye + A, jnp.broadcast_to(eye, A.shape), left_side=True, lower=True)
    u = jnp.einsum("nbhij,nbhje->nbhie", Tinv, vc * bc[..., None])
    w = jnp.einsum("nbhij,nbhjd->nbhid", Tinv, kb * jnp.exp(gcum)[..., None])
    qk = jnp.einsum("nbhid,nbhjd->nbhij", qc, kc) * decay_mat
    q_dec = qc * jnp.exp(gcum)[..., None]
    k_dec = kc * jnp.exp(gcum[..., -1:] - gcum)[..., None]
    g_last = jnp.exp(gcum[..., -1])

    def step(S, xs):
        q_i, k_i, u_i, w_i, qk_i, gl_i = xs
        v_new = u_i - jnp.einsum("bhcd,bhde->bhce", w_i, S)
        o_i = jnp.einsum("bhcd,bhde->bhce", q_i, S) + jnp.einsum("bhij,bhje->bhie", qk_i, v_new)
        S = S * gl_i[..., None, None] + jnp.einsum("bhcd,bhce->bhde", k_i, v_new)
        return S, o_i

    S_fin, o = lax.scan(step, S0.astype(f32), (q_dec, k_dec, u, w, qk, g_last))
    o = jnp.moveaxis(o, (0, 2), (1, 3)).reshape(B, T, H, o.shape[-1])
    return o.astype(q.dtype), S_fin.astype(q.dtype)


def deltanet_branch(qkv, g_gate, beta_logit, a_logit, S0, conv_qkv, a_log, dt_bias, dn_norm_g):
    B, T, _ = qkv.shape
    qkv = jax.nn.silu(short_conv(qkv, conv_qkv))
    q, k, v = jnp.split(qkv, 3, axis=-1)
    q = (l2norm(q.reshape(B, T, H_A, DK)) * (DK ** -0.5)).astype(qkv.dtype)
    k = l2norm(k.reshape(B, T, H_A, DK)).astype(qkv.dtype)
    v = v.reshape(B, T, H_A, DV)
    beta = jax.nn.sigmoid(beta_logit.reshape(B, T, 2, H_A))
    g = -jnp.exp(a_log.astype(jnp.float32)) * jax.nn.softplus(
        a_logit.reshape(B, T, 2, H_A).astype(jnp.float32) + dt_bias.astype(jnp.float32))
    o_f, S_f = chunk_delta_rule(q, k, v, beta[:, :, 0], g[:, :, 0], S0[:, 0])
    o_b, S_b = chunk_delta_rule(q[:, ::-1], k[:, ::-1], v[:, ::-1], beta[:, ::-1, 1], g[:, ::-1, 1], S0[:, 1])
    o = rmsnorm(o_f + o_b[:, ::-1], dn_norm_g)
    y = o.reshape(B, T, W_A) * jax.nn.silu(g_gate)
    return y, jnp.stack([S_f, S_b], axis=1)


def sgu_branch(uv, g_gate, sgu_norm_g, sgu_w, sgu_b):
    B, T, _ = uv.shape
    N = T // SGU_CHUNK
    u, v = jnp.split(jax.nn.gelu(uv), 2, axis=-1)
    v = rmsnorm(v, sgu_norm_g).reshape(B, N, SGU_CHUNK, G_B, W_B // G_B)
    sv = jnp.einsum("gpq,bnqgc->bnpgc", sgu_w, v) + sgu_b.T[None, None, :, :, None]
    return u * sv.reshape(B, T, W_B) * jax.nn.silu(g_gate)


def pool_branch(x_c, g_gate, pool_w, pool_scale):
    B, T, _ = x_c.shape
    xg = x_c.reshape(B, T, len(POOL_WINDOWS), C_C)
    cs = jnp.concatenate([jnp.zeros((B, 1, len(POOL_WINDOWS), C_C), jnp.float32),
                          jnp.cumsum(xg.astype(jnp.float32), axis=1)], axis=1)
    t = jnp.arange(T)
    means = []
    for gi, wsize in enumerate(POOL_WINDOWS):
        lo = jnp.clip(t - wsize // 2, 0, T)
        hi = jnp.clip(t + wsize - wsize // 2, 0, T)
        cs_g = cs[:, :, gi]
        means.append((cs_g[:, hi] - cs_g[:, lo]) / (hi - lo).astype(jnp.float32)[None, :, None])
    pooled = (jnp.stack(means, axis=2) - xg.astype(jnp.float32)).astype(x_c.dtype)
    y = jnp.einsum("btgc,gcd->btgd", pooled, pool_w).reshape(B, T, W_C)
    return y * pool_scale * jax.nn.silu(g_gate)


def fourier_branch(x_d, g_gate, fourier_w):
    B, T, _ = x_d.shape
    xg = x_d.reshape(B, T, G_D, C_D).astype(jnp.float32)
    f = jnp.fft.fftn(xg, axes=(1, 3), norm="ortho").real.astype(x_d.dtype)
    y = jnp.einsum("btgc,gcd->btgd", f, fourier_w).reshape(B, T, W_D)
    return y * jax.nn.silu(g_gate)


def trunk_layer(x, cond, S0, ada_w, ada_b, norm_g, w_in, conv_qkv, a_log, dt_bias, dn_norm_g,
                sgu_norm_g, sgu_w, sgu_b, pool_w, pool_scale, fourier_w, w_out):
    mod = jax.nn.silu(cond) @ ada_w + ada_b
    shift, scale, gate = jnp.split(mod, 3, axis=-1)
    h = rmsnorm(x, norm_g) * (1 + scale[:, None]) + shift[:, None]
    p = h @ w_in
    qkv, g_a, beta_logit, a_logit, uv, g_b, x_c, g_c, x_d, g_d = jnp.split(p, _split_points(), axis=-1)
    y_a, S_new = deltanet_branch(qkv, g_a, beta_logit, a_logit, S0, conv_qkv, a_log, dt_bias, dn_norm_g)
    y_b = sgu_branch(uv, g_b, sgu_norm_g, sgu_w, sgu_b)
    y_c = pool_branch(x_c, g_c, pool_w, pool_scale)
    y_d = fourier_branch(x_d, g_d, fourier_w)
    out = jnp.concatenate([y_a, y_b, y_c, y_d], axis=-1) @ w_out
    return x + gate[:, None] * out, S_new


def setup_inputs(seed: int = 0) -> dict:
    key = jax.random.key(seed)
    ks = jax.random.split(key, 24)
    f32 = jnp.float32

    def nrm(k, shape, s):
        return s * jax.random.normal(k, shape, f32)

    dt = jnp.exp(jax.random.uniform(ks[10], (DEPTH, 2, H_A), f32, minval=math.log(1e-3), maxval=math.log(1e-1)))
    return {
        "x_prompt": nrm(ks[0], (BATCH, SEQ, D_MODEL), 1.0),
        "x_sample": nrm(ks[1], (DEC_BATCH, DEC_SEQ, D_MODEL), 1.0),
        "state_delta": nrm(ks[2], (DEC_BATCH, DEPTH, 2, H_A, DK, DV), 1.0),
        "c": nrm(ks[3], (DEC_BATCH, D_MODEL), 1.0),
        "c_ctx": nrm(ks[4], (D_MODEL,), 1.0),
        "ada_w": nrm(ks[5], (DEPTH, D_MODEL, 3 * D_MODEL), 0.5 * D_MODEL ** -0.5),
        "ada_b": nrm(ks[6], (DEPTH, 3 * D_MODEL), 0.02),
        "norm_g": 1.0 + nrm(ks[7], (DEPTH, D_MODEL), 0.02),
        "w_in": nrm(ks[8], (DEPTH, D_MODEL, P_IN), D_MODEL ** -0.5),
        "conv_qkv": nrm(ks[9], (DEPTH, CONV_W, 3 * W_A), CONV_W ** -0.5),
        "a_log": jnp.log(jax.random.uniform(ks[11], (DEPTH, 2, H_A), f32, minval=1.0, maxval=16.0)),
        "dt_bias": dt + jnp.log(-jnp.expm1(-dt)),
        "dn_norm_g": 1.0 + nrm(ks[12], (DEPTH, DV), 0.02),
        "sgu_norm_g": 1.0 + nrm(ks[13], (DEPTH, W_B), 0.02),
        "sgu_w": nrm(ks[14], (DEPTH, G_B, SGU_CHUNK, SGU_CHUNK), SGU_CHUNK ** -0.5),
        "sgu_b": 1.0 + nrm(ks[15], (DEPTH, G_B, SGU_CHUNK), 0.02),
        "pool_w": nrm(ks[16], (DEPTH, len(POOL_WINDOWS), C_C, C_C), C_C ** -0.5),
        "pool_scale": 1.0 + nrm(ks[17], (DEPTH, W_C), 0.02),
        "fourier_w": nrm(ks[18], (DEPTH, G_D, C_D, C_D), C_D ** -0.5),
        "w_out": nrm(ks[19], (DEPTH, MIX_W, D_MODEL), MIX_W ** -0.5),
        "final_norm_g": 1.0 + nrm(ks[20], (D_MODEL,), 0.02),
    }


def reference(x_prompt, x_sample, state_delta, c, c_ctx, ada_w, ada_b, norm_g, w_in, conv_qkv, a_log, dt_bias,
              dn_norm_g, sgu_norm_g, sgu_w, sgu_b, pool_w, pool_scale, fourier_w, w_out, final_norm_g):
    per_layer = [dict(ada_w=ada_w[l], ada_b=ada_b[l], norm_g=norm_g[l], w_in=w_in[l], conv_qkv=conv_qkv[l],
                      a_log=a_log[l], dt_bias=dt_bias[l], dn_norm_g=dn_norm_g[l], sgu_norm_g=sgu_norm_g[l],
                      sgu_w=sgu_w[l], sgu_b=sgu_b[l], pool_w=pool_w[l], pool_scale=pool_scale[l],
                      fourier_w=fourier_w[l], w_out=w_out[l]) for l in range(DEPTH)]

    bp = x_prompt.shape[0]
    cond_ctx = jnp.broadcast_to(c_ctx, (bp, D_MODEL))
    xp = x_prompt
    ctx_states = []
    for l in range(DEPTH):
        S0 = jnp.zeros((bp, 2, H_A, DK, DV), x_prompt.dtype)
        xp, S_l = trunk_layer(xp, cond_ctx, S0, **per_layer[l])
        ctx_states.append(S_l)
    y_prompt = rmsnorm(xp, final_norm_g)
    new_state_delta = jnp.stack(ctx_states, axis=1)

    xs = x_sample + grid_pos_embed(x_sample.shape[1], x_sample.dtype)[None]
    for l in range(DEPTH):
        xs, _ = trunk_layer(xs, c, state_delta[:, l], **per_layer[l])
    y_sample = rmsnorm(xs, final_norm_g)
    return (y_prompt, y_sample, new_state_delta)
```

```python
import contextlib
import numpy as np
import concourse.bass as bass
import concourse.mybir as mybir
from concourse.bass_utils import run_bass_kernel_spmd

F32 = mybir.dt.float32
BF16 = mybir.dt.bfloat16
AF = mybir.ActivationFunctionType
ALU = mybir.AluOpType
AX = mybir.AxisListType
ENGS = ("pe", "act", "dve", "pool", "sp")
EPS = 1e-6
NL = 4
TSAMP = 4096
TPR = 256
NTOK = TSAMP + 4 * TPR


PSUM_KEYS = {"PA", "PB", "PC", "PD", "PE_", "PF", "PG", "PT"}


class Prog:
    def __init__(self, nc):
        self.nc = nc
        self.ops = []
        self.last_w = {}
        self.readers = {}

    limit = None
    in_region = False
    region_count = 0

    rec = None

    def add(self, eng, fn, reads=(), writes=(), dma=0, sem_key=None):
        if self.rec is not None:
            self.rec.append((eng, fn, list(reads), list(writes), dma, sem_key))
            return
        if self.in_region and self.limit is not None:
            if self.region_count >= self.limit:
                return
            self.region_count += 1
        i = len(self.ops)
        deps = []
        for r in reads:
            lw = self.last_w.get(r)
            if lw is not None:
                deps.append((lw, "raw"))
            if r in PSUM_KEYS:
                for rd in self.readers.get(r, ()):
                    deps.append((rd, "war"))
        for w in writes:
            lw = self.last_w.get(w)
            if lw is not None:
                deps.append((lw, "waw"))
            for rd in self.readers.get(w, ()):
                deps.append((rd, "war"))
        for r in reads:
            self.readers.setdefault(r, []).append(i)
        for w in writes:
            self.last_w[w] = i
            self.readers[w] = []
        assert (not dma) or sem_key is not None
        self.ops.append(dict(eng=eng, fn=fn, deps=deps, dma=dma, sem_key=sem_key, signal=False))
        return i

    def build(self):
        nc = self.nc
        ops = self.ops
        cnt = {e: 0 for e in ENGS}
        for o in ops:
            o["eidx"] = cnt[o["eng"]]
            cnt[o["eng"]] += 1
        dma_cum = {}
        for o in ops:
            if o["dma"]:
                k = o["sem_key"]
                dma_cum[k] = dma_cum.get(k, 0) + 16 * o["dma"]
                o["dma_val"] = dma_cum[k]
        known = {e: {f: -1 for f in ENGS} for e in ENGS}
        known_dma = {e: {} for e in ENGS}
        for o in ops:
            e = o["eng"]
            w_eng = {}
            w_dma = {}
            for (d, kind) in o["deps"]:
                D = ops[d]
                if D["dma"]:
                    k = D["sem_key"]
                    if known_dma[e].get(k, 0) >= D["dma_val"]:
                        continue
                    w_dma[k] = max(w_dma.get(k, 0), D["dma_val"])
                else:
                    f = D["eng"]
                    if f == e:
                        if e == "pe" or e == "sp":
                            continue
                        if kind == "war" and e != "pool":
                            continue
                    if known[e][f] >= D["eidx"]:
                        continue
                    if f not in w_eng or ops[w_eng[f]]["eidx"] < D["eidx"]:
                        w_eng[f] = d
            o["w_eng"] = w_eng
            o["w_dma"] = w_dma
            for f, d in w_eng.items():
                ops[d]["signal"] = True
                known[e][f] = ops[d]["eidx"]
            for k, v in w_dma.items():
                known_dma[e][k] = v
        sig = {e: 0 for e in ENGS}
        for o in ops:
            if o["signal"] and not o["dma"]:
                sig[o["eng"]] += 1
                o["sig_val"] = sig[o["eng"]]
        with contextlib.ExitStack() as st:
            esem = {e: st.enter_context(nc.semaphore("s_" + e)) for e in ENGS}
            dsem = {}
            for k in dma_cum:
                dsem[k] = st.enter_context(nc.semaphore("d%d" % len(dsem)))
            block = st.enter_context(nc.Block())
            per = {e: [o for o in ops if o["eng"] == e] for e in ENGS}

            def emit(engh, lst):
                for o in lst:
                    for f, d in o["w_eng"].items():
                        engh.wait_ge(esem[f], ops[d]["sig_val"])
                    for k, v in o["w_dma"].items():
                        engh.wait_ge(dsem[k], v)
                    if o["fn"] is None:
                        continue
                    ins = o["fn"](engh)
                    if o["dma"]:
                        if not isinstance(ins, (list, tuple)):
                            ins = [ins]
                        assert len(ins) == o["dma"], (len(ins), o["dma"])
                        for x in ins:
                            x.then_inc(dsem[o["sem_key"]], 16)
                    elif o["signal"]:
                        ins.then_inc(esem[o["eng"]], 1)

            @block.sync
            def _(eng):
                emit(eng, per["sp"])

            @block.scalar
            def _(eng):
                emit(eng, per["act"])

            @block.vector
            def _(eng):
                emit(eng, per["dve"])

            @block.gpsimd
            def _(eng):
                emit(eng, per["pool"])

            @block.tensor
            def _(eng):
                emit(eng, per["pe"])


def host_consts():
    c = {}
    c["c_ident"] = np.eye(128, dtype=np.float32)
    r = np.arange(64)[:, None]
    cc = np.arange(64)[None, :]
    su = np.where(cc > r, -1.0, 0.0).astype(np.float32)
    sl = np.where(cc < r, -1.0, 0.0).astype(np.float32)
    mf = np.where(cc >= r, 0.0, -30000.0).astype(np.float32)
    mb = np.where(cc <= r, 0.0, -30000.0).astype(np.float32)
    i64 = np.eye(64, dtype=np.float32)
    t4 = lambda m: m
    blk = lambda b: (r // b == cc // b)
    md8 = blk(8).astype(np.float32)
    moff = [(blk(2 * b) & ~blk(b)).astype(np.float32) for b in (8, 16, 32)]
    c["c_m64"] = np.stack([su, sl, mf, mb, i64, md8] + moff, axis=1).astype(np.float32)
    tri = np.stack([(r <= cc), (r >= cc), np.ones((64, 64), bool)], axis=1).astype(np.float32)
    c["c_tri"] = tri
    T = TSAMP
    rows = T // 64
    rr = np.repeat(np.arange(rows, dtype=np.float32), 64)
    col = np.tile(np.arange(64, dtype=np.float32), rows)
    nf = 256
    freqs = np.power(np.float32(10000.0), -np.arange(nf, dtype=np.float32) / np.float32(nf)).astype(np.float32)
    ar = rr[:, None] * freqs[None]
    ac = col[:, None] * freqs[None]
    c["c_pos"] = np.concatenate([np.sin(ar), np.cos(ar), np.sin(ac), np.cos(ac)], axis=-1).astype(np.float32)
    band = np.zeros((128, 5, 4, 128), np.float32)
    for gi, w in enumerate((2, 4, 8, 16)):
        Tn = 384
        Fm = np.zeros((Tn, Tn), np.float64)
        for t in range(Tn):
            lo = min(max(t - w // 2, 0), Tn)
            hi = min(max(t + w - w // 2, 0), Tn)
            Fm[t, lo:hi] = 1.0 / (hi - lo)
        Fm -= np.eye(Tn)
        band[:, 0, gi, :] = Fm[0:128, 0:128].T
        band[:, 1, gi, :] = Fm[128:256, 128:256].T
        band[:, 2, gi, :] = Fm[256:384, 256:384].T
        band[:, 3, gi, :] = Fm[128:256, 0:128].T
        band[:, 4, gi, :] = Fm[128:256, 256:384].T
    c["c_band"] = band
    k64 = np.arange(64)
    ang = 2 * np.pi * np.outer(k64, k64) / 64.0
    C64 = np.cos(ang)
    S64 = np.sin(ang)
    pad = np.zeros((64, 2, 2, 128), np.float32)
    for half in range(2):
        pad[:, half, 0, half * 64:(half + 1) * 64] = C64
        pad[:, half, 1, half * 64:(half + 1) * 64] = S64
    c["c_dftpad"] = pad
    dA = np.zeros((64, 2, 3, 64), np.float32)
    for ai, A in enumerate((64, 4)):
        a = np.arange(A)
        an = 2 * np.pi * np.outer(a, a) / A
        dA[:A, ai, 0, :A] = np.cos(an)
        dA[:A, ai, 1, :A] = np.sin(an)
        dA[:A, ai, 2, :A] = -np.sin(an)
        Tt = 64 * A
        p = np.arange(A)[:, None, None]
        b = np.arange(64)[None, :, None]
        q = np.arange(64)[None, None, :]
        be = 2 * np.pi * ((b * (p + A * q)) % Tt) / Tt
        nrm = 1.0 / np.sqrt(64.0 * Tt)
        e2 = np.stack([np.cos(be) * nrm, -np.sin(be) * nrm], axis=2).astype(np.float32)
        c["c_e2_%d" % A] = e2
    c["c_dfta"] = dA
    return c


import os
DBG_REGION = os.environ.get("DBG_REGION", "")
DBG_LIMIT = os.environ.get("DBG_LIMIT")


def build_program(nl_run=NL, seq_sel=(0, 1, 2, 3, 4), skip=()):
    nc = bass.Bass("TRN2", target_bir_lowering=False)
    P = Prog(nc)
    P.limit = int(DBG_LIMIT) if DBG_LIMIT else None
    consts = host_consts()

    def din(name, shape):
        return nc.dram_tensor(name, list(shape), F32, kind="ExternalInput").ap()

    xs = din("xs", [TSAMP, 1024])
    xp = din("xp", [4 * TPR, 1024])
    sd = din("sd", [NL, 2, 4, 64, 64])
    ccd = din("cc", [2, 1024])
    ada_w = din("ada_w", [NL, 1024, 3072])
    ada_b = din("ada_b", [NL, 3072])
    norm_g = din("norm_g", [NL, 1024])
    w_in = din("w_in", [NL, 1024, 2832])
    conv_qkv = din("conv_qkv", [NL, 4, 768])
    a_log = din("a_log", [NL, 8])
    dt_bias = din("dt_bias", [NL, 8])
    dn_norm_g = din("dn_norm_g", [NL, 64])
    sgu_norm_g = din("sgu_norm_g", [NL, 256])
    sgu_w = din("sgu_w", [NL, 4, 128, 128])
    sgu_b = din("sgu_b", [NL, 4, 128])
    pool_w = din("pool_w", [NL, 4, 64, 64])
    pool_scale = din("pool_scale", [NL, 256])
    fourier_w = din("fourier_w", [NL, 4, 64, 64])
    w_out = din("w_out", [NL, 1024, 1024])
    final_norm_g = din("final_norm_g", [1024])
    cd = {k: din(k, v.shape) for k, v in consts.items()}

    ys = nc.dram_tensor("ys", [TSAMP, 1024], F32, kind="ExternalOutput").ap()
    yp = nc.dram_tensor("yp", [4 * TPR, 1024], F32, kind="ExternalOutput").ap()
    nsd = nc.dram_tensor("ns", [4, NL, 2, 4, 64, 64], F32, kind="ExternalOutput").ap()

    XS = nc.dram_tensor("XS", [NTOK, 1024], F32, kind="Internal").ap()
    MIXD = nc.dram_tensor("MIXD", [TSAMP, 1024], F32, kind="Internal").ap()
    QKVD = nc.dram_tensor("QKVD", [TSAMP, 768], BF16, kind="Internal").ap()
    OFD = nc.dram_tensor("OFD", [2, TSAMP, 256], F32, kind="Internal").ap()
    ZD = nc.dram_tensor("ZD", [2, TSAMP, 256], F32, kind="Internal").ap()
    VD = nc.dram_tensor("VD", [2, 64, 64, 256], F32, kind="Internal").ap()

    st = contextlib.ExitStack()
    with st:
        def sb(name, shape, dt=F32):
            return st.enter_context(nc.sbuf_tensor(name, list(shape), dt))

        def ps(name, shape, dt=F32):
            return st.enter_context(nc.psum_tensor(name, list(shape), dt))

        HT = sb("HT", [128, 8, TSAMP], BF16)
        WG = sb("WG", [128, 8, 1040], BF16)
        AW = sb("AW", [128, 8, 256], BF16)
        GATE = sb("GATE", [128, 2, 1024])
        XT = sb("XT", [128, 1024])
        XT2 = sb("XT2", [128, 1024])
        XN = sb("XN", [128, 1024], BF16)
        MXT = sb("MXT", [128, 8, 128], BF16)
        JNK = MXT[:].rearrange("p k t -> p (k t)")
        IDN = sb("IDN", [128, 128])
        IDB = sb("IDB", [128, 128], BF16)
        M64 = sb("M64", [128, 9, 64])
        TRI = sb("TRI", [128, 3, 64])
        BAND = sb("BAND", [128, 5, 4, 128])
        DFTPAD = sb("DFTPAD", [64, 2, 2, 128])
        DFTA = sb("DFTA", [64, 2, 3, 64])
        SCB = sb("SCB", [128, 8, 2, 128], BF16)
        TMPL = sb("TMPL", [128, 128])
        CCT = sb("CCT", [128, 16])
        SCF = sb("SCF", [128, 16])
        SCb = sb("SCb", [128, 8, 2], BF16)
        NG = sb("NG", [128, 8])
        AB = sb("AB", [128, 24])
        ABG = sb("ABG", [128, 1024])
        MODT = sb("MODT", [128, 24, 2])
        AM = sb("AM", [128, 8, 2])
        CW = sb("CW", [128, 24])
        ALB = sb("ALB", [128, 8])
        DTB = sb("DTB", [128, 8])
        NEGA = sb("NEGA", [128, 8])
        DNG = sb("DNG", [128, 256])
        SGNG = sb("SGNG", [128, 256])
        SWT = sb("SWT", [128, 4, 128])
        SGBT = sb("SGBT", [128, 4])
        PWB = sb("PWB", [128, 2, 128])
        PSC = sb("PSC", [128, 256])
        FW = sb("FW", [64, 4, 64])
        PCS = sb("PCS", [128, 2, 2, 128])
        ST = sb("ST", [128, 8])
        TL = 128
        RAW = sb("RAW", [128, 6, TL + 3])
        CS = sb("CS", [128, 6, TL])
        QKVF = sb("QKVF", [128, 768])
        QKV3 = sb("QKV3", [128, 768], BF16)
        SCL = sb("SCL", [128, 32])
        GCGL = sb("GCGL", [128, 8])
        EX = sb("EX", [128, 16])
        KS = sb("KS", [128, 256], BF16)
        RH = sb("RH", [128, 4, 128], BF16)
        KDEC = sb("KDEC", [128, 256], BF16)
        QDEC = sb("QDEC", [128, 256], BF16)
        TR = sb("TR", [128, 1024], BF16)
        SQ = sb("SQ", [128, 512])
        XA = sb("XA", [128, 256], BF16)
        XB = sb("XB", [128, 256], BF16)
        COF = sb("COF", [128, 256], BF16)
        SSb = sb("SSb", [128, 256], BF16)
        DG = sb("DG", [128, 256])
        NDG = sb("NDG", [128, 256])
        DTT = sb("DTT", [128, 256])
        Cb = [sb("C_a", [128, 256], BF16), sb("C_b", [128, 256], BF16)]
        Bb = [sb("B_a", [128, 256], BF16), sb("B_b", [128, 256], BF16)]
        Tb = [sb("T_a", [128, 256], BF16), sb("T_b", [128, 256], BF16)]
        QKT = sb("QKT", [128, 256], BF16)
        MTT = sb("MTT", [128, 256], BF16)
        UU = sb("UU", [128, 256])
        WB_ = sb("WB_", [128, 256], BF16)
        WT = sb("WT", [128, 256], BF16)
        VN = sb("VN", [128, 256], BF16)
        OO = sb("OO", [128, 256])
        SS = sb("SS", [128, 256])
        STMP = sb("STMP", [128, 256])
        GA = STMP
        RN = sb("RN", [128, 16])
        UVG = sb("UVG", [128, 512])
        VNS = sb("VNS", [128, 256])
        SGB = sb("SGB", [128, 256])
        YB = sb("YB", [128, 256])
        XCT = sb("XCT", [128, 2, 128])
        ZR = [sb("ZR%d" % i, [128, 256]) for i in range(3)]
        ZCS = UVG
        ZA = sb("ZA", [64, 2, 512])
        VV = sb("VV", [64, 2, 512])
        VB = sb("VB", [64, 2, 256])
        E2 = sb("E2", [64, 2, 64])
        YD = YB[0:64, :]
        PA = ps("PA", [128, 512])
        PB = ps("PB", [128, 512])
        PC = ps("PC", [128, 512])
        PD = ps("PD", [128, 512])
        PE_ = ps("PE_", [128, 512])
        PF = ps("PF", [128, 512])
        PG = ps("PG", [128, 512])
        PT = ps("PT", [128, 512])
        PTb = PT[:].bitcast(BF16)

        def ld(out, in_, r, w, q="sp", key=None, n=1, nc_ok=False):
            kw = dict(allow_slow_non_contiguous=True) if nc_ok else {}
            P.add(q, lambda e: e.dma_start(out=out, in_=in_, **kw), reads=r, writes=w, dma=1, sem_key=key or ("ld", w[0]))

        def stq(out, in_, r, w, key):
            P.add("sp", lambda e: e.dma_start(out=out, in_=in_), reads=r, writes=w, dma=1, sem_key=key)

        def mm(out, lhsT, rhs, r, w, start=True, stop=True):
            P.add("pe", lambda e: e.matmul(out, lhsT=lhsT, rhs=rhs, start=start, stop=stop), reads=r, writes=w)

        def tr(out, in_, ident, r, w):
            P.add("pe", lambda e: e.transpose(out=out, in_=in_, identity=ident), reads=r, writes=w)

        def act(out, in_, func, r, w, bias=None, scale=None, accum=None):
            kw = {}
            if bias is not None:
                kw["bias"] = bias
            if scale is not None:
                kw["scale"] = scale
            if accum is not None:
                kw["accum_out"] = accum
            P.add("act", lambda e: e.activation(out=out, in_=in_, func=func, **kw), reads=r, writes=w)

        def tt(out, in0, in1, op, r, w, eng="dve"):
            P.add(eng, lambda e: e.tensor_tensor(out=out, in0=in0, in1=in1, op=op), reads=r, writes=w)

        def ts(out, in0, s1, s2, op0, op1, r, w, eng="dve"):
            if op1 is None:
                P.add(eng, lambda e: e.tensor_scalar(out=out, in0=in0, scalar1=s1, scalar2=None, op0=op0), reads=r, writes=w)
            else:
                P.add(eng, lambda e: e.tensor_scalar(out=out, in0=in0, scalar1=s1, scalar2=s2, op0=op0, op1=op1), reads=r, writes=w)

        def stt(out, in0, scalar, in1, op0, op1, r, w):
            P.add("dve", lambda e: e.scalar_tensor_tensor(out=out, in0=in0, scalar=scalar, in1=in1, op0=op0, op1=op1), reads=r, writes=w)

        def cp(out, in_, r, w, eng="dve"):
            if eng == "act":
                P.add(eng, lambda e: e.activation(out=out, in_=in_, func=AF.Identity), reads=r, writes=w)
            else:
                P.add(eng, lambda e: e.tensor_copy(out=out, in_=in_), reads=r, writes=w)

        def recip(out, in_, r, w):
            P.add("dve", lambda e: e.reciprocal(out=out, in_=in_), reads=r, writes=w)

        def rstd_from_ssq(ssq_ap, out_ap, n, key):
            ts(ssq_ap, ssq_ap, 1.0 / n, EPS, ALU.mult, ALU.add, [key], [key])
            act(ssq_ap, ssq_ap, AF.Sqrt, [key], [key])
            recip(out_ap, ssq_ap, [key], [key])

        def load_T(dst_ap, src_ap, n, dkey):
            ld(TMPL[0:n, :], src_ap, [], ["TMPL"])
            tr(PG[:, 0:n], TMPL[0:n, :], IDN[0:n, 0:n], ["TMPL", "IDN"], ["PG"])
            cp(dst_ap, PG[:, 0:n], ["PG"], [dkey])

        def bc3(ap, shape):
            return ap.unsqueeze(2).to_broadcast(shape)

        ld(IDN[:], cd["c_ident"], [], ["IDN"])
        cp(IDB[:], IDN[:], ["IDN"], ["IDB"])
        for hf in range(2):
            ld(M64[hf * 64:hf * 64 + 64], cd["c_m64"], [], ["M64"])
            ld(TRI[hf * 64:hf * 64 + 64], cd["c_tri"], [], ["TRI"])
        ld(BAND[:], cd["c_band"], [], ["BAND"])
        ld(DFTPAD[:], cd["c_dftpad"], [], ["DFTPAD"])
        ld(DFTA[:], cd["c_dfta"], [], ["DFTA"])
        def masks(hf):
            psl = slice(hf * 64, hf * 64 + 64)
            bh = lambda i: M64[psl, i, :].unsqueeze(1).to_broadcast([64, 4, 64])
            return dict(SUL=(bh(0), bh(1)), MINC=(bh(2), bh(3)), ID4=bh(4), MD8=bh(5), MOFF=[bh(6), bh(7), bh(8)])
        MSK = [masks(0), masks(1)]

        load_T(CCT[:], ccd.rearrange("c (k p) -> (c k) p", p=128), 16, "CCT")
        act(SCF[:], CCT[:], AF.Silu, ["CCT"], ["SCF"])
        cp(SCb[:].rearrange("p k c -> p c k"), SCF[:].rearrange("p (c k) -> p c k", c=2), ["SCF"], ["SCb"])
        cp(SCB[:], SCb[:].unsqueeze(3).to_broadcast([128, 8, 2, 128]), ["SCb"], ["SCB"])

        for t in (range(TSAMP // 128) if 0 in seq_sel else []):
            ld(XT[:], xs[t * 128:(t + 1) * 128, :], [], ["XT"])
            ld(XT2[:], cd["c_pos"][t * 128:(t + 1) * 128, :], [], ["XT2"])
            tt(XT[:], XT[:], XT2[:], ALU.add, ["XT", "XT2"], ["XT"])
            stq(XS[t * 128:(t + 1) * 128, :], XT[:], ["XT"], [("XS", t)], ("st", "XT"))
        for t in range(8):
            if (1 + t // 2) in seq_sel:
                ld(XT[:], xp[t * 128:(t + 1) * 128, :], [], ["XT"])
                stq(XS[TSAMP + t * 128:TSAMP + (t + 1) * 128, :], XT[:], ["XT"], [("XS", 32 + t)], ("st", "XT"))

        seqs = [(0, TSAMP, 0, None)] + [(TSAMP + i * TPR, TPR, 1, i) for i in range(4)]
        seqs = [seqs[i] for i in seq_sel]

        for l in range(nl_run):
            load_T(NG[:], norm_g[l].rearrange("(k p) -> k p", p=128), 8, "NG")
            load_T(AB[:], ada_b[l].rearrange("(k p) -> k p", p=128), 24, "AB")
            load_T(CW[:], conv_qkv[l].rearrange("j (c p) -> (j c) p", p=128), 24, "CW")
            ld(ABG[:], ada_b[l, 2048:3072].partition_broadcast(128), [], ["ABG"])
            ld(ALB[:], a_log[l].partition_broadcast(128), [], ["ALB"])
            ld(DTB[:], dt_bias[l].partition_broadcast(128), [], ["DTB"])
            act(NEGA[:], ALB[:], AF.Exp, ["ALB"], ["NEGA"])
            ts(NEGA[:], NEGA[:], -1.0, None, ALU.mult, None, ["NEGA"], ["NEGA"])
            for h in range(4):
                ld(DNG[:, h * 64:(h + 1) * 64], dn_norm_g[l].partition_broadcast(128), [], ["DNG"], key=("ld", "DNG"))
            ld(SGNG[:], sgu_norm_g[l].partition_broadcast(128), [], ["SGNG"])
            ld(PSC[:], pool_scale[l].partition_broadcast(128), [], ["PSC"])
            load_T(SGBT[:], sgu_b[l], 4, "SGBT")
            for g in range(4):
                ld(TMPL[:], sgu_w[l, g], [], ["TMPL"])
                tr(PG[:, 0:128], TMPL[:], IDN[:], ["TMPL", "IDN"], ["PG"])
                cp(SWT[:, g, :], PG[:, 0:128], ["PG"], ["SWT"])
            P.add("pool", lambda e: e.memset(PWB[:], 0.0), reads=[], writes=["PWB"])
            for g in range(4):
                hb = (g % 2) * 64
                ld(PWB[hb:hb + 64, g // 2, hb:hb + 64], pool_w[l, g], [], ["PWB"], key=("ld", "PWB"))
            for c in range(2):
                tt(PWB[:, c, :], PWB[:, c, :], PSC[:, c * 128:(c + 1) * 128], ALU.mult, ["PWB", "PSC"], ["PWB"])
            ld(FW[:], fourier_w[l].rearrange("g c d -> c g d"), [], ["FW"])
            for g in range(4):
                for cs in range(2):
                    mm(PG[:, 0:64], DFTPAD[:, g % 2, cs, :], FW[:, g, :], ["DFTPAD", "FW"], ["PG"])
                    cp(PCS[:, cs, g // 2, (g % 2) * 64:(g % 2) * 64 + 64], PG[:, 0:64], ["PG"], ["PCS"])
            wo_v = w_out[l].rearrange("(k p) n -> p k n", p=128)
            aw_v = ada_w[l].rearrange("(k p) n -> p k n", p=128)
            for n in range(12):
                P.add("pool", lambda e, n=n, aw_v=aw_v: e.dma_start(out=AW[:], in_=aw_v[:, :, n * 256:(n + 1) * 256]),
                      reads=[], writes=["AW"], dma=1, sem_key=("ld", "AW"))
                for c in range(2):
                    for k in range(8):
                        mm(PG[:, 0:2], AW[:, k, c * 128:(c + 1) * 128], SCb[:, k, :], ["AW", "SCb"], ["PG"], start=(k == 0), stop=(k == 7))
                    ts(MODT[:, n * 2 + c, :], PG[:, 0:2], AB[:, n * 2 + c:n * 2 + c + 1], None, ALU.add, None, ["PG", "AB"], ["MODT"])
                if n >= 8:
                    for cond in range(2):
                        for k in range(8):
                            mm(PA[:, 0:256], SCB[:, k, cond, :], AW[:, k, :], ["SCB", "AW"], ["PA"], start=(k == 0), stop=(k == 7))
                        tt(GATE[:, cond, (n - 8) * 256:(n - 7) * 256], PA[:, 0:256], ABG[:, (n - 8) * 256:(n - 7) * 256], ALU.add,
                           ["PA", "ABG"], ["GATE"])
            ts(AM[:], MODT[:, 8:16, :], 1.0, None, ALU.add, None, ["MODT"], ["AM"])
            tt(AM[:], AM[:], bc3(NG[:], [128, 8, 2]), ALU.mult, ["AM", "NG"], ["AM"])

            win_v = w_in[l].rearrange("(k p) n -> p k n", p=128)

            def load_wg(c0, ncol):
                P.add("pool", lambda e, wv=win_v: e.dma_start(out=WG[:, :, 0:ncol], in_=wv[:, :, c0:c0 + ncol]),
                      reads=[], writes=["WG"], dma=1, sem_key=("ld", "WG"))

            for (tok0, T, cond, pidx) in seqs:
                ntile = T // 128
                for t in range(ntile):
                    g0 = tok0 // 128 + t
                    ld(XT[:], XS[g0 * 128:(g0 + 1) * 128, :], [("XS", g0)], ["XT"])
                    act(JNK, XT[:], AF.Square, ["XT"], ["MXT", "ST"], accum=ST[:, 0:1])
                    rstd_from_ssq(ST[:, 0:1], ST[:, 2:3], 1024.0, "ST")
                    ts(XN[:], XT[:], ST[:, 2:3], None, ALU.mult, None, ["XT", "ST"], ["XN"])
                    for k in range(8):
                        tr(PTb[:, k * 128:(k + 1) * 128], XN[:, k * 128:(k + 1) * 128], IDB[:], ["XN", "IDB"], ["PT"])
                    for k in range(8):
                        act(HT[:, k, t * 128:(t + 1) * 128], PTb[:, k * 128:(k + 1) * 128], AF.Identity, ["PT", "AM", "MODT"],
                            [("HT", t)], bias=MODT[:, k, cond:cond + 1], scale=AM[:, k, cond:cond + 1])
                HTall = [("HT", t) for t in range(ntile)]

                def proj_tm(out_ps, tok_sl, c0, ncol, M, okey):
                    for k in range(8):
                        mm(out_ps, HT[:, k, tok_sl], WG[:, k, c0:c0 + ncol], HTall + ["WG"], [okey], start=(k == 0), stop=(k == 7))

                def proj_fm(out_ps, tok_sl, c0, ncol, okey, start=True):
                    for k in range(8):
                        mm(out_ps, WG[:, k, c0:c0 + ncol], HT[:, k, tok_sl], HTall + ["WG"], [okey], start=(k == 0), stop=(k == 7))

                load_wg(0, 1040)
                nch = T // 64
                P.in_region = ("dn" == DBG_REGION)
                dn_on = "dn" not in skip
                for tl in (range(T // TL) if dn_on else []):
                    s0 = tl * TL
                    for c in range(6):
                        proj_fm(PA[:, 0:TL], slice(s0, s0 + TL), c * 128, 128, "PA")
                        cp(RAW[:, c, 2:2 + TL], PA[:, 0:TL], ["PA"], ["RAW"], eng="act")
                        if s0 > 0:
                            proj_fm(PB[:, 0:2], slice(s0 - 2, s0), c * 128, 128, "PB")
                            cp(RAW[:, c, 0:2], PB[:, 0:2], ["PB"], ["RAW"])
                        else:
                            P.add("pool", lambda e, c=c: e.memset(RAW[:, c, 0:2], 0.0), reads=[], writes=["RAW"])
                        if s0 + TL < T:
                            proj_fm(PB[:, 2:4], slice(s0 + TL, s0 + TL + 2), c * 128, 128, "PB")
                            cp(RAW[:, c, 2 + TL:3 + TL], PB[:, 2:3], ["PB"], ["RAW"])
                        else:
                            P.add("pool", lambda e, c=c: e.memset(RAW[:, c, 2 + TL:3 + TL], 0.0), reads=[], writes=["RAW"])
                        ts(CS[:, c, :], RAW[:, c, 0:TL], CW[:, 0 * 6 + c:0 * 6 + c + 1], None, ALU.mult, None, ["RAW", "CW"], ["CS"])
                        for j in range(1, 4):
                            stt(CS[:, c, :], RAW[:, c, j:j + TL], CW[:, j * 6 + c:j * 6 + c + 1], CS[:, c, :], ALU.mult, ALU.add,
                                ["RAW", "CW", "CS"], ["CS"])
                        act(CS[:, c, :], CS[:, c, :], AF.Silu, ["CS"], ["CS"])
                    for c in range(6):
                        dst = (PC if c < 4 else PD)
                        off = (c % 4) * 128
                        tr(dst[:, off:off + 128], CS[:, c, :], IDN[:], ["CS", "IDN"], ["PC" if c < 4 else "PD"])
                    cp(QKVF[:, 0:512], PC[:, :], ["PC"], ["QKVF"])
                    cp(QKVF[:, 512:768], PD[:, 0:256], ["PD"], ["QKVF"], eng="act")
                    tt(SQ[:], QKVF[:, 0:512], QKVF[:, 0:512], ALU.mult, ["QKVF"], ["SQ"])
                    P.add("dve", lambda e: e.tensor_reduce(out=RN[:, 0:8], in_=SQ[:].rearrange("p (h c) -> p h c", h=8), axis=AX.X, op=ALU.add),
                          reads=["SQ"], writes=["RN"])
                    ts(RN[:, 0:8], RN[:, 0:8], EPS, None, ALU.add, None, ["RN"], ["RN"])
                    act(RN[:, 0:8], RN[:, 0:8], AF.Sqrt, ["RN"], ["RN"])
                    recip(RN[:, 8:16], RN[:, 0:8], ["RN"], ["RN"])
                    ts(RN[:, 8:12], RN[:, 8:12], 0.125, None, ALU.mult, None, ["RN"], ["RN"])
                    tt(QKV3[:, 0:512].rearrange("p (h c) -> p h c", h=8), QKVF[:, 0:512].rearrange("p (h c) -> p h c", h=8),
                       bc3(RN[:, 8:16], [128, 8, 64]), ALU.mult, ["QKVF", "RN"], ["QKV3_0", "QKV3_1"])
                    cp(QKV3[:, 512:768], QKVF[:, 512:768], ["QKVF"], ["QKV3_0", "QKV3_1"], eng="pool")
                    stq(QKVD[s0:s0 + 128, :], QKV3[:], ["QKV3_0", "QKV3_1"], [("QKVD", tl)], ("st", "QKV3"))

                def dn_unit(ch, d):
                    hf = d
                    psl = slice(hf * 64, hf * 64 + 64)
                    sx = "_%d" % hf
                    K_ = lambda *n: [x + sx for x in n]
                    b1, b2, b3, b4 = (PC, PD, PF, PG) if hf == 0 else (PA, PB, PE_, PT)
                    k1, k2, k3, k4 = ("PC", "PD", "PF", "PG") if hf == 0 else ("PA", "PB", "PE_", "PT")
                    M = MSK[hf]
                    v4 = lambda ap: ap.rearrange("p (h c) -> p h c", h=4)
                    sh = [64, 4, 64]
                    ld(QKV3[psl, :], QKVD[ch * 64:ch * 64 + 64, :], [("QKVD", ch // 2)], K_("QKV3"), key=("ld", "QKV3" + sx))
                    Q = QKV3[psl, 0:256]
                    K = QKV3[psl, 256:512]
                    V = QKV3[psl, 512:768]
                    proj_tm(b4[psl, 256:272], slice(ch * 64, ch * 64 + 64), 1024, 16, 64, k4)
                    act(SCL[psl, 0:4], b4[psl, 256 + d * 4:256 + d * 4 + 4], AF.Sigmoid, [k4], K_("SCL"))
                    act(SCL[psl, 4:8], SCL[psl, 0:4], AF.Sqrt, K_("SCL"), K_("SCL"))
                    tt(SCL[psl, 12:16], b4[psl, 264 + d * 4:268 + d * 4], DTB[psl, d * 4:d * 4 + 4], ALU.add, [k4, "DTB"], K_("SCL"))
                    act(SCL[psl, 12:16], SCL[psl, 12:16], AF.Exp, K_("SCL"), K_("SCL"))
                    act(SCL[psl, 12:16], SCL[psl, 12:16], AF.Ln, K_("SCL"), K_("SCL"), bias=1.0)
                    tt(SCL[psl, 8:12], SCL[psl, 12:16], NEGA[psl, d * 4:d * 4 + 4], ALU.mult, K_("SCL") + ["NEGA"], K_("SCL"))
                    mm(b4[psl, 272:276], TRI[psl, d, :], SCL[psl, 8:12], ["TRI"] + K_("SCL"), [k4])
                    mm(b4[psl, 276:280], TRI[psl, 2, :], SCL[psl, 8:12], ["TRI"] + K_("SCL"), [k4])
                    cp(GCGL[psl, :], b4[psl, 272:280], [k4], K_("GCGL"))
                    act(EX[psl, 0:4], GCGL[psl, 0:4], AF.Exp, K_("GCGL"), K_("EX"))
                    tt(EX[psl, 12:16], GCGL[psl, 4:8], GCGL[psl, 0:4], ALU.subtract, K_("GCGL"), K_("EX"))
                    act(EX[psl, 4:8], EX[psl, 12:16], AF.Exp, K_("EX"), K_("EX"))
                    act(EX[psl, 8:12], GCGL[psl, 4:8], AF.Exp, K_("GCGL"), K_("EX"))
                    tt(SCL[psl, 16:20], SCL[psl, 4:8], EX[psl, 0:4], ALU.mult, K_("SCL", "EX"), K_("SCL"))
                    tt(v4(KS[psl, :]), v4(K), bc3(SCL[psl, 4:8], sh), ALU.mult, K_("QKV3", "SCL"), K_("KS"))
                    tt(RH[psl, :, 0:64], v4(V), bc3(SCL[psl, 4:8], sh), ALU.mult, K_("QKV3", "SCL"), K_("RH"), eng="pool")
                    tt(RH[psl, :, 64:128], v4(K), bc3(SCL[psl, 16:20], sh), ALU.mult, K_("QKV3", "SCL"), K_("RH"))
                    tt(v4(KDEC[psl, :]), v4(K), bc3(EX[psl, 4:8], sh), ALU.mult, K_("QKV3", "EX"), K_("KDEC"), eng="pool")
                    tt(v4(QDEC[psl, :]), v4(Q), bc3(EX[psl, 0:4], sh), ALU.mult, K_("QKV3", "EX"), K_("QDEC"))
                    I64 = IDB[psl, hf * 64:hf * 64 + 64]
                    b1b = b1[psl, :].bitcast(BF16)
                    for h in range(4):
                        hs = slice(h * 64, h * 64 + 64)
                        tr(b1b[:, h * 64:h * 64 + 64], KS[psl, hs], I64, K_("KS") + ["IDB"], [k1])
                        tr(b1b[:, 256 + h * 64:320 + h * 64], K[:, hs], I64, K_("QKV3") + ["IDB"], [k1])
                        tr(b1b[:, 512 + h * 64:576 + h * 64], Q[:, hs], I64, K_("QKV3") + ["IDB"], [k1])
                        tr(b1b[:, 768 + h * 64:832 + h * 64], QDEC[psl, hs], I64, K_("QDEC") + ["IDB"], [k1])
                    cp(TR[psl, :], b1b, [k1], K_("TR"), eng="act")
                    KST = lambda h: TR[psl, h * 64:h * 64 + 64]
                    KT = lambda h: TR[psl, 256 + h * 64:320 + h * 64]
                    QT = lambda h: TR[psl, 512 + h * 64:576 + h * 64]
                    QDT = lambda h: TR[psl, 768 + h * 64:832 + h * 64]
                    for h in range(4):
                        mm(b3[psl, h * 64:h * 64 + 64], KST(h), KST(h), K_("TR"), [k3])
                        mm(b3[psl, 256 + h * 64:320 + h * 64], KT(h), QT(h), K_("TR"), [k3])
                    tt(v4(DG[psl, :]), M["ID4"], bc3(GCGL[psl, 0:4], sh), ALU.mult, ["M64"] + K_("GCGL"), K_("DG"))
                    ts(NDG[psl, :], DG[psl, :], -1.0, None, ALU.mult, None, K_("DG"), K_("NDG"), eng="pool")
                    for h in range(4):
                        hs = slice(h * 64, h * 64 + 64)
                        mm(b4[psl, hs], TRI[psl, 2, :], DG[psl, hs], ["TRI"] + K_("DG"), [k4], start=True, stop=False)
                        mm(b4[psl, hs], NDG[psl, hs], TRI[psl, 2, :], ["TRI"] + K_("NDG"), [k4], start=False, stop=True)
                    tt(v4(DTT[psl, :]), v4(b4[psl, 0:256]), M["MINC"][d], ALU.add, [k4, "M64"], K_("DTT"))
                    act(DTT[psl, :], DTT[psl, :], AF.Exp, K_("DTT"), K_("DTT"))
                    C0_, B0_ = Cb[0], Bb[0]
                    CD, BD = Cb[1], Bb[1]
                    TT_, TN_ = Tb[0], Tb[1]
                    tt(v4(C0_[psl, :]), v4(b3[psl, 0:256]), M["SUL"][d], ALU.mult, [k3, "M64"], K_("C_a"))
                    tt(v4(B0_[psl, :]), v4(b3[psl, 0:256]), M["SUL"][1 - d], ALU.mult, [k3, "M64"], K_("B_a"))
                    tt(QKT[psl, :], b3[psl, 256:512], DTT[psl, :], ALU.mult, [k3] + K_("DTT"), K_("QKT"))
                    tt(v4(CD[psl, :]), v4(C0_[psl, :]), M["MD8"], ALU.mult, K_("C_a") + ["M64"], K_("C_b"))
                    tt(v4(BD[psl, :]), v4(B0_[psl, :]), M["MD8"], ALU.mult, K_("B_a") + ["M64"], K_("B_b"), eng="pool")
                    tt(v4(TT_[psl, :]), v4(CD[psl, :]), M["ID4"], ALU.add, K_("C_b") + ["M64"], K_("T_a"))
                    tt(v4(TN_[psl, :]), v4(BD[psl, :]), M["ID4"], ALU.add, K_("B_b") + ["M64"], K_("T_b"), eng="pool")

                    def grp(dst, dk, off, lt, lk, rt, rk):
                        for h in range(4):
                            hs = slice(h * 64, h * 64 + 64)
                            mm(dst[psl, off + h * 64:off + h * 64 + 64], lt[psl, hs], rt[psl, hs], K_(lk, rk), [dk])

                    for lev in range(2):
                        grp(b1, k1, 0, CD, "C_b", BD, "B_b")
                        grp(b1, k1, 256, BD, "B_b", CD, "C_b")
                        cp(BD[psl, :], b1[psl, 0:256], [k1], K_("B_b"))
                        cp(CD[psl, :], b1[psl, 256:512], [k1], K_("C_b"), eng="act")
                        grp(b2, k2, 0, BD, "B_b", TT_, "T_a")
                        grp(b2, k2, 256, CD, "C_b", TN_, "T_b")
                        tt(TT_[psl, :], TT_[psl, :], b2[psl, 0:256], ALU.add, K_("T_a") + [k2], K_("T_a"))
                        tt(TN_[psl, :], TN_[psl, :], b2[psl, 256:512], ALU.add, K_("T_b") + [k2], K_("T_b"))
                    BOF = VN
                    for li in range(3):
                        last = (li == 2)
                        tt(v4(BOF[psl, :]), v4(B0_[psl, :]), M["MOFF"][li], ALU.mult, K_("B_a") + ["M64"], K_("VN"))
                        grp(b1, k1, 0, BOF, "VN", TT_, "T_a")
                        cp(XA[psl, :], b1[psl, 0:256], [k1], K_("XA"))
                        if not last:
                            tt(v4(COF[psl, :]), v4(C0_[psl, :]), M["MOFF"][li], ALU.mult, K_("C_a") + ["M64"], K_("COF"), eng="pool")
                            grp(b1, k1, 256, COF, "COF", TN_, "T_b")
                            cp(XB[psl, :], b1[psl, 256:512], [k1], K_("XB"), eng="act")
                        grp(b2, k2, 0, TN_, "T_b", XA, "XA")
                        if not last:
                            grp(b2, k2, 256, TT_, "T_a", XB, "XB")
                        tt(TT_[psl, :], TT_[psl, :], b2[psl, 0:256], ALU.add, K_("T_a") + [k2], K_("T_a"))
                        if not last:
                            tt(TN_[psl, :], TN_[psl, :], b2[psl, 256:512], ALU.add, K_("T_b") + [k2], K_("T_b"))
                    tt(MTT[psl, :], TT_[psl, :], DTT[psl, :], ALU.mult, K_("T_a", "DTT"), K_("MTT"))
                    for h in range(4):
                        mm(b3[psl, h * 128:(h + 1) * 128], MTT[psl, h * 64:h * 64 + 64], RH[psl, h, :], K_("MTT", "RH"), [k3])
                    b3v = b3[psl, :].rearrange("p (h c) -> p h c", h=4)
                    tt(v4(UU[psl, :]), b3v[:, :, 0:64], bc3(SCL[psl, 4:8], sh), ALU.mult, [k3] + K_("SCL"), K_("UU"))
                    tt(v4(WB_[psl, :]), b3v[:, :, 64:128], bc3(SCL[psl, 4:8], sh), ALU.mult, [k3] + K_("SCL"), K_("WB_"))
                    b4b = b4[psl, 0:128].bitcast(BF16)
                    for h in range(4):
                        tr(b4b[:, h * 64:h * 64 + 64], WB_[psl, h * 64:h * 64 + 64], I64, K_("WB_") + ["IDB"], [k4])
                    cp(WT[psl, :], b4b, [k4], K_("WT"), eng="act")
                    for h in range(4):
                        hs = slice(h * 64, h * 64 + 64)
                        mm(b1[psl, hs], WT[psl, hs], SSb[psl, hs], K_("WT", "SSb"), [k1])
                    tt(VN[psl, :], UU[psl, :], b1[psl, 0:256], ALU.subtract, K_("UU") + [k1], K_("VN"))
                    for h in range(4):
                        hs = slice(h * 64, h * 64 + 64)
                        mm(b2[psl, hs], QDT(h), SSb[psl, hs], K_("TR", "SSb"), [k2], start=True, stop=False)
                        mm(b2[psl, hs], QKT[psl, hs], VN[psl, hs], K_("QKT", "VN"), [k2], start=False, stop=True)
                    cp(OO[psl, :], b2[psl, 0:256], [k2], K_("OO"), eng="act")
                    for h in range(4):
                        hs = slice(h * 64, h * 64 + 64)
                        mm(b4[psl, 256 + h * 64:320 + h * 64], KDEC[psl, hs], VN[psl, hs], K_("KDEC", "VN"), [k4])
                    tt(v4(STMP[psl, :]), v4(SS[psl, :]), bc3(EX[psl, 8:12], sh), ALU.mult, K_("SS", "EX"), K_("STMP"), eng="pool")
                    tt(SS[psl, :], STMP[psl, :], b4[psl, 256:512], ALU.add, K_("STMP") + [k4], K_("SS"))
                    cp(SSb[psl, :], SS[psl, :], K_("SS"), K_("SSb"), eng="pool")
                    stq(OFD[d, ch * 64:ch * 64 + 64, :], OO[psl, :], K_("OO"), [("OFD", d, ch)], ("st", "OO" + sx))

                def init_S(d):
                    psl = slice(d * 64, d * 64 + 64)
                    sx = "_%d" % d
                    if pidx is None:
                        ld(SS[psl, :].rearrange("p (h c) -> p h c", h=4), sd[l, d].rearrange("h k v -> k h v"), [], ["SS" + sx], key=("ld", "SS" + sx))
                    else:
                        P.add("pool", lambda e: e.memset(SS[psl, :], 0.0), reads=[], writes=["SS" + sx])
                    cp(SSb[psl, :], SS[psl, :], ["SS" + sx], ["SSb" + sx], eng="pool")

                def store_S(d):
                    psl = slice(d * 64, d * 64 + 64)
                    sx = "_%d" % d
                    if pidx is not None:
                        stq(nsd[pidx, l, d].rearrange("h k v -> k h v"), SS[psl, :].rearrange("p (h c) -> p h c", h=4), ["SS" + sx],
                            [("ns", pidx, l, d)], ("st", "SS" + sx))

                if dn_on:
                    init_S(0)
                    init_S(1)
                    for i in range(nch):
                        P.rec = []
                        dn_unit(i, 0)
                        ra = P.rec
                        P.rec = []
                        dn_unit(nch - 1 - i, 1)
                        rb = P.rec
                        P.rec = None
                        na, nb = len(ra), len(rb)
                        for j in range(max(na, nb)):
                            if j < na:
                                P.add(*ra[j][:2], reads=ra[j][2], writes=ra[j][3], dma=ra[j][4], sem_key=ra[j][5])
                            if j < nb:
                                P.add(*rb[j][:2], reads=rb[j][2], writes=rb[j][3], dma=rb[j][4], sem_key=rb[j][5])
                    store_S(0)
                    store_S(1)
                    for t in range(ntile):
                        tsl = slice(t * 128, t * 128 + 128)
                        ld(DG[:], OFD[0, tsl, :], [("OFD", 0, 2 * t), ("OFD", 0, 2 * t + 1)], ["DG_0", "DG_1"], key=("ld", "DGc"))
                        ld(NDG[:], OFD[1, tsl, :], [("OFD", 1, 2 * t), ("OFD", 1, 2 * t + 1)], ["NDG_0", "NDG_1"], key=("ld", "NDGc"))
                        tt(DG[:], DG[:], NDG[:], ALU.add, ["DG_0", "DG_1", "NDG_0", "NDG_1"], ["DG_0", "DG_1"])
                        tt(SQ[:, 0:256], DG[:], DG[:], ALU.mult, ["DG_0", "DG_1"], ["SQ"])
                        P.add("dve", lambda e: e.tensor_reduce(out=RN[:, 0:4], in_=SQ[:, 0:256].rearrange("p (h c) -> p h c", h=4), axis=AX.X, op=ALU.add),
                              reads=["SQ"], writes=["RN"])
                        ts(RN[:, 0:4], RN[:, 0:4], 1.0 / 64.0, EPS, ALU.mult, ALU.add, ["RN"], ["RN"])
                        act(RN[:, 0:4], RN[:, 0:4], AF.Sqrt, ["RN"], ["RN"])
                        recip(RN[:, 8:12], RN[:, 0:4], ["RN"], ["RN"])
                        tt(DG[:].rearrange("p (h c) -> p h c", h=4), DG[:].rearrange("p (h c) -> p h c", h=4), bc3(RN[:, 8:12], [128, 4, 64]),
                           ALU.mult, ["DG_0", "DG_1", "RN"], ["DG_0", "DG_1"])
                        tt(DG[:], DG[:], DNG[:], ALU.mult, ["DG_0", "DG_1", "DNG"], ["DG_0", "DG_1"])
                        proj_tm(PC[:, 0:256], tsl, 768, 256, 128, "PC")
                        act(GA[:], PC[:, 0:256], AF.Silu, ["PC"], ["STMP_0", "STMP_1"])
                        tt(DG[:], DG[:], GA[:], ALU.mult, ["DG_0", "DG_1", "STMP_0", "STMP_1"], ["DG_0", "DG_1"])
                        stq(MIXD[tsl, 0:256], DG[:], ["DG_0", "DG_1"], [("MIXD", t, 0)], ("st", "DGc"))
                P.in_region = False

                load_wg(1040, 768)
                P.in_region = ("sgu" == DBG_REGION)
                for t in (range(ntile) if "sgu" not in skip else []):
                    tsl = slice(t * 128, t * 128 + 128)
                    proj_tm(PA[:], tsl, 0, 512, 128, "PA")
                    act(UVG[:], PA[:], AF.Gelu, ["PA"], ["UVG"])
                    act(JNK[:, 0:256], UVG[:, 256:512], AF.Square, ["UVG"], ["MXT", "ST"], accum=ST[:, 0:1])
                    rstd_from_ssq(ST[:, 0:1], ST[:, 2:3], 256.0, "ST")
                    stt(VNS[:], UVG[:, 256:512], ST[:, 2:3], SGNG[:], ALU.mult, ALU.mult, ["UVG", "ST", "SGNG"], ["VNS"])
                    for g in range(4):
                        mm(PB[:, g * 64:g * 64 + 64], SWT[:, g, :], VNS[:, g * 64:g * 64 + 64], ["SWT", "VNS"], ["PB"])
                    proj_tm(PB[:, 256:512], tsl, 512, 256, 128, "PB")
                    act(SGB[:], PB[:, 256:512], AF.Silu, ["PB"], ["SGB"])
                    for g in range(4):
                        stt(YB[:, g * 64:g * 64 + 64], PB[:, g * 64:g * 64 + 64], SGBT[:, g:g + 1], UVG[:, g * 64:g * 64 + 64], ALU.add, ALU.mult,
                            ["PB", "SGBT", "UVG"], ["YB"])
                    tt(YB[:], YB[:], SGB[:], ALU.mult, ["YB", "SGB"], ["YB"])
                    stq(MIXD[t * 128:t * 128 + 128, 256:512], YB[:], ["YB"], [("MIXD", t, 1)], ("st", "YB"))

                P.in_region = False
                load_wg(1808, 512)

                def pool_out(j):
                    typ = 0 if j == 0 else (2 if j == ntile - 1 else 1)
                    for g in range(4):
                        gs = slice(g * 64, g * 64 + 64)
                        terms = []
                        if j > 0:
                            terms.append((3, (j - 1) % 3))
                        terms.append((typ, j % 3))
                        if j < ntile - 1:
                            terms.append((4, (j + 1) % 3))
                        for i, (ty, zi) in enumerate(terms):
                            mm(PC[:, gs], BAND[:, ty, g, :], ZR[zi][:, gs], ["BAND", "ZR%d" % zi], ["PC"], start=(i == 0), stop=(i == len(terms) - 1))
                    proj_tm(PC[:, 256:512], slice(j * 128, j * 128 + 128), 256, 256, 128, "PC")
                    act(SGB[:], PC[:, 256:512], AF.Silu, ["PC"], ["SGB"])
                    tt(YB[:], PC[:, 0:256], SGB[:], ALU.mult, ["PC", "SGB"], ["YB"])
                    stq(MIXD[j * 128:j * 128 + 128, 512:768], YB[:], ["YB"], [("MIXD", j, 2)], ("st", "YB"))

                for t in (range(ntile) if "pool" not in skip else []):
                    tsl = slice(t * 128, t * 128 + 128)
                    for c in range(2):
                        proj_fm(PA[:, c * 128:c * 128 + 128], tsl, c * 128, 128, "PA")
                    cp(XCT[:].rearrange("p c t -> p (c t)"), PA[:, 0:256], ["PA"], ["XCT"], eng="act")
                    for c in range(2):
                        mm(PB[:, c * 128:c * 128 + 128], XCT[:, c, :], PWB[:, c, :], ["XCT", "PWB"], ["PB"])
                    cp(ZR[t % 3][:], PB[:, 0:256], ["PB"], ["ZR%d" % (t % 3)])
                    if t >= 1:
                        pool_out(t - 1)
                if "pool" not in skip:
                    pool_out(ntile - 1)

                load_wg(2320, 512)
                A = T // 64
                ai = 0 if A == 64 else 1
                e2d = cd["c_e2_%d" % A]
                for t in (range(ntile) if "fourier" not in skip else []):
                    tsl = slice(t * 128, t * 128 + 128)
                    for c in range(2):
                        proj_fm(PA[:, c * 128:c * 128 + 128], tsl, c * 128, 128, "PA")
                    cp(XCT[:].rearrange("p c t -> p (c t)"), PA[:, 0:256], ["PA"], ["XCT"], eng="act")
                    for cs in range(2):
                        for c in range(2):
                            mm(PB[:, cs * 256 + c * 128:cs * 256 + c * 128 + 128], XCT[:, c, :], PCS[:, cs, c, :], ["XCT", "PCS"], ["PB"])
                    cp(ZCS[:], PB[:], ["PB"], ["UVG"])
                    P.add("sp", lambda e, t=t: [e.dma_start(out=ZD[cs, t * 128:t * 128 + 128, :], in_=ZCS[:, cs * 256:cs * 256 + 256]) for cs in range(2)],
                          reads=["UVG"], writes=[("ZD", t)], dma=2, sem_key=("st", "UVG"))
                ZDall = [("ZD", t) for t in range(ntile)]
                zdv = ZD.rearrange("s (a b) c -> s a b c", b=64)
                for bb in (range(32) if "fourier" not in skip and "f1" not in skip else []):
                    P.add("sp", lambda e, bb=bb, A=A, zdv=zdv: [e.dma_start(out=ZA[0:A, cs, :].rearrange("a (b c) -> a b c", b=2), in_=zdv[cs, 0:A, bb * 2:bb * 2 + 2, :]) for cs in range(2)],
                          reads=ZDall, writes=["ZA"], dma=2, sem_key=("ld", "ZA"))
                    CA_, SA_, NSA_ = DFTA[0:A, ai, 0, 0:A], DFTA[0:A, ai, 1, 0:A], DFTA[0:A, ai, 2, 0:A]
                    mm(PC[0:A, :], CA_, ZA[0:A, 0, :], ["DFTA", "ZA"], ["PC"], start=True, stop=False)
                    mm(PC[0:A, :], NSA_, ZA[0:A, 1, :], ["DFTA", "ZA"], ["PC"], start=False, stop=True)
                    mm(PD[0:A, :], CA_, ZA[0:A, 1, :], ["DFTA", "ZA"], ["PD"], start=True, stop=False)
                    mm(PD[0:A, :], SA_, ZA[0:A, 0, :], ["DFTA", "ZA"], ["PD"], start=False, stop=True)
                    cp(VV[0:A, 0, :], PC[0:A, :], ["PC"], ["VV"])
                    cp(VV[0:A, 1, :], PD[0:A, :], ["PD"], ["VV"], eng="act")
                    P.add("sp", lambda e, bb=bb, A=A: [e.dma_start(out=VD[ri, 0:A, bb * 2:bb * 2 + 2, :], in_=VV[0:A, ri, :].rearrange("a (b c) -> a b c", b=2)) for ri in range(2)],
                          reads=["VV"], writes=[("VD", bb)], dma=2, sem_key=("st", "VV"))
                VDall = [("VD", bb) for bb in range(32)]
                mixv = MIXD[0:T, :].rearrange("(q a) c -> a q c", a=A)
                for p in (range(A) if "fourier" not in skip and "f2" not in skip else []):
                    P.add("sp", lambda e, p=p: [e.dma_start(out=VB[:, ri, :], in_=VD[ri, p, :, :]) for ri in range(2)],
                          reads=VDall, writes=["VB"], dma=2, sem_key=("ld", "VB"))
                    ld(E2[:], e2d[p], [], ["E2"])
                    mm(PE_[0:64, 0:256], E2[:, 0, :], VB[:, 0, :], ["E2", "VB"], ["PE_"], start=True, stop=False)
                    mm(PE_[0:64, 0:256], E2[:, 1, :], VB[:, 1, :], ["E2", "VB"], ["PE_"], start=False, stop=True)
                    proj_tm(PE_[0:64, 256:512], slice(p, p + A * 63 + 1, A), 256, 256, 64, "PE_")
                    act(GA[0:64, :], PE_[0:64, 256:512], AF.Silu, ["PE_"], ["STMP_0"])
                    tt(YD, PE_[0:64, 0:256], GA[0:64, :], ALU.mult, ["PE_", "STMP_0"], ["YB"])
                    stq(mixv[p, :, 768:1024], YD, ["YB"], [("MIXD", "f", p)], ("st", "YB"))

                mix_keys_f = [("MIXD", "f", p) for p in range(A)]
                P.add("pool", lambda e, wo_v=wo_v: e.dma_start(out=WG[:, :, 0:1024], in_=wo_v), reads=[], writes=["WG"], dma=1, sem_key=("ld", "WG"))
                for t in (range(ntile) if "p3" not in skip else []):
                    g0 = tok0 // 128 + t
                    ld(XT2[:], MIXD[t * 128:(t + 1) * 128, :],
                       [("MIXD", t, 0), ("MIXD", t, 1), ("MIXD", t, 2)] + mix_keys_f, ["XT2"])
                    for k in range(8):
                        dst = PA if k < 4 else PB
                        tr(dst[:, (k % 4) * 128:(k % 4) * 128 + 128], XT2[:, k * 128:(k + 1) * 128], IDN[:], ["XT2", "IDN"], ["PA" if k < 4 else "PB"])
                    cp(MXT[:, 0:4, :].rearrange("p k t -> p (k t)"), PA[:], ["PA"], ["MXT"])
                    cp(MXT[:, 4:8, :].rearrange("p k t -> p (k t)"), PB[:], ["PB"], ["MXT"], eng="act")
                    ld(XT[:], XS[g0 * 128:(g0 + 1) * 128, :], [("XS", g0)], ["XT"])
                    for n in range(2):
                        dst, dk = (PC, "PC") if n == 0 else (PD, "PD")
                        for k in range(8):
                            mm(dst[:], MXT[:, k, :], WG[:, k, n * 512:(n + 1) * 512], ["MXT", "WG"], [dk], start=(k == 0), stop=(k == 7))
                        tt(XT2[:, n * 512:(n + 1) * 512], dst[:], GATE[:, cond, n * 512:(n + 1) * 512], ALU.mult, [dk, "GATE"], ["XT2"])
                    tt(XT[:], XT[:], XT2[:], ALU.add, ["XT", "XT2"], ["XT"], eng="pool")
                    stq(XS[g0 * 128:(g0 + 1) * 128, :], XT[:], ["XT"], [("XS", g0)], ("st", "XT"))

        outs = []
        ld(ABG[:], final_norm_g.partition_broadcast(128), [], ["ABG"])
        fin_tiles = []
        for (tok0, T, cond, pidx) in seqs:
            fin_tiles += list(range(tok0 // 128, (tok0 + T) // 128))
        for g0 in fin_tiles:
            ld(XT[:], XS[g0 * 128:(g0 + 1) * 128, :], [("XS", g0)], ["XT"])
            act(JNK, XT[:], AF.Square, ["XT"], ["MXT", "ST"], accum=ST[:, 0:1])
            rstd_from_ssq(ST[:, 0:1], ST[:, 2:3], 1024.0, "ST")
            stt(XT2[:], XT[:], ST[:, 2:3], ABG[:], ALU.mult, ALU.mult, ["XT", "ST", "ABG"], ["XT2"])
            if g0 < 32:
                dst = ys[g0 * 128:(g0 + 1) * 128, :]
            else:
                dst = yp[(g0 - 32) * 128:(g0 - 31) * 128, :]
            stq(dst, XT2[:], ["XT2"], [("Y", g0)], ("st", "XT2"))
            outs.append(("Y", g0))
        for (tok0, T, cond, pidx) in seqs:
            for l in range(nl_run):
                for d in range(2):
                    if pidx is not None:
                        outs.append(("ns", pidx, l, d))
        P.add("sp", None, reads=outs)
        P.build()
    return nc, consts


_CACHE = {}


def kernel(**inputs):
    if "nc" not in _CACHE:
        _CACHE["nc"] = build_program()
    nc, consts = _CACHE["nc"]
    f = lambda a: np.ascontiguousarray(np.asarray(a, dtype=np.float32))
    shared = {k: f(inputs[k]) for k in ["ada_w", "ada_b", "norm_g", "w_in", "conv_qkv", "dn_norm_g", "sgu_norm_g", "sgu_w", "sgu_b",
                                         "pool_w", "pool_scale", "fourier_w", "w_out", "final_norm_g"]}
    shared["a_log"] = f(inputs["a_log"]).reshape(NL, 8)
    shared["dt_bias"] = f(inputs["dt_bias"]).reshape(NL, 8)
    for k, v in consts.items():
        shared[k] = f(v)
    xsm = f(inputs["x_sample"])
    xpr = f(inputs["x_prompt"])
    sdl = f(inputs["state_delta"])
    c = f(inputs["c"])
    cctx = f(inputs["c_ctx"])
    in_maps = []
    for i in range(8):
        m = dict(shared)
        m["xs"] = xsm[i]
        m["xp"] = np.ascontiguousarray(xpr[4 * i:4 * i + 4].reshape(4 * TPR, 1024))
        m["sd"] = sdl[i]
        m["cc"] = np.ascontiguousarray(np.stack([c[i], cctx], axis=0))
        in_maps.append(m)
    res = run_bass_kernel_spmd(nc, in_maps, core_ids=list(range(8)))
    y_sample = np.stack([np.asarray(r["ys"], np.float32) for r in res.results], axis=0)
    y_prompt = np.concatenate([np.asarray(r["yp"], np.float32).reshape(4, TPR, 1024) for r in res.results], axis=0)
    ns = np.concatenate([np.asarray(r["ns"], np.float32) for r in res.results], axis=0)
    return (y_prompt, y_sample, ns)
```

```python
import contextlib
import numpy as np
import concourse.bass as bass
import concourse.mybir as mybir
from concourse.bass_utils import run_bass_kernel_spmd

F32 = mybir.dt.float32
BF16 = mybir.dt.bfloat16
AF = mybir.ActivationFunctionType
ALU = mybir.AluOpType
AX = mybir.AxisListType
ENGS = ("pe", "act", "dve", "pool", "sp")
EPS = 1e-6
NL = 4
TSAMP = 4096
TPR = 256
NTOK = TSAMP + 4 * TPR


PSUM_KEYS = {"PA", "PB", "PC", "PD", "PE_", "PF", "PG", "PT"}


class Prog:
    def __init__(self, nc):
        self.nc = nc
        self.ops = []
        self.last_w = {}
        self.readers = {}

    limit = None
    in_region = False
    region_count = 0

    rec = None

    def add(self, eng, fn, reads=(), writes=(), dma=0, sem_key=None):
        if self.rec is not None:
            self.rec.append((eng, fn, list(reads), list(writes), dma, sem_key))
            return
        if self.in_region and self.limit is not None:
            if self.region_count >= self.limit:
                return
            self.region_count += 1
        i = len(self.ops)
        deps = []
        for r in reads:
            lw = self.last_w.get(r)
            if lw is not None:
                deps.append((lw, "raw"))
            if r in PSUM_KEYS:
                for rd in self.readers.get(r, ()):
                    deps.append((rd, "war"))
        for w in writes:
            lw = self.last_w.get(w)
            if lw is not None:
                deps.append((lw, "waw"))
            for rd in self.readers.get(w, ()):
                deps.append((rd, "war"))
        for r in reads:
            self.readers.setdefault(r, []).append(i)
        for w in writes:
            self.last_w[w] = i
            self.readers[w] = []
        assert (not dma) or sem_key is not None
        self.ops.append(dict(eng=eng, fn=fn, deps=deps, dma=dma, sem_key=sem_key, signal=False))
        return i

    def build(self):
        nc = self.nc
        ops = self.ops
        cnt = {e: 0 for e in ENGS}
        for o in ops:
            o["eidx"] = cnt[o["eng"]]
            cnt[o["eng"]] += 1
        dma_cum = {}
        for o in ops:
            if o["dma"]:
                k = o["sem_key"]
                dma_cum[k] = dma_cum.get(k, 0) + 16 * o["dma"]
                o["dma_val"] = dma_cum[k]
        known = {e: {f: -1 for f in ENGS} for e in ENGS}
        known_dma = {e: {} for e in ENGS}
        for o in ops:
            e = o["eng"]
            w_eng = {}
            w_dma = {}
            for (d, kind) in o["deps"]:
                D = ops[d]
                if D["dma"]:
                    k = D["sem_key"]
                    if known_dma[e].get(k, 0) >= D["dma_val"]:
                        continue
                    w_dma[k] = max(w_dma.get(k, 0), D["dma_val"])
                else:
                    f = D["eng"]
                    if f == e:
                        if e == "pe" or e == "sp":
                            continue
                        pass
                    if known[e][f] >= D["eidx"]:
                        continue
                    if f not in w_eng or ops[w_eng[f]]["eidx"] < D["eidx"]:
                        w_eng[f] = d
            o["w_eng"] = w_eng
            o["w_dma"] = w_dma
            for f, d in w_eng.items():
                ops[d]["signal"] = True
                known[e][f] = ops[d]["eidx"]
            for k, v in w_dma.items():
                known_dma[e][k] = v
        sig = {e: 0 for e in ENGS}
        for o in ops:
            if o["signal"] and not o["dma"]:
                sig[o["eng"]] += 1
                o["sig_val"] = sig[o["eng"]]
        with contextlib.ExitStack() as st:
            esem = {e: st.enter_context(nc.semaphore("s_" + e)) for e in ENGS}
            dsem = {}
            for k in dma_cum:
                dsem[k] = st.enter_context(nc.semaphore("d%d" % len(dsem)))
            block = st.enter_context(nc.Block())
            per = {e: [o for o in ops if o["eng"] == e] for e in ENGS}

            def emit(engh, lst):
                for o in lst:
                    for f, d in o["w_eng"].items():
                        engh.wait_ge(esem[f], ops[d]["sig_val"])
                    for k, v in o["w_dma"].items():
                        engh.wait_ge(dsem[k], v)
                    if o["fn"] is None:
                        continue
                    ins = o["fn"](engh)
                    if o["dma"]:
                        if not isinstance(ins, (list, tuple)):
                            ins = [ins]
                        assert len(ins) == o["dma"], (len(ins), o["dma"])
                        for x in ins:
                            x.then_inc(dsem[o["sem_key"]], 16)
                    elif o["signal"]:
                        ins.then_inc(esem[o["eng"]], 1)

            @block.sync
            def _(eng):
                emit(eng, per["sp"])

            @block.scalar
            def _(eng):
                emit(eng, per["act"])

            @block.vector
            def _(eng):
                emit(eng, per["dve"])

            @block.gpsimd
            def _(eng):
                emit(eng, per["pool"])

            @block.tensor
            def _(eng):
                emit(eng, per["pe"])


def host_consts():
    c = {}
    c["c_ident"] = np.eye(128, dtype=np.float32)
    r = np.arange(64)[:, None]
    cc = np.arange(64)[None, :]
    su = np.where(cc > r, -1.0, 0.0).astype(np.float32)
    sl = np.where(cc < r, -1.0, 0.0).astype(np.float32)
    mf = np.where(cc >= r, 0.0, -30000.0).astype(np.float32)
    mb = np.where(cc <= r, 0.0, -30000.0).astype(np.float32)
    i64 = np.eye(64, dtype=np.float32)
    t4 = lambda m: m
    blk = lambda b: (r // b == cc // b)
    md8 = blk(8).astype(np.float32)
    moff = [(blk(2 * b) & ~blk(b)).astype(np.float32) for b in (8, 16, 32)]
    c["c_m64"] = np.stack([su, sl, mf, mb, i64, md8] + moff, axis=1).astype(np.float32)
    tri = np.stack([(r <= cc), (r >= cc), np.ones((64, 64), bool)], axis=1).astype(np.float32)
    c["c_tri"] = tri
    T = TSAMP
    rows = T // 64
    rr = np.repeat(np.arange(rows, dtype=np.float32), 64)
    col = np.tile(np.arange(64, dtype=np.float32), rows)
    nf = 256
    freqs = np.power(np.float32(10000.0), -np.arange(nf, dtype=np.float32) / np.float32(nf)).astype(np.float32)
    ar = rr[:, None] * freqs[None]
    ac = col[:, None] * freqs[None]
    c["c_pos"] = np.concatenate([np.sin(ar), np.cos(ar), np.sin(ac), np.cos(ac)], axis=-1).astype(np.float32)
    band = np.zeros((128, 5, 4, 128), np.float32)
    for gi, w in enumerate((2, 4, 8, 16)):
        Tn = 384
        Fm = np.zeros((Tn, Tn), np.float64)
        for t in range(Tn):
            lo = min(max(t - w // 2, 0), Tn)
            hi = min(max(t + w - w // 2, 0), Tn)
            Fm[t, lo:hi] = 1.0 / (hi - lo)
        Fm -= np.eye(Tn)
        band[:, 0, gi, :] = Fm[0:128, 0:128].T
        band[:, 1, gi, :] = Fm[128:256, 128:256].T
        band[:, 2, gi, :] = Fm[256:384, 256:384].T
        band[:, 3, gi, :] = Fm[128:256, 0:128].T
        band[:, 4, gi, :] = Fm[128:256, 256:384].T
    c["c_band"] = band
    k64 = np.arange(64)
    ang = 2 * np.pi * np.outer(k64, k64) / 64.0
    C64 = np.cos(ang)
    S64 = np.sin(ang)
    pad = np.zeros((64, 2, 2, 128), np.float32)
    for half in range(2):
        pad[:, half, 0, half * 64:(half + 1) * 64] = C64
        pad[:, half, 1, half * 64:(half + 1) * 64] = S64
    c["c_dftpad"] = pad
    dA = np.zeros((64, 2, 3, 64), np.float32)
    for ai, A in enumerate((64, 4)):
        a = np.arange(A)
        an = 2 * np.pi * np.outer(a, a) / A
        dA[:A, ai, 0, :A] = np.cos(an)
        dA[:A, ai, 1, :A] = np.sin(an)
        dA[:A, ai, 2, :A] = -np.sin(an)
        Tt = 64 * A
        p = np.arange(A)[:, None, None]
        b = np.arange(64)[None, :, None]
        q = np.arange(64)[None, None, :]
        be = 2 * np.pi * ((b * (p + A * q)) % Tt) / Tt
        nrm = 1.0 / np.sqrt(64.0 * Tt)
        e2 = np.stack([np.cos(be) * nrm, -np.sin(be) * nrm], axis=2).astype(np.float32)
        c["c_e2_%d" % A] = e2
    c["c_dfta"] = dA
    return c


import os
DBG_REGION = os.environ.get("DBG_REGION", "")
DBG_LIMIT = os.environ.get("DBG_LIMIT")


def build_program(nl_run=NL, seq_sel=(0, 1, 2, 3, 4), skip=()):
    nc = bass.Bass("TRN2", target_bir_lowering=False)
    P = Prog(nc)
    P.limit = int(DBG_LIMIT) if DBG_LIMIT else None
    consts = host_consts()

    def din(name, shape):
        return nc.dram_tensor(name, list(shape), F32, kind="ExternalInput").ap()

    xs = din("xs", [TSAMP, 1024])
    xp = din("xp", [4 * TPR, 1024])
    sd = din("sd", [NL, 2, 4, 64, 64])
    ccd = din("cc", [2, 1024])
    ada_w = din("ada_w", [NL, 1024, 3072])
    ada_b = din("ada_b", [NL, 3072])
    norm_g = din("norm_g", [NL, 1024])
    w_in = din("w_in", [NL, 1024, 2832])
    conv_qkv = din("conv_qkv", [NL, 4, 768])
    a_log = din("a_log", [NL, 8])
    dt_bias = din("dt_bias", [NL, 8])
    dn_norm_g = din("dn_norm_g", [NL, 64])
    sgu_norm_g = din("sgu_norm_g", [NL, 256])
    sgu_w = din("sgu_w", [NL, 4, 128, 128])
    sgu_b = din("sgu_b", [NL, 4, 128])
    pool_w = din("pool_w", [NL, 4, 64, 64])
    pool_scale = din("pool_scale", [NL, 256])
    fourier_w = din("fourier_w", [NL, 4, 64, 64])
    w_out = din("w_out", [NL, 1024, 1024])
    final_norm_g = din("final_norm_g", [1024])
    cd = {k: din(k, v.shape) for k, v in consts.items()}

    ys = nc.dram_tensor("ys", [TSAMP, 1024], F32, kind="ExternalOutput").ap()
    yp = nc.dram_tensor("yp", [4 * TPR, 1024], F32, kind="ExternalOutput").ap()
    nsd = nc.dram_tensor("ns", [4, NL, 2, 4, 64, 64], F32, kind="ExternalOutput").ap()

    XS = nc.dram_tensor("XS", [NTOK, 1024], F32, kind="Internal").ap()
    MIXD = nc.dram_tensor("MIXD", [TSAMP, 1024], F32, kind="Internal").ap()
    QKVD = nc.dram_tensor("QKVD", [TSAMP, 768], BF16, kind="Internal").ap()
    OFD = nc.dram_tensor("OFD", [2, TSAMP, 256], F32, kind="Internal").ap()
    ZD = nc.dram_tensor("ZD", [2, TSAMP, 256], F32, kind="Internal").ap()
    VD = nc.dram_tensor("VD", [2, 64, 64, 256], F32, kind="Internal").ap()

    st = contextlib.ExitStack()
    with st:
        def sb(name, shape, dt=F32):
            return st.enter_context(nc.sbuf_tensor(name, list(shape), dt))

        def ps(name, shape, dt=F32):
            return st.enter_context(nc.psum_tensor(name, list(shape), dt))

        HT = sb("HT", [128, 8, TSAMP], BF16)
        WG = sb("WG", [128, 8, 1040], BF16)
        AW = sb("AW", [128, 8, 256], BF16)
        GATE = sb("GATE", [128, 2, 1024])
        XT = sb("XT", [128, 1024])
        XT2 = sb("XT2", [128, 1024])
        XN = sb("XN", [128, 1024], BF16)
        MXT = sb("MXT", [128, 8, 128], BF16)
        JNK = MXT[:].rearrange("p k t -> p (k t)")
        IDN = sb("IDN", [128, 128])
        IDB = sb("IDB", [128, 128], BF16)
        M64 = sb("M64", [128, 9, 64])
        TRI = sb("TRI", [128, 3, 64])
        BAND = sb("BAND", [128, 5, 4, 128])
        DFTPAD = sb("DFTPAD", [64, 2, 2, 128])
        DFTA = sb("DFTA", [64, 2, 3, 64])
        SCB = sb("SCB", [128, 8, 2, 128], BF16)
        TMPL = sb("TMPL", [128, 128])
        CCT = sb("CCT", [128, 16])
        SCF = sb("SCF", [128, 16])
        SCb = sb("SCb", [128, 8, 2], BF16)
        NG = sb("NG", [128, 8])
        AB = sb("AB", [128, 24])
        ABG = sb("ABG", [128, 1024])
        MODT = sb("MODT", [128, 24, 2])
        AM = sb("AM", [128, 8, 2])
        CW = sb("CW", [128, 24])
        ALB = sb("ALB", [128, 8])
        DTB = sb("DTB", [128, 8])
        NEGA = sb("NEGA", [128, 8])
        DNG = sb("DNG", [128, 256])
        SGNG = sb("SGNG", [128, 256])
        SWT = sb("SWT", [128, 4, 128])
        SGBT = sb("SGBT", [128, 4])
        PWB = sb("PWB", [128, 2, 128])
        PSC = sb("PSC", [128, 256])
        FW = sb("FW", [64, 4, 64])
        PCS = sb("PCS", [128, 2, 2, 128])
        ST = sb("ST", [128, 8])
        TL = 128
        RAW = sb("RAW", [128, 6, TL + 3])
        CS = sb("CS", [128, 6, TL])
        QKVF = sb("QKVF", [128, 768])
        QKV3 = sb("QKV3", [128, 768], BF16)
        SALL = sb("SALL", [128, 64, 24])
        TB_ = sb("TB_", [128, 32, 4])
        TZ_ = sb("TZ_", [128, 32, 4])
        TG_ = sb("TG_", [128, 32, 4])
        KS = sb("KS", [128, 256], BF16)
        RH = sb("RH", [128, 4, 128], BF16)
        KDEC = sb("KDEC", [128, 256], BF16)
        QDEC = sb("QDEC", [128, 256], BF16)
        TR = sb("TR", [128, 1024], BF16)
        SQ = sb("SQ", [128, 512])
        XA = sb("XA", [128, 256], BF16)
        XB = sb("XB", [128, 256], BF16)
        COF = sb("COF", [128, 256], BF16)
        SSb = sb("SSb", [128, 256], BF16)
        DG = sb("DG", [128, 256])
        NDG = sb("NDG", [128, 256])
        DTT = sb("DTT", [128, 256])
        Cb = [sb("C_a", [128, 256], BF16), sb("C_b", [128, 256], BF16)]
        Bb = [sb("B_a", [128, 256], BF16), sb("B_b", [128, 256], BF16)]
        Tb = [sb("T_a", [128, 256], BF16), sb("T_b", [128, 256], BF16)]
        QKT = sb("QKT", [128, 256], BF16)
        MTT = sb("MTT", [128, 256], BF16)
        UU = sb("UU", [128, 256])
        WB_ = sb("WB_", [128, 256], BF16)
        WT = sb("WT", [128, 256], BF16)
        VN = sb("VN", [128, 256], BF16)
        OO = sb("OO", [128, 256])
        SS = sb("SS", [128, 256])
        STMP = sb("STMP", [128, 256])
        GA = STMP
        RN = sb("RN", [128, 16])
        UVG = sb("UVG", [128, 512])
        VNS = sb("VNS", [128, 256])
        SGB = sb("SGB", [128, 256])
        YB = sb("YB", [128, 256])
        XCT = sb("XCT", [128, 2, 128])
        ZR = [sb("ZR%d" % i, [128, 256]) for i in range(3)]
        ZCS = UVG
        ZA = sb("ZA", [64, 2, 512])
        VV = sb("VV", [64, 2, 512])
        VB = sb("VB", [64, 2, 256])
        E2 = sb("E2", [64, 2, 64])
        YD = YB[0:64, :]
        PA = ps("PA", [128, 512])
        PB = ps("PB", [128, 512])
        PC = ps("PC", [128, 512])
        PD = ps("PD", [128, 512])
        PE_ = ps("PE_", [128, 512])
        PF = ps("PF", [128, 512])
        PG = ps("PG", [128, 512])
        PT = ps("PT", [128, 512])
        PTb = PT[:].bitcast(BF16)

        def ld(out, in_, r, w, q="sp", key=None, n=1, nc_ok=False):
            kw = dict(allow_slow_non_contiguous=True) if nc_ok else {}
            P.add(q, lambda e: e.dma_start(out=out, in_=in_, **kw), reads=r, writes=w, dma=1, sem_key=key or ("ld", w[0]))

        def stq(out, in_, r, w, key):
            P.add("sp", lambda e: e.dma_start(out=out, in_=in_), reads=r, writes=w, dma=1, sem_key=key)

        def mm(out, lhsT, rhs, r, w, start=True, stop=True):
            P.add("pe", lambda e: e.matmul(out, lhsT=lhsT, rhs=rhs, start=start, stop=stop), reads=r, writes=w)

        def tr(out, in_, ident, r, w):
            P.add("pe", lambda e: e.transpose(out=out, in_=in_, identity=ident), reads=r, writes=w)

        def act(out, in_, func, r, w, bias=None, scale=None, accum=None):
            kw = {}
            if bias is not None:
                kw["bias"] = bias
            if scale is not None:
                kw["scale"] = scale
            if accum is not None:
                kw["accum_out"] = accum
            P.add("act", lambda e: e.activation(out=out, in_=in_, func=func, **kw), reads=r, writes=w)

        def tt(out, in0, in1, op, r, w, eng="dve"):
            P.add(eng, lambda e: e.tensor_tensor(out=out, in0=in0, in1=in1, op=op), reads=r, writes=w)

        def ts(out, in0, s1, s2, op0, op1, r, w, eng="dve"):
            if op1 is None:
                P.add(eng, lambda e: e.tensor_scalar(out=out, in0=in0, scalar1=s1, scalar2=None, op0=op0), reads=r, writes=w)
            else:
                P.add(eng, lambda e: e.tensor_scalar(out=out, in0=in0, scalar1=s1, scalar2=s2, op0=op0, op1=op1), reads=r, writes=w)

        def stt(out, in0, scalar, in1, op0, op1, r, w):
            P.add("dve", lambda e: e.scalar_tensor_tensor(out=out, in0=in0, scalar=scalar, in1=in1, op0=op0, op1=op1), reads=r, writes=w)

        def cp(out, in_, r, w, eng="dve"):
            if eng == "act":
                P.add(eng, lambda e: e.activation(out=out, in_=in_, func=AF.Identity), reads=r, writes=w)
            else:
                P.add(eng, lambda e: e.tensor_copy(out=out, in_=in_), reads=r, writes=w)

        def recip(out, in_, r, w):
            P.add("dve", lambda e: e.reciprocal(out=out, in_=in_), reads=r, writes=w)

        def rstd_from_ssq(ssq_ap, out_ap, n, key):
            ts(ssq_ap, ssq_ap, 1.0 / n, EPS, ALU.mult, ALU.add, [key], [key])
            act(ssq_ap, ssq_ap, AF.Sqrt, [key], [key])
            recip(out_ap, ssq_ap, [key], [key])

        def load_T(dst_ap, src_ap, n, dkey):
            ld(TMPL[0:n, :], src_ap, [], ["TMPL"])
            tr(PG[:, 0:n], TMPL[0:n, :], IDN[0:n, 0:n], ["TMPL", "IDN"], ["PG"])
            cp(dst_ap, PG[:, 0:n], ["PG"], [dkey])

        def bc3(ap, shape):
            return ap.unsqueeze(2).to_broadcast(shape)

        ld(IDN[:], cd["c_ident"], [], ["IDN"])
        cp(IDB[:], IDN[:], ["IDN"], ["IDB"])
        for hf in range(2):
            ld(M64[hf * 64:hf * 64 + 64], cd["c_m64"], [], ["M64"])
            ld(TRI[hf * 64:hf * 64 + 64], cd["c_tri"], [], ["TRI"])
        ld(BAND[:], cd["c_band"], [], ["BAND"])
        ld(DFTPAD[:], cd["c_dftpad"], [], ["DFTPAD"])
        ld(DFTA[:], cd["c_dfta"], [], ["DFTA"])
        def masks(hf):
            psl = slice(hf * 64, hf * 64 + 64)
            bh = lambda i: M64[psl, i, :].unsqueeze(1).to_broadcast([64, 2, 64])
            return dict(SUL=(bh(0), bh(1)), MINC=(bh(2), bh(3)), ID4=bh(4), MD8=bh(5), MOFF=[bh(6), bh(7), bh(8)])
        MSK = [masks(0), masks(1)]

        load_T(CCT[:], ccd.rearrange("c (k p) -> (c k) p", p=128), 16, "CCT")
        act(SCF[:], CCT[:], AF.Silu, ["CCT"], ["SCF"])
        cp(SCb[:].rearrange("p k c -> p c k"), SCF[:].rearrange("p (c k) -> p c k", c=2), ["SCF"], ["SCb"])
        cp(SCB[:], SCb[:].unsqueeze(3).to_broadcast([128, 8, 2, 128]), ["SCb"], ["SCB"])

        for t in (range(TSAMP // 128) if 0 in seq_sel else []):
            ld(XT[:], xs[t * 128:(t + 1) * 128, :], [], ["XT"])
            ld(XT2[:], cd["c_pos"][t * 128:(t + 1) * 128, :], [], ["XT2"])
            tt(XT[:], XT[:], XT2[:], ALU.add, ["XT", "XT2"], ["XT"])
            stq(XS[t * 128:(t + 1) * 128, :], XT[:], ["XT"], [("XS", t)], ("st", "XT"))
        for t in range(8):
            if (1 + t // 2) in seq_sel:
                ld(XT[:], xp[t * 128:(t + 1) * 128, :], [], ["XT"])
                stq(XS[TSAMP + t * 128:TSAMP + (t + 1) * 128, :], XT[:], ["XT"], [("XS", 32 + t)], ("st", "XT"))

        seqs = [(0, TSAMP, 0, None)] + [(TSAMP + i * TPR, TPR, 1, i) for i in range(4)]
        seqs = [seqs[i] for i in seq_sel]

        for l in range(nl_run):
            load_T(NG[:], norm_g[l].rearrange("(k p) -> k p", p=128), 8, "NG")
            load_T(AB[:], ada_b[l].rearrange("(k p) -> k p", p=128), 24, "AB")
            load_T(CW[:], conv_qkv[l].rearrange("j (c p) -> (j c) p", p=128), 24, "CW")
            ld(ABG[:], ada_b[l, 2048:3072].partition_broadcast(128), [], ["ABG"])
            ld(ALB[:], a_log[l].partition_broadcast(128), [], ["ALB"])
            ld(DTB[:], dt_bias[l].partition_broadcast(128), [], ["DTB"])
            act(NEGA[:], ALB[:], AF.Exp, ["ALB"], ["NEGA"])
            ts(NEGA[:], NEGA[:], -1.0, None, ALU.mult, None, ["NEGA"], ["NEGA"])
            for h in range(4):
                ld(DNG[:, h * 64:(h + 1) * 64], dn_norm_g[l].partition_broadcast(128), [], ["DNG"], key=("ld", "DNG"))
            ld(SGNG[:], sgu_norm_g[l].partition_broadcast(128), [], ["SGNG"])
            ld(PSC[:], pool_scale[l].partition_broadcast(128), [], ["PSC"])
            load_T(SGBT[:], sgu_b[l], 4, "SGBT")
            for g in range(4):
                ld(TMPL[:], sgu_w[l, g], [], ["TMPL"])
                tr(PG[:, 0:128], TMPL[:], IDN[:], ["TMPL", "IDN"], ["PG"])
                cp(SWT[:, g, :], PG[:, 0:128], ["PG"], ["SWT"])
            P.add("pool", lambda e: e.memset(PWB[:], 0.0), reads=[], writes=["PWB"])
            for g in range(4):
                hb = (g % 2) * 64
                ld(PWB[hb:hb + 64, g // 2, hb:hb + 64], pool_w[l, g], [], ["PWB"], key=("ld", "PWB"))
            for c in range(2):
                tt(PWB[:, c, :], PWB[:, c, :], PSC[:, c * 128:(c + 1) * 128], ALU.mult, ["PWB", "PSC"], ["PWB"])
            ld(FW[:], fourier_w[l].rearrange("g c d -> c g d"), [], ["FW"])
            for g in range(4):
                for cs in range(2):
                    mm(PG[:, 0:64], DFTPAD[:, g % 2, cs, :], FW[:, g, :], ["DFTPAD", "FW"], ["PG"])
                    cp(PCS[:, cs, g // 2, (g % 2) * 64:(g % 2) * 64 + 64], PG[:, 0:64], ["PG"], ["PCS"])
            wo_v = w_out[l].rearrange("(k p) n -> p k n", p=128)
            aw_v = ada_w[l].rearrange("(k p) n -> p k n", p=128)
            for n in range(12):
                P.add("pool", lambda e, n=n, aw_v=aw_v: e.dma_start(out=AW[:], in_=aw_v[:, :, n * 256:(n + 1) * 256]),
                      reads=[], writes=["AW"], dma=1, sem_key=("ld", "AW"))
                for c in range(2):
                    for k in range(8):
                        mm(PG[:, 0:2], AW[:, k, c * 128:(c + 1) * 128], SCb[:, k, :], ["AW", "SCb"], ["PG"], start=(k == 0), stop=(k == 7))
                    ts(MODT[:, n * 2 + c, :], PG[:, 0:2], AB[:, n * 2 + c:n * 2 + c + 1], None, ALU.add, None, ["PG", "AB"], ["MODT"])
                if n >= 8:
                    for cond in range(2):
                        for k in range(8):
                            mm(PA[:, 0:256], SCB[:, k, cond, :], AW[:, k, :], ["SCB", "AW"], ["PA"], start=(k == 0), stop=(k == 7))
                        tt(GATE[:, cond, (n - 8) * 256:(n - 7) * 256], PA[:, 0:256], ABG[:, (n - 8) * 256:(n - 7) * 256], ALU.add,
                           ["PA", "ABG"], ["GATE"])
            ts(AM[:], MODT[:, 8:16, :], 1.0, None, ALU.add, None, ["MODT"], ["AM"])
            tt(AM[:], AM[:], bc3(NG[:], [128, 8, 2]), ALU.mult, ["AM", "NG"], ["AM"])

            win_v = w_in[l].rearrange("(k p) n -> p k n", p=128)

            def load_wg(c0, ncol):
                P.add("pool", lambda e, wv=win_v: e.dma_start(out=WG[:, :, 0:ncol], in_=wv[:, :, c0:c0 + ncol]),
                      reads=[], writes=["WG"], dma=1, sem_key=("ld", "WG"))

            for (tok0, T, cond, pidx) in seqs:
                ntile = T // 128
                for t in range(ntile):
                    g0 = tok0 // 128 + t
                    ld(XT[:], XS[g0 * 128:(g0 + 1) * 128, :], [("XS", g0)], ["XT"])
                    act(JNK, XT[:], AF.Square, ["XT"], ["MXT", "ST"], accum=ST[:, 0:1])
                    rstd_from_ssq(ST[:, 0:1], ST[:, 2:3], 1024.0, "ST")
                    ts(XN[:], XT[:], ST[:, 2:3], None, ALU.mult, None, ["XT", "ST"], ["XN"])
                    for k in range(8):
                        tr(PTb[:, k * 128:(k + 1) * 128], XN[:, k * 128:(k + 1) * 128], IDB[:], ["XN", "IDB"], ["PT"])
                    for k in range(8):
                        act(HT[:, k, t * 128:(t + 1) * 128], PTb[:, k * 128:(k + 1) * 128], AF.Identity, ["PT", "AM", "MODT"],
                            [("HT", t)], bias=MODT[:, k, cond:cond + 1], scale=AM[:, k, cond:cond + 1])
                HTall = [("HT", t) for t in range(ntile)]

                def proj_tm(out_ps, tok_sl, c0, ncol, M, okey):
                    for k in range(8):
                        mm(out_ps, HT[:, k, tok_sl], WG[:, k, c0:c0 + ncol], HTall + ["WG"], [okey], start=(k == 0), stop=(k == 7))

                def proj_fm(out_ps, tok_sl, c0, ncol, okey, start=True):
                    for k in range(8):
                        mm(out_ps, WG[:, k, c0:c0 + ncol], HT[:, k, tok_sl], HTall + ["WG"], [okey], start=(k == 0), stop=(k == 7))

                load_wg(0, 1040)
                nch = T // 64
                P.in_region = ("dn" == DBG_REGION)
                dn_on = "dn" not in skip
                for tl in (range(T // TL) if dn_on else []):
                    s0 = tl * TL
                    for c in range(6):
                        proj_fm(PA[:, 0:TL], slice(s0, s0 + TL), c * 128, 128, "PA")
                        cp(RAW[:, c, 2:2 + TL], PA[:, 0:TL], ["PA"], ["RAW"], eng="act")
                        if s0 > 0:
                            proj_fm(PB[:, 0:2], slice(s0 - 2, s0), c * 128, 128, "PB")
                            cp(RAW[:, c, 0:2], PB[:, 0:2], ["PB"], ["RAW"])
                        else:
                            P.add("pool", lambda e, c=c: e.memset(RAW[:, c, 0:2], 0.0), reads=[], writes=["RAW"])
                        if s0 + TL < T:
                            proj_fm(PB[:, 2:4], slice(s0 + TL, s0 + TL + 2), c * 128, 128, "PB")
                            cp(RAW[:, c, 2 + TL:3 + TL], PB[:, 2:3], ["PB"], ["RAW"])
                        else:
                            P.add("pool", lambda e, c=c: e.memset(RAW[:, c, 2 + TL:3 + TL], 0.0), reads=[], writes=["RAW"])
                        ts(CS[:, c, :], RAW[:, c, 0:TL], CW[:, 0 * 6 + c:0 * 6 + c + 1], None, ALU.mult, None, ["RAW", "CW"], ["CS"])
                        for j in range(1, 4):
                            stt(CS[:, c, :], RAW[:, c, j:j + TL], CW[:, j * 6 + c:j * 6 + c + 1], CS[:, c, :], ALU.mult, ALU.add,
                                ["RAW", "CW", "CS"], ["CS"])
                        act(CS[:, c, :], CS[:, c, :], AF.Silu, ["CS"], ["CS"])
                    for c in range(6):
                        dst = (PC if c < 4 else PD)
                        off = (c % 4) * 128
                        tr(dst[:, off:off + 128], CS[:, c, :], IDN[:], ["CS", "IDN"], ["PC" if c < 4 else "PD"])
                    cp(QKVF[:, 0:512], PC[:, :], ["PC"], ["QKVF"])
                    cp(QKVF[:, 512:768], PD[:, 0:256], ["PD"], ["QKVF"], eng="act")
                    tt(SQ[:], QKVF[:, 0:512], QKVF[:, 0:512], ALU.mult, ["QKVF"], ["SQ"])
                    P.add("dve", lambda e: e.tensor_reduce(out=RN[:, 0:8], in_=SQ[:].rearrange("p (h c) -> p h c", h=8), axis=AX.X, op=ALU.add),
                          reads=["SQ"], writes=["RN"])
                    ts(RN[:, 0:8], RN[:, 0:8], EPS, None, ALU.add, None, ["RN"], ["RN"])
                    act(RN[:, 0:8], RN[:, 0:8], AF.Sqrt, ["RN"], ["RN"])
                    recip(RN[:, 8:16], RN[:, 0:8], ["RN"], ["RN"])
                    ts(RN[:, 8:12], RN[:, 8:12], 0.125, None, ALU.mult, None, ["RN"], ["RN"])
                    tt(QKV3[:, 0:512].rearrange("p (h c) -> p h c", h=8), QKVF[:, 0:512].rearrange("p (h c) -> p h c", h=8),
                       bc3(RN[:, 8:16], [128, 8, 64]), ALU.mult, ["QKVF", "RN"], ["QKV3_0", "QKV3_1"])
                    cp(QKV3[:, 512:768], QKVF[:, 512:768], ["QKVF"], ["QKV3_0", "QKV3_1"], eng="pool")
                    stq(QKVD[s0:s0 + 128, :], QKV3[:], ["QKV3_0", "QKV3_1"], [("QKVD", tl)], ("st", "QKV3"))

                def dn_scal_all(d):
                    hf = d
                    psl = slice(hf * 64, hf * 64 + 64)
                    sx = "_%d" % hf
                    bk, bkk = (PA, "PA") if hf == 0 else (PB, "PB")
                    bk2, bkk2 = (PC, "PC") if hf == 0 else (PD, "PD")
                    for g0 in range(0, nch, 32):
                        n = min(32, nch - g0)
                        for j in range(n):
                            ch = g0 + j
                            proj_tm(bk[psl, j * 16:(j + 1) * 16], slice(ch * 64, ch * 64 + 64), 1024, 16, 64, bkk)
                        pv = bk[psl, 0:n * 16].rearrange("p (n c) -> p n c", c=16)
                        TB, TZ, TG = TB_[psl, 0:n, :], TZ_[psl, 0:n, :], TG_[psl, 0:n, :]
                        SA = lambda o: SALL[psl, g0:g0 + n, o:o + 4]
                        kk = ["TB_" + sx]
                        act(TB, pv[:, :, d * 4:d * 4 + 4], AF.Sigmoid, [bkk], kk)
                        act(SA(0), TB, AF.Sqrt, kk, ["SALL" + sx])
                        tt(TZ, pv[:, :, 8 + d * 4:12 + d * 4], DTB[psl, d * 4:d * 4 + 4].unsqueeze(1).to_broadcast([64, n, 4]), ALU.add, [bkk, "DTB"], kk)
                        act(TZ, TZ, AF.Exp, kk, kk)
                        act(TZ, TZ, AF.Ln, kk, kk, bias=1.0)
                        tt(TG, TZ, NEGA[psl, d * 4:d * 4 + 4].unsqueeze(1).to_broadcast([64, n, 4]), ALU.mult, kk + ["NEGA"], kk)
                        gflat = TG_[psl, 0:n, :].rearrange("p n c -> p (n c)")
                        mm(bk2[psl, 0:n * 4], TRI[psl, d, :], gflat, ["TRI"] + kk, [bkk2])
                        mm(bk2[psl, 128:128 + n * 4], TRI[psl, 2, :], gflat, ["TRI"] + kk, [bkk2])
                        gcv = bk2[psl, 0:n * 4].rearrange("p (n c) -> p n c", c=4)
                        glv = bk2[psl, 128:128 + n * 4].rearrange("p (n c) -> p n c", c=4)
                        cp(SA(20), gcv, [bkk2], ["SALL" + sx])
                        act(SA(8), gcv, AF.Exp, [bkk2], ["SALL" + sx])
                        tt(TZ, glv, SA(20), ALU.subtract, [bkk2, "SALL" + sx], kk)
                        act(SA(12), TZ, AF.Exp, kk, ["SALL" + sx])
                        act(SA(16), glv, AF.Exp, [bkk2], ["SALL" + sx])
                        tt(SA(4), SA(0), SA(8), ALU.mult, ["SALL" + sx], ["SALL" + sx])

                def dn_unit(ch, d, hp):
                    hf = d
                    psl = slice(hf * 64, hf * 64 + 64)
                    sx = "_%d" % hf
                    sy = "_%d%d" % (hf, hp)
                    K_ = lambda *n: [x + sy for x in n]
                    bx, by = [[(PC, PD), (PF, PG)], [(PA, PB), (PE_, PT)]][hf][hp]
                    kx, ky = [[("PC", "PD"), ("PF", "PG")], [("PA", "PB"), ("PE_", "PT")]][hf][hp]
                    M = MSK[hf]
                    v2 = lambda ap: ap.rearrange("p (h c) -> p h c", h=2)
                    sh = [64, 2, 64]
                    cs_ = slice(hp * 128, hp * 128 + 128)
                    SAk = "SALL" + sx
                    SA = lambda o: SALL[psl, ch, o + 2 * hp:o + 2 * hp + 2]
                    if hp == 0:
                        ld(QKV3[psl, :], QKVD[ch * 64:ch * 64 + 64, :], [("QKVD", ch // 2)], ["QKV3" + sx], key=("ld", "QKV3" + sx))
                    Q = QKV3[psl, 0 + hp * 128:128 + hp * 128]
                    K = QKV3[psl, 256 + hp * 128:384 + hp * 128]
                    V = QKV3[psl, 512 + hp * 128:640 + hp * 128]
                    QK3 = ["QKV3" + sx]
                    tt(v2(KS[psl, cs_]), v2(K), bc3(SA(0), sh), ALU.mult, QK3 + [SAk], K_("KS"))
                    tt(RH[psl, 2 * hp:2 * hp + 2, 0:64], v2(V), bc3(SA(0), sh), ALU.mult, QK3 + [SAk], K_("RH"), eng="pool")
                    tt(RH[psl, 2 * hp:2 * hp + 2, 64:128], v2(K), bc3(SA(4), sh), ALU.mult, QK3 + [SAk], K_("RH"))
                    tt(v2(KDEC[psl, cs_]), v2(K), bc3(SA(12), sh), ALU.mult, QK3 + [SAk], K_("KDEC"), eng="pool")
                    tt(v2(QDEC[psl, cs_]), v2(Q), bc3(SA(8), sh), ALU.mult, QK3 + [SAk], K_("QDEC"))
                    I64 = IDB[psl, hf * 64:hf * 64 + 64]
                    bxb = bx[psl, 0:256].bitcast(BF16)
                    c0 = hp * 128
                    for h in range(2):
                        hs = slice(c0 + h * 64, c0 + h * 64 + 64)
                        hq = slice(h * 64, h * 64 + 64)
                        tr(bxb[:, h * 64:h * 64 + 64], KS[psl, hs], I64, K_("KS") + ["IDB"], [kx])
                        tr(bxb[:, 128 + h * 64:192 + h * 64], K[:, hq], I64, QK3 + ["IDB"], [kx])
                        tr(bxb[:, 256 + h * 64:320 + h * 64], Q[:, hq], I64, QK3 + ["IDB"], [kx])
                        tr(bxb[:, 384 + h * 64:448 + h * 64], QDEC[psl, hs], I64, K_("QDEC") + ["IDB"], [kx])
                    TRs = TR[psl, hp * 512:hp * 512 + 512]
                    cp(TRs, bxb, [kx], K_("TR"), eng="act")
                    KST = lambda h: TRs[:, h * 64:h * 64 + 64]
                    KT = lambda h: TRs[:, 128 + h * 64:192 + h * 64]
                    QT = lambda h: TRs[:, 256 + h * 64:320 + h * 64]
                    QDT = lambda h: TRs[:, 384 + h * 64:448 + h * 64]
                    for h in range(2):
                        mm(by[psl, h * 64:h * 64 + 64], KST(h), KST(h), K_("TR"), [ky])
                        mm(by[psl, 128 + h * 64:192 + h * 64], KT(h), QT(h), K_("TR"), [ky])
                    tt(v2(DG[psl, cs_]), M["ID4"], bc3(SA(20), sh), ALU.mult, ["M64", SAk], K_("DG"), eng="pool")
                    act(NDG[psl, cs_], DG[psl, cs_], AF.Identity, K_("DG"), K_("NDG"), scale=-1.0)
                    for h in range(2):
                        hs = slice(c0 + h * 64, c0 + h * 64 + 64)
                        mm(by[psl, 256 + h * 64:320 + h * 64], TRI[psl, 2, :], DG[psl, hs], ["TRI"] + K_("DG"), [ky], start=True, stop=False)
                        mm(by[psl, 256 + h * 64:320 + h * 64], NDG[psl, hs], TRI[psl, 2, :], ["TRI"] + K_("NDG"), [ky], start=False, stop=True)
                    tt(v2(DTT[psl, cs_]), v2(by[psl, 256:384]), M["MINC"][d], ALU.add, [ky, "M64"], K_("DTT"))
                    act(DTT[psl, cs_], DTT[psl, cs_], AF.Exp, K_("DTT"), K_("DTT"))
                    C0_, B0_ = Cb[0], Bb[0]
                    CD, BD = Cb[1], Bb[1]
                    TT_, TN_ = Tb[0], Tb[1]
                    tt(v2(C0_[psl, cs_]), v2(by[psl, 0:128]), M["SUL"][d], ALU.mult, [ky, "M64"], K_("C_a"))
                    tt(v2(B0_[psl, cs_]), v2(by[psl, 0:128]), M["SUL"][1 - d], ALU.mult, [ky, "M64"], K_("B_a"))
                    tt(QKT[psl, cs_], by[psl, 128:256], DTT[psl, cs_], ALU.mult, [ky] + K_("DTT"), K_("QKT"))
                    tt(v2(CD[psl, cs_]), v2(C0_[psl, cs_]), M["MD8"], ALU.mult, K_("C_a") + ["M64"], K_("C_b"), eng="pool")
                    tt(v2(BD[psl, cs_]), v2(B0_[psl, cs_]), M["MD8"], ALU.mult, K_("B_a") + ["M64"], K_("B_b"), eng="pool")
                    tt(v2(TT_[psl, cs_]), v2(CD[psl, cs_]), M["ID4"], ALU.add, K_("C_b") + ["M64"], K_("T_a"), eng="pool")
                    tt(v2(TN_[psl, cs_]), v2(BD[psl, cs_]), M["ID4"], ALU.add, K_("B_b") + ["M64"], K_("T_b"), eng="pool")

                    def grp(dst, dk, off, lt, lk, rt, rk):
                        for h in range(2):
                            hs = slice(c0 + h * 64, c0 + h * 64 + 64)
                            mm(dst[psl, off + h * 64:off + h * 64 + 64], lt[psl, hs], rt[psl, hs], K_(lk, rk), [dk])

                    for lev in range(2):
                        grp(bx, kx, 0, CD, "C_b", BD, "B_b")
                        grp(bx, kx, 128, BD, "B_b", CD, "C_b")
                        cp(BD[psl, cs_], bx[psl, 0:128], [kx], K_("B_b"), eng="act")
                        cp(CD[psl, cs_], bx[psl, 128:256], [kx], K_("C_b"), eng="act")
                        grp(by, ky, 0, BD, "B_b", TT_, "T_a")
                        grp(by, ky, 128, CD, "C_b", TN_, "T_b")
                        tt(TT_[psl, cs_], TT_[psl, cs_], by[psl, 0:128], ALU.add, K_("T_a") + [ky], K_("T_a"))
                        tt(TN_[psl, cs_], TN_[psl, cs_], by[psl, 128:256], ALU.add, K_("T_b") + [ky], K_("T_b"))
                    BOF = VN
                    for li in range(3):
                        last = (li == 2)
                        tt(v2(BOF[psl, cs_]), v2(B0_[psl, cs_]), M["MOFF"][li], ALU.mult, K_("B_a") + ["M64"], K_("VN"), eng="pool")
                        grp(bx, kx, 0, BOF, "VN", TT_, "T_a")
                        cp(XA[psl, cs_], bx[psl, 0:128], [kx], K_("XA"), eng="act")
                        if not last:
                            tt(v2(COF[psl, cs_]), v2(C0_[psl, cs_]), M["MOFF"][li], ALU.mult, K_("C_a") + ["M64"], K_("COF"), eng="pool")
                            grp(bx, kx, 128, COF, "COF", TN_, "T_b")
                            cp(XB[psl, cs_], bx[psl, 128:256], [kx], K_("XB"), eng="act")
                        grp(by, ky, 0, TN_, "T_b", XA, "XA")
                        if not last:
                            grp(by, ky, 128, TT_, "T_a", XB, "XB")
                        tt(TT_[psl, cs_], TT_[psl, cs_], by[psl, 0:128], ALU.add, K_("T_a") + [ky], K_("T_a"))
                        if not last:
                            tt(TN_[psl, cs_], TN_[psl, cs_], by[psl, 128:256], ALU.add, K_("T_b") + [ky], K_("T_b"))
                    tt(MTT[psl, cs_], TT_[psl, cs_], DTT[psl, cs_], ALU.mult, K_("T_a", "DTT"), K_("MTT"))
                    for h in range(2):
                        hs = slice(c0 + h * 64, c0 + h * 64 + 64)
                        mm(by[psl, h * 128:(h + 1) * 128], MTT[psl, hs], RH[psl, 2 * hp + h, :], K_("MTT", "RH"), [ky])
                    byv = by[psl, 0:256].rearrange("p (h c) -> p h c", h=2)
                    tt(v2(UU[psl, cs_]), byv[:, :, 0:64], bc3(SA(0), sh), ALU.mult, [ky, SAk], K_("UU"))
                    tt(v2(WB_[psl, cs_]), byv[:, :, 64:128], bc3(SA(0), sh), ALU.mult, [ky, SAk], K_("WB_"))
                    bxw = bx[psl, 0:64].bitcast(BF16)
                    for h in range(2):
                        hs = slice(c0 + h * 64, c0 + h * 64 + 64)
                        tr(bxw[:, h * 64:h * 64 + 64], WB_[psl, hs], I64, K_("WB_") + ["IDB"], [kx])
                    cp(WT[psl, cs_], bxw, [kx], K_("WT"), eng="act")
                    for h in range(2):
                        hs = slice(c0 + h * 64, c0 + h * 64 + 64)
                        mm(bx[psl, 128 + h * 64:192 + h * 64], WT[psl, hs], SSb[psl, hs], K_("WT", "SSb"), [kx])
                    tt(VN[psl, cs_], UU[psl, cs_], bx[psl, 128:256], ALU.subtract, K_("UU") + [kx], K_("VN"))
                    for h in range(2):
                        hs = slice(c0 + h * 64, c0 + h * 64 + 64)
                        mm(by[psl, h * 64:h * 64 + 64], QDT(h), SSb[psl, hs], K_("TR", "SSb"), [ky], start=True, stop=False)
                        mm(by[psl, h * 64:h * 64 + 64], QKT[psl, hs], VN[psl, hs], K_("QKT", "VN"), [ky], start=False, stop=True)
                    cp(OO[psl, cs_], by[psl, 0:128], [ky], K_("OO"), eng="act")
                    for h in range(2):
                        hs = slice(c0 + h * 64, c0 + h * 64 + 64)
                        mm(bx[psl, h * 64:h * 64 + 64], KDEC[psl, hs], VN[psl, hs], K_("KDEC", "VN"), [kx])
                    tt(v2(STMP[psl, cs_]), v2(SS[psl, cs_]), bc3(SA(16), sh), ALU.mult, K_("SS") + [SAk], K_("STMP"), eng="pool")
                    tt(SS[psl, cs_], STMP[psl, cs_], bx[psl, 0:128], ALU.add, K_("STMP") + [kx], K_("SS"))
                    cp(SSb[psl, cs_], SS[psl, cs_], K_("SS"), K_("SSb"), eng="pool")
                    stq(OFD[d, ch * 64:ch * 64 + 64, cs_], OO[psl, cs_], K_("OO"), [("OFD", d, ch, hp)], ("st", "OO" + sy))

                def init_S(d):
                    psl = slice(d * 64, d * 64 + 64)
                    sx = "_%d" % d
                    if pidx is None:
                        ld(SS[psl, :].rearrange("p (h c) -> p h c", h=4), sd[l, d].rearrange("h k v -> k h v"), [], ["SS" + sx + "0", "SS" + sx + "1"], key=("ld", "SS" + sx))
                    else:
                        P.add("pool", lambda e: e.memset(SS[psl, :], 0.0), reads=[], writes=["SS" + sx + "0", "SS" + sx + "1"])
                    cp(SSb[psl, :], SS[psl, :], ["SS" + sx + "0", "SS" + sx + "1"], ["SSb" + sx + "0", "SSb" + sx + "1"], eng="pool")

                def store_S(d):
                    psl = slice(d * 64, d * 64 + 64)
                    sx = "_%d" % d
                    if pidx is not None:
                        stq(nsd[pidx, l, d].rearrange("h k v -> k h v"), SS[psl, :].rearrange("p (h c) -> p h c", h=4), ["SS" + sx + "0", "SS" + sx + "1"],
                            [("ns", pidx, l, d)], ("st", "SS" + sx))

                if dn_on:
                    init_S(0)
                    init_S(1)
                    dn_scal_all(0)
                    dn_scal_all(1)
                    for i in range(nch):
                        recs = []
                        for (cch, dd, hh) in ((i, 0, 0), (nch - 1 - i, 1, 0), (i, 0, 1), (nch - 1 - i, 1, 1)):
                            P.rec = []
                            dn_unit(cch, dd, hh)
                            recs.append(P.rec)
                        P.rec = None
                        for j in range(max(len(r) for r in recs)):
                            for r in recs:
                                if j < len(r):
                                    P.add(*r[j][:2], reads=r[j][2], writes=r[j][3], dma=r[j][4], sem_key=r[j][5])
                    store_S(0)
                    store_S(1)
                    DGK = ["DG_00", "DG_01", "DG_10", "DG_11"]
                    NDGK = ["NDG_00", "NDG_01", "NDG_10", "NDG_11"]
                    STK = ["STMP_00", "STMP_01", "STMP_10", "STMP_11"]
                    for t in range(ntile):
                        tsl = slice(t * 128, t * 128 + 128)
                        ld(DG[:], OFD[0, tsl, :], [("OFD", 0, 2 * t + a, b) for a in range(2) for b in range(2)], DGK, key=("ld", "DGc"))
                        ld(NDG[:], OFD[1, tsl, :], [("OFD", 1, 2 * t + a, b) for a in range(2) for b in range(2)], NDGK, key=("ld", "NDGc"))
                        tt(DG[:], DG[:], NDG[:], ALU.add, DGK + NDGK, DGK)
                        tt(SQ[:, 0:256], DG[:], DG[:], ALU.mult, DGK, ["SQ"])
                        P.add("dve", lambda e: e.tensor_reduce(out=RN[:, 0:4], in_=SQ[:, 0:256].rearrange("p (h c) -> p h c", h=4), axis=AX.X, op=ALU.add),
                              reads=["SQ"], writes=["RN"])
                        ts(RN[:, 0:4], RN[:, 0:4], 1.0 / 64.0, EPS, ALU.mult, ALU.add, ["RN"], ["RN"])
                        act(RN[:, 0:4], RN[:, 0:4], AF.Sqrt, ["RN"], ["RN"])
                        recip(RN[:, 8:12], RN[:, 0:4], ["RN"], ["RN"])
                        tt(DG[:].rearrange("p (h c) -> p h c", h=4), DG[:].rearrange("p (h c) -> p h c", h=4), bc3(RN[:, 8:12], [128, 4, 64]),
                           ALU.mult, DGK + ["RN"], DGK)
                        tt(DG[:], DG[:], DNG[:], ALU.mult, DGK + ["DNG"], DGK)
                        proj_tm(PC[:, 0:256], tsl, 768, 256, 128, "PC")
                        act(GA[:], PC[:, 0:256], AF.Silu, ["PC"], STK)
                        tt(DG[:], DG[:], GA[:], ALU.mult, DGK + STK, DGK)
                        stq(MIXD[tsl, 0:256], DG[:], DGK, [("MIXD", t, 0)], ("st", "DGc"))
                P.in_region = False

                load_wg(1040, 768)
                P.in_region = ("sgu" == DBG_REGION)
                for t in (range(ntile) if "sgu" not in skip else []):
                    tsl = slice(t * 128, t * 128 + 128)
                    proj_tm(PA[:], tsl, 0, 512, 128, "PA")
                    act(UVG[:], PA[:], AF.Gelu, ["PA"], ["UVG"])
                    act(JNK[:, 0:256], UVG[:, 256:512], AF.Square, ["UVG"], ["MXT", "ST"], accum=ST[:, 0:1])
                    rstd_from_ssq(ST[:, 0:1], ST[:, 2:3], 256.0, "ST")
                    stt(VNS[:], UVG[:, 256:512], ST[:, 2:3], SGNG[:], ALU.mult, ALU.mult, ["UVG", "ST", "SGNG"], ["VNS"])
                    for g in range(4):
                        mm(PB[:, g * 64:g * 64 + 64], SWT[:, g, :], VNS[:, g * 64:g * 64 + 64], ["SWT", "VNS"], ["PB"])
                    proj_tm(PB[:, 256:512], tsl, 512, 256, 128, "PB")
                    act(SGB[:], PB[:, 256:512], AF.Silu, ["PB"], ["SGB"])
                    for g in range(4):
                        stt(YB[:, g * 64:g * 64 + 64], PB[:, g * 64:g * 64 + 64], SGBT[:, g:g + 1], UVG[:, g * 64:g * 64 + 64], ALU.add, ALU.mult,
                            ["PB", "SGBT", "UVG"], ["YB"])
                    tt(YB[:], YB[:], SGB[:], ALU.mult, ["YB", "SGB"], ["YB"])
                    stq(MIXD[t * 128:t * 128 + 128, 256:512], YB[:], ["YB"], [("MIXD", t, 1)], ("st", "YB"))

                P.in_region = False
                load_wg(1808, 512)

                def pool_out(j):
                    typ = 0 if j == 0 else (2 if j == ntile - 1 else 1)
                    for g in range(4):
                        gs = slice(g * 64, g * 64 + 64)
                        terms = []
                        if j > 0:
                            terms.append((3, (j - 1) % 3))
                        terms.append((typ, j % 3))
                        if j < ntile - 1:
                            terms.append((4, (j + 1) % 3))
                        for i, (ty, zi) in enumerate(terms):
                            mm(PC[:, gs], BAND[:, ty, g, :], ZR[zi][:, gs], ["BAND", "ZR%d" % zi], ["PC"], start=(i == 0), stop=(i == len(terms) - 1))
                    proj_tm(PC[:, 256:512], slice(j * 128, j * 128 + 128), 256, 256, 128, "PC")
                    act(SGB[:], PC[:, 256:512], AF.Silu, ["PC"], ["SGB"])
                    tt(YB[:], PC[:, 0:256], SGB[:], ALU.mult, ["PC", "SGB"], ["YB"])
                    stq(MIXD[j * 128:j * 128 + 128, 512:768], YB[:], ["YB"], [("MIXD", j, 2)], ("st", "YB"))

                for t in (range(ntile) if "pool" not in skip else []):
                    tsl = slice(t * 128, t * 128 + 128)
                    for c in range(2):
                        proj_fm(PA[:, c * 128:c * 128 + 128], tsl, c * 128, 128, "PA")
                    cp(XCT[:].rearrange("p c t -> p (c t)"), PA[:, 0:256], ["PA"], ["XCT"], eng="act")
                    for c in range(2):
                        mm(PB[:, c * 128:c * 128 + 128], XCT[:, c, :], PWB[:, c, :], ["XCT", "PWB"], ["PB"])
                    cp(ZR[t % 3][:], PB[:, 0:256], ["PB"], ["ZR%d" % (t % 3)])
                    if t >= 1:
                        pool_out(t - 1)
                if "pool" not in skip:
                    pool_out(ntile - 1)

                load_wg(2320, 512)
                A = T // 64
                ai = 0 if A == 64 else 1
                e2d = cd["c_e2_%d" % A]
                for t in (range(ntile) if "fourier" not in skip else []):
                    tsl = slice(t * 128, t * 128 + 128)
                    for c in range(2):
                        proj_fm(PA[:, c * 128:c * 128 + 128], tsl, c * 128, 128, "PA")
                    cp(XCT[:].rearrange("p c t -> p (c t)"), PA[:, 0:256], ["PA"], ["XCT"], eng="act")
                    for cs in range(2):
                        for c in range(2):
                            mm(PB[:, cs * 256 + c * 128:cs * 256 + c * 128 + 128], XCT[:, c, :], PCS[:, cs, c, :], ["XCT", "PCS"], ["PB"])
                    cp(ZCS[:], PB[:], ["PB"], ["UVG"])
                    P.add("sp", lambda e, t=t: [e.dma_start(out=ZD[cs, t * 128:t * 128 + 128, :], in_=ZCS[:, cs * 256:cs * 256 + 256]) for cs in range(2)],
                          reads=["UVG"], writes=[("ZD", t)], dma=2, sem_key=("st", "UVG"))
                ZDall = [("ZD", t) for t in range(ntile)]
                zdv = ZD.rearrange("s (a b) c -> s a b c", b=64)
                for bb in (range(32) if "fourier" not in skip and "f1" not in skip else []):
                    P.add("sp", lambda e, bb=bb, A=A, zdv=zdv: [e.dma_start(out=ZA[0:A, cs, :].rearrange("a (b c) -> a b c", b=2), in_=zdv[cs, 0:A, bb * 2:bb * 2 + 2, :]) for cs in range(2)],
                          reads=ZDall, writes=["ZA"], dma=2, sem_key=("ld", "ZA"))
                    CA_, SA_, NSA_ = DFTA[0:A, ai, 0, 0:A], DFTA[0:A, ai, 1, 0:A], DFTA[0:A, ai, 2, 0:A]
                    mm(PC[0:A, :], CA_, ZA[0:A, 0, :], ["DFTA", "ZA"], ["PC"], start=True, stop=False)
                    mm(PC[0:A, :], NSA_, ZA[0:A, 1, :], ["DFTA", "ZA"], ["PC"], start=False, stop=True)
                    mm(PD[0:A, :], CA_, ZA[0:A, 1, :], ["DFTA", "ZA"], ["PD"], start=True, stop=False)
                    mm(PD[0:A, :], SA_, ZA[0:A, 0, :], ["DFTA", "ZA"], ["PD"], start=False, stop=True)
                    cp(VV[0:A, 0, :], PC[0:A, :], ["PC"], ["VV"])
                    cp(VV[0:A, 1, :], PD[0:A, :], ["PD"], ["VV"], eng="act")
                    P.add("sp", lambda e, bb=bb, A=A: [e.dma_start(out=VD[ri, 0:A, bb * 2:bb * 2 + 2, :], in_=VV[0:A, ri, :].rearrange("a (b c) -> a b c", b=2)) for ri in range(2)],
                          reads=["VV"], writes=[("VD", bb)], dma=2, sem_key=("st", "VV"))
                VDall = [("VD", bb) for bb in range(32)]
                mixv = MIXD[0:T, :].rearrange("(q a) c -> a q c", a=A)
                for p in (range(A) if "fourier" not in skip and "f2" not in skip else []):
                    P.add("sp", lambda e, p=p: [e.dma_start(out=VB[:, ri, :], in_=VD[ri, p, :, :]) for ri in range(2)],
                          reads=VDall, writes=["VB"], dma=2, sem_key=("ld", "VB"))
                    ld(E2[:], e2d[p], [], ["E2"])
                    mm(PE_[0:64, 0:256], E2[:, 0, :], VB[:, 0, :], ["E2", "VB"], ["PE_"], start=True, stop=False)
                    mm(PE_[0:64, 0:256], E2[:, 1, :], VB[:, 1, :], ["E2", "VB"], ["PE_"], start=False, stop=True)
                    proj_tm(PE_[0:64, 256:512], slice(p, p + A * 63 + 1, A), 256, 256, 64, "PE_")
                    act(GA[0:64, :], PE_[0:64, 256:512], AF.Silu, ["PE_"], ["STMP_00", "STMP_01"])
                    tt(YD, PE_[0:64, 0:256], GA[0:64, :], ALU.mult, ["PE_", "STMP_00", "STMP_01"], ["YB"])
                    stq(mixv[p, :, 768:1024], YD, ["YB"], [("MIXD", "f", p)], ("st", "YB"))

                mix_keys_f = [("MIXD", "f", p) for p in range(A)]
                P.add("pool", lambda e, wo_v=wo_v: e.dma_start(out=WG[:, :, 0:1024], in_=wo_v), reads=[], writes=["WG"], dma=1, sem_key=("ld", "WG"))
                for t in (range(ntile) if "p3" not in skip else []):
                    g0 = tok0 // 128 + t
                    ld(XT2[:], MIXD[t * 128:(t + 1) * 128, :],
                       [("MIXD", t, 0), ("MIXD", t, 1), ("MIXD", t, 2)] + mix_keys_f, ["XT2"])
                    for k in range(8):
                        dst = PA if k < 4 else PB
                        tr(dst[:, (k % 4) * 128:(k % 4) * 128 + 128], XT2[:, k * 128:(k + 1) * 128], IDN[:], ["XT2", "IDN"], ["PA" if k < 4 else "PB"])
                    cp(MXT[:, 0:4, :].rearrange("p k t -> p (k t)"), PA[:], ["PA"], ["MXT"])
                    cp(MXT[:, 4:8, :].rearrange("p k t -> p (k t)"), PB[:], ["PB"], ["MXT"], eng="act")
                    ld(XT[:], XS[g0 * 128:(g0 + 1) * 128, :], [("XS", g0)], ["XT"])
                    for n in range(2):
                        dst, dk = (PC, "PC") if n == 0 else (PD, "PD")
                        for k in range(8):
                            mm(dst[:], MXT[:, k, :], WG[:, k, n * 512:(n + 1) * 512], ["MXT", "WG"], [dk], start=(k == 0), stop=(k == 7))
                        tt(XT2[:, n * 512:(n + 1) * 512], dst[:], GATE[:, cond, n * 512:(n + 1) * 512], ALU.mult, [dk, "GATE"], ["XT2"])
                    tt(XT[:], XT[:], XT2[:], ALU.add, ["XT", "XT2"], ["XT"], eng="pool")
                    stq(XS[g0 * 128:(g0 + 1) * 128, :], XT[:], ["XT"], [("XS", g0)], ("st", "XT"))

        outs = []
        ld(ABG[:], final_norm_g.partition_broadcast(128), [], ["ABG"])
        fin_tiles = []
        for (tok0, T, cond, pidx) in seqs:
            fin_tiles += list(range(tok0 // 128, (tok0 + T) // 128))
        for g0 in fin_tiles:
            ld(XT[:], XS[g0 * 128:(g0 + 1) * 128, :], [("XS", g0)], ["XT"])
            act(JNK, XT[:], AF.Square, ["XT"], ["MXT", "ST"], accum=ST[:, 0:1])
            rstd_from_ssq(ST[:, 0:1], ST[:, 2:3], 1024.0, "ST")
            stt(XT2[:], XT[:], ST[:, 2:3], ABG[:], ALU.mult, ALU.mult, ["XT", "ST", "ABG"], ["XT2"])
            if g0 < 32:
                dst = ys[g0 * 128:(g0 + 1) * 128, :]
            else:
                dst = yp[(g0 - 32) * 128:(g0 - 31) * 128, :]
            stq(dst, XT2[:], ["XT2"], [("Y", g0)], ("st", "XT2"))
            outs.append(("Y", g0))
        for (tok0, T, cond, pidx) in seqs:
            for l in range(nl_run):
                for d in range(2):
                    if pidx is not None:
                        outs.append(("ns", pidx, l, d))
        P.add("sp", None, reads=outs)
        P.build()
    return nc, consts


_CACHE = {}


def kernel(**inputs):
    if "nc" not in _CACHE:
        _CACHE["nc"] = build_program()
    nc, consts = _CACHE["nc"]
    f = lambda a: np.ascontiguousarray(np.asarray(a, dtype=np.float32))
    shared = {k: f(inputs[k]) for k in ["ada_w", "ada_b", "norm_g", "w_in", "conv_qkv", "dn_norm_g", "sgu_norm_g", "sgu_w", "sgu_b",
                                         "pool_w", "pool_scale", "fourier_w", "w_out", "final_norm_g"]}
    shared["a_log"] = f(inputs["a_log"]).reshape(NL, 8)
    shared["dt_bias"] = f(inputs["dt_bias"]).reshape(NL, 8)
    for k, v in consts.items():
        shared[k] = f(v)
    xsm = f(inputs["x_sample"])
    xpr = f(inputs["x_prompt"])
    sdl = f(inputs["state_delta"])
    c = f(inputs["c"])
    cctx = f(inputs["c_ctx"])
    in_maps = []
    for i in range(8):
        m = dict(shared)
        m["xs"] = xsm[i]
        m["xp"] = np.ascontiguousarray(xpr[4 * i:4 * i + 4].reshape(4 * TPR, 1024))
        m["sd"] = sdl[i]
        m["cc"] = np.ascontiguousarray(np.stack([c[i], cctx], axis=0))
        in_maps.append(m)
    res = run_bass_kernel_spmd(nc, in_maps, core_ids=list(range(8)))
    y_sample = np.stack([np.asarray(r["ys"], np.float32) for r in res.results], axis=0)
    y_prompt = np.concatenate([np.asarray(r["yp"], np.float32).reshape(4, TPR, 1024) for r in res.results], axis=0)
    ns = np.concatenate([np.asarray(r["ns"], np.float32) for r in res.results], axis=0)
    return (y_prompt, y_sample, ns)
```

```python
import contextlib
import numpy as np
import concourse.bass as bass
import concourse.mybir as mybir
from concourse.bass_utils import run_bass_kernel_spmd

F32 = mybir.dt.float32
BF16 = mybir.dt.bfloat16
AF = mybir.ActivationFunctionType
ALU = mybir.AluOpType
AX = mybir.AxisListType
ENGS = ("pe", "act", "dve", "pool", "sp")
EPS = 1e-6
NL = 4
TSAMP = 4096
TPR = 256
NTOK = TSAMP + 4 * TPR


PSUM_KEYS = {"PA", "PB", "PC", "PD", "PE_", "PF", "PG", "PT"}


class Prog:
    def __init__(self, nc):
        self.nc = nc
        self.ops = []
        self.last_w = {}
        self.readers = {}

    limit = None
    in_region = False
    region_count = 0

    rec = None

    def add(self, eng, fn, reads=(), writes=(), dma=0, sem_key=None):
        if self.rec is not None:
            self.rec.append((eng, fn, list(reads), list(writes), dma, sem_key))
            return
        if self.in_region and self.limit is not None:
            if self.region_count >= self.limit:
                return
            self.region_count += 1
        i = len(self.ops)
        deps = []
        for r in reads:
            lw = self.last_w.get(r)
            if lw is not None:
                deps.append((lw, "raw"))
            if r in PSUM_KEYS:
                for rd in self.readers.get(r, ()):
                    deps.append((rd, "war"))
        for w in writes:
            lw = self.last_w.get(w)
            if lw is not None:
                deps.append((lw, "waw"))
            for rd in self.readers.get(w, ()):
                deps.append((rd, "war"))
        for r in reads:
            self.readers.setdefault(r, []).append(i)
        for w in writes:
            self.last_w[w] = i
            self.readers[w] = []
        assert (not dma) or sem_key is not None
        self.ops.append(dict(eng=eng, fn=fn, deps=deps, dma=dma, sem_key=sem_key, signal=False))
        return i

    def build(self):
        nc = self.nc
        ops = self.ops
        cnt = {e: 0 for e in ENGS}
        for o in ops:
            o["eidx"] = cnt[o["eng"]]
            cnt[o["eng"]] += 1
        dma_cum = {}
        for o in ops:
            if o["dma"]:
                k = o["sem_key"]
                dma_cum[k] = dma_cum.get(k, 0) + 16 * o["dma"]
                o["dma_val"] = dma_cum[k]
        known = {e: {f: -1 for f in ENGS} for e in ENGS}
        known_dma = {e: {} for e in ENGS}
        for o in ops:
            e = o["eng"]
            w_eng = {}
            w_dma = {}
            for (d, kind) in o["deps"]:
                D = ops[d]
                if D["dma"]:
                    k = D["sem_key"]
                    if known_dma[e].get(k, 0) >= D["dma_val"]:
                        continue
                    w_dma[k] = max(w_dma.get(k, 0), D["dma_val"])
                else:
                    f = D["eng"]
                    if f == e:
                        if e == "pe" or e == "sp":
                            continue
                        pass
                    if known[e][f] >= D["eidx"]:
                        continue
                    if f not in w_eng or ops[w_eng[f]]["eidx"] < D["eidx"]:
                        w_eng[f] = d
            o["w_eng"] = w_eng
            o["w_dma"] = w_dma
            for f, d in w_eng.items():
                ops[d]["signal"] = True
                known[e][f] = ops[d]["eidx"]
            for k, v in w_dma.items():
                known_dma[e][k] = v
        sig = {e: 0 for e in ENGS}
        for o in ops:
            if o["signal"] and not o["dma"]:
                sig[o["eng"]] += 1
                o["sig_val"] = sig[o["eng"]]
        with contextlib.ExitStack() as st:
            esem = {e: st.enter_context(nc.semaphore("s_" + e)) for e in ENGS}
            dsem = {}
            for k in dma_cum:
                dsem[k] = st.enter_context(nc.semaphore("d%d" % len(dsem)))
            block = st.enter_context(nc.Block())
            per = {e: [o for o in ops if o["eng"] == e] for e in ENGS}

            def emit(engh, lst):
                for o in lst:
                    for f, d in o["w_eng"].items():
                        engh.wait_ge(esem[f], ops[d]["sig_val"])
                    for k, v in o["w_dma"].items():
                        engh.wait_ge(dsem[k], v)
                    if o["fn"] is None:
                        continue
                    ins = o["fn"](engh)
                    if o["dma"]:
                        if not isinstance(ins, (list, tuple)):
                            ins = [ins]
                        assert len(ins) == o["dma"], (len(ins), o["dma"])
                        for x in ins:
                            x.then_inc(dsem[o["sem_key"]], 16)
                    elif o["signal"]:
                        ins.then_inc(esem[o["eng"]], 1)

            @block.sync
            def _(eng):
                emit(eng, per["sp"])

            @block.scalar
            def _(eng):
                emit(eng, per["act"])

            @block.vector
            def _(eng):
                emit(eng, per["dve"])

            @block.gpsimd
            def _(eng):
                emit(eng, per["pool"])

            @block.tensor
            def _(eng):
                emit(eng, per["pe"])


def host_consts():
    c = {}
    c["c_ident"] = np.eye(128, dtype=np.float32)
    r = np.arange(64)[:, None]
    cc = np.arange(64)[None, :]
    su = np.where(cc > r, -1.0, 0.0).astype(np.float32)
    sl = np.where(cc < r, -1.0, 0.0).astype(np.float32)
    mf = np.where(cc >= r, 0.0, -30000.0).astype(np.float32)
    mb = np.where(cc <= r, 0.0, -30000.0).astype(np.float32)
    i64 = np.eye(64, dtype=np.float32)
    t4 = lambda m: m
    blk = lambda b: (r // b == cc // b)
    md8 = blk(8).astype(np.float32)
    moff = [(blk(2 * b) & ~blk(b)).astype(np.float32) for b in (8, 16, 32)]
    c["c_m64"] = np.stack([su, sl, mf, mb, i64, md8] + moff, axis=1).astype(np.float32)
    tri = np.stack([(r <= cc), (r >= cc), np.ones((64, 64), bool)], axis=1).astype(np.float32)
    c["c_tri"] = tri
    T = TSAMP
    rows = T // 64
    rr = np.repeat(np.arange(rows, dtype=np.float32), 64)
    col = np.tile(np.arange(64, dtype=np.float32), rows)
    nf = 256
    freqs = np.power(np.float32(10000.0), -np.arange(nf, dtype=np.float32) / np.float32(nf)).astype(np.float32)
    ar = rr[:, None] * freqs[None]
    ac = col[:, None] * freqs[None]
    c["c_pos"] = np.concatenate([np.sin(ar), np.cos(ar), np.sin(ac), np.cos(ac)], axis=-1).astype(np.float32)
    band = np.zeros((128, 5, 4, 128), np.float32)
    for gi, w in enumerate((2, 4, 8, 16)):
        Tn = 384
        Fm = np.zeros((Tn, Tn), np.float64)
        for t in range(Tn):
            lo = min(max(t - w // 2, 0), Tn)
            hi = min(max(t + w - w // 2, 0), Tn)
            Fm[t, lo:hi] = 1.0 / (hi - lo)
        Fm -= np.eye(Tn)
        band[:, 0, gi, :] = Fm[0:128, 0:128].T
        band[:, 1, gi, :] = Fm[128:256, 128:256].T
        band[:, 2, gi, :] = Fm[256:384, 256:384].T
        band[:, 3, gi, :] = Fm[128:256, 0:128].T
        band[:, 4, gi, :] = Fm[128:256, 256:384].T
    c["c_band"] = band
    k64 = np.arange(64)
    ang = 2 * np.pi * np.outer(k64, k64) / 64.0
    C64 = np.cos(ang)
    S64 = np.sin(ang)
    pad = np.zeros((64, 2, 2, 128), np.float32)
    for half in range(2):
        pad[:, half, 0, half * 64:(half + 1) * 64] = C64
        pad[:, half, 1, half * 64:(half + 1) * 64] = S64
    c["c_dftpad"] = pad
    dA = np.zeros((64, 2, 3, 64), np.float32)
    for ai, A in enumerate((64, 4)):
        a = np.arange(A)
        an = 2 * np.pi * np.outer(a, a) / A
        dA[:A, ai, 0, :A] = np.cos(an)
        dA[:A, ai, 1, :A] = np.sin(an)
        dA[:A, ai, 2, :A] = -np.sin(an)
        Tt = 64 * A
        p = np.arange(A)[:, None, None]
        b = np.arange(64)[None, :, None]
        q = np.arange(64)[None, None, :]
        be = 2 * np.pi * ((b * (p + A * q)) % Tt) / Tt
        nrm = 1.0 / np.sqrt(64.0 * Tt)
        e2 = np.stack([np.cos(be) * nrm, -np.sin(be) * nrm], axis=2).astype(np.float32)
        c["c_e2_%d" % A] = e2
    c["c_dfta"] = dA
    return c


import os
DBG_REGION = os.environ.get("DBG_REGION", "")
DBG_LIMIT = os.environ.get("DBG_LIMIT")


def build_program(nl_run=NL, seq_sel=(0, 1, 2, 3, 4), skip=()):
    nc = bass.Bass("TRN2", target_bir_lowering=False)
    P = Prog(nc)
    P.limit = int(DBG_LIMIT) if DBG_LIMIT else None
    consts = host_consts()

    def din(name, shape):
        return nc.dram_tensor(name, list(shape), F32, kind="ExternalInput").ap()

    xs = din("xs", [TSAMP, 1024])
    xp = din("xp", [4 * TPR, 1024])
    sd = din("sd", [NL, 2, 4, 64, 64])
    ccd = din("cc", [2, 1024])
    ada_w = din("ada_w", [NL, 1024, 3072])
    ada_b = din("ada_b", [NL, 3072])
    norm_g = din("norm_g", [NL, 1024])
    w_in = din("w_in", [NL, 1024, 2832])
    conv_qkv = din("conv_qkv", [NL, 4, 768])
    a_log = din("a_log", [NL, 8])
    dt_bias = din("dt_bias", [NL, 8])
    dn_norm_g = din("dn_norm_g", [NL, 64])
    sgu_norm_g = din("sgu_norm_g", [NL, 256])
    sgu_w = din("sgu_w", [NL, 4, 128, 128])
    sgu_b = din("sgu_b", [NL, 4, 128])
    pool_w = din("pool_w", [NL, 4, 64, 64])
    pool_scale = din("pool_scale", [NL, 256])
    fourier_w = din("fourier_w", [NL, 4, 64, 64])
    w_out = din("w_out", [NL, 1024, 1024])
    final_norm_g = din("final_norm_g", [1024])
    cd = {k: din(k, v.shape) for k, v in consts.items()}

    ys = nc.dram_tensor("ys", [TSAMP, 1024], F32, kind="ExternalOutput").ap()
    yp = nc.dram_tensor("yp", [4 * TPR, 1024], F32, kind="ExternalOutput").ap()
    nsd = nc.dram_tensor("ns", [4, NL, 2, 4, 64, 64], F32, kind="ExternalOutput").ap()

    XS = nc.dram_tensor("XS", [NTOK, 1024], F32, kind="Internal").ap()
    MIXD = nc.dram_tensor("MIXD", [TSAMP, 1024], F32, kind="Internal").ap()
    QKVD = nc.dram_tensor("QKVD", [TSAMP, 768], BF16, kind="Internal").ap()
    OFD = nc.dram_tensor("OFD", [2, TSAMP, 256], F32, kind="Internal").ap()
    ZD = nc.dram_tensor("ZD", [2, TSAMP, 256], F32, kind="Internal").ap()
    VD = nc.dram_tensor("VD", [2, 64, 64, 256], F32, kind="Internal").ap()

    st = contextlib.ExitStack()
    with st:
        def sb(name, shape, dt=F32):
            return st.enter_context(nc.sbuf_tensor(name, list(shape), dt))

        def ps(name, shape, dt=F32):
            return st.enter_context(nc.psum_tensor(name, list(shape), dt))

        HT = sb("HT", [128, 8, TSAMP], BF16)
        WG = sb("WG", [128, 8, 1040], BF16)
        AW = sb("AW", [128, 8, 256], BF16)
        GATE = sb("GATE", [128, 2, 1024])
        XT = sb("XT", [128, 1024])
        XT2 = sb("XT2", [128, 1024])
        XN = sb("XN", [128, 1024], BF16)
        MXT = sb("MXT", [128, 8, 128], BF16)
        JNK = MXT[:].rearrange("p k t -> p (k t)")
        IDN = sb("IDN", [128, 128])
        IDB = sb("IDB", [128, 128], BF16)
        M64 = sb("M64", [128, 9, 64])
        TRI = sb("TRI", [128, 3, 64])
        BAND = sb("BAND", [128, 5, 4, 128])
        DFTPAD = sb("DFTPAD", [64, 2, 2, 128])
        DFTA = sb("DFTA", [64, 2, 3, 64])
        SCB = sb("SCB", [128, 8, 2, 128], BF16)
        TMPL = sb("TMPL", [128, 128])
        CCT = sb("CCT", [128, 16])
        SCF = sb("SCF", [128, 16])
        SCb = sb("SCb", [128, 8, 2], BF16)
        NG = sb("NG", [128, 8])
        AB = sb("AB", [128, 24])
        ABG = sb("ABG", [128, 1024])
        MODT = sb("MODT", [128, 24, 2])
        AM = sb("AM", [128, 8, 2])
        CW = sb("CW", [128, 24])
        ALB = sb("ALB", [128, 8])
        DTB = sb("DTB", [128, 8])
        NEGA = sb("NEGA", [128, 8])
        DNG = sb("DNG", [128, 256])
        SGNG = sb("SGNG", [128, 256])
        SWT = sb("SWT", [128, 4, 128])
        SGBT = sb("SGBT", [128, 4])
        PWB = sb("PWB", [128, 2, 128])
        PSC = sb("PSC", [128, 256])
        FW = sb("FW", [64, 4, 64])
        PCS = sb("PCS", [128, 2, 2, 128])
        ST = sb("ST0", [128, 8])
        TL = 128
        RAW = sb("RAW", [128, 6, TL + 3])
        CS = sb("CS", [128, 6, TL])
        QKVF = sb("QKVF", [128, 1024])
        QKV3 = sb("QKV3", [128, 768], BF16)
        SALL = sb("SALL", [128, 64, 24])
        TB_ = sb("TB_", [128, 32, 4])
        TZ_ = sb("TZ_", [128, 32, 4])
        TG_ = sb("TG_", [128, 32, 4])
        KS = sb("KS", [128, 256], BF16)
        RH = sb("RH", [128, 4, 128], BF16)
        KDEC = sb("KDEC", [128, 256], BF16)
        QDEC = sb("QDEC", [128, 256], BF16)
        TR = sb("TR", [128, 1024], BF16)
        SQ = sb("SQ", [128, 1024])
        XA = sb("XA", [128, 256], BF16)
        XB = sb("XB", [128, 256], BF16)
        COF = sb("COF", [128, 256], BF16)
        SSb = sb("SSb", [128, 256], BF16)
        DG = sb("DG", [128, 256])
        NDG = sb("NDG", [128, 256])
        DTT = sb("DTT", [128, 256])
        Cb = [sb("C_a", [128, 256], BF16), sb("C_b", [128, 256], BF16)]
        Bb = [sb("B_a", [128, 256], BF16), sb("B_b", [128, 256], BF16)]
        Tb = [sb("T_a", [128, 256], BF16), sb("T_b", [128, 256], BF16)]
        QKT = sb("QKT", [128, 256], BF16)
        MTT = sb("MTT", [128, 256], BF16)
        UU = sb("UU", [128, 256])
        WB_ = sb("WB_", [128, 256], BF16)
        WT = sb("WT", [128, 256], BF16)
        VN = sb("VN", [128, 256], BF16)
        OO = sb("OO", [128, 256])
        SS = sb("SS", [128, 256])
        STMP = sb("STMP", [128, 256])
        GA = STMP
        RN = sb("RN", [128, 16])
        UVG = sb("UVG", [128, 512])
        VNS = sb("VNS", [128, 256])
        SGB = sb("SGB", [128, 256])
        YB = sb("YB", [128, 256])
        XCT = sb("XCT", [128, 2, 128])
        ZR = [sb("ZR%d" % i, [128, 256]) for i in range(3)]
        ZCS = UVG
        ZA = sb("ZA", [64, 2, 512])
        VV = sb("VV", [64, 2, 512])
        VB = sb("VB", [64, 2, 256])
        E2 = sb("E2", [64, 2, 64])
        VB2 = sb("VB2", [64, 2, 256])
        E2_2 = sb("E2_2", [64, 2, 64])
        XCT2 = sb("XCT2", [128, 2, 128])
        UVG2 = sb("UVG2", [128, 512])
        YD = YB[0:64, :]
        if os.environ.get("EXTRA_SB"):
            sb("EXTRA", [128, int(os.environ["EXTRA_SB"]) // 4])
        PA = ps("PA", [128, 512])
        PB = ps("PB", [128, 512])
        PC = ps("PC", [128, 512])
        PD = ps("PD", [128, 512])
        PE_ = ps("PE_", [128, 512])
        PF = ps("PF", [128, 512])
        PG = ps("PG", [128, 512])
        PT = ps("PT", [128, 512])
        PTb = PT[:].bitcast(BF16)

        def ld(out, in_, r, w, q="sp", key=None, n=1, nc_ok=False):
            kw = dict(allow_slow_non_contiguous=True) if nc_ok else {}
            P.add(q, lambda e: e.dma_start(out=out, in_=in_, **kw), reads=r, writes=w, dma=1, sem_key=key or ("ld", w[0]))

        def stq(out, in_, r, w, key, q="sp"):
            P.add(q, lambda e: e.dma_start(out=out, in_=in_), reads=r, writes=w, dma=1, sem_key=(key, q))

        def mm(out, lhsT, rhs, r, w, start=True, stop=True):
            P.add("pe", lambda e: e.matmul(out, lhsT=lhsT, rhs=rhs, start=start, stop=stop), reads=r, writes=w)

        def tr(out, in_, ident, r, w):
            P.add("pe", lambda e: e.transpose(out=out, in_=in_, identity=ident), reads=r, writes=w)

        def act(out, in_, func, r, w, bias=None, scale=None, accum=None):
            kw = {}
            if bias is not None:
                kw["bias"] = bias
            if scale is not None:
                kw["scale"] = scale
            if accum is not None:
                kw["accum_out"] = accum
            P.add("act", lambda e: e.activation(out=out, in_=in_, func=func, **kw), reads=r, writes=w)

        def tt(out, in0, in1, op, r, w, eng="dve"):
            P.add(eng, lambda e: e.tensor_tensor(out=out, in0=in0, in1=in1, op=op), reads=r, writes=w)

        def ts(out, in0, s1, s2, op0, op1, r, w, eng="dve"):
            if op1 is None:
                P.add(eng, lambda e: e.tensor_scalar(out=out, in0=in0, scalar1=s1, scalar2=None, op0=op0), reads=r, writes=w)
            else:
                P.add(eng, lambda e: e.tensor_scalar(out=out, in0=in0, scalar1=s1, scalar2=s2, op0=op0, op1=op1), reads=r, writes=w)

        def stt(out, in0, scalar, in1, op0, op1, r, w):
            P.add("dve", lambda e: e.scalar_tensor_tensor(out=out, in0=in0, scalar=scalar, in1=in1, op0=op0, op1=op1), reads=r, writes=w)

        def cp(out, in_, r, w, eng="dve"):
            if eng == "act":
                P.add(eng, lambda e: e.activation(out=out, in_=in_, func=AF.Identity), reads=r, writes=w)
            else:
                P.add(eng, lambda e: e.tensor_copy(out=out, in_=in_), reads=r, writes=w)

        def recip(out, in_, r, w):
            P.add("dve", lambda e: e.reciprocal(out=out, in_=in_), reads=r, writes=w)

        def rstd_from_ssq(ssq_ap, out_ap, n, key):
            ts(ssq_ap, ssq_ap, 1.0 / n, EPS, ALU.mult, ALU.add, [key], [key])
            act(ssq_ap, ssq_ap, AF.Sqrt, [key], [key])
            recip(out_ap, ssq_ap, [key], [key])

        def load_T(dst_ap, src_ap, n, dkey):
            ld(TMPL[0:n, :], src_ap, [], ["TMPL"])
            tr(PG[:, 0:n], TMPL[0:n, :], IDN[0:n, 0:n], ["TMPL", "IDN"], ["PG"])
            cp(dst_ap, PG[:, 0:n], ["PG"], [dkey])

        def bc3(ap, shape):
            return ap.unsqueeze(2).to_broadcast(shape)

        ld(IDN[:], cd["c_ident"], [], ["IDN"])
        cp(IDB[:], IDN[:], ["IDN"], ["IDB"])
        for hf in range(2):
            ld(M64[hf * 64:hf * 64 + 64], cd["c_m64"], [], ["M64"])
            ld(TRI[hf * 64:hf * 64 + 64], cd["c_tri"], [], ["TRI"])
        ld(BAND[:], cd["c_band"], [], ["BAND"])
        ld(DFTPAD[:], cd["c_dftpad"], [], ["DFTPAD"])
        ld(DFTA[:], cd["c_dfta"], [], ["DFTA"])
        def masks(hf):
            psl = slice(hf * 64, hf * 64 + 64)
            bh = lambda i: M64[psl, i, :].unsqueeze(1).to_broadcast([64, 2, 64])
            return dict(SUL=(bh(0), bh(1)), MINC=(bh(2), bh(3)), ID4=bh(4), MD8=bh(5), MOFF=[bh(6), bh(7), bh(8)])
        MSK = [masks(0), masks(1)]

        load_T(CCT[:], ccd.rearrange("c (k p) -> (c k) p", p=128), 16, "CCT")
        act(SCF[:], CCT[:], AF.Silu, ["CCT"], ["SCF"])
        cp(SCb[:].rearrange("p k c -> p c k"), SCF[:].rearrange("p (c k) -> p c k", c=2), ["SCF"], ["SCb"])
        cp(SCB[:], SCb[:].unsqueeze(3).to_broadcast([128, 8, 2, 128]), ["SCb"], ["SCB"])

        for t in (range(TSAMP // 128) if 0 in seq_sel else []):
            ld(XT[:], xs[t * 128:(t + 1) * 128, :], [], ["XT"])
            ld(XT2[:], cd["c_pos"][t * 128:(t + 1) * 128, :], [], ["XT2"])
            tt(XT[:], XT[:], XT2[:], ALU.add, ["XT", "XT2"], ["XT"])
            stq(XS[t * 128:(t + 1) * 128, :], XT[:], ["XT"], [("XS", t)], ("st", "XT"))
        for t in range(8):
            if (1 + t // 2) in seq_sel:
                ld(XT[:], xp[t * 128:(t + 1) * 128, :], [], ["XT"])
                stq(XS[TSAMP + t * 128:TSAMP + (t + 1) * 128, :], XT[:], ["XT"], [("XS", 32 + t)], ("st", "XT"))

        seqs = [(0, TSAMP, 0, None)] + [(TSAMP + i * TPR, TPR, 1, i) for i in range(4)]
        seqs = [seqs[i] for i in seq_sel]

        for l in range(nl_run):
            load_T(NG[:], norm_g[l].rearrange("(k p) -> k p", p=128), 8, "NG")
            load_T(AB[:], ada_b[l].rearrange("(k p) -> k p", p=128), 24, "AB")
            load_T(CW[:], conv_qkv[l].rearrange("j (c p) -> (j c) p", p=128), 24, "CW")
            ld(ABG[:], ada_b[l, 2048:3072].partition_broadcast(128), [], ["ABG"])
            ld(ALB[:], a_log[l].partition_broadcast(128), [], ["ALB"])
            ld(DTB[:], dt_bias[l].partition_broadcast(128), [], ["DTB"])
            act(NEGA[:], ALB[:], AF.Exp, ["ALB"], ["NEGA"])
            ts(NEGA[:], NEGA[:], -1.0, None, ALU.mult, None, ["NEGA"], ["NEGA"])
            for h in range(4):
                ld(DNG[:, h * 64:(h + 1) * 64], dn_norm_g[l].partition_broadcast(128), [], ["DNG"], key=("ld", "DNG"))
            ld(SGNG[:], sgu_norm_g[l].partition_broadcast(128), [], ["SGNG"])
            ld(PSC[:], pool_scale[l].partition_broadcast(128), [], ["PSC"])
            load_T(SGBT[:], sgu_b[l], 4, "SGBT")
            for g in range(4):
                ld(TMPL[:], sgu_w[l, g], [], ["TMPL"])
                tr(PG[:, 0:128], TMPL[:], IDN[:], ["TMPL", "IDN"], ["PG"])
                cp(SWT[:, g, :], PG[:, 0:128], ["PG"], ["SWT"])
            P.add("pool", lambda e: e.memset(PWB[:], 0.0), reads=[], writes=["PWB"])
            for g in range(4):
                hb = (g % 2) * 64
                ld(PWB[hb:hb + 64, g // 2, hb:hb + 64], pool_w[l, g], [], ["PWB"], key=("ld", "PWB"))
            for c in range(2):
                tt(PWB[:, c, :], PWB[:, c, :], PSC[:, c * 128:(c + 1) * 128], ALU.mult, ["PWB", "PSC"], ["PWB"])
            ld(FW[:], fourier_w[l].rearrange("g c d -> c g d"), [], ["FW"])
            for g in range(4):
                for cs in range(2):
                    mm(PG[:, 0:64], DFTPAD[:, g % 2, cs, :], FW[:, g, :], ["DFTPAD", "FW"], ["PG"])
                    cp(PCS[:, cs, g // 2, (g % 2) * 64:(g % 2) * 64 + 64], PG[:, 0:64], ["PG"], ["PCS"])
            wo_v = w_out[l].rearrange("(k p) n -> p k n", p=128)
            aw_v = ada_w[l].rearrange("(k p) n -> p k n", p=128)
            for n in range(12):
                P.add("pool", lambda e, n=n, aw_v=aw_v: e.dma_start(out=AW[:], in_=aw_v[:, :, n * 256:(n + 1) * 256]),
                      reads=[], writes=["AW"], dma=1, sem_key=("ld", "AW"))
                for c in range(2):
                    for k in range(8):
                        mm(PG[:, 0:2], AW[:, k, c * 128:(c + 1) * 128], SCb[:, k, :], ["AW", "SCb"], ["PG"], start=(k == 0), stop=(k == 7))
                    ts(MODT[:, n * 2 + c, :], PG[:, 0:2], AB[:, n * 2 + c:n * 2 + c + 1], None, ALU.add, None, ["PG", "AB"], ["MODT"])
                if n >= 8:
                    for cond in range(2):
                        for k in range(8):
                            mm(PA[:, 0:256], SCB[:, k, cond, :], AW[:, k, :], ["SCB", "AW"], ["PA"], start=(k == 0), stop=(k == 7))
                        tt(GATE[:, cond, (n - 8) * 256:(n - 7) * 256], PA[:, 0:256], ABG[:, (n - 8) * 256:(n - 7) * 256], ALU.add,
                           ["PA", "ABG"], ["GATE"])
            ts(AM[:], MODT[:, 8:16, :], 1.0, None, ALU.add, None, ["MODT"], ["AM"])
            tt(AM[:], AM[:], bc3(NG[:], [128, 8, 2]), ALU.mult, ["AM", "NG"], ["AM"])

            win_v = w_in[l].rearrange("(k p) n -> p k n", p=128)

            def load_wg(c0, ncol):
                P.add("pool", lambda e, wv=win_v: e.dma_start(out=WG[:, :, 0:ncol], in_=wv[:, :, c0:c0 + ncol]),
                      reads=[], writes=["WG"], dma=1, sem_key=("ld", "WG"))

            for (tok0, T, cond, pidx) in seqs:
                ntile = T // 128
                CSj = CS[:].rearrange("p a b -> p (a b)").bitcast(BF16)[:, 0:1024]
                for t in range(ntile):
                    g0 = tok0 // 128 + t
                    xt, xtk = ((XT, "XT"), (XT2, "XT2"))[t % 2]
                    xn, xnk = ((XN[:], "XN"), (MXT[:].rearrange("p k t -> p (k t)"), "MXT"))[t % 2]
                    ptb, ptk = ((PTb, "PT"), (PG[:].bitcast(BF16), "PG"))[t % 2]
                    stc = (t % 2) * 4
                    ld(xt[:], XS[g0 * 128:(g0 + 1) * 128, :], [("XS", g0)], [xtk])
                    act(CSj, xt[:], AF.Square, [xtk], ["CS", "ST%d" % (t % 2)], accum=ST[:, stc:stc + 1])
                    rstd_from_ssq(ST[:, stc:stc + 1], ST[:, stc + 2:stc + 3], 1024.0, "ST%d" % (t % 2))
                    ts(xn, xt[:], ST[:, stc + 2:stc + 3], None, ALU.mult, None, [xtk, "ST%d" % (t % 2)], [xnk])
                    for k in range(8):
                        tr(ptb[:, k * 128:(k + 1) * 128], xn[:, k * 128:(k + 1) * 128], IDB[:], [xnk, "IDB"], [ptk])
                    for k in range(8):
                        act(HT[:, k, t * 128:(t + 1) * 128], ptb[:, k * 128:(k + 1) * 128], AF.Identity, [ptk, "AM", "MODT"],
                            [("HT", t)], bias=MODT[:, k, cond:cond + 1], scale=AM[:, k, cond:cond + 1])
                HTall = [("HT", t) for t in range(ntile)]

                def proj_tm(out_ps, tok_sl, c0, ncol, M, okey):
                    for k in range(8):
                        mm(out_ps, HT[:, k, tok_sl], WG[:, k, c0:c0 + ncol], HTall + ["WG"], [okey], start=(k == 0), stop=(k == 7))

                def proj_fm(out_ps, tok_sl, c0, ncol, okey, start=True):
                    for k in range(8):
                        mm(out_ps, WG[:, k, c0:c0 + ncol], HT[:, k, tok_sl], HTall + ["WG"], [okey], start=(k == 0), stop=(k == 7))

                load_wg(0, 1040)
                nch = T // 64
                P.in_region = ("dn" == DBG_REGION)
                dn_on = "dn" not in skip
                for tl in (range(T // TL) if dn_on else []):
                    s0 = tl * TL
                    for c in range(6):
                        proj_fm(PA[:, 0:TL], slice(s0, s0 + TL), c * 128, 128, "PA")
                        cp(RAW[:, c, 2:2 + TL], PA[:, 0:TL], ["PA"], ["RAW"], eng="act")
                        if s0 > 0:
                            proj_fm(PB[:, 0:2], slice(s0 - 2, s0), c * 128, 128, "PB")
                            cp(RAW[:, c, 0:2], PB[:, 0:2], ["PB"], ["RAW"])
                        else:
                            P.add("pool", lambda e, c=c: e.memset(RAW[:, c, 0:2], 0.0), reads=[], writes=["RAW"])
                        if s0 + TL < T:
                            proj_fm(PB[:, 2:4], slice(s0 + TL, s0 + TL + 2), c * 128, 128, "PB")
                            cp(RAW[:, c, 2 + TL:3 + TL], PB[:, 2:3], ["PB"], ["RAW"])
                        else:
                            P.add("pool", lambda e, c=c: e.memset(RAW[:, c, 2 + TL:3 + TL], 0.0), reads=[], writes=["RAW"])
                        ts(CS[:, c, :], RAW[:, c, 0:TL], CW[:, 0 * 6 + c:0 * 6 + c + 1], None, ALU.mult, None, ["RAW", "CW"], ["CS"])
                        for j in range(1, 4):
                            stt(CS[:, c, :], RAW[:, c, j:j + TL], CW[:, j * 6 + c:j * 6 + c + 1], CS[:, c, :], ALU.mult, ALU.add,
                                ["RAW", "CW", "CS"], ["CS"])
                        act(CS[:, c, :], CS[:, c, :], AF.Silu, ["CS"], ["CS"])
                    for c in range(6):
                        dst = (PC if c < 4 else PD)
                        off = (c % 4) * 128
                        tr(dst[:, off:off + 128], CS[:, c, :], IDN[:], ["CS", "IDN"], ["PC" if c < 4 else "PD"])
                    cp(QKVF[:, 0:512], PC[:, :], ["PC"], ["QKVF"])
                    cp(QKVF[:, 512:768], PD[:, 0:256], ["PD"], ["QKVF"], eng="act")
                    tt(SQ[:, 0:512], QKVF[:, 0:512], QKVF[:, 0:512], ALU.mult, ["QKVF"], ["SQ"])
                    P.add("dve", lambda e: e.tensor_reduce(out=RN[:, 0:8], in_=SQ[:, 0:512].rearrange("p (h c) -> p h c", h=8), axis=AX.X, op=ALU.add),
                          reads=["SQ"], writes=["RN"])
                    ts(RN[:, 0:8], RN[:, 0:8], EPS, None, ALU.add, None, ["RN"], ["RN"])
                    act(RN[:, 0:8], RN[:, 0:8], AF.Sqrt, ["RN"], ["RN"])
                    recip(RN[:, 8:16], RN[:, 0:8], ["RN"], ["RN"])
                    ts(RN[:, 8:12], RN[:, 8:12], 0.125, None, ALU.mult, None, ["RN"], ["RN"])
                    tt(QKV3[:, 0:512].rearrange("p (h c) -> p h c", h=8), QKVF[:, 0:512].rearrange("p (h c) -> p h c", h=8),
                       bc3(RN[:, 8:16], [128, 8, 64]), ALU.mult, ["QKVF", "RN"], ["QKV3_00", "QKV3_01", "QKV3_10", "QKV3_11"])
                    cp(QKV3[:, 512:768], QKVF[:, 512:768], ["QKVF"], ["QKV3_00", "QKV3_01", "QKV3_10", "QKV3_11"], eng="pool")
                    stq(QKVD[s0:s0 + 128, :], QKV3[:], ["QKV3_00", "QKV3_01", "QKV3_10", "QKV3_11"], [("QKVD", tl)], ("st", "QKV3"))

                def dn_scal_all(d):
                    hf = d
                    psl = slice(hf * 64, hf * 64 + 64)
                    sx = "_%d" % hf
                    bk, bkk = (PA, "PA") if hf == 0 else (PB, "PB")
                    bk2, bkk2 = (PC, "PC") if hf == 0 else (PD, "PD")
                    for g0 in range(0, nch, 32):
                        n = min(32, nch - g0)
                        for j in range(n):
                            ch = g0 + j
                            proj_tm(bk[psl, j * 16:(j + 1) * 16], slice(ch * 64, ch * 64 + 64), 1024, 16, 64, bkk)
                        pv = bk[psl, 0:n * 16].rearrange("p (n c) -> p n c", c=16)
                        TB, TZ, TG = TB_[psl, 0:n, :], TZ_[psl, 0:n, :], TG_[psl, 0:n, :]
                        SA = lambda o: SALL[psl, g0:g0 + n, o:o + 4]
                        kk = ["TB_" + sx]
                        act(TB, pv[:, :, d * 4:d * 4 + 4], AF.Sigmoid, [bkk], kk)
                        act(SA(0), TB, AF.Sqrt, kk, ["SALL" + sx])
                        tt(TZ, pv[:, :, 8 + d * 4:12 + d * 4], DTB[psl, d * 4:d * 4 + 4].unsqueeze(1).to_broadcast([64, n, 4]), ALU.add, [bkk, "DTB"], kk)
                        act(TZ, TZ, AF.Exp, kk, kk)
                        act(TZ, TZ, AF.Ln, kk, kk, bias=1.0)
                        tt(TG, TZ, NEGA[psl, d * 4:d * 4 + 4].unsqueeze(1).to_broadcast([64, n, 4]), ALU.mult, kk + ["NEGA"], kk)
                        gflat = TG_[psl, 0:n, :].rearrange("p n c -> p (n c)")
                        mm(bk2[psl, 0:n * 4], TRI[psl, d, :], gflat, ["TRI"] + kk, [bkk2])
                        mm(bk2[psl, 128:128 + n * 4], TRI[psl, 2, :], gflat, ["TRI"] + kk, [bkk2])
                        gcv = bk2[psl, 0:n * 4].rearrange("p (n c) -> p n c", c=4)
                        glv = bk2[psl, 128:128 + n * 4].rearrange("p (n c) -> p n c", c=4)
                        cp(SA(20), gcv, [bkk2], ["SALL" + sx])
                        act(SA(8), gcv, AF.Exp, [bkk2], ["SALL" + sx])
                        tt(TZ, glv, SA(20), ALU.subtract, [bkk2, "SALL" + sx], kk)
                        act(SA(12), TZ, AF.Exp, kk, ["SALL" + sx])
                        act(SA(16), glv, AF.Exp, [bkk2], ["SALL" + sx])
                        tt(SA(4), SA(0), SA(8), ALU.mult, ["SALL" + sx], ["SALL" + sx])

                def dn_unit(ch, d, hp):
                    hf = d
                    psl = slice(hf * 64, hf * 64 + 64)
                    sx = "_%d" % hf
                    sy = "_%d%d" % (hf, hp)
                    K_ = lambda *n: [x + sy for x in n]
                    bx, by = [[(PC, PD), (PF, PG)], [(PA, PB), (PE_, PT)]][hf][hp]
                    kx, ky = [[("PC", "PD"), ("PF", "PG")], [("PA", "PB"), ("PE_", "PT")]][hf][hp]
                    M = MSK[hf]
                    v2 = lambda ap: ap.rearrange("p (h c) -> p h c", h=2)
                    sh = [64, 2, 64]
                    cs_ = slice(hp * 128, hp * 128 + 128)
                    SAk = "SALL" + sx
                    SA = lambda o: SALL[psl, ch, o + 2 * hp:o + 2 * hp + 2]
                    ld(QKV3[psl, :].rearrange("p (t c) -> p t c", t=3)[:, :, hp * 128:hp * 128 + 128],
                       QKVD[ch * 64:ch * 64 + 64, :].rearrange("p (t c) -> p t c", t=3)[:, :, hp * 128:hp * 128 + 128],
                       [("QKVD", ch // 2)], ["QKV3" + sy], key=("ld", "QKV3" + sy))
                    Q = QKV3[psl, 0 + hp * 128:128 + hp * 128]
                    K = QKV3[psl, 256 + hp * 128:384 + hp * 128]
                    V = QKV3[psl, 512 + hp * 128:640 + hp * 128]
                    QK3 = ["QKV3" + sy]
                    tt(v2(KS[psl, cs_]), v2(K), bc3(SA(0), sh), ALU.mult, QK3 + [SAk], K_("KS"))
                    tt(RH[psl, 2 * hp:2 * hp + 2, 0:64], v2(V), bc3(SA(0), sh), ALU.mult, QK3 + [SAk], K_("RH"), eng="pool")
                    tt(RH[psl, 2 * hp:2 * hp + 2, 64:128], v2(K), bc3(SA(4), sh), ALU.mult, QK3 + [SAk], K_("RH"))
                    tt(v2(KDEC[psl, cs_]), v2(K), bc3(SA(12), sh), ALU.mult, QK3 + [SAk], K_("KDEC"), eng="pool")
                    tt(v2(QDEC[psl, cs_]), v2(Q), bc3(SA(8), sh), ALU.mult, QK3 + [SAk], K_("QDEC"))
                    I64 = IDB[psl, hf * 64:hf * 64 + 64]
                    bxb = bx[psl, 0:256].bitcast(BF16)
                    c0 = hp * 128
                    for h in range(2):
                        hs = slice(c0 + h * 64, c0 + h * 64 + 64)
                        hq = slice(h * 64, h * 64 + 64)
                        tr(bxb[:, h * 64:h * 64 + 64], KS[psl, hs], I64, K_("KS") + ["IDB"], [kx])
                        tr(bxb[:, 128 + h * 64:192 + h * 64], K[:, hq], I64, QK3 + ["IDB"], [kx])
                        tr(bxb[:, 256 + h * 64:320 + h * 64], Q[:, hq], I64, QK3 + ["IDB"], [kx])
                        tr(bxb[:, 384 + h * 64:448 + h * 64], QDEC[psl, hs], I64, K_("QDEC") + ["IDB"], [kx])
                    TRs = TR[psl, hp * 512:hp * 512 + 512]
                    cp(TRs, bxb, [kx], K_("TR"), eng="act")
                    KST = lambda h: TRs[:, h * 64:h * 64 + 64]
                    KT = lambda h: TRs[:, 128 + h * 64:192 + h * 64]
                    QT = lambda h: TRs[:, 256 + h * 64:320 + h * 64]
                    QDT = lambda h: TRs[:, 384 + h * 64:448 + h * 64]
                    for h in range(2):
                        mm(by[psl, h * 64:h * 64 + 64], KST(h), KST(h), K_("TR"), [ky])
                        mm(by[psl, 128 + h * 64:192 + h * 64], KT(h), QT(h), K_("TR"), [ky])
                    tt(v2(DG[psl, cs_]), M["ID4"], bc3(SA(20), sh), ALU.mult, ["M64", SAk], K_("DG"), eng="pool")
                    act(NDG[psl, cs_], DG[psl, cs_], AF.Identity, K_("DG"), K_("NDG"), scale=-1.0)
                    for h in range(2):
                        hs = slice(c0 + h * 64, c0 + h * 64 + 64)
                        mm(by[psl, 256 + h * 64:320 + h * 64], TRI[psl, 2, :], DG[psl, hs], ["TRI"] + K_("DG"), [ky], start=True, stop=False)
                        mm(by[psl, 256 + h * 64:320 + h * 64], NDG[psl, hs], TRI[psl, 2, :], ["TRI"] + K_("NDG"), [ky], start=False, stop=True)
                    tt(v2(DTT[psl, cs_]), v2(by[psl, 256:384]), M["MINC"][d], ALU.add, [ky, "M64"], K_("DTT"))
                    act(DTT[psl, cs_], DTT[psl, cs_], AF.Exp, K_("DTT"), K_("DTT"))
                    C0_, B0_ = Cb[0], Bb[0]
                    CD, BD = Cb[1], Bb[1]
                    TT_, TN_ = Tb[0], Tb[1]
                    tt(v2(C0_[psl, cs_]), v2(by[psl, 0:128]), M["SUL"][d], ALU.mult, [ky, "M64"], K_("C_a"))
                    tt(v2(B0_[psl, cs_]), v2(by[psl, 0:128]), M["SUL"][1 - d], ALU.mult, [ky, "M64"], K_("B_a"))
                    tt(QKT[psl, cs_], by[psl, 128:256], DTT[psl, cs_], ALU.mult, [ky] + K_("DTT"), K_("QKT"))
                    tt(v2(CD[psl, cs_]), v2(C0_[psl, cs_]), M["MD8"], ALU.mult, K_("C_a") + ["M64"], K_("C_b"), eng="pool")
                    tt(v2(BD[psl, cs_]), v2(B0_[psl, cs_]), M["MD8"], ALU.mult, K_("B_a") + ["M64"], K_("B_b"), eng="pool")
                    tt(v2(TT_[psl, cs_]), v2(CD[psl, cs_]), M["ID4"], ALU.add, K_("C_b") + ["M64"], K_("T_a"), eng="pool")
                    tt(v2(TN_[psl, cs_]), v2(BD[psl, cs_]), M["ID4"], ALU.add, K_("B_b") + ["M64"], K_("T_b"), eng="pool")

                    def grp(dst, dk, off, lt, lk, rt, rk):
                        for h in range(2):
                            hs = slice(c0 + h * 64, c0 + h * 64 + 64)
                            mm(dst[psl, off + h * 64:off + h * 64 + 64], lt[psl, hs], rt[psl, hs], K_(lk, rk), [dk])

                    for lev in range(2):
                        grp(bx, kx, 0, CD, "C_b", BD, "B_b")
                        grp(bx, kx, 128, BD, "B_b", CD, "C_b")
                        cp(BD[psl, cs_], bx[psl, 0:128], [kx], K_("B_b"), eng="act")
                        cp(CD[psl, cs_], bx[psl, 128:256], [kx], K_("C_b"), eng="act")
                        grp(by, ky, 0, BD, "B_b", TT_, "T_a")
                        grp(by, ky, 128, CD, "C_b", TN_, "T_b")
                        tt(TT_[psl, cs_], TT_[psl, cs_], by[psl, 0:128], ALU.add, K_("T_a") + [ky], K_("T_a"))
                        tt(TN_[psl, cs_], TN_[psl, cs_], by[psl, 128:256], ALU.add, K_("T_b") + [ky], K_("T_b"))
                    BOF = VN
                    for li in range(3):
                        last = (li == 2)
                        tt(v2(BOF[psl, cs_]), v2(B0_[psl, cs_]), M["MOFF"][li], ALU.mult, K_("B_a") + ["M64"], K_("VN"), eng="pool")
                        grp(bx, kx, 0, BOF, "VN", TT_, "T_a")
                        cp(XA[psl, cs_], bx[psl, 0:128], [kx], K_("XA"), eng="act")
                        if not last:
                            tt(v2(COF[psl, cs_]), v2(C0_[psl, cs_]), M["MOFF"][li], ALU.mult, K_("C_a") + ["M64"], K_("COF"), eng="pool")
                            grp(bx, kx, 128, COF, "COF", TN_, "T_b")
                            cp(XB[psl, cs_], bx[psl, 128:256], [kx], K_("XB"), eng="act")
                        grp(by, ky, 0, TN_, "T_b", XA, "XA")
                        if not last:
                            grp(by, ky, 128, TT_, "T_a", XB, "XB")
                        tt(TT_[psl, cs_], TT_[psl, cs_], by[psl, 0:128], ALU.add, K_("T_a") + [ky], K_("T_a"))
                        if not last:
                            tt(TN_[psl, cs_], TN_[psl, cs_], by[psl, 128:256], ALU.add, K_("T_b") + [ky], K_("T_b"))
                    tt(MTT[psl, cs_], TT_[psl, cs_], DTT[psl, cs_], ALU.mult, K_("T_a", "DTT"), K_("MTT"))
                    for h in range(2):
                        hs = slice(c0 + h * 64, c0 + h * 64 + 64)
                        mm(by[psl, h * 128:(h + 1) * 128], MTT[psl, hs], RH[psl, 2 * hp + h, :], K_("MTT", "RH"), [ky])
                    byv = by[psl, 0:256].rearrange("p (h c) -> p h c", h=2)
                    tt(v2(UU[psl, cs_]), byv[:, :, 0:64], bc3(SA(0), sh), ALU.mult, [ky, SAk], K_("UU"))
                    tt(v2(WB_[psl, cs_]), byv[:, :, 64:128], bc3(SA(0), sh), ALU.mult, [ky, SAk], K_("WB_"))
                    bxw = bx[psl, 0:64].bitcast(BF16)
                    for h in range(2):
                        hs = slice(c0 + h * 64, c0 + h * 64 + 64)
                        tr(bxw[:, h * 64:h * 64 + 64], WB_[psl, hs], I64, K_("WB_") + ["IDB"], [kx])
                    cp(WT[psl, cs_], bxw, [kx], K_("WT"), eng="act")
                    for h in range(2):
                        hs = slice(c0 + h * 64, c0 + h * 64 + 64)
                        mm(bx[psl, 128 + h * 64:192 + h * 64], WT[psl, hs], SSb[psl, hs], K_("WT", "SSb"), [kx])
                    tt(VN[psl, cs_], UU[psl, cs_], bx[psl, 128:256], ALU.subtract, K_("UU") + [kx], K_("VN"))
                    for h in range(2):
                        hs = slice(c0 + h * 64, c0 + h * 64 + 64)
                        mm(by[psl, h * 64:h * 64 + 64], QDT(h), SSb[psl, hs], K_("TR", "SSb"), [ky], start=True, stop=False)
                        mm(by[psl, h * 64:h * 64 + 64], QKT[psl, hs], VN[psl, hs], K_("QKT", "VN"), [ky], start=False, stop=True)
                    cp(OO[psl, cs_], by[psl, 0:128], [ky], K_("OO"), eng="act")
                    for h in range(2):
                        hs = slice(c0 + h * 64, c0 + h * 64 + 64)
                        mm(bx[psl, h * 64:h * 64 + 64], KDEC[psl, hs], VN[psl, hs], K_("KDEC", "VN"), [kx])
                    tt(v2(STMP[psl, cs_]), v2(SS[psl, cs_]), bc3(SA(16), sh), ALU.mult, K_("SS") + [SAk], K_("STMP"), eng="pool")
                    tt(SS[psl, cs_], STMP[psl, cs_], bx[psl, 0:128], ALU.add, K_("STMP") + [kx], K_("SS"))
                    cp(SSb[psl, cs_], SS[psl, cs_], K_("SS"), K_("SSb"), eng="pool")
                    stq(OFD[d, ch * 64:ch * 64 + 64, cs_], OO[psl, cs_], K_("OO"), [("OFD", d, ch, hp)], ("st", "OO" + sy))

                def init_S(d):
                    psl = slice(d * 64, d * 64 + 64)
                    sx = "_%d" % d
                    if pidx is None:
                        ld(SS[psl, :].rearrange("p (h c) -> p h c", h=4), sd[l, d].rearrange("h k v -> k h v"), [], ["SS" + sx + "0", "SS" + sx + "1"], key=("ld", "SS" + sx))
                    else:
                        P.add("pool", lambda e: e.memset(SS[psl, :], 0.0), reads=[], writes=["SS" + sx + "0", "SS" + sx + "1"])
                    cp(SSb[psl, :], SS[psl, :], ["SS" + sx + "0", "SS" + sx + "1"], ["SSb" + sx + "0", "SSb" + sx + "1"], eng="pool")

                def store_S(d):
                    psl = slice(d * 64, d * 64 + 64)
                    sx = "_%d" % d
                    if pidx is not None:
                        stq(nsd[pidx, l, d].rearrange("h k v -> k h v"), SS[psl, :].rearrange("p (h c) -> p h c", h=4), ["SS" + sx + "0", "SS" + sx + "1"],
                            [("ns", pidx, l, d)], ("st", "SS" + sx))

                if dn_on:
                    init_S(0)
                    init_S(1)
                    dn_scal_all(0)
                    dn_scal_all(1)
                    streams = []
                    for (dd, hh) in ((0, 0), (1, 0), (0, 1), (1, 1)):
                        P.rec = []
                        for i in range(nch):
                            dn_unit(i if dd == 0 else nch - 1 - i, dd, hh)
                        streams.append(P.rec)
                    P.rec = None
                    ulen = len(streams[0]) // nch
                    offs = [0, ulen // 4, ulen // 2, (3 * ulen) // 4]
                    for j in range(max(len(r) for r in streams) + offs[-1]):
                        for r, o in zip(streams, offs):
                            jj = j - o
                            if 0 <= jj < len(r):
                                P.add(*r[jj][:2], reads=r[jj][2], writes=r[jj][3], dma=r[jj][4], sem_key=r[jj][5])
                    store_S(0)
                    store_S(1)
                    DGK = ["DG_00", "DG_01", "DG_10", "DG_11"]
                    NDGK = ["NDG_00", "NDG_01", "NDG_10", "NDG_11"]
                    STK = ["STMP_00", "STMP_01", "STMP_10", "STMP_11"]
                    for t in range(ntile):
                        tsl = slice(t * 128, t * 128 + 128)
                        ld(DG[:], OFD[0, tsl, :], [("OFD", 0, 2 * t + a, b) for a in range(2) for b in range(2)], DGK, key=("ld", "DGc"))
                        ld(NDG[:], OFD[1, tsl, :], [("OFD", 1, 2 * t + a, b) for a in range(2) for b in range(2)], NDGK, key=("ld", "NDGc"))
                        tt(DG[:], DG[:], NDG[:], ALU.add, DGK + NDGK, DGK)
                        tt(SQ[:, 0:256], DG[:], DG[:], ALU.mult, DGK, ["SQ"])
                        P.add("dve", lambda e: e.tensor_reduce(out=RN[:, 0:4], in_=SQ[:, 0:256].rearrange("p (h c) -> p h c", h=4), axis=AX.X, op=ALU.add),
                              reads=["SQ"], writes=["RN"])
                        ts(RN[:, 0:4], RN[:, 0:4], 1.0 / 64.0, EPS, ALU.mult, ALU.add, ["RN"], ["RN"])
                        act(RN[:, 0:4], RN[:, 0:4], AF.Sqrt, ["RN"], ["RN"])
                        recip(RN[:, 8:12], RN[:, 0:4], ["RN"], ["RN"])
                        tt(DG[:].rearrange("p (h c) -> p h c", h=4), DG[:].rearrange("p (h c) -> p h c", h=4), bc3(RN[:, 8:12], [128, 4, 64]),
                           ALU.mult, DGK + ["RN"], DGK)
                        tt(DG[:], DG[:], DNG[:], ALU.mult, DGK + ["DNG"], DGK)
                        proj_tm(PC[:, 0:256], tsl, 768, 256, 128, "PC")
                        act(GA[:], PC[:, 0:256], AF.Silu, ["PC"], STK)
                        tt(DG[:], DG[:], GA[:], ALU.mult, DGK + STK, DGK)
                        stq(MIXD[tsl, 0:256], DG[:], DGK, [("MIXD", t, 0)], ("st", "DGc"))
                P.in_region = False

                load_wg(1040, 768)
                P.in_region = ("sgu" == DBG_REGION)
                for t in (range(ntile) if "sgu" not in skip else []):
                    tsl = slice(t * 128, t * 128 + 128)
                    proj_tm(PA[:], tsl, 0, 512, 128, "PA")
                    act(UVG[:], PA[:], AF.Gelu, ["PA"], ["UVG"])
                    act(JNK[:, 0:256], UVG[:, 256:512], AF.Square, ["UVG"], ["MXT", "ST0"], accum=ST[:, 0:1])
                    rstd_from_ssq(ST[:, 0:1], ST[:, 2:3], 256.0, "ST0")
                    stt(VNS[:], UVG[:, 256:512], ST[:, 2:3], SGNG[:], ALU.mult, ALU.mult, ["UVG", "ST0", "SGNG"], ["VNS"])
                    for g in range(4):
                        mm(PB[:, g * 64:g * 64 + 64], SWT[:, g, :], VNS[:, g * 64:g * 64 + 64], ["SWT", "VNS"], ["PB"])
                    proj_tm(PB[:, 256:512], tsl, 512, 256, 128, "PB")
                    act(SGB[:], PB[:, 256:512], AF.Silu, ["PB"], ["SGB"])
                    for g in range(4):
                        stt(YB[:, g * 64:g * 64 + 64], PB[:, g * 64:g * 64 + 64], SGBT[:, g:g + 1], UVG[:, g * 64:g * 64 + 64], ALU.add, ALU.mult,
                            ["PB", "SGBT", "UVG"], ["YB"])
                    tt(YB[:], YB[:], SGB[:], ALU.mult, ["YB", "SGB"], ["YB"])
                    stq(MIXD[t * 128:t * 128 + 128, 256:512], YB[:], ["YB"], [("MIXD", t, 1)], ("st", "YB"), q="pool")

                P.in_region = False
                load_wg(1808, 512)

                def pool_out(j):
                    typ = 0 if j == 0 else (2 if j == ntile - 1 else 1)
                    for g in range(4):
                        gs = slice(g * 64, g * 64 + 64)
                        terms = []
                        if j > 0:
                            terms.append((3, (j - 1) % 3))
                        terms.append((typ, j % 3))
                        if j < ntile - 1:
                            terms.append((4, (j + 1) % 3))
                        for i, (ty, zi) in enumerate(terms):
                            mm(PC[:, gs], BAND[:, ty, g, :], ZR[zi][:, gs], ["BAND", "ZR%d" % zi], ["PC"], start=(i == 0), stop=(i == len(terms) - 1))
                    proj_tm(PC[:, 256:512], slice(j * 128, j * 128 + 128), 256, 256, 128, "PC")
                    act(SGB[:], PC[:, 256:512], AF.Silu, ["PC"], ["SGB"])
                    tt(YB[:], PC[:, 0:256], SGB[:], ALU.mult, ["PC", "SGB"], ["YB"])
                    stq(MIXD[j * 128:j * 128 + 128, 512:768], YB[:], ["YB"], [("MIXD", j, 2)], ("st", "YB"), q="pool")

                for t in (range(ntile) if "pool" not in skip else []):
                    tsl = slice(t * 128, t * 128 + 128)
                    for c in range(2):
                        proj_fm(PA[:, c * 128:c * 128 + 128], tsl, c * 128, 128, "PA")
                    cp(XCT[:].rearrange("p c t -> p (c t)"), PA[:, 0:256], ["PA"], ["XCT"], eng="act")
                    for c in range(2):
                        mm(PB[:, c * 128:c * 128 + 128], XCT[:, c, :], PWB[:, c, :], ["XCT", "PWB"], ["PB"])
                    cp(ZR[t % 3][:], PB[:, 0:256], ["PB"], ["ZR%d" % (t % 3)])
                    if t >= 1:
                        pool_out(t - 1)
                if "pool" not in skip:
                    pool_out(ntile - 1)

                load_wg(2320, 512)
                A = T // 64
                ai = 0 if A == 64 else 1
                e2d = cd["c_e2_%d" % A]
                f_on = "fourier" not in skip
                XCTb = [(XCT, "XCT"), (XCT2, "XCT2")]
                ZCSb = [(UVG, "UVG"), (UVG2, "UVG2")]
                for t in (range(ntile) if f_on else []):
                    tsl = slice(t * 128, t * 128 + 128)
                    xc, xk = XCTb[t % 2]
                    zc, zk = ZCSb[t % 2]
                    pa, pak = (PA, "PA") if t % 2 == 0 else (PC, "PC")
                    pb, pbk = (PB, "PB") if t % 2 == 0 else (PD, "PD")
                    for c in range(2):
                        proj_fm(pa[:, c * 128:c * 128 + 128], tsl, c * 128, 128, pak)
                    cp(xc[:].rearrange("p c t -> p (c t)"), pa[:, 0:256], [pak], [xk], eng="act")
                    for cs in range(2):
                        for c in range(2):
                            mm(pb[:, cs * 256 + c * 128:cs * 256 + c * 128 + 128], xc[:, c, :], PCS[:, cs, c, :], [xk, "PCS"], [pbk])
                    cp(zc[:], pb[:], [pbk], [zk])
                    P.add("pool", lambda e, t=t, zc=zc: [e.dma_start(out=ZD[cs, t * 128:t * 128 + 128, :], in_=zc[:, cs * 256:cs * 256 + 256]) for cs in range(2)],
                          reads=[zk], writes=[("ZD", t)], dma=2, sem_key=("st", zk, "pool"))
                ZDall = [("ZD", t) for t in range(ntile)]
                zdv = ZD.rearrange("s (a b) c -> s a b c", b=64)
                ZAb = [(ZA[0:A], "ZA"), (QKVF[0:A, :].rearrange("p (s c) -> p s c", s=2), "QKVF")]
                VVb = [(VV[0:A], "VV"), (SQ[0:A, :].rearrange("p (s c) -> p s c", s=2), "SQ")]
                CA_, SA_, NSA_ = DFTA[0:A, ai, 0, 0:A], DFTA[0:A, ai, 1, 0:A], DFTA[0:A, ai, 2, 0:A]
                for bb in (range(32) if f_on and "f1" not in skip else []):
                    za, zak = ZAb[bb % 2]
                    vv, vvk = VVb[bb % 2]
                    pc, pck = (PE_, "PE_") if bb % 2 == 0 else (PF, "PF")
                    pd, pdk = (PG, "PG") if bb % 2 == 0 else (PT, "PT")
                    P.add("sp", lambda e, bb=bb, A=A, zdv=zdv, za=za: [e.dma_start(out=za[:, cs, :].rearrange("a (b c) -> a b c", b=2), in_=zdv[cs, 0:A, bb * 2:bb * 2 + 2, :]) for cs in range(2)],
                          reads=ZDall, writes=[zak], dma=2, sem_key=("ld", zak))
                    mm(pc[0:A, :], CA_, za[:, 0, :], ["DFTA", zak], [pck], start=True, stop=False)
                    mm(pc[0:A, :], NSA_, za[:, 1, :], ["DFTA", zak], [pck], start=False, stop=True)
                    mm(pd[0:A, :], CA_, za[:, 1, :], ["DFTA", zak], [pdk], start=True, stop=False)
                    mm(pd[0:A, :], SA_, za[:, 0, :], ["DFTA", zak], [pdk], start=False, stop=True)
                    cp(vv[:, 0, :], pc[0:A, :], [pck], [vvk])
                    cp(vv[:, 1, :], pd[0:A, :], [pdk], [vvk], eng="act")
                    P.add("pool", lambda e, bb=bb, A=A, vv=vv: [e.dma_start(out=VD[ri, 0:A, bb * 2:bb * 2 + 2, :], in_=vv[:, ri, :].rearrange("a (b c) -> a b c", b=2)) for ri in range(2)],
                          reads=[vvk], writes=[("VD", bb)], dma=2, sem_key=("st", vvk, "pool"))
                VDall = [("VD", bb) for bb in range(32)]
                mixv = MIXD[0:T, :].rearrange("(q a) c -> a q c", a=A)
                VBb = [(VB, "VB"), (VB2, "VB2")]
                E2b = [(E2, "E2"), (E2_2, "E2_2")]
                GAb = [(STMP[0:64, :], ["STMP_00", "STMP_01"]), (SGB[0:64, :], ["SGB"])]
                YDb = [(YB[0:64, :], "YB"), (VNS[0:64, :], "VNS")]
                for p in (range(A) if f_on and "f2" not in skip else []):
                    vb, vbk = VBb[p % 2]
                    e2, e2k = E2b[p % 2]
                    ga, gak = GAb[p % 2]
                    yd, ydk = YDb[p % 2]
                    pe, pek = (PA, "PA") if p % 2 == 0 else (PB, "PB")
                    pg, pgk = (PC, "PC") if p % 2 == 0 else (PD, "PD")
                    P.add("sp", lambda e, p=p, vb=vb: [e.dma_start(out=vb[:, ri, :], in_=VD[ri, p, :, :]) for ri in range(2)],
                          reads=VDall, writes=[vbk], dma=2, sem_key=("ld", vbk))
                    ld(e2[:], e2d[p], [], [e2k])
                    mm(pe[0:64, 0:256], e2[:, 0, :], vb[:, 0, :], [e2k, vbk], [pek], start=True, stop=False)
                    mm(pe[0:64, 0:256], e2[:, 1, :], vb[:, 1, :], [e2k, vbk], [pek], start=False, stop=True)
                    proj_tm(pg[0:64, 0:256], slice(p, p + A * 63 + 1, A), 256, 256, 64, pgk)
                    act(ga, pg[0:64, 0:256], AF.Silu, [pgk], gak)
                    tt(yd, pe[0:64, 0:256], ga, ALU.mult, [pek] + gak, [ydk])
                    stq(mixv[p, :, 768:1024], yd, [ydk], [("MIXD", "f", p)], ("st", ydk), q="pool")

                mix_keys_f = [("MIXD", "f", p) for p in range(A)]
                P.add("pool", lambda e, wo_v=wo_v: e.dma_start(out=WG[:, :, 0:1024], in_=wo_v), reads=[], writes=["WG"], dma=1, sem_key=("ld", "WG"))
                for t in (range(ntile) if "p3" not in skip else []):
                    g0 = tok0 // 128 + t
                    mx, mxk = ((XT2, "XT2"), (QKVF, "QKVF"))[t % 2]
                    xt, xtk = ((XT, "XT"), (SQ, "SQ"))[t % 2]
                    mt, mtk = ((MXT, "MXT"), (XN[:].rearrange("p (k t) -> p k t", k=8), "XN"))[t % 2]
                    (pa, pak), (pb, pbk) = (((PA, "PA"), (PB, "PB")), ((PE_, "PE_"), (PF, "PF")))[t % 2]
                    pouts = (((PC, "PC"), (PD, "PD")), ((PG, "PG"), (PT, "PT")))[t % 2]
                    ld(mx[:], MIXD[t * 128:(t + 1) * 128, :],
                       [("MIXD", t, 0), ("MIXD", t, 1), ("MIXD", t, 2)] + mix_keys_f, [mxk])
                    for k in range(8):
                        dst, dk = (pa, pak) if k < 4 else (pb, pbk)
                        tr(dst[:, (k % 4) * 128:(k % 4) * 128 + 128], mx[:, k * 128:(k + 1) * 128], IDN[:], [mxk, "IDN"], [dk])
                    cp(mt[:, 0:4, :].rearrange("p k t -> p (k t)"), pa[:], [pak], [mtk])
                    cp(mt[:, 4:8, :].rearrange("p k t -> p (k t)"), pb[:], [pbk], [mtk], eng="act")
                    ld(xt[:], XS[g0 * 128:(g0 + 1) * 128, :], [("XS", g0)], [xtk])
                    for n in range(2):
                        dst, dk = pouts[n]
                        for k in range(8):
                            mm(dst[:], mt[:, k, :], WG[:, k, n * 512:(n + 1) * 512], [mtk, "WG"], [dk], start=(k == 0), stop=(k == 7))
                        tt(mx[:, n * 512:(n + 1) * 512], dst[:], GATE[:, cond, n * 512:(n + 1) * 512], ALU.mult, [dk, "GATE"], [mxk])
                    tt(xt[:], xt[:], mx[:], ALU.add, [xtk, mxk], [xtk], eng="pool")
                    stq(XS[g0 * 128:(g0 + 1) * 128, :], xt[:], [xtk], [("XS", g0)], ("st", xtk), q="pool")

        outs = []
        ld(ABG[:], final_norm_g.partition_broadcast(128), [], ["ABG"])
        fin_tiles = []
        for (tok0, T, cond, pidx) in seqs:
            fin_tiles += list(range(tok0 // 128, (tok0 + T) // 128))
        for g0 in fin_tiles:
            ld(XT[:], XS[g0 * 128:(g0 + 1) * 128, :], [("XS", g0)], ["XT"])
            act(JNK, XT[:], AF.Square, ["XT"], ["MXT", "ST0"], accum=ST[:, 0:1])
            rstd_from_ssq(ST[:, 0:1], ST[:, 2:3], 1024.0, "ST0")
            stt(XT2[:], XT[:], ST[:, 2:3], ABG[:], ALU.mult, ALU.mult, ["XT", "ST0", "ABG"], ["XT2"])
            if g0 < 32:
                dst = ys[g0 * 128:(g0 + 1) * 128, :]
            else:
                dst = yp[(g0 - 32) * 128:(g0 - 31) * 128, :]
            stq(dst, XT2[:], ["XT2"], [("Y", g0)], ("st", "XT2"), q="pool")
            outs.append(("Y", g0))
        for (tok0, T, cond, pidx) in seqs:
            for l in range(nl_run):
                for d in range(2):
                    if pidx is not None:
                        outs.append(("ns", pidx, l, d))
        P.add("sp", None, reads=outs)
        P.build()
    return nc, consts


_CACHE = {}


def kernel(**inputs):
    if "nc" not in _CACHE:
        _CACHE["nc"] = build_program()
    nc, consts = _CACHE["nc"]
    f = lambda a: np.ascontiguousarray(np.asarray(a, dtype=np.float32))
    shared = {k: f(inputs[k]) for k in ["ada_w", "ada_b", "norm_g", "w_in", "conv_qkv", "dn_norm_g", "sgu_norm_g", "sgu_w", "sgu_b",
                                         "pool_w", "pool_scale", "fourier_w", "w_out", "final_norm_g"]}
    shared["a_log"] = f(inputs["a_log"]).reshape(NL, 8)
    shared["dt_bias"] = f(inputs["dt_bias"]).reshape(NL, 8)
    for k, v in consts.items():
        shared[k] = f(v)
    xsm = f(inputs["x_sample"])
    xpr = f(inputs["x_prompt"])
    sdl = f(inputs["state_delta"])
    c = f(inputs["c"])
    cctx = f(inputs["c_ctx"])
    in_maps = []
    for i in range(8):
        m = dict(shared)
        m["xs"] = xsm[i]
        m["xp"] = np.ascontiguousarray(xpr[4 * i:4 * i + 4].reshape(4 * TPR, 1024))
        m["sd"] = sdl[i]
        m["cc"] = np.ascontiguousarray(np.stack([c[i], cctx], axis=0))
        in_maps.append(m)
    res = run_bass_kernel_spmd(nc, in_maps, core_ids=list(range(8)))
    y_sample = np.stack([np.asarray(r["ys"], np.float32) for r in res.results], axis=0)
    y_prompt = np.concatenate([np.asarray(r["yp"], np.float32).reshape(4, TPR, 1024) for r in res.results], axis=0)
    ns = np.concatenate([np.asarray(r["ns"], np.float32) for r in res.results], axis=0)
    return (y_prompt, y_sample, ns)
```

```python
import contextlib
import numpy as np
import concourse.bass as bass
import concourse.mybir as mybir
from concourse.bass_utils import run_bass_kernel_spmd

F32 = mybir.dt.float32
BF16 = mybir.dt.bfloat16
AF = mybir.ActivationFunctionType
ALU = mybir.AluOpType
AX = mybir.AxisListType
ENGS = ("pe", "act", "dve", "pool", "sp")
EPS = 1e-6
NL = 4
TSAMP = 4096
TPR = 256
NTOK = TSAMP + 4 * TPR


PSUM_KEYS = {"PA", "PB", "PC", "PD", "PE_", "PF", "PG", "PT"}


class Prog:
    def __init__(self, nc):
        self.nc = nc
        self.ops = []
        self.last_w = {}
        self.readers = {}

    limit = None
    in_region = False
    region_count = 0

    rec = None

    def add(self, eng, fn, reads=(), writes=(), dma=0, sem_key=None):
        if self.rec is not None:
            self.rec.append((eng, fn, list(reads), list(writes), dma, sem_key))
            return
        if self.in_region and self.limit is not None:
            if self.region_count >= self.limit:
                return
            self.region_count += 1
        i = len(self.ops)
        deps = []
        for r in reads:
            lw = self.last_w.get(r)
            if lw is not None:
                deps.append((lw, "raw"))
            if r in PSUM_KEYS:
                for rd in self.readers.get(r, ()):
                    deps.append((rd, "war"))
        for w in writes:
            lw = self.last_w.get(w)
            if lw is not None:
                deps.append((lw, "waw"))
            for rd in self.readers.get(w, ()):
                deps.append((rd, "war"))
        for r in reads:
            self.readers.setdefault(r, []).append(i)
        for w in writes:
            self.last_w[w] = i
            self.readers[w] = []
        assert (not dma) or sem_key is not None
        self.ops.append(dict(eng=eng, fn=fn, deps=deps, dma=dma, sem_key=sem_key, signal=False))
        return i

    def build(self):
        nc = self.nc
        ops = self.ops
        cnt = {e: 0 for e in ENGS}
        for o in ops:
            o["eidx"] = cnt[o["eng"]]
            cnt[o["eng"]] += 1
        dma_cum = {}
        for o in ops:
            if o["dma"]:
                k = o["sem_key"]
                dma_cum[k] = dma_cum.get(k, 0) + 16 * o["dma"]
                o["dma_val"] = dma_cum[k]
        known = {e: {f: -1 for f in ENGS} for e in ENGS}
        known_dma = {e: {} for e in ENGS}
        for o in ops:
            e = o["eng"]
            w_eng = {}
            w_dma = {}
            for (d, kind) in o["deps"]:
                D = ops[d]
                if D["dma"]:
                    k = D["sem_key"]
                    if known_dma[e].get(k, 0) >= D["dma_val"]:
                        continue
                    w_dma[k] = max(w_dma.get(k, 0), D["dma_val"])
                else:
                    f = D["eng"]
                    if f == e:
                        if e == "pe" or e == "sp":
                            continue
                        pass
                    if known[e][f] >= D["eidx"]:
                        continue
                    if f not in w_eng or ops[w_eng[f]]["eidx"] < D["eidx"]:
                        w_eng[f] = d
            o["w_eng"] = w_eng
            o["w_dma"] = w_dma
            for f, d in w_eng.items():
                ops[d]["signal"] = True
                known[e][f] = ops[d]["eidx"]
            for k, v in w_dma.items():
                known_dma[e][k] = v
        sig = {e: 0 for e in ENGS}
        for o in ops:
            if o["signal"] and not o["dma"]:
                sig[o["eng"]] += 1
                o["sig_val"] = sig[o["eng"]]
        with contextlib.ExitStack() as st:
            esem = {e: st.enter_context(nc.semaphore("s_" + e)) for e in ENGS}
            dsem = {}
            for k in dma_cum:
                dsem[k] = st.enter_context(nc.semaphore("d%d" % len(dsem)))
            block = st.enter_context(nc.Block())
            per = {e: [o for o in ops if o["eng"] == e] for e in ENGS}

            def emit(engh, lst):
                for o in lst:
                    for f, d in o["w_eng"].items():
                        engh.wait_ge(esem[f], ops[d]["sig_val"])
                    for k, v in o["w_dma"].items():
                        engh.wait_ge(dsem[k], v)
                    if o["fn"] is None:
                        continue
                    ins = o["fn"](engh)
                    if o["dma"]:
                        if not isinstance(ins, (list, tuple)):
                            ins = [ins]
                        assert len(ins) == o["dma"], (len(ins), o["dma"])
                        for x in ins:
                            x.then_inc(dsem[o["sem_key"]], 16)
                    elif o["signal"]:
                        ins.then_inc(esem[o["eng"]], 1)

            @block.sync
            def _(eng):
                emit(eng, per["sp"])

            @block.scalar
            def _(eng):
                emit(eng, per["act"])

            @block.vector
            def _(eng):
                emit(eng, per["dve"])

            @block.gpsimd
            def _(eng):
                emit(eng, per["pool"])

            @block.tensor
            def _(eng):
                emit(eng, per["pe"])


def host_consts():
    c = {}
    c["c_ident"] = np.eye(128, dtype=np.float32)
    r = np.arange(64)[:, None]
    cc = np.arange(64)[None, :]
    su = np.where(cc > r, -1.0, 0.0).astype(np.float32)
    sl = np.where(cc < r, -1.0, 0.0).astype(np.float32)
    mf = np.where(cc >= r, 0.0, -30000.0).astype(np.float32)
    mb = np.where(cc <= r, 0.0, -30000.0).astype(np.float32)
    i64 = np.eye(64, dtype=np.float32)
    t4 = lambda m: m
    blk = lambda b: (r // b == cc // b)
    md8 = blk(8).astype(np.float32)
    moff = [(blk(2 * b) & ~blk(b)).astype(np.float32) for b in (8, 16, 32)]
    c["c_m64"] = np.stack([su, sl, mf, mb, i64, md8] + moff, axis=1).astype(np.float32)
    tri = np.stack([(r <= cc), (r >= cc), np.ones((64, 64), bool)], axis=1).astype(np.float32)
    c["c_tri"] = tri
    T = TSAMP
    rows = T // 64
    rr = np.repeat(np.arange(rows, dtype=np.float32), 64)
    col = np.tile(np.arange(64, dtype=np.float32), rows)
    nf = 256
    freqs = np.power(np.float32(10000.0), -np.arange(nf, dtype=np.float32) / np.float32(nf)).astype(np.float32)
    ar = rr[:, None] * freqs[None]
    ac = col[:, None] * freqs[None]
    c["c_pos"] = np.concatenate([np.sin(ar), np.cos(ar), np.sin(ac), np.cos(ac)], axis=-1).astype(np.float32)
    band = np.zeros((128, 5, 4, 128), np.float32)
    for gi, w in enumerate((2, 4, 8, 16)):
        Tn = 384
        Fm = np.zeros((Tn, Tn), np.float64)
        for t in range(Tn):
            lo = min(max(t - w // 2, 0), Tn)
            hi = min(max(t + w - w // 2, 0), Tn)
            Fm[t, lo:hi] = 1.0 / (hi - lo)
        Fm -= np.eye(Tn)
        band[:, 0, gi, :] = Fm[0:128, 0:128].T
        band[:, 1, gi, :] = Fm[128:256, 128:256].T
        band[:, 2, gi, :] = Fm[256:384, 256:384].T
        band[:, 3, gi, :] = Fm[128:256, 0:128].T
        band[:, 4, gi, :] = Fm[128:256, 256:384].T
    c["c_band"] = band
    k64 = np.arange(64)
    ang = 2 * np.pi * np.outer(k64, k64) / 64.0
    C64 = np.cos(ang)
    S64 = np.sin(ang)
    pad = np.zeros((64, 2, 2, 128), np.float32)
    for half in range(2):
        pad[:, half, 0, half * 64:(half + 1) * 64] = C64
        pad[:, half, 1, half * 64:(half + 1) * 64] = S64
    c["c_dftpad"] = pad
    dA = np.zeros((64, 2, 3, 64), np.float32)
    for ai, A in enumerate((64, 4)):
        a = np.arange(A)
        an = 2 * np.pi * np.outer(a, a) / A
        dA[:A, ai, 0, :A] = np.cos(an)
        dA[:A, ai, 1, :A] = np.sin(an)
        dA[:A, ai, 2, :A] = -np.sin(an)
        Tt = 64 * A
        p = np.arange(A)[:, None, None]
        b = np.arange(64)[None, :, None]
        q = np.arange(64)[None, None, :]
        be = 2 * np.pi * ((b * (p + A * q)) % Tt) / Tt
        nrm = 1.0 / np.sqrt(64.0 * Tt)
        e2 = np.stack([np.cos(be) * nrm, -np.sin(be) * nrm], axis=2).astype(np.float32)
        c["c_e2_%d" % A] = e2
    c["c_dfta"] = dA
    tt_ = np.arange(256)
    th = 2 * np.pi * np.outer(tt_, tt_) / 256.0
    nrm = 1.0 / np.sqrt(64.0 * 256.0)
    e256 = np.stack([np.cos(th) * nrm, -np.sin(th) * nrm], axis=1)
    e256 = e256.reshape(2, 128, 2, 256).transpose(1, 0, 2, 3)
    c["c_e256"] = np.ascontiguousarray(e256.reshape(128, 1024)).astype(np.float32)
    return c


import os
DBG_REGION = os.environ.get("DBG_REGION", "")
DBG_LIMIT = os.environ.get("DBG_LIMIT")


def build_program(nl_run=NL, seq_sel=(0, 1, 2, 3, 4), skip=()):
    nc = bass.Bass("TRN2", target_bir_lowering=False)
    P = Prog(nc)
    P.limit = int(DBG_LIMIT) if DBG_LIMIT else None
    consts = host_consts()

    def din(name, shape):
        return nc.dram_tensor(name, list(shape), F32, kind="ExternalInput").ap()

    xs = din("xs", [TSAMP, 1024])
    xp = din("xp", [4 * TPR, 1024])
    sd = din("sd", [NL, 2, 4, 64, 64])
    ccd = din("cc", [2, 1024])
    ada_w = din("ada_w", [NL, 1024, 3072])
    ada_b = din("ada_b", [NL, 3072])
    norm_g = din("norm_g", [NL, 1024])
    w_in = din("w_in", [NL, 1024, 2832])
    conv_qkv = din("conv_qkv", [NL, 4, 768])
    a_log = din("a_log", [NL, 8])
    dt_bias = din("dt_bias", [NL, 8])
    dn_norm_g = din("dn_norm_g", [NL, 64])
    sgu_norm_g = din("sgu_norm_g", [NL, 256])
    sgu_w = din("sgu_w", [NL, 4, 128, 128])
    sgu_b = din("sgu_b", [NL, 4, 128])
    pool_w = din("pool_w", [NL, 4, 64, 64])
    pool_scale = din("pool_scale", [NL, 256])
    fourier_w = din("fourier_w", [NL, 4, 64, 64])
    w_out = din("w_out", [NL, 1024, 1024])
    final_norm_g = din("final_norm_g", [1024])
    cd = {k: din(k, v.shape) for k, v in consts.items()}

    ys = nc.dram_tensor("ys", [TSAMP, 1024], F32, kind="ExternalOutput").ap()
    yp = nc.dram_tensor("yp", [4 * TPR, 1024], F32, kind="ExternalOutput").ap()
    nsd = nc.dram_tensor("ns", [4, NL, 2, 4, 64, 64], F32, kind="ExternalOutput").ap()

    XS = nc.dram_tensor("XS", [NTOK, 1024], F32, kind="Internal").ap()
    MIXD = nc.dram_tensor("MIXD", [TSAMP, 1024], F32, kind="Internal").ap()
    QKVD = nc.dram_tensor("QKVD", [TSAMP, 768], BF16, kind="Internal").ap()
    OFD = nc.dram_tensor("OFD", [2, TSAMP, 256], F32, kind="Internal").ap()
    ZD = nc.dram_tensor("ZD", [2, TSAMP, 256], F32, kind="Internal").ap()
    VD = nc.dram_tensor("VD", [2, 64, 64, 256], F32, kind="Internal").ap()

    st = contextlib.ExitStack()
    with st:
        def sb(name, shape, dt=F32):
            return st.enter_context(nc.sbuf_tensor(name, list(shape), dt))

        def ps(name, shape, dt=F32):
            return st.enter_context(nc.psum_tensor(name, list(shape), dt))

        HT = sb("HT", [128, 8, TSAMP], BF16)
        WG = sb("WG", [128, 8, 1040], BF16)
        AW = sb("AW", [128, 8, 256], BF16)
        GATE = sb("GATE", [128, 2, 1024])
        XT = sb("XT", [128, 1024])
        XT2 = sb("XT2", [128, 1024])
        XN = sb("XN", [128, 1024], BF16)
        MXT = sb("MXT", [128, 8, 128], BF16)
        JNK = MXT[:].rearrange("p k t -> p (k t)")
        IDN = sb("IDN", [128, 128])
        IDB = sb("IDB", [128, 128], BF16)
        M64 = sb("M64", [128, 9, 64])
        TRI = sb("TRI", [128, 3, 64])
        BAND = sb("BAND", [128, 5, 4, 128])
        DFTPAD = sb("DFTPAD", [64, 2, 2, 128])
        DFTA = sb("DFTA", [64, 2, 3, 64])
        SCB = sb("SCB", [128, 8, 2, 128], BF16)
        TMPL = sb("TMPL", [128, 128])
        CCT = sb("CCT", [128, 16])
        SCF = sb("SCF", [128, 16])
        SCb = sb("SCb", [128, 8, 2], BF16)
        NG = sb("NG", [128, 8])
        AB = sb("AB", [128, 24])
        ABG = sb("ABG", [128, 1024])
        MODT = sb("MODT", [128, 24, 2])
        AM = sb("AM", [128, 8, 2])
        CW = sb("CW", [128, 24])
        ALB = sb("ALB", [128, 8])
        DTB = sb("DTB", [128, 8])
        NEGA = sb("NEGA", [128, 8])
        DNG = sb("DNG", [128, 256])
        SGNG = sb("SGNG", [128, 256])
        SWT = sb("SWT", [128, 4, 128])
        SGBT = sb("SGBT", [128, 4])
        PWB = sb("PWB", [128, 2, 128])
        PSC = sb("PSC", [128, 256])
        FW = sb("FW", [64, 4, 64])
        PCS = sb("PCS", [128, 2, 2, 128])
        ST = sb("ST0", [128, 8])
        TL = 128
        RAW = sb("RAW", [128, 6, TL + 3])
        CS = sb("CS", [128, 6, TL])
        QKVF = sb("QKVF", [128, 1024])
        QKV3 = sb("QKV3", [128, 768], BF16)
        SALL = sb("SALL", [128, 64, 24])
        TB_ = sb("TB_", [128, 32, 4])
        TZ_ = sb("TZ_", [128, 32, 4])
        TG_ = sb("TG_", [128, 32, 4])
        KS = sb("KS", [128, 256], BF16)
        RH = sb("RH", [128, 4, 128], BF16)
        KDEC = sb("KDEC", [128, 256], BF16)
        QDEC = sb("QDEC", [128, 256], BF16)
        TR = sb("TR", [128, 1024], BF16)
        SQ = sb("SQ", [128, 1024])
        XA = sb("XA", [128, 256], BF16)
        XB = sb("XB", [128, 256], BF16)
        COF = sb("COF", [128, 256], BF16)
        SSb = sb("SSb", [128, 256], BF16)
        DG = sb("DG", [128, 256])
        NDG = sb("NDG", [128, 256])
        DTT = sb("DTT", [128, 256])
        Cb = [sb("C_a", [128, 256], BF16), sb("C_b", [128, 256], BF16)]
        Bb = [sb("B_a", [128, 256], BF16), sb("B_b", [128, 256], BF16)]
        Tb = [sb("T_a", [128, 256], BF16), sb("T_b", [128, 256], BF16)]
        QKT = sb("QKT", [128, 256], BF16)
        MTT = sb("MTT", [128, 256], BF16)
        UU = sb("UU", [128, 256])
        WB_ = sb("WB_", [128, 256], BF16)
        WT = sb("WT", [128, 256], BF16)
        VN = sb("VN", [128, 256], BF16)
        OO = sb("OO", [128, 256])
        SS = sb("SS", [128, 256])
        STMP = sb("STMP", [128, 256])
        GA = STMP
        RN = sb("RN", [128, 16])
        UVG = sb("UVG", [128, 512])
        VNS = sb("VNS", [128, 256])
        SGB = sb("SGB", [128, 256])
        YB = sb("YB", [128, 256])
        XCT = sb("XCT", [128, 2, 128])
        ZR = [sb("ZR%d" % i, [128, 256]) for i in range(3)]
        ZCS = UVG
        ZA = sb("ZA", [64, 2, 512])
        VV = sb("VV", [64, 2, 512])
        VB = sb("VB", [64, 2, 256])
        E2 = sb("E2", [64, 2, 64])
        VB2 = sb("VB2", [64, 2, 256])
        E2_2 = sb("E2_2", [64, 2, 64])
        XCT2 = sb("XCT2", [128, 2, 128])
        UVG2 = sb("UVG2", [128, 512])
        YD = YB[0:64, :]
        if os.environ.get("EXTRA_SB"):
            sb("EXTRA", [128, int(os.environ["EXTRA_SB"]) // 4])
        PA = ps("PA", [128, 512])
        PB = ps("PB", [128, 512])
        PC = ps("PC", [128, 512])
        PD = ps("PD", [128, 512])
        PE_ = ps("PE_", [128, 512])
        PF = ps("PF", [128, 512])
        PG = ps("PG", [128, 512])
        PT = ps("PT", [128, 512])
        PTb = PT[:].bitcast(BF16)

        def ld(out, in_, r, w, q="sp", key=None, n=1, nc_ok=False):
            kw = dict(allow_slow_non_contiguous=True) if nc_ok else {}
            P.add(q, lambda e: e.dma_start(out=out, in_=in_, **kw), reads=r, writes=w, dma=1, sem_key=key or ("ld", w[0]))

        def stq(out, in_, r, w, key, q="sp"):
            P.add(q, lambda e: e.dma_start(out=out, in_=in_), reads=r, writes=w, dma=1, sem_key=(key, q))

        def mm(out, lhsT, rhs, r, w, start=True, stop=True):
            P.add("pe", lambda e: e.matmul(out, lhsT=lhsT, rhs=rhs, start=start, stop=stop), reads=r, writes=w)

        def tr(out, in_, ident, r, w):
            P.add("pe", lambda e: e.transpose(out=out, in_=in_, identity=ident), reads=r, writes=w)

        def act(out, in_, func, r, w, bias=None, scale=None, accum=None):
            kw = {}
            if bias is not None:
                kw["bias"] = bias
            if scale is not None:
                kw["scale"] = scale
            if accum is not None:
                kw["accum_out"] = accum
            P.add("act", lambda e: e.activation(out=out, in_=in_, func=func, **kw), reads=r, writes=w)

        def tt(out, in0, in1, op, r, w, eng="dve"):
            P.add(eng, lambda e: e.tensor_tensor(out=out, in0=in0, in1=in1, op=op), reads=r, writes=w)

        def ts(out, in0, s1, s2, op0, op1, r, w, eng="dve"):
            if op1 is None:
                P.add(eng, lambda e: e.tensor_scalar(out=out, in0=in0, scalar1=s1, scalar2=None, op0=op0), reads=r, writes=w)
            else:
                P.add(eng, lambda e: e.tensor_scalar(out=out, in0=in0, scalar1=s1, scalar2=s2, op0=op0, op1=op1), reads=r, writes=w)

        def stt(out, in0, scalar, in1, op0, op1, r, w):
            P.add("dve", lambda e: e.scalar_tensor_tensor(out=out, in0=in0, scalar=scalar, in1=in1, op0=op0, op1=op1), reads=r, writes=w)

        def cp(out, in_, r, w, eng="dve"):
            if eng == "act":
                P.add(eng, lambda e: e.activation(out=out, in_=in_, func=AF.Identity), reads=r, writes=w)
            else:
                P.add(eng, lambda e: e.tensor_copy(out=out, in_=in_), reads=r, writes=w)

        def recip(out, in_, r, w):
            P.add("dve", lambda e: e.reciprocal(out=out, in_=in_), reads=r, writes=w)

        def rstd_from_ssq(ssq_ap, out_ap, n, key):
            ts(ssq_ap, ssq_ap, 1.0 / n, EPS, ALU.mult, ALU.add, [key], [key])
            act(ssq_ap, ssq_ap, AF.Sqrt, [key], [key])
            recip(out_ap, ssq_ap, [key], [key])

        def load_T(dst_ap, src_ap, n, dkey):
            ld(TMPL[0:n, :], src_ap, [], ["TMPL"])
            tr(PG[:, 0:n], TMPL[0:n, :], IDN[0:n, 0:n], ["TMPL", "IDN"], ["PG"])
            cp(dst_ap, PG[:, 0:n], ["PG"], [dkey])

        def bc3(ap, shape):
            return ap.unsqueeze(2).to_broadcast(shape)

        ld(IDN[:], cd["c_ident"], [], ["IDN"])
        cp(IDB[:], IDN[:], ["IDN"], ["IDB"])
        for hf in range(2):
            ld(M64[hf * 64:hf * 64 + 64], cd["c_m64"], [], ["M64"])
            ld(TRI[hf * 64:hf * 64 + 64], cd["c_tri"], [], ["TRI"])
        ld(BAND[:], cd["c_band"], [], ["BAND"])
        ld(DFTPAD[:], cd["c_dftpad"], [], ["DFTPAD"])
        ld(DFTA[:], cd["c_dfta"], [], ["DFTA"])
        def masks(hf):
            psl = slice(hf * 64, hf * 64 + 64)
            bh = lambda i: M64[psl, i, :].unsqueeze(1).to_broadcast([64, 2, 64])
            return dict(SUL=(bh(0), bh(1)), MINC=(bh(2), bh(3)), ID4=bh(4), MD8=bh(5), MOFF=[bh(6), bh(7), bh(8)])
        MSK = [masks(0), masks(1)]

        load_T(CCT[:], ccd.rearrange("c (k p) -> (c k) p", p=128), 16, "CCT")
        act(SCF[:], CCT[:], AF.Silu, ["CCT"], ["SCF"])
        cp(SCb[:].rearrange("p k c -> p c k"), SCF[:].rearrange("p (c k) -> p c k", c=2), ["SCF"], ["SCb"])
        cp(SCB[:], SCb[:].unsqueeze(3).to_broadcast([128, 8, 2, 128]), ["SCb"], ["SCB"])

        for t in (range(TSAMP // 128) if 0 in seq_sel else []):
            ld(XT[:], xs[t * 128:(t + 1) * 128, :], [], ["XT"])
            ld(XT2[:], cd["c_pos"][t * 128:(t + 1) * 128, :], [], ["XT2"])
            tt(XT[:], XT[:], XT2[:], ALU.add, ["XT", "XT2"], ["XT"])
            stq(XS[t * 128:(t + 1) * 128, :], XT[:], ["XT"], [("XS", t)], ("st", "XT"))
        for t in range(8):
            if (1 + t // 2) in seq_sel:
                ld(XT[:], xp[t * 128:(t + 1) * 128, :], [], ["XT"])
                stq(XS[TSAMP + t * 128:TSAMP + (t + 1) * 128, :], XT[:], ["XT"], [("XS", 32 + t)], ("st", "XT"))

        seqs = [(0, TSAMP, 0, None)] + [(TSAMP + i * TPR, TPR, 1, i) for i in range(4)]
        seqs = [seqs[i] for i in seq_sel]

        for l in range(nl_run):
            load_T(NG[:], norm_g[l].rearrange("(k p) -> k p", p=128), 8, "NG")
            load_T(AB[:], ada_b[l].rearrange("(k p) -> k p", p=128), 24, "AB")
            load_T(CW[:], conv_qkv[l].rearrange("j (c p) -> (j c) p", p=128), 24, "CW")
            ld(ABG[:], ada_b[l, 2048:3072].partition_broadcast(128), [], ["ABG"])
            ld(ALB[:], a_log[l].partition_broadcast(128), [], ["ALB"])
            ld(DTB[:], dt_bias[l].partition_broadcast(128), [], ["DTB"])
            act(NEGA[:], ALB[:], AF.Exp, ["ALB"], ["NEGA"])
            ts(NEGA[:], NEGA[:], -1.0, None, ALU.mult, None, ["NEGA"], ["NEGA"])
            for h in range(4):
                ld(DNG[:, h * 64:(h + 1) * 64], dn_norm_g[l].partition_broadcast(128), [], ["DNG"], key=("ld", "DNG"))
            ld(SGNG[:], sgu_norm_g[l].partition_broadcast(128), [], ["SGNG"])
            ld(PSC[:], pool_scale[l].partition_broadcast(128), [], ["PSC"])
            load_T(SGBT[:], sgu_b[l], 4, "SGBT")
            for g in range(4):
                ld(TMPL[:], sgu_w[l, g], [], ["TMPL"])
                tr(PG[:, 0:128], TMPL[:], IDN[:], ["TMPL", "IDN"], ["PG"])
                cp(SWT[:, g, :], PG[:, 0:128], ["PG"], ["SWT"])
            P.add("pool", lambda e: e.memset(PWB[:], 0.0), reads=[], writes=["PWB"])
            for g in range(4):
                hb = (g % 2) * 64
                ld(PWB[hb:hb + 64, g // 2, hb:hb + 64], pool_w[l, g], [], ["PWB"], key=("ld", "PWB"))
            for c in range(2):
                tt(PWB[:, c, :], PWB[:, c, :], PSC[:, c * 128:(c + 1) * 128], ALU.mult, ["PWB", "PSC"], ["PWB"])
            ld(FW[:], fourier_w[l].rearrange("g c d -> c g d"), [], ["FW"])
            for g in range(4):
                for cs in range(2):
                    mm(PG[:, 0:64], DFTPAD[:, g % 2, cs, :], FW[:, g, :], ["DFTPAD", "FW"], ["PG"])
                    cp(PCS[:, cs, g // 2, (g % 2) * 64:(g % 2) * 64 + 64], PG[:, 0:64], ["PG"], ["PCS"])
            wo_v = w_out[l].rearrange("(k p) n -> p k n", p=128)
            aw_v = ada_w[l].rearrange("(k p) n -> p k n", p=128)
            for n in range(12):
                P.add("pool", lambda e, n=n, aw_v=aw_v: e.dma_start(out=AW[:], in_=aw_v[:, :, n * 256:(n + 1) * 256]),
                      reads=[], writes=["AW"], dma=1, sem_key=("ld", "AW"))
                for c in range(2):
                    for k in range(8):
                        mm(PG[:, 0:2], AW[:, k, c * 128:(c + 1) * 128], SCb[:, k, :], ["AW", "SCb"], ["PG"], start=(k == 0), stop=(k == 7))
                    ts(MODT[:, n * 2 + c, :], PG[:, 0:2], AB[:, n * 2 + c:n * 2 + c + 1], None, ALU.add, None, ["PG", "AB"], ["MODT"])
                if n >= 8:
                    for cond in range(2):
                        for k in range(8):
                            mm(PA[:, 0:256], SCB[:, k, cond, :], AW[:, k, :], ["SCB", "AW"], ["PA"], start=(k == 0), stop=(k == 7))
                        tt(GATE[:, cond, (n - 8) * 256:(n - 7) * 256], PA[:, 0:256], ABG[:, (n - 8) * 256:(n - 7) * 256], ALU.add,
                           ["PA", "ABG"], ["GATE"])
            ts(AM[:], MODT[:, 8:16, :], 1.0, None, ALU.add, None, ["MODT"], ["AM"])
            tt(AM[:], AM[:], bc3(NG[:], [128, 8, 2]), ALU.mult, ["AM", "NG"], ["AM"])

            win_v = w_in[l].rearrange("(k p) n -> p k n", p=128)

            def load_wg(c0, ncol):
                P.add("pool", lambda e, wv=win_v: e.dma_start(out=WG[:, :, 0:ncol], in_=wv[:, :, c0:c0 + ncol]),
                      reads=[], writes=["WG"], dma=1, sem_key=("ld", "WG"))

            for (tok0, T, cond, pidx) in seqs:
                ntile = T // 128
                CSj = CS[:].rearrange("p a b -> p (a b)").bitcast(BF16)[:, 0:1024]
                for t in range(ntile):
                    g0 = tok0 // 128 + t
                    xt, xtk = ((XT, "XT"), (XT2, "XT2"))[t % 2]
                    xn, xnk = ((XN[:], "XN"), (MXT[:].rearrange("p k t -> p (k t)"), "MXT"))[t % 2]
                    ptb, ptk = ((PTb, "PT"), (PG[:].bitcast(BF16), "PG"))[t % 2]
                    stc = (t % 2) * 4
                    ld(xt[:], XS[g0 * 128:(g0 + 1) * 128, :], [("XS", g0)], [xtk])
                    act(CSj, xt[:], AF.Square, [xtk], ["CS", "ST%d" % (t % 2)], accum=ST[:, stc:stc + 1])
                    rstd_from_ssq(ST[:, stc:stc + 1], ST[:, stc + 2:stc + 3], 1024.0, "ST%d" % (t % 2))
                    ts(xn, xt[:], ST[:, stc + 2:stc + 3], None, ALU.mult, None, [xtk, "ST%d" % (t % 2)], [xnk])
                    for k in range(8):
                        tr(ptb[:, k * 128:(k + 1) * 128], xn[:, k * 128:(k + 1) * 128], IDB[:], [xnk, "IDB"], [ptk])
                    for k in range(8):
                        act(HT[:, k, t * 128:(t + 1) * 128], ptb[:, k * 128:(k + 1) * 128], AF.Identity, [ptk, "AM", "MODT"],
                            [("HT", t)], bias=MODT[:, k, cond:cond + 1], scale=AM[:, k, cond:cond + 1])
                HTall = [("HT", t) for t in range(ntile)]

                def proj_tm(out_ps, tok_sl, c0, ncol, M, okey):
                    for k in range(8):
                        mm(out_ps, HT[:, k, tok_sl], WG[:, k, c0:c0 + ncol], HTall + ["WG"], [okey], start=(k == 0), stop=(k == 7))

                def proj_fm(out_ps, tok_sl, c0, ncol, okey, start=True):
                    for k in range(8):
                        mm(out_ps, WG[:, k, c0:c0 + ncol], HT[:, k, tok_sl], HTall + ["WG"], [okey], start=(k == 0), stop=(k == 7))

                load_wg(0, 1040)
                nch = T // 64
                P.in_region = ("dn" == DBG_REGION)
                dn_on = "dn" not in skip
                for tl in (range(T // TL) if dn_on else []):
                    s0 = tl * TL
                    for c in range(6):
                        proj_fm(PA[:, 0:TL], slice(s0, s0 + TL), c * 128, 128, "PA")
                        cp(RAW[:, c, 2:2 + TL], PA[:, 0:TL], ["PA"], ["RAW"], eng="act")
                        if s0 > 0:
                            proj_fm(PB[:, 0:2], slice(s0 - 2, s0), c * 128, 128, "PB")
                            cp(RAW[:, c, 0:2], PB[:, 0:2], ["PB"], ["RAW"])
                        else:
                            P.add("pool", lambda e, c=c: e.memset(RAW[:, c, 0:2], 0.0), reads=[], writes=["RAW"])
                        if s0 + TL < T:
                            proj_fm(PB[:, 2:4], slice(s0 + TL, s0 + TL + 2), c * 128, 128, "PB")
                            cp(RAW[:, c, 2 + TL:3 + TL], PB[:, 2:3], ["PB"], ["RAW"])
                        else:
                            P.add("pool", lambda e, c=c: e.memset(RAW[:, c, 2 + TL:3 + TL], 0.0), reads=[], writes=["RAW"])
                        ts(CS[:, c, :], RAW[:, c, 0:TL], CW[:, 0 * 6 + c:0 * 6 + c + 1], None, ALU.mult, None, ["RAW", "CW"], ["CS"])
                        for j in range(1, 4):
                            stt(CS[:, c, :], RAW[:, c, j:j + TL], CW[:, j * 6 + c:j * 6 + c + 1], CS[:, c, :], ALU.mult, ALU.add,
                                ["RAW", "CW", "CS"], ["CS"])
                        act(CS[:, c, :], CS[:, c, :], AF.Silu, ["CS"], ["CS"])
                    for c in range(6):
                        dst = (PC if c < 4 else PD)
                        off = (c % 4) * 128
                        tr(dst[:, off:off + 128], CS[:, c, :], IDN[:], ["CS", "IDN"], ["PC" if c < 4 else "PD"])
                    cp(QKVF[:, 0:512], PC[:, :], ["PC"], ["QKVF"])
                    cp(QKVF[:, 512:768], PD[:, 0:256], ["PD"], ["QKVF"], eng="act")
                    tt(SQ[:, 0:512], QKVF[:, 0:512], QKVF[:, 0:512], ALU.mult, ["QKVF"], ["SQ"])
                    P.add("dve", lambda e: e.tensor_reduce(out=RN[:, 0:8], in_=SQ[:, 0:512].rearrange("p (h c) -> p h c", h=8), axis=AX.X, op=ALU.add),
                          reads=["SQ"], writes=["RN"])
                    ts(RN[:, 0:8], RN[:, 0:8], EPS, None, ALU.add, None, ["RN"], ["RN"])
                    act(RN[:, 0:8], RN[:, 0:8], AF.Sqrt, ["RN"], ["RN"])
                    recip(RN[:, 8:16], RN[:, 0:8], ["RN"], ["RN"])
                    ts(RN[:, 8:12], RN[:, 8:12], 0.125, None, ALU.mult, None, ["RN"], ["RN"])
                    tt(QKV3[:, 0:512].rearrange("p (h c) -> p h c", h=8), QKVF[:, 0:512].rearrange("p (h c) -> p h c", h=8),
                       bc3(RN[:, 8:16], [128, 8, 64]), ALU.mult, ["QKVF", "RN"], ["QKV3_00", "QKV3_01", "QKV3_10", "QKV3_11"])
                    cp(QKV3[:, 512:768], QKVF[:, 512:768], ["QKVF"], ["QKV3_00", "QKV3_01", "QKV3_10", "QKV3_11"], eng="pool")
                    stq(QKVD[s0:s0 + 128, :], QKV3[:], ["QKV3_00", "QKV3_01", "QKV3_10", "QKV3_11"], [("QKVD", tl)], ("st", "QKV3"))

                def dn_scal_all(d):
                    hf = d
                    psl = slice(hf * 64, hf * 64 + 64)
                    sx = "_%d" % hf
                    bk, bkk = (PA, "PA") if hf == 0 else (PB, "PB")
                    bk2, bkk2 = (PC, "PC") if hf == 0 else (PD, "PD")
                    for g0 in range(0, nch, 32):
                        n = min(32, nch - g0)
                        for j in range(n):
                            ch = g0 + j
                            proj_tm(bk[psl, j * 16:(j + 1) * 16], slice(ch * 64, ch * 64 + 64), 1024, 16, 64, bkk)
                        pv = bk[psl, 0:n * 16].rearrange("p (n c) -> p n c", c=16)
                        TB, TZ, TG = TB_[psl, 0:n, :], TZ_[psl, 0:n, :], TG_[psl, 0:n, :]
                        SA = lambda o: SALL[psl, g0:g0 + n, o:o + 4]
                        kk = ["TB_" + sx]
                        act(TB, pv[:, :, d * 4:d * 4 + 4], AF.Sigmoid, [bkk], kk)
                        act(SA(0), TB, AF.Sqrt, kk, ["SALL" + sx])
                        tt(TZ, pv[:, :, 8 + d * 4:12 + d * 4], DTB[psl, d * 4:d * 4 + 4].unsqueeze(1).to_broadcast([64, n, 4]), ALU.add, [bkk, "DTB"], kk)
                        act(TZ, TZ, AF.Exp, kk, kk)
                        act(TZ, TZ, AF.Ln, kk, kk, bias=1.0)
                        tt(TG, TZ, NEGA[psl, d * 4:d * 4 + 4].unsqueeze(1).to_broadcast([64, n, 4]), ALU.mult, kk + ["NEGA"], kk)
                        gflat = TG_[psl, 0:n, :].rearrange("p n c -> p (n c)")
                        mm(bk2[psl, 0:n * 4], TRI[psl, d, :], gflat, ["TRI"] + kk, [bkk2])
                        mm(bk2[psl, 128:128 + n * 4], TRI[psl, 2, :], gflat, ["TRI"] + kk, [bkk2])
                        gcv = bk2[psl, 0:n * 4].rearrange("p (n c) -> p n c", c=4)
                        glv = bk2[psl, 128:128 + n * 4].rearrange("p (n c) -> p n c", c=4)
                        cp(SA(20), gcv, [bkk2], ["SALL" + sx])
                        act(SA(8), gcv, AF.Exp, [bkk2], ["SALL" + sx])
                        tt(TZ, glv, SA(20), ALU.subtract, [bkk2, "SALL" + sx], kk)
                        act(SA(12), TZ, AF.Exp, kk, ["SALL" + sx])
                        act(SA(16), glv, AF.Exp, [bkk2], ["SALL" + sx])
                        tt(SA(4), SA(0), SA(8), ALU.mult, ["SALL" + sx], ["SALL" + sx])

                def dn_unit(ch, d, hp):
                    hf = d
                    psl = slice(hf * 64, hf * 64 + 64)
                    sx = "_%d" % hf
                    sy = "_%d%d" % (hf, hp)
                    K_ = lambda *n: [x + sy for x in n]
                    bx, by = [[(PC, PD), (PF, PG)], [(PA, PB), (PE_, PT)]][hf][hp]
                    kx, ky = [[("PC", "PD"), ("PF", "PG")], [("PA", "PB"), ("PE_", "PT")]][hf][hp]
                    M = MSK[hf]
                    v2 = lambda ap: ap.rearrange("p (h c) -> p h c", h=2)
                    sh = [64, 2, 64]
                    cs_ = slice(hp * 128, hp * 128 + 128)
                    SAk = "SALL" + sx
                    SA = lambda o: SALL[psl, ch, o + 2 * hp:o + 2 * hp + 2]
                    ld(QKV3[psl, :].rearrange("p (t c) -> p t c", t=3)[:, :, hp * 128:hp * 128 + 128],
                       QKVD[ch * 64:ch * 64 + 64, :].rearrange("p (t c) -> p t c", t=3)[:, :, hp * 128:hp * 128 + 128],
                       [("QKVD", ch // 2)], ["QKV3" + sy], key=("ld", "QKV3" + sy))
                    Q = QKV3[psl, 0 + hp * 128:128 + hp * 128]
                    K = QKV3[psl, 256 + hp * 128:384 + hp * 128]
                    V = QKV3[psl, 512 + hp * 128:640 + hp * 128]
                    QK3 = ["QKV3" + sy]
                    tt(v2(KS[psl, cs_]), v2(K), bc3(SA(0), sh), ALU.mult, QK3 + [SAk], K_("KS"))
                    tt(RH[psl, 2 * hp:2 * hp + 2, 0:64], v2(V), bc3(SA(0), sh), ALU.mult, QK3 + [SAk], K_("RH"), eng="pool")
                    tt(RH[psl, 2 * hp:2 * hp + 2, 64:128], v2(K), bc3(SA(4), sh), ALU.mult, QK3 + [SAk], K_("RH"))
                    tt(v2(KDEC[psl, cs_]), v2(K), bc3(SA(12), sh), ALU.mult, QK3 + [SAk], K_("KDEC"), eng="pool")
                    tt(v2(QDEC[psl, cs_]), v2(Q), bc3(SA(8), sh), ALU.mult, QK3 + [SAk], K_("QDEC"))
                    I64 = IDB[psl, hf * 64:hf * 64 + 64]
                    bxb = bx[psl, 0:256].bitcast(BF16)
                    c0 = hp * 128
                    for h in range(2):
                        hs = slice(c0 + h * 64, c0 + h * 64 + 64)
                        hq = slice(h * 64, h * 64 + 64)
                        tr(bxb[:, h * 64:h * 64 + 64], KS[psl, hs], I64, K_("KS") + ["IDB"], [kx])
                        tr(bxb[:, 128 + h * 64:192 + h * 64], K[:, hq], I64, QK3 + ["IDB"], [kx])
                        tr(bxb[:, 256 + h * 64:320 + h * 64], Q[:, hq], I64, QK3 + ["IDB"], [kx])
                        tr(bxb[:, 384 + h * 64:448 + h * 64], QDEC[psl, hs], I64, K_("QDEC") + ["IDB"], [kx])
                    TRs = TR[psl, hp * 512:hp * 512 + 512]
                    cp(TRs, bxb, [kx], K_("TR"), eng="act")
                    KST = lambda h: TRs[:, h * 64:h * 64 + 64]
                    KT = lambda h: TRs[:, 128 + h * 64:192 + h * 64]
                    QT = lambda h: TRs[:, 256 + h * 64:320 + h * 64]
                    QDT = lambda h: TRs[:, 384 + h * 64:448 + h * 64]
                    for h in range(2):
                        mm(by[psl, h * 64:h * 64 + 64], KST(h), KST(h), K_("TR"), [ky])
                        mm(by[psl, 128 + h * 64:192 + h * 64], KT(h), QT(h), K_("TR"), [ky])
                    tt(v2(DG[psl, cs_]), M["ID4"], bc3(SA(20), sh), ALU.mult, ["M64", SAk], K_("DG"), eng="pool")
                    act(NDG[psl, cs_], DG[psl, cs_], AF.Identity, K_("DG"), K_("NDG"), scale=-1.0)
                    for h in range(2):
                        hs = slice(c0 + h * 64, c0 + h * 64 + 64)
                        mm(by[psl, 256 + h * 64:320 + h * 64], TRI[psl, 2, :], DG[psl, hs], ["TRI"] + K_("DG"), [ky], start=True, stop=False)
                        mm(by[psl, 256 + h * 64:320 + h * 64], NDG[psl, hs], TRI[psl, 2, :], ["TRI"] + K_("NDG"), [ky], start=False, stop=True)
                    tt(v2(DTT[psl, cs_]), v2(by[psl, 256:384]), M["MINC"][d], ALU.add, [ky, "M64"], K_("DTT"))
                    act(DTT[psl, cs_], DTT[psl, cs_], AF.Exp, K_("DTT"), K_("DTT"))
                    C0_, B0_ = Cb[0], Bb[0]
                    CD, BD = Cb[1], Bb[1]
                    TT_, TN_ = Tb[0], Tb[1]
                    tt(v2(C0_[psl, cs_]), v2(by[psl, 0:128]), M["SUL"][d], ALU.mult, [ky, "M64"], K_("C_a"))
                    tt(v2(B0_[psl, cs_]), v2(by[psl, 0:128]), M["SUL"][1 - d], ALU.mult, [ky, "M64"], K_("B_a"))
                    tt(QKT[psl, cs_], by[psl, 128:256], DTT[psl, cs_], ALU.mult, [ky] + K_("DTT"), K_("QKT"))
                    tt(v2(CD[psl, cs_]), v2(C0_[psl, cs_]), M["MD8"], ALU.mult, K_("C_a") + ["M64"], K_("C_b"), eng="pool")
                    tt(v2(BD[psl, cs_]), v2(B0_[psl, cs_]), M["MD8"], ALU.mult, K_("B_a") + ["M64"], K_("B_b"), eng="pool")
                    tt(v2(TT_[psl, cs_]), v2(CD[psl, cs_]), M["ID4"], ALU.add, K_("C_b") + ["M64"], K_("T_a"), eng="pool")
                    tt(v2(TN_[psl, cs_]), v2(BD[psl, cs_]), M["ID4"], ALU.add, K_("B_b") + ["M64"], K_("T_b"), eng="pool")

                    def grp(dst, dk, off, lt, lk, rt, rk):
                        for h in range(2):
                            hs = slice(c0 + h * 64, c0 + h * 64 + 64)
                            mm(dst[psl, off + h * 64:off + h * 64 + 64], lt[psl, hs], rt[psl, hs], K_(lk, rk), [dk])

                    for lev in range(2):
                        grp(bx, kx, 0, CD, "C_b", BD, "B_b")
                        grp(bx, kx, 128, BD, "B_b", CD, "C_b")
                        cp(BD[psl, cs_], bx[psl, 0:128], [kx], K_("B_b"), eng="act")
                        cp(CD[psl, cs_], bx[psl, 128:256], [kx], K_("C_b"), eng="act")
                        grp(by, ky, 0, BD, "B_b", TT_, "T_a")
                        grp(by, ky, 128, CD, "C_b", TN_, "T_b")
                        tt(TT_[psl, cs_], TT_[psl, cs_], by[psl, 0:128], ALU.add, K_("T_a") + [ky], K_("T_a"))
                        tt(TN_[psl, cs_], TN_[psl, cs_], by[psl, 128:256], ALU.add, K_("T_b") + [ky], K_("T_b"))
                    BOF = VN
                    for li in range(3):
                        last = (li == 2)
                        tt(v2(BOF[psl, cs_]), v2(B0_[psl, cs_]), M["MOFF"][li], ALU.mult, K_("B_a") + ["M64"], K_("VN"), eng="pool")
                        grp(bx, kx, 0, BOF, "VN", TT_, "T_a")
                        cp(XA[psl, cs_], bx[psl, 0:128], [kx], K_("XA"), eng="act")
                        if not last:
                            tt(v2(COF[psl, cs_]), v2(C0_[psl, cs_]), M["MOFF"][li], ALU.mult, K_("C_a") + ["M64"], K_("COF"), eng="pool")
                            grp(bx, kx, 128, COF, "COF", TN_, "T_b")
                            cp(XB[psl, cs_], bx[psl, 128:256], [kx], K_("XB"), eng="act")
                        grp(by, ky, 0, TN_, "T_b", XA, "XA")
                        if not last:
                            grp(by, ky, 128, TT_, "T_a", XB, "XB")
                        tt(TT_[psl, cs_], TT_[psl, cs_], by[psl, 0:128], ALU.add, K_("T_a") + [ky], K_("T_a"))
                        if not last:
                            tt(TN_[psl, cs_], TN_[psl, cs_], by[psl, 128:256], ALU.add, K_("T_b") + [ky], K_("T_b"))
                    tt(MTT[psl, cs_], TT_[psl, cs_], DTT[psl, cs_], ALU.mult, K_("T_a", "DTT"), K_("MTT"))
                    for h in range(2):
                        hs = slice(c0 + h * 64, c0 + h * 64 + 64)
                        mm(by[psl, h * 128:(h + 1) * 128], MTT[psl, hs], RH[psl, 2 * hp + h, :], K_("MTT", "RH"), [ky])
                    byv = by[psl, 0:256].rearrange("p (h c) -> p h c", h=2)
                    tt(v2(UU[psl, cs_]), byv[:, :, 0:64], bc3(SA(0), sh), ALU.mult, [ky, SAk], K_("UU"))
                    tt(v2(WB_[psl, cs_]), byv[:, :, 64:128], bc3(SA(0), sh), ALU.mult, [ky, SAk], K_("WB_"))
                    bxw = bx[psl, 0:64].bitcast(BF16)
                    for h in range(2):
                        hs = slice(c0 + h * 64, c0 + h * 64 + 64)
                        tr(bxw[:, h * 64:h * 64 + 64], WB_[psl, hs], I64, K_("WB_") + ["IDB"], [kx])
                    cp(WT[psl, cs_], bxw, [kx], K_("WT"), eng="act")
                    for h in range(2):
                        hs = slice(c0 + h * 64, c0 + h * 64 + 64)
                        mm(bx[psl, 128 + h * 64:192 + h * 64], WT[psl, hs], SSb[psl, hs], K_("WT", "SSb"), [kx])
                    tt(VN[psl, cs_], UU[psl, cs_], bx[psl, 128:256], ALU.subtract, K_("UU") + [kx], K_("VN"))
                    for h in range(2):
                        hs = slice(c0 + h * 64, c0 + h * 64 + 64)
                        mm(by[psl, h * 64:h * 64 + 64], QDT(h), SSb[psl, hs], K_("TR", "SSb"), [ky], start=True, stop=False)
                        mm(by[psl, h * 64:h * 64 + 64], QKT[psl, hs], VN[psl, hs], K_("QKT", "VN"), [ky], start=False, stop=True)
                    cp(OO[psl, cs_], by[psl, 0:128], [ky], K_("OO"), eng="act")
                    for h in range(2):
                        hs = slice(c0 + h * 64, c0 + h * 64 + 64)
                        mm(bx[psl, h * 64:h * 64 + 64], KDEC[psl, hs], VN[psl, hs], K_("KDEC", "VN"), [kx])
                    tt(v2(STMP[psl, cs_]), v2(SS[psl, cs_]), bc3(SA(16), sh), ALU.mult, K_("SS") + [SAk], K_("STMP"), eng="pool")
                    tt(SS[psl, cs_], STMP[psl, cs_], bx[psl, 0:128], ALU.add, K_("STMP") + [kx], K_("SS"))
                    cp(SSb[psl, cs_], SS[psl, cs_], K_("SS"), K_("SSb"), eng="pool")
                    stq(OFD[d, ch * 64:ch * 64 + 64, cs_], OO[psl, cs_], K_("OO"), [("OFD", d, ch, hp)], ("st", "OO" + sy))

                def init_S(d):
                    psl = slice(d * 64, d * 64 + 64)
                    sx = "_%d" % d
                    if pidx is None:
                        ld(SS[psl, :].rearrange("p (h c) -> p h c", h=4), sd[l, d].rearrange("h k v -> k h v"), [], ["SS" + sx + "0", "SS" + sx + "1"], key=("ld", "SS" + sx))
                    else:
                        P.add("pool", lambda e: e.memset(SS[psl, :], 0.0), reads=[], writes=["SS" + sx + "0", "SS" + sx + "1"])
                    cp(SSb[psl, :], SS[psl, :], ["SS" + sx + "0", "SS" + sx + "1"], ["SSb" + sx + "0", "SSb" + sx + "1"], eng="pool")

                def store_S(d):
                    psl = slice(d * 64, d * 64 + 64)
                    sx = "_%d" % d
                    if pidx is not None:
                        stq(nsd[pidx, l, d].rearrange("h k v -> k h v"), SS[psl, :].rearrange("p (h c) -> p h c", h=4), ["SS" + sx + "0", "SS" + sx + "1"],
                            [("ns", pidx, l, d)], ("st", "SS" + sx))

                if dn_on:
                    init_S(0)
                    init_S(1)
                    dn_scal_all(0)
                    dn_scal_all(1)
                    streams = []
                    for (dd, hh) in ((0, 0), (1, 0), (0, 1), (1, 1)):
                        P.rec = []
                        for i in range(nch):
                            dn_unit(i if dd == 0 else nch - 1 - i, dd, hh)
                        streams.append(P.rec)
                    P.rec = None
                    ulen = len(streams[0]) // nch
                    offs = [0, ulen // 4, ulen // 2, (3 * ulen) // 4]
                    for j in range(max(len(r) for r in streams) + offs[-1]):
                        for r, o in zip(streams, offs):
                            jj = j - o
                            if 0 <= jj < len(r):
                                P.add(*r[jj][:2], reads=r[jj][2], writes=r[jj][3], dma=r[jj][4], sem_key=r[jj][5])
                    store_S(0)
                    store_S(1)
                    DGK = ["DG_00", "DG_01", "DG_10", "DG_11"]
                    NDGK = ["NDG_00", "NDG_01", "NDG_10", "NDG_11"]
                    STK = ["STMP_00", "STMP_01", "STMP_10", "STMP_11"]
                    for t in range(ntile):
                        tsl = slice(t * 128, t * 128 + 128)
                        ld(DG[:], OFD[0, tsl, :], [("OFD", 0, 2 * t + a, b) for a in range(2) for b in range(2)], DGK, key=("ld", "DGc"))
                        ld(NDG[:], OFD[1, tsl, :], [("OFD", 1, 2 * t + a, b) for a in range(2) for b in range(2)], NDGK, key=("ld", "NDGc"))
                        tt(DG[:], DG[:], NDG[:], ALU.add, DGK + NDGK, DGK)
                        tt(SQ[:, 0:256], DG[:], DG[:], ALU.mult, DGK, ["SQ"])
                        P.add("dve", lambda e: e.tensor_reduce(out=RN[:, 0:4], in_=SQ[:, 0:256].rearrange("p (h c) -> p h c", h=4), axis=AX.X, op=ALU.add),
                              reads=["SQ"], writes=["RN"])
                        ts(RN[:, 0:4], RN[:, 0:4], 1.0 / 64.0, EPS, ALU.mult, ALU.add, ["RN"], ["RN"])
                        act(RN[:, 0:4], RN[:, 0:4], AF.Sqrt, ["RN"], ["RN"])
                        recip(RN[:, 8:12], RN[:, 0:4], ["RN"], ["RN"])
                        tt(DG[:].rearrange("p (h c) -> p h c", h=4), DG[:].rearrange("p (h c) -> p h c", h=4), bc3(RN[:, 8:12], [128, 4, 64]),
                           ALU.mult, DGK + ["RN"], DGK)
                        tt(DG[:], DG[:], DNG[:], ALU.mult, DGK + ["DNG"], DGK)
                        proj_tm(PC[:, 0:256], tsl, 768, 256, 128, "PC")
                        act(GA[:], PC[:, 0:256], AF.Silu, ["PC"], STK)
                        tt(DG[:], DG[:], GA[:], ALU.mult, DGK + STK, DGK)
                        stq(MIXD[tsl, 0:256], DG[:], DGK, [("MIXD", t, 0)], ("st", "DGc"))
                P.in_region = False

                load_wg(1040, 768)
                P.in_region = ("sgu" == DBG_REGION)
                K4 = lambda n: [n + "_00", n + "_01", n + "_10", n + "_11"]
                for t in (range(ntile) if "sgu" not in skip else []):
                    tsl = slice(t * 128, t * 128 + 128)
                    o = t % 2
                    uvg, uvk = ((UVG, ["UVG"]), (UVG2, ["UVG2"]))[o]
                    vns, vnk = ((VNS, ["VNS"]), (DG, K4("DG")))[o]
                    sgb, sgk = ((SGB, ["SGB"]), (NDG, K4("NDG")))[o]
                    yb, ybk = ((YB, ["YB"]), (DTT, K4("DTT")))[o]
                    (pa, pak), (pb, pbk) = (((PA, "PA"), (PB, "PB")), ((PC, "PC"), (PD, "PD")))[o]
                    stc, stk = o * 4, "ST%d" % o
                    proj_tm(pa[:], tsl, 0, 512, 128, pak)
                    act(uvg[:], pa[:], AF.Gelu, [pak], uvk)
                    act(JNK[:, 0:256], uvg[:, 256:512], AF.Square, uvk, ["MXT", stk], accum=ST[:, stc:stc + 1])
                    rstd_from_ssq(ST[:, stc:stc + 1], ST[:, stc + 2:stc + 3], 256.0, stk)
                    stt(vns[:], uvg[:, 256:512], ST[:, stc + 2:stc + 3], SGNG[:], ALU.mult, ALU.mult, uvk + [stk, "SGNG"], vnk)
                    for g in range(4):
                        mm(pb[:, g * 64:g * 64 + 64], SWT[:, g, :], vns[:, g * 64:g * 64 + 64], ["SWT"] + vnk, [pbk])
                    proj_tm(pb[:, 256:512], tsl, 512, 256, 128, pbk)
                    act(sgb[:], pb[:, 256:512], AF.Silu, [pbk], sgk)
                    for g in range(4):
                        stt(yb[:, g * 64:g * 64 + 64], pb[:, g * 64:g * 64 + 64], SGBT[:, g:g + 1], uvg[:, g * 64:g * 64 + 64], ALU.add, ALU.mult,
                            [pbk, "SGBT"] + uvk, ybk)
                    tt(yb[:], yb[:], sgb[:], ALU.mult, ybk + sgk, ybk)
                    stq(MIXD[t * 128:t * 128 + 128, 256:512], yb[:], ybk, [("MIXD", t, 1)], ("st", ybk[0]), q="pool")

                P.in_region = False
                load_wg(1808, 512)

                def pool_out(j):
                    o = j % 2
                    PCx, pck = ((PC, "PC"), (PD, "PD"))[o]
                    sgb, sgk = ((SGB, ["SGB"]), (NDG, K4("NDG")))[o]
                    yb, ybk = ((YB, ["YB"]), (DTT, K4("DTT")))[o]
                    typ = 0 if j == 0 else (2 if j == ntile - 1 else 1)
                    for g in range(4):
                        gs = slice(g * 64, g * 64 + 64)
                        terms = []
                        if j > 0:
                            terms.append((3, (j - 1) % 3))
                        terms.append((typ, j % 3))
                        if j < ntile - 1:
                            terms.append((4, (j + 1) % 3))
                        for i, (ty, zi) in enumerate(terms):
                            mm(PCx[:, gs], BAND[:, ty, g, :], ZR[zi][:, gs], ["BAND", "ZR%d" % zi], [pck], start=(i == 0), stop=(i == len(terms) - 1))
                    proj_tm(PCx[:, 256:512], slice(j * 128, j * 128 + 128), 256, 256, 128, pck)
                    act(sgb[:], PCx[:, 256:512], AF.Silu, [pck], sgk)
                    tt(yb[:], PCx[:, 0:256], sgb[:], ALU.mult, [pck] + sgk, ybk)
                    stq(MIXD[j * 128:j * 128 + 128, 512:768], yb[:], ybk, [("MIXD", j, 2)], ("st", ybk[0]), q="pool")

                for t in (range(ntile) if "pool" not in skip else []):
                    tsl = slice(t * 128, t * 128 + 128)
                    xc, xk = ((XCT, "XCT"), (XCT2, "XCT2"))[t % 2]
                    (pa, pak), (pb, pbk) = (((PA, "PA"), (PB, "PB")), ((PE_, "PE_"), (PF, "PF")))[t % 2]
                    for c in range(2):
                        proj_fm(pa[:, c * 128:c * 128 + 128], tsl, c * 128, 128, pak)
                    cp(xc[:].rearrange("p c t -> p (c t)"), pa[:, 0:256], [pak], [xk], eng="act")
                    for c in range(2):
                        mm(pb[:, c * 128:c * 128 + 128], xc[:, c, :], PWB[:, c, :], [xk, "PWB"], [pbk])
                    cp(ZR[t % 3][:], pb[:, 0:256], [pbk], ["ZR%d" % (t % 3)])
                    if t >= 1:
                        pool_out(t - 1)
                if "pool" not in skip:
                    pool_out(ntile - 1)

                load_wg(2320, 512)
                A = T // 64
                ai = 0 if A == 64 else 1
                e2d = cd["c_e2_%d" % A]
                f_on = "fourier" not in skip
                XCTb = [(XCT, "XCT"), (XCT2, "XCT2")]
                ZCSb = [(UVG, "UVG"), (UVG2, "UVG2")]
                for t in (range(ntile) if f_on else []):
                    tsl = slice(t * 128, t * 128 + 128)
                    xc, xk = XCTb[t % 2]
                    zc, zk = ZCSb[t % 2]
                    pa, pak = (PA, "PA") if t % 2 == 0 else (PC, "PC")
                    pb, pbk = (PB, "PB") if t % 2 == 0 else (PD, "PD")
                    for c in range(2):
                        proj_fm(pa[:, c * 128:c * 128 + 128], tsl, c * 128, 128, pak)
                    cp(xc[:].rearrange("p c t -> p (c t)"), pa[:, 0:256], [pak], [xk], eng="act")
                    for cs in range(2):
                        for c in range(2):
                            mm(pb[:, cs * 256 + c * 128:cs * 256 + c * 128 + 128], xc[:, c, :], PCS[:, cs, c, :], [xk, "PCS"], [pbk])
                    cp(zc[:], pb[:], [pbk], [zk])
                    if A != 4:
                        P.add("pool", lambda e, t=t, zc=zc: [e.dma_start(out=ZD[cs, t * 128:t * 128 + 128, :], in_=zc[:, cs * 256:cs * 256 + 256]) for cs in range(2)],
                              reads=[zk], writes=[("ZD", t)], dma=2, sem_key=("st", zk, "pool"))
                if A == 4 and f_on:
                    ld(QKVF[:], cd["c_e256"], [], ["QKVF"])
                    E256 = QKVF[:].rearrange("p (t s k) -> p t s k", t=2, s=2)
                    for kt in range(2):
                        pe, pek = (PE_, "PE_") if kt == 0 else (PF, "PF")
                        pg, pgk = (PG, "PG") if kt == 0 else (PT, "PT")
                        ga, gak = (STMP[:], ["STMP_00", "STMP_01", "STMP_10", "STMP_11"]) if kt == 0 else (SGB[:], ["SGB"])
                        yd, ydk = (YB[:], "YB") if kt == 0 else (VNS[:], "VNS")
                        i_ = 0
                        for tt2 in range(2):
                            zc, zk = ZCSb[tt2]
                            for cs in range(2):
                                mm(pe[:, 0:256], E256[:, tt2, cs, kt * 128:(kt + 1) * 128], zc[:, cs * 256:(cs + 1) * 256], ["QKVF", zk], [pek],
                                   start=(i_ == 0), stop=(i_ == 3))
                                i_ += 1
                        proj_tm(pg[:, 0:256], slice(kt * 128, kt * 128 + 128), 256, 256, 128, pgk)
                        act(ga, pg[:, 0:256], AF.Silu, [pgk], gak)
                        tt(yd, pe[:, 0:256], ga, ALU.mult, [pek] + gak, [ydk])
                        stq(MIXD[kt * 128:(kt + 1) * 128, 768:1024], yd, [ydk], [("MIXD", "f", 2 * kt), ("MIXD", "f", 2 * kt + 1)], ("st", ydk), q="pool")
                ZDall = [("ZD", t) for t in range(ntile)]
                zdv = ZD.rearrange("s (a b) c -> s a b c", b=64)
                ZAb = [(ZA[0:A], "ZA"), (QKVF[0:A, :].rearrange("p (s c) -> p s c", s=2), "QKVF")]
                VVb = [(VV[0:A], "VV"), (SQ[0:A, :].rearrange("p (s c) -> p s c", s=2), "SQ")]
                CA_, SA_, NSA_ = DFTA[0:A, ai, 0, 0:A], DFTA[0:A, ai, 1, 0:A], DFTA[0:A, ai, 2, 0:A]
                for bb in (range(32) if f_on and "f1" not in skip and A != 4 else []):
                    za, zak = ZAb[bb % 2]
                    vv, vvk = VVb[bb % 2]
                    pc, pck = (PE_, "PE_") if bb % 2 == 0 else (PF, "PF")
                    pd, pdk = (PG, "PG") if bb % 2 == 0 else (PT, "PT")
                    P.add("sp", lambda e, bb=bb, A=A, zdv=zdv, za=za: [e.dma_start(out=za[:, cs, :].rearrange("a (b c) -> a b c", b=2), in_=zdv[cs, 0:A, bb * 2:bb * 2 + 2, :]) for cs in range(2)],
                          reads=ZDall, writes=[zak], dma=2, sem_key=("ld", zak))
                    mm(pc[0:A, :], CA_, za[:, 0, :], ["DFTA", zak], [pck], start=True, stop=False)
                    mm(pc[0:A, :], NSA_, za[:, 1, :], ["DFTA", zak], [pck], start=False, stop=True)
                    mm(pd[0:A, :], CA_, za[:, 1, :], ["DFTA", zak], [pdk], start=True, stop=False)
                    mm(pd[0:A, :], SA_, za[:, 0, :], ["DFTA", zak], [pdk], start=False, stop=True)
                    cp(vv[:, 0, :], pc[0:A, :], [pck], [vvk])
                    cp(vv[:, 1, :], pd[0:A, :], [pdk], [vvk], eng="act")
                    P.add("pool", lambda e, bb=bb, A=A, vv=vv: [e.dma_start(out=VD[ri, 0:A, bb * 2:bb * 2 + 2, :], in_=vv[:, ri, :].rearrange("a (b c) -> a b c", b=2)) for ri in range(2)],
                          reads=[vvk], writes=[("VD", bb)], dma=2, sem_key=("st", vvk, "pool"))
                VDall = [("VD", bb) for bb in range(32)]
                mixv = MIXD[0:T, :].rearrange("(q a) c -> a q c", a=A)
                VBb = [(VB, "VB"), (VB2, "VB2")]
                E2b = [(E2, "E2"), (E2_2, "E2_2")]
                GAb = [(STMP[0:64, :], ["STMP_00", "STMP_01"]), (SGB[0:64, :], ["SGB"])]
                YDb = [(YB[0:64, :], "YB"), (VNS[0:64, :], "VNS")]
                for p in (range(A) if f_on and "f2" not in skip and A != 4 else []):
                    vb, vbk = VBb[p % 2]
                    e2, e2k = E2b[p % 2]
                    ga, gak = GAb[p % 2]
                    yd, ydk = YDb[p % 2]
                    pe, pek = (PA, "PA") if p % 2 == 0 else (PB, "PB")
                    pg, pgk = (PC, "PC") if p % 2 == 0 else (PD, "PD")
                    P.add("sp", lambda e, p=p, vb=vb: [e.dma_start(out=vb[:, ri, :], in_=VD[ri, p, :, :]) for ri in range(2)],
                          reads=VDall, writes=[vbk], dma=2, sem_key=("ld", vbk))
                    ld(e2[:], e2d[p], [], [e2k])
                    mm(pe[0:64, 0:256], e2[:, 0, :], vb[:, 0, :], [e2k, vbk], [pek], start=True, stop=False)
                    mm(pe[0:64, 0:256], e2[:, 1, :], vb[:, 1, :], [e2k, vbk], [pek], start=False, stop=True)
                    proj_tm(pg[0:64, 0:256], slice(p, p + A * 63 + 1, A), 256, 256, 64, pgk)
                    act(ga, pg[0:64, 0:256], AF.Silu, [pgk], gak)
                    tt(yd, pe[0:64, 0:256], ga, ALU.mult, [pek] + gak, [ydk])
                    stq(mixv[p, :, 768:1024], yd, [ydk], [("MIXD", "f", p)], ("st", ydk), q="pool")

                mix_keys_f = [("MIXD", "f", p) for p in range(A)]
                P.add("pool", lambda e, wo_v=wo_v: e.dma_start(out=WG[:, :, 0:1024], in_=wo_v), reads=[], writes=["WG"], dma=1, sem_key=("ld", "WG"))
                for t in (range(ntile) if "p3" not in skip else []):
                    g0 = tok0 // 128 + t
                    mx, mxk = ((XT2, "XT2"), (QKVF, "QKVF"))[t % 2]
                    xt, xtk = ((XT, "XT"), (SQ, "SQ"))[t % 2]
                    mt, mtk = ((MXT, "MXT"), (XN[:].rearrange("p (k t) -> p k t", k=8), "XN"))[t % 2]
                    (pa, pak), (pb, pbk) = (((PA, "PA"), (PB, "PB")), ((PE_, "PE_"), (PF, "PF")))[t % 2]
                    pouts = (((PC, "PC"), (PD, "PD")), ((PG, "PG"), (PT, "PT")))[t % 2]
                    ld(mx[:], MIXD[t * 128:(t + 1) * 128, :],
                       [("MIXD", t, 0), ("MIXD", t, 1), ("MIXD", t, 2)] + mix_keys_f, [mxk])
                    for k in range(8):
                        dst, dk = (pa, pak) if k < 4 else (pb, pbk)
                        tr(dst[:, (k % 4) * 128:(k % 4) * 128 + 128], mx[:, k * 128:(k + 1) * 128], IDN[:], [mxk, "IDN"], [dk])
                    cp(mt[:, 0:4, :].rearrange("p k t -> p (k t)"), pa[:], [pak], [mtk])
                    cp(mt[:, 4:8, :].rearrange("p k t -> p (k t)"), pb[:], [pbk], [mtk], eng="act")
                    ld(xt[:], XS[g0 * 128:(g0 + 1) * 128, :], [("XS", g0)], [xtk])
                    for n in range(2):
                        dst, dk = pouts[n]
                        for k in range(8):
                            mm(dst[:], mt[:, k, :], WG[:, k, n * 512:(n + 1) * 512], [mtk, "WG"], [dk], start=(k == 0), stop=(k == 7))
                        tt(mx[:, n * 512:(n + 1) * 512], dst[:], GATE[:, cond, n * 512:(n + 1) * 512], ALU.mult, [dk, "GATE"], [mxk])
                    tt(xt[:], xt[:], mx[:], ALU.add, [xtk, mxk], [xtk], eng="pool")
                    stq(XS[g0 * 128:(g0 + 1) * 128, :], xt[:], [xtk], [("XS", g0)], ("st", xtk), q="pool")

        outs = []
        ld(ABG[:], final_norm_g.partition_broadcast(128), [], ["ABG"])
        fin_tiles = []
        for (tok0, T, cond, pidx) in seqs:
            fin_tiles += list(range(tok0 // 128, (tok0 + T) // 128))
        for g0 in fin_tiles:
            ld(XT[:], XS[g0 * 128:(g0 + 1) * 128, :], [("XS", g0)], ["XT"])
            act(JNK, XT[:], AF.Square, ["XT"], ["MXT", "ST0"], accum=ST[:, 0:1])
            rstd_from_ssq(ST[:, 0:1], ST[:, 2:3], 1024.0, "ST0")
            stt(XT2[:], XT[:], ST[:, 2:3], ABG[:], ALU.mult, ALU.mult, ["XT", "ST0", "ABG"], ["XT2"])
            if g0 < 32:
                dst = ys[g0 * 128:(g0 + 1) * 128, :]
            else:
                dst = yp[(g0 - 32) * 128:(g0 - 31) * 128, :]
            stq(dst, XT2[:], ["XT2"], [("Y", g0)], ("st", "XT2"), q="pool")
            outs.append(("Y", g0))
        for (tok0, T, cond, pidx) in seqs:
            for l in range(nl_run):
                for d in range(2):
                    if pidx is not None:
                        outs.append(("ns", pidx, l, d))
        P.add("sp", None, reads=outs)
        P.build()
    return nc, consts


_CACHE = {}


def kernel(**inputs):
    if "nc" not in _CACHE:
        _CACHE["nc"] = build_program()
    nc, consts = _CACHE["nc"]
    f = lambda a: np.ascontiguousarray(np.asarray(a, dtype=np.float32))
    shared = {k: f(inputs[k]) for k in ["ada_w", "ada_b", "norm_g", "w_in", "conv_qkv", "dn_norm_g", "sgu_norm_g", "sgu_w", "sgu_b",
                                         "pool_w", "pool_scale", "fourier_w", "w_out", "final_norm_g"]}
    shared["a_log"] = f(inputs["a_log"]).reshape(NL, 8)
    shared["dt_bias"] = f(inputs["dt_bias"]).reshape(NL, 8)
    for k, v in consts.items():
        shared[k] = f(v)
    xsm = f(inputs["x_sample"])
    xpr = f(inputs["x_prompt"])
    sdl = f(inputs["state_delta"])
    c = f(inputs["c"])
    cctx = f(inputs["c_ctx"])
    in_maps = []
    for i in range(8):
        m = dict(shared)
        m["xs"] = xsm[i]
        m["xp"] = np.ascontiguousarray(xpr[4 * i:4 * i + 4].reshape(4 * TPR, 1024))
        m["sd"] = sdl[i]
        m["cc"] = np.ascontiguousarray(np.stack([c[i], cctx], axis=0))
        in_maps.append(m)
    res = run_bass_kernel_spmd(nc, in_maps, core_ids=list(range(8)))
    y_sample = np.stack([np.asarray(r["ys"], np.float32) for r in res.results], axis=0)
    y_prompt = np.concatenate([np.asarray(r["yp"], np.float32).reshape(4, TPR, 1024) for r in res.results], axis=0)
    ns = np.concatenate([np.asarray(r["ns"], np.float32) for r in res.results], axis=0)
    return (y_prompt, y_sample, ns)
```

```python
import contextlib
import numpy as np
import concourse.bass as bass
import concourse.mybir as mybir
from concourse.bass_utils import run_bass_kernel_spmd

F32 = mybir.dt.float32
BF16 = mybir.dt.bfloat16
AF = mybir.ActivationFunctionType
ALU = mybir.AluOpType
AX = mybir.AxisListType
ENGS = ("pe", "act", "dve", "pool", "sp")
EPS = 1e-6
NL = 4
TSAMP = 4096
TPR = 256
NTOK = TSAMP + 4 * TPR


PSUM_KEYS = {"PA", "PB", "PC", "PD", "PE_", "PF", "PG", "PT"}


class Prog:
    def __init__(self, nc):
        self.nc = nc
        self.ops = []
        self.last_w = {}
        self.readers = {}

    limit = None
    in_region = False
    region_count = 0

    rec = None

    def add(self, eng, fn, reads=(), writes=(), dma=0, sem_key=None):
        if self.rec is not None:
            self.rec.append((eng, fn, list(reads), list(writes), dma, sem_key))
            return
        if self.in_region and self.limit is not None:
            if self.region_count >= self.limit:
                return
            self.region_count += 1
        i = len(self.ops)
        deps = []
        for r in reads:
            lw = self.last_w.get(r)
            if lw is not None:
                deps.append((lw, "raw"))
            if r in PSUM_KEYS:
                for rd in self.readers.get(r, ()):
                    deps.append((rd, "war"))
        for w in writes:
            lw = self.last_w.get(w)
            if lw is not None:
                deps.append((lw, "waw"))
            for rd in self.readers.get(w, ()):
                deps.append((rd, "war"))
        for r in reads:
            self.readers.setdefault(r, []).append(i)
        for w in writes:
            self.last_w[w] = i
            self.readers[w] = []
        assert (not dma) or sem_key is not None
        self.ops.append(dict(eng=eng, fn=fn, deps=deps, dma=dma, sem_key=sem_key, signal=False))
        return i

    def build(self):
        nc = self.nc
        ops = self.ops
        cnt = {e: 0 for e in ENGS}
        for o in ops:
            o["eidx"] = cnt[o["eng"]]
            cnt[o["eng"]] += 1
        dma_cum = {}
        for o in ops:
            if o["dma"]:
                k = o["sem_key"]
                dma_cum[k] = dma_cum.get(k, 0) + 16 * o["dma"]
                o["dma_val"] = dma_cum[k]
        known = {e: {f: -1 for f in ENGS} for e in ENGS}
        known_dma = {e: {} for e in ENGS}
        for o in ops:
            e = o["eng"]
            w_eng = {}
            w_dma = {}
            for (d, kind) in o["deps"]:
                D = ops[d]
                if D["dma"]:
                    k = D["sem_key"]
                    if known_dma[e].get(k, 0) >= D["dma_val"]:
                        continue
                    w_dma[k] = max(w_dma.get(k, 0), D["dma_val"])
                else:
                    f = D["eng"]
                    if f == e:
                        if e == "pe" or e == "sp":
                            continue
                        pass
                    if known[e][f] >= D["eidx"]:
                        continue
                    if f not in w_eng or ops[w_eng[f]]["eidx"] < D["eidx"]:
                        w_eng[f] = d
            o["w_eng"] = w_eng
            o["w_dma"] = w_dma
            for f, d in w_eng.items():
                ops[d]["signal"] = True
                known[e][f] = ops[d]["eidx"]
            for k, v in w_dma.items():
                known_dma[e][k] = v
        sig = {e: 0 for e in ENGS}
        for o in ops:
            if o["signal"] and not o["dma"]:
                sig[o["eng"]] += 1
                o["sig_val"] = sig[o["eng"]]
        with contextlib.ExitStack() as st:
            esem = {e: st.enter_context(nc.semaphore("s_" + e)) for e in ENGS}
            dsem = {}
            for k in dma_cum:
                dsem[k] = st.enter_context(nc.semaphore("d%d" % len(dsem)))
            block = st.enter_context(nc.Block())
            per = {e: [o for o in ops if o["eng"] == e] for e in ENGS}

            def emit(engh, lst):
                for o in lst:
                    for f, d in o["w_eng"].items():
                        engh.wait_ge(esem[f], ops[d]["sig_val"])
                    for k, v in o["w_dma"].items():
                        engh.wait_ge(dsem[k], v)
                    if o["fn"] is None:
                        continue
                    ins = o["fn"](engh)
                    if o["dma"]:
                        if not isinstance(ins, (list, tuple)):
                            ins = [ins]
                        assert len(ins) == o["dma"], (len(ins), o["dma"])
                        for x in ins:
                            x.then_inc(dsem[o["sem_key"]], 16)
                    elif o["signal"]:
                        ins.then_inc(esem[o["eng"]], 1)

            @block.sync
            def _(eng):
                emit(eng, per["sp"])

            @block.scalar
            def _(eng):
                emit(eng, per["act"])

            @block.vector
            def _(eng):
                emit(eng, per["dve"])

            @block.gpsimd
            def _(eng):
                emit(eng, per["pool"])

            @block.tensor
            def _(eng):
                emit(eng, per["pe"])


def host_consts():
    c = {}
    c["c_ident"] = np.eye(128, dtype=np.float32)
    r = np.arange(64)[:, None]
    cc = np.arange(64)[None, :]
    su = np.where(cc > r, -1.0, 0.0).astype(np.float32)
    sl = np.where(cc < r, -1.0, 0.0).astype(np.float32)
    mf = np.where(cc >= r, 0.0, -30000.0).astype(np.float32)
    mb = np.where(cc <= r, 0.0, -30000.0).astype(np.float32)
    i64 = np.eye(64, dtype=np.float32)
    t4 = lambda m: m
    blk = lambda b: (r // b == cc // b)
    md8 = blk(8).astype(np.float32)
    moff = [(blk(2 * b) & ~blk(b)).astype(np.float32) for b in (8, 16, 32)]
    c["c_m64"] = np.stack([su, sl, mf, mb, i64, md8] + moff, axis=1).astype(np.float32)
    tri = np.stack([(r <= cc), (r >= cc), np.ones((64, 64), bool)], axis=1).astype(np.float32)
    c["c_tri"] = tri
    T = TSAMP
    rows = T // 64
    rr = np.repeat(np.arange(rows, dtype=np.float32), 64)
    col = np.tile(np.arange(64, dtype=np.float32), rows)
    nf = 256
    freqs = np.power(np.float32(10000.0), -np.arange(nf, dtype=np.float32) / np.float32(nf)).astype(np.float32)
    ar = rr[:, None] * freqs[None]
    ac = col[:, None] * freqs[None]
    c["c_pos"] = np.concatenate([np.sin(ar), np.cos(ar), np.sin(ac), np.cos(ac)], axis=-1).astype(np.float32)
    band = np.zeros((128, 5, 4, 128), np.float32)
    for gi, w in enumerate((2, 4, 8, 16)):
        Tn = 384
        Fm = np.zeros((Tn, Tn), np.float64)
        for t in range(Tn):
            lo = min(max(t - w // 2, 0), Tn)
            hi = min(max(t + w - w // 2, 0), Tn)
            Fm[t, lo:hi] = 1.0 / (hi - lo)
        Fm -= np.eye(Tn)
        band[:, 0, gi, :] = Fm[0:128, 0:128].T
        band[:, 1, gi, :] = Fm[128:256, 128:256].T
        band[:, 2, gi, :] = Fm[256:384, 256:384].T
        band[:, 3, gi, :] = Fm[128:256, 0:128].T
        band[:, 4, gi, :] = Fm[128:256, 256:384].T
    c["c_band"] = band
    k64 = np.arange(64)
    ang = 2 * np.pi * np.outer(k64, k64) / 64.0
    C64 = np.cos(ang)
    S64 = np.sin(ang)
    pad = np.zeros((64, 2, 2, 128), np.float32)
    for half in range(2):
        pad[:, half, 0, half * 64:(half + 1) * 64] = C64
        pad[:, half, 1, half * 64:(half + 1) * 64] = S64
    c["c_dftpad"] = pad
    dA = np.zeros((64, 2, 3, 64), np.float32)
    for ai, A in enumerate((64, 4)):
        a = np.arange(A)
        an = 2 * np.pi * np.outer(a, a) / A
        dA[:A, ai, 0, :A] = np.cos(an)
        dA[:A, ai, 1, :A] = np.sin(an)
        dA[:A, ai, 2, :A] = -np.sin(an)
        Tt = 64 * A
        p = np.arange(A)[:, None, None]
        b = np.arange(64)[None, :, None]
        q = np.arange(64)[None, None, :]
        be = 2 * np.pi * ((b * (p + A * q)) % Tt) / Tt
        nrm = 1.0 / np.sqrt(64.0 * Tt)
        e2 = np.stack([np.cos(be) * nrm, -np.sin(be) * nrm], axis=2).astype(np.float32)
        c["c_e2_%d" % A] = e2
    c["c_dfta"] = dA
    tt_ = np.arange(256)
    th = 2 * np.pi * np.outer(tt_, tt_) / 256.0
    nrm = 1.0 / np.sqrt(64.0 * 256.0)
    e256 = np.stack([np.cos(th) * nrm, -np.sin(th) * nrm], axis=1)
    e256 = e256.reshape(2, 128, 2, 256).transpose(1, 0, 2, 3)
    c["c_e256"] = np.ascontiguousarray(e256.reshape(128, 1024)).astype(np.float32)
    return c


import os
DBG_REGION = os.environ.get("DBG_REGION", "")
DBG_LIMIT = os.environ.get("DBG_LIMIT")


def build_program(nl_run=NL, seq_sel=(0, 1, 2, 3, 4), skip=()):
    nc = bass.Bass("TRN2", target_bir_lowering=False)
    P = Prog(nc)
    P.limit = int(DBG_LIMIT) if DBG_LIMIT else None
    consts = host_consts()

    def din(name, shape):
        return nc.dram_tensor(name, list(shape), F32, kind="ExternalInput").ap()

    xs = din("xs", [TSAMP, 1024])
    xp = din("xp", [4 * TPR, 1024])
    sd = din("sd", [NL, 2, 4, 64, 64])
    ccd = din("cc", [2, 1024])
    ada_w = din("ada_w", [NL, 1024, 3072])
    ada_b = din("ada_b", [NL, 3072])
    norm_g = din("norm_g", [NL, 1024])
    w_in = din("w_in", [NL, 1024, 2832])
    conv_qkv = din("conv_qkv", [NL, 4, 768])
    a_log = din("a_log", [NL, 8])
    dt_bias = din("dt_bias", [NL, 8])
    dn_norm_g = din("dn_norm_g", [NL, 64])
    sgu_norm_g = din("sgu_norm_g", [NL, 256])
    sgu_w = din("sgu_w", [NL, 4, 128, 128])
    sgu_b = din("sgu_b", [NL, 4, 128])
    pool_w = din("pool_w", [NL, 4, 64, 64])
    pool_scale = din("pool_scale", [NL, 256])
    fourier_w = din("fourier_w", [NL, 4, 64, 64])
    w_out = din("w_out", [NL, 1024, 1024])
    final_norm_g = din("final_norm_g", [1024])
    cd = {k: din(k, v.shape) for k, v in consts.items()}

    ys = nc.dram_tensor("ys", [TSAMP, 1024], F32, kind="ExternalOutput").ap()
    yp = nc.dram_tensor("yp", [4 * TPR, 1024], F32, kind="ExternalOutput").ap()
    nsd = nc.dram_tensor("ns", [4, NL, 2, 4, 64, 64], F32, kind="ExternalOutput").ap()

    XS = nc.dram_tensor("XS", [NTOK, 1024], F32, kind="Internal").ap()
    MIXD = nc.dram_tensor("MIXD", [TSAMP, 1024], F32, kind="Internal").ap()
    QKVD = nc.dram_tensor("QKVD", [TSAMP, 768], BF16, kind="Internal").ap()
    OFD = nc.dram_tensor("OFD", [2, TSAMP, 256], F32, kind="Internal").ap()
    ZD = nc.dram_tensor("ZD", [2, TSAMP, 256], F32, kind="Internal").ap()
    VD = nc.dram_tensor("VD", [2, 64, 64, 256], F32, kind="Internal").ap()

    st = contextlib.ExitStack()
    with st:
        def sb(name, shape, dt=F32):
            return st.enter_context(nc.sbuf_tensor(name, list(shape), dt))

        def ps(name, shape, dt=F32):
            return st.enter_context(nc.psum_tensor(name, list(shape), dt))

        HT = sb("HT", [128, 8, TSAMP], BF16)
        WG = sb("WG", [128, 8, 1040], BF16)
        AW = sb("AW", [128, 8, 256], BF16)
        GATE = sb("GATE", [128, 2, 1024])
        XT = sb("XT", [128, 1024])
        XT2 = sb("XT2", [128, 1024])
        XN = sb("XN", [128, 1024], BF16)
        MXT = sb("MXT", [128, 8, 128], BF16)
        JNK = MXT[:].rearrange("p k t -> p (k t)")
        IDN = sb("IDN", [128, 128])
        IDB = sb("IDB", [128, 128], BF16)
        M64 = sb("M64", [128, 9, 64])
        TRI = sb("TRI", [128, 3, 64])
        BAND = sb("BAND", [128, 5, 4, 128])
        DFTPAD = sb("DFTPAD", [64, 2, 2, 128])
        DFTA = sb("DFTA", [64, 2, 3, 64])
        SCB = sb("SCB", [128, 8, 2, 128], BF16)
        TMPL = sb("TMPL", [128, 128])
        CCT = sb("CCT", [128, 16])
        SCF = sb("SCF", [128, 16])
        SCb = sb("SCb", [128, 8, 2], BF16)
        NG = sb("NG", [128, 8])
        AB = sb("AB", [128, 24])
        ABG = sb("ABG", [128, 1024])
        MODT = sb("MODT", [128, 24, 2])
        AM = sb("AM", [128, 8, 2])
        CW = sb("CW", [128, 24])
        ALB = sb("ALB", [128, 8])
        DTB = sb("DTB", [128, 8])
        NEGA = sb("NEGA", [128, 8])
        DNG = sb("DNG", [128, 256])
        SGNG = sb("SGNG", [128, 256])
        SWT = sb("SWT", [128, 4, 128])
        SGBT = sb("SGBT", [128, 4])
        PWB = sb("PWB", [128, 2, 128])
        PSC = sb("PSC", [128, 256])
        FW = sb("FW", [64, 4, 64])
        PCS = sb("PCS", [128, 2, 2, 128])
        ST = sb("ST0", [128, 8])
        TL = 128
        RAW = sb("RAW", [128, 6, TL + 3])
        CS = sb("CS", [128, 6, TL])
        QKVF = sb("QKVF", [128, 1024])
        QKV3 = sb("QKV3", [128, 768], BF16)
        SALL = sb("SALL", [128, 64, 24])
        TB_ = sb("TB_", [128, 32, 4])
        TZ_ = sb("TZ_", [128, 32, 4])
        TG_ = sb("TG_", [128, 32, 4])
        KS = sb("KS", [128, 256], BF16)
        RH = sb("RH", [128, 4, 128], BF16)
        KDEC = sb("KDEC", [128, 256], BF16)
        QDEC = sb("QDEC", [128, 256], BF16)
        TR = sb("TR", [128, 1024], BF16)
        SQ = sb("SQ", [128, 1024])
        XA = sb("XA", [128, 256], BF16)
        XB = sb("XB", [128, 256], BF16)
        COF = sb("COF", [128, 256], BF16)
        SSb = sb("SSb", [128, 256], BF16)
        DG = sb("DG", [128, 256])
        NDG = sb("NDG", [128, 256])
        DTT = sb("DTT", [128, 256])
        Cb = [sb("C_a", [128, 256], BF16), sb("C_b", [128, 256], BF16)]
        Bb = [sb("B_a", [128, 256], BF16), sb("B_b", [128, 256], BF16)]
        Tb = [sb("T_a", [128, 256], BF16), sb("T_b", [128, 256], BF16)]
        QKT = sb("QKT", [128, 256], BF16)
        MTT = sb("MTT", [128, 256], BF16)
        UU = sb("UU", [128, 256])
        WB_ = sb("WB_", [128, 256], BF16)
        WT = sb("WT", [128, 256], BF16)
        VN = sb("VN", [128, 256], BF16)
        OO = sb("OO", [128, 256])
        SS = sb("SS", [128, 256])
        STMP = sb("STMP", [128, 256])
        GA = STMP
        RN = sb("RN", [128, 16])
        UVG = sb("UVG", [128, 512])
        VNS = sb("VNS", [128, 256])
        SGB = sb("SGB", [128, 256])
        YB = sb("YB", [128, 256])
        XCT = sb("XCT", [128, 2, 128])
        ZR = [sb("ZR%d" % i, [128, 256]) for i in range(3)]
        ZCS = UVG
        ZA = sb("ZA", [64, 2, 512])
        VV = sb("VV", [64, 2, 512])
        VB = sb("VB", [64, 2, 256])
        E2 = sb("E2", [64, 2, 64])
        VB2 = sb("VB2", [64, 2, 256])
        E2_2 = sb("E2_2", [64, 2, 64])
        XCT2 = sb("XCT2", [128, 2, 128])
        UVG2 = sb("UVG2", [128, 512])
        YD = YB[0:64, :]
        if os.environ.get("EXTRA_SB"):
            sb("EXTRA", [128, int(os.environ["EXTRA_SB"]) // 4])
        PA = ps("PA", [128, 512])
        PB = ps("PB", [128, 512])
        PC = ps("PC", [128, 512])
        PD = ps("PD", [128, 512])
        PE_ = ps("PE_", [128, 512])
        PF = ps("PF", [128, 512])
        PG = ps("PG", [128, 512])
        PT = ps("PT", [128, 512])
        PTb = PT[:].bitcast(BF16)

        def ld(out, in_, r, w, q="sp", key=None, n=1, nc_ok=False):
            kw = dict(allow_slow_non_contiguous=True) if nc_ok else {}
            P.add(q, lambda e: e.dma_start(out=out, in_=in_, **kw), reads=r, writes=w, dma=1, sem_key=key or ("ld", w[0]))

        def stq(out, in_, r, w, key, q="sp"):
            P.add(q, lambda e: e.dma_start(out=out, in_=in_), reads=r, writes=w, dma=1, sem_key=(key, q))

        def mm(out, lhsT, rhs, r, w, start=True, stop=True):
            P.add("pe", lambda e: e.matmul(out, lhsT=lhsT, rhs=rhs, start=start, stop=stop), reads=r, writes=w)

        def tr(out, in_, ident, r, w):
            P.add("pe", lambda e: e.transpose(out=out, in_=in_, identity=ident), reads=r, writes=w)

        def act(out, in_, func, r, w, bias=None, scale=None, accum=None):
            kw = {}
            if bias is not None:
                kw["bias"] = bias
            if scale is not None:
                kw["scale"] = scale
            if accum is not None:
                kw["accum_out"] = accum
            P.add("act", lambda e: e.activation(out=out, in_=in_, func=func, **kw), reads=r, writes=w)

        def tt(out, in0, in1, op, r, w, eng="dve"):
            P.add(eng, lambda e: e.tensor_tensor(out=out, in0=in0, in1=in1, op=op), reads=r, writes=w)

        def ts(out, in0, s1, s2, op0, op1, r, w, eng="dve"):
            if op1 is None:
                P.add(eng, lambda e: e.tensor_scalar(out=out, in0=in0, scalar1=s1, scalar2=None, op0=op0), reads=r, writes=w)
            else:
                P.add(eng, lambda e: e.tensor_scalar(out=out, in0=in0, scalar1=s1, scalar2=s2, op0=op0, op1=op1), reads=r, writes=w)

        def stt(out, in0, scalar, in1, op0, op1, r, w):
            P.add("dve", lambda e: e.scalar_tensor_tensor(out=out, in0=in0, scalar=scalar, in1=in1, op0=op0, op1=op1), reads=r, writes=w)

        def cp(out, in_, r, w, eng="dve"):
            if eng == "act":
                P.add(eng, lambda e: e.activation(out=out, in_=in_, func=AF.Identity), reads=r, writes=w)
            else:
                P.add(eng, lambda e: e.tensor_copy(out=out, in_=in_), reads=r, writes=w)

        def recip(out, in_, r, w):
            P.add("dve", lambda e: e.reciprocal(out=out, in_=in_), reads=r, writes=w)

        def rstd_from_ssq(ssq_ap, out_ap, n, key):
            ts(ssq_ap, ssq_ap, 1.0 / n, EPS, ALU.mult, ALU.add, [key], [key])
            act(ssq_ap, ssq_ap, AF.Sqrt, [key], [key])
            recip(out_ap, ssq_ap, [key], [key])

        def load_T(dst_ap, src_ap, n, dkey):
            ld(TMPL[0:n, :], src_ap, [], ["TMPL"])
            tr(PG[:, 0:n], TMPL[0:n, :], IDN[0:n, 0:n], ["TMPL", "IDN"], ["PG"])
            cp(dst_ap, PG[:, 0:n], ["PG"], [dkey])

        def bc3(ap, shape):
            return ap.unsqueeze(2).to_broadcast(shape)

        ld(IDN[:], cd["c_ident"], [], ["IDN"])
        cp(IDB[:], IDN[:], ["IDN"], ["IDB"])
        for hf in range(2):
            ld(M64[hf * 64:hf * 64 + 64], cd["c_m64"], [], ["M64"])
            ld(TRI[hf * 64:hf * 64 + 64], cd["c_tri"], [], ["TRI"])
        ld(BAND[:], cd["c_band"], [], ["BAND"])
        ld(DFTPAD[:], cd["c_dftpad"], [], ["DFTPAD"])
        ld(DFTA[:], cd["c_dfta"], [], ["DFTA"])
        def masks(hf):
            psl = slice(hf * 64, hf * 64 + 64)
            bh = lambda i: M64[psl, i, :].unsqueeze(1).to_broadcast([64, 2, 64])
            return dict(SUL=(bh(0), bh(1)), MINC=(bh(2), bh(3)), ID4=bh(4), MD8=bh(5), MOFF=[bh(6), bh(7), bh(8)])
        MSK = [masks(0), masks(1)]

        load_T(CCT[:], ccd.rearrange("c (k p) -> (c k) p", p=128), 16, "CCT")
        act(SCF[:], CCT[:], AF.Silu, ["CCT"], ["SCF"])
        cp(SCb[:].rearrange("p k c -> p c k"), SCF[:].rearrange("p (c k) -> p c k", c=2), ["SCF"], ["SCb"])
        cp(SCB[:], SCb[:].unsqueeze(3).to_broadcast([128, 8, 2, 128]), ["SCb"], ["SCB"])

        for t in (range(TSAMP // 128) if 0 in seq_sel else []):
            ld(XT[:], xs[t * 128:(t + 1) * 128, :], [], ["XT"])
            ld(XT2[:], cd["c_pos"][t * 128:(t + 1) * 128, :], [], ["XT2"])
            tt(XT[:], XT[:], XT2[:], ALU.add, ["XT", "XT2"], ["XT"])
            stq(XS[t * 128:(t + 1) * 128, :], XT[:], ["XT"], [("XS", t)], ("st", "XT"))
        for t in range(8):
            if (1 + t // 2) in seq_sel:
                ld(XT[:], xp[t * 128:(t + 1) * 128, :], [], ["XT"])
                stq(XS[TSAMP + t * 128:TSAMP + (t + 1) * 128, :], XT[:], ["XT"], [("XS", 32 + t)], ("st", "XT"))

        seqs = [(0, TSAMP, 0, None)] + [(TSAMP + i * TPR, TPR, 1, i) for i in range(4)]
        seqs = [seqs[i] for i in seq_sel]

        for l in range(nl_run):
            load_T(NG[:], norm_g[l].rearrange("(k p) -> k p", p=128), 8, "NG")
            load_T(AB[:], ada_b[l].rearrange("(k p) -> k p", p=128), 24, "AB")
            load_T(CW[:], conv_qkv[l].rearrange("j (c p) -> (j c) p", p=128), 24, "CW")
            ld(ABG[:], ada_b[l, 2048:3072].partition_broadcast(128), [], ["ABG"])
            ld(ALB[:], a_log[l].partition_broadcast(128), [], ["ALB"])
            ld(DTB[:], dt_bias[l].partition_broadcast(128), [], ["DTB"])
            act(NEGA[:], ALB[:], AF.Exp, ["ALB"], ["NEGA"])
            ts(NEGA[:], NEGA[:], -1.0, None, ALU.mult, None, ["NEGA"], ["NEGA"])
            for h in range(4):
                ld(DNG[:, h * 64:(h + 1) * 64], dn_norm_g[l].partition_broadcast(128), [], ["DNG"], key=("ld", "DNG"))
            ld(SGNG[:], sgu_norm_g[l].partition_broadcast(128), [], ["SGNG"])
            ld(PSC[:], pool_scale[l].partition_broadcast(128), [], ["PSC"])
            load_T(SGBT[:], sgu_b[l], 4, "SGBT")
            for g in range(4):
                ld(TMPL[:], sgu_w[l, g], [], ["TMPL"])
                tr(PG[:, 0:128], TMPL[:], IDN[:], ["TMPL", "IDN"], ["PG"])
                cp(SWT[:, g, :], PG[:, 0:128], ["PG"], ["SWT"])
            P.add("pool", lambda e: e.memset(PWB[:], 0.0), reads=[], writes=["PWB"])
            for g in range(4):
                hb = (g % 2) * 64
                ld(PWB[hb:hb + 64, g // 2, hb:hb + 64], pool_w[l, g], [], ["PWB"], key=("ld", "PWB"))
            for c in range(2):
                tt(PWB[:, c, :], PWB[:, c, :], PSC[:, c * 128:(c + 1) * 128], ALU.mult, ["PWB", "PSC"], ["PWB"])
            ld(FW[:], fourier_w[l].rearrange("g c d -> c g d"), [], ["FW"])
            for g in range(4):
                for cs in range(2):
                    mm(PG[:, 0:64], DFTPAD[:, g % 2, cs, :], FW[:, g, :], ["DFTPAD", "FW"], ["PG"])
                    cp(PCS[:, cs, g // 2, (g % 2) * 64:(g % 2) * 64 + 64], PG[:, 0:64], ["PG"], ["PCS"])
            wo_v = w_out[l].rearrange("(k p) n -> p k n", p=128)
            aw_v = ada_w[l].rearrange("(k p) n -> p k n", p=128)
            for n in range(12):
                P.add("pool", lambda e, n=n, aw_v=aw_v: e.dma_start(out=AW[:], in_=aw_v[:, :, n * 256:(n + 1) * 256]),
                      reads=[], writes=["AW"], dma=1, sem_key=("ld", "AW"))
                for c in range(2):
                    for k in range(8):
                        mm(PG[:, 0:2], AW[:, k, c * 128:(c + 1) * 128], SCb[:, k, :], ["AW", "SCb"], ["PG"], start=(k == 0), stop=(k == 7))
                    ts(MODT[:, n * 2 + c, :], PG[:, 0:2], AB[:, n * 2 + c:n * 2 + c + 1], None, ALU.add, None, ["PG", "AB"], ["MODT"])
                if n >= 8:
                    for cond in range(2):
                        for k in range(8):
                            mm(PA[:, 0:256], SCB[:, k, cond, :], AW[:, k, :], ["SCB", "AW"], ["PA"], start=(k == 0), stop=(k == 7))
                        tt(GATE[:, cond, (n - 8) * 256:(n - 7) * 256], PA[:, 0:256], ABG[:, (n - 8) * 256:(n - 7) * 256], ALU.add,
                           ["PA", "ABG"], ["GATE"])
            ts(AM[:], MODT[:, 8:16, :], 1.0, None, ALU.add, None, ["MODT"], ["AM"])
            tt(AM[:], AM[:], bc3(NG[:], [128, 8, 2]), ALU.mult, ["AM", "NG"], ["AM"])

            win_v = w_in[l].rearrange("(k p) n -> p k n", p=128)

            def load_wg(c0, ncol):
                P.add("pool", lambda e, wv=win_v: e.dma_start(out=WG[:, :, 0:ncol], in_=wv[:, :, c0:c0 + ncol]),
                      reads=[], writes=["WG"], dma=1, sem_key=("ld", "WG"))

            for (tok0, T, cond, pidx) in seqs:
                ntile = T // 128
                CSj = CS[:].rearrange("p a b -> p (a b)").bitcast(BF16)[:, 0:1024]
                for t in range(ntile):
                    g0 = tok0 // 128 + t
                    xt, xtk = ((XT, "XT"), (XT2, "XT2"))[t % 2]
                    xn, xnk = ((XN[:], "XN"), (MXT[:].rearrange("p k t -> p (k t)"), "MXT"))[t % 2]
                    ptb, ptk = ((PTb, "PT"), (PG[:].bitcast(BF16), "PG"))[t % 2]
                    stc = (t % 2) * 4
                    ld(xt[:], XS[g0 * 128:(g0 + 1) * 128, :], [("XS", g0)], [xtk])
                    act(CSj, xt[:], AF.Square, [xtk], ["CS%d" % c_ for c_ in range(6)] + ["ST%d" % (t % 2)], accum=ST[:, stc:stc + 1])
                    rstd_from_ssq(ST[:, stc:stc + 1], ST[:, stc + 2:stc + 3], 1024.0, "ST%d" % (t % 2))
                    ts(xn, xt[:], ST[:, stc + 2:stc + 3], None, ALU.mult, None, [xtk, "ST%d" % (t % 2)], [xnk])
                    for k in range(8):
                        tr(ptb[:, k * 128:(k + 1) * 128], xn[:, k * 128:(k + 1) * 128], IDB[:], [xnk, "IDB"], [ptk])
                    for k in range(8):
                        act(HT[:, k, t * 128:(t + 1) * 128], ptb[:, k * 128:(k + 1) * 128], AF.Identity, [ptk, "AM", "MODT"],
                            [("HT", t)], bias=MODT[:, k, cond:cond + 1], scale=AM[:, k, cond:cond + 1])
                HTall = [("HT", t) for t in range(ntile)]

                def proj_tm(out_ps, tok_sl, c0, ncol, M, okey):
                    for k in range(8):
                        mm(out_ps, HT[:, k, tok_sl], WG[:, k, c0:c0 + ncol], HTall + ["WG"], [okey], start=(k == 0), stop=(k == 7))

                def proj_fm(out_ps, tok_sl, c0, ncol, okey, start=True):
                    for k in range(8):
                        mm(out_ps, WG[:, k, c0:c0 + ncol], HT[:, k, tok_sl], HTall + ["WG"], [okey], start=(k == 0), stop=(k == 7))

                load_wg(0, 1040)
                nch = T // 64
                P.in_region = ("dn" == DBG_REGION)
                dn_on = "dn" not in skip
                for tl in (range(T // TL) if dn_on else []):
                    s0 = tl * TL
                    lo = s0 - 2 if s0 > 0 else 0
                    hi = s0 + TL + 1 if s0 + TL < T else T
                    c_lo = lo - (s0 - 2)
                    ncol = hi - lo
                    for c in range(6):
                        pm, pmk = ((PA, "PA"), (PB, "PB"), (PE_, "PE_"), (PF, "PF"), (PG, "PG"), (PT, "PT"))[c]
                        rk, ck = "RAW%d" % c, "CS%d" % c
                        proj_fm(pm[:, 0:ncol], slice(lo, hi), c * 128, 128, pmk)
                        if c_lo > 0:
                            P.add("pool", lambda e, c=c: e.memset(RAW[:, c, 0:2], 0.0), reads=[], writes=[rk])
                        if hi == T:
                            P.add("pool", lambda e, c=c: e.memset(RAW[:, c, 2 + TL:3 + TL], 0.0), reads=[], writes=[rk])
                        cp(RAW[:, c, c_lo:c_lo + ncol], pm[:, 0:ncol], [pmk], [rk], eng="act")
                        ts(CS[:, c, :], RAW[:, c, 0:TL], CW[:, 0 * 6 + c:0 * 6 + c + 1], None, ALU.mult, None, [rk, "CW"], [ck])
                        for j in range(1, 4):
                            stt(CS[:, c, :], RAW[:, c, j:j + TL], CW[:, j * 6 + c:j * 6 + c + 1], CS[:, c, :], ALU.mult, ALU.add,
                                [rk, "CW", ck], [ck])
                        act(CS[:, c, :], CS[:, c, :], AF.Silu, [ck], [ck])
                    for c in range(6):
                        dst = (PC if c < 4 else PD)
                        off = (c % 4) * 128
                        tr(dst[:, off:off + 128], CS[:, c, :], IDN[:], ["CS%d" % c, "IDN"], ["PC" if c < 4 else "PD"])
                    cp(QKVF[:, 0:512], PC[:, :], ["PC"], ["QKVF"])
                    cp(QKVF[:, 512:768], PD[:, 0:256], ["PD"], ["QKVF"], eng="act")
                    tt(SQ[:, 0:512], QKVF[:, 0:512], QKVF[:, 0:512], ALU.mult, ["QKVF"], ["SQ"])
                    P.add("dve", lambda e: e.tensor_reduce(out=RN[:, 0:8], in_=SQ[:, 0:512].rearrange("p (h c) -> p h c", h=8), axis=AX.X, op=ALU.add),
                          reads=["SQ"], writes=["RN"])
                    ts(RN[:, 0:8], RN[:, 0:8], EPS, None, ALU.add, None, ["RN"], ["RN"])
                    act(RN[:, 0:8], RN[:, 0:8], AF.Sqrt, ["RN"], ["RN"])
                    recip(RN[:, 8:16], RN[:, 0:8], ["RN"], ["RN"])
                    ts(RN[:, 8:12], RN[:, 8:12], 0.125, None, ALU.mult, None, ["RN"], ["RN"])
                    tt(QKV3[:, 0:512].rearrange("p (h c) -> p h c", h=8), QKVF[:, 0:512].rearrange("p (h c) -> p h c", h=8),
                       bc3(RN[:, 8:16], [128, 8, 64]), ALU.mult, ["QKVF", "RN"], ["QKV3_00", "QKV3_01", "QKV3_10", "QKV3_11"])
                    cp(QKV3[:, 512:768], QKVF[:, 512:768], ["QKVF"], ["QKV3_00", "QKV3_01", "QKV3_10", "QKV3_11"], eng="pool")
                    stq(QKVD[s0:s0 + 128, :], QKV3[:], ["QKV3_00", "QKV3_01", "QKV3_10", "QKV3_11"], [("QKVD", tl)], ("st", "QKV3"))

                def dn_scal_all(d):
                    hf = d
                    psl = slice(hf * 64, hf * 64 + 64)
                    sx = "_%d" % hf
                    bk, bkk = (PA, "PA") if hf == 0 else (PB, "PB")
                    bk2, bkk2 = (PC, "PC") if hf == 0 else (PD, "PD")
                    for g0 in range(0, nch, 32):
                        n = min(32, nch - g0)
                        for j in range(n):
                            ch = g0 + j
                            proj_tm(bk[psl, j * 16:(j + 1) * 16], slice(ch * 64, ch * 64 + 64), 1024, 16, 64, bkk)
                        pv = bk[psl, 0:n * 16].rearrange("p (n c) -> p n c", c=16)
                        TB, TZ, TG = TB_[psl, 0:n, :], TZ_[psl, 0:n, :], TG_[psl, 0:n, :]
                        SA = lambda o: SALL[psl, g0:g0 + n, o:o + 4]
                        kk = ["TB_" + sx]
                        act(TB, pv[:, :, d * 4:d * 4 + 4], AF.Sigmoid, [bkk], kk)
                        act(SA(0), TB, AF.Sqrt, kk, ["SALL" + sx])
                        tt(TZ, pv[:, :, 8 + d * 4:12 + d * 4], DTB[psl, d * 4:d * 4 + 4].unsqueeze(1).to_broadcast([64, n, 4]), ALU.add, [bkk, "DTB"], kk)
                        act(TZ, TZ, AF.Exp, kk, kk)
                        act(TZ, TZ, AF.Ln, kk, kk, bias=1.0)
                        tt(TG, TZ, NEGA[psl, d * 4:d * 4 + 4].unsqueeze(1).to_broadcast([64, n, 4]), ALU.mult, kk + ["NEGA"], kk)
                        gflat = TG_[psl, 0:n, :].rearrange("p n c -> p (n c)")
                        mm(bk2[psl, 0:n * 4], TRI[psl, d, :], gflat, ["TRI"] + kk, [bkk2])
                        mm(bk2[psl, 128:128 + n * 4], TRI[psl, 2, :], gflat, ["TRI"] + kk, [bkk2])
                        gcv = bk2[psl, 0:n * 4].rearrange("p (n c) -> p n c", c=4)
                        glv = bk2[psl, 128:128 + n * 4].rearrange("p (n c) -> p n c", c=4)
                        cp(SA(20), gcv, [bkk2], ["SALL" + sx])
                        act(SA(8), gcv, AF.Exp, [bkk2], ["SALL" + sx])
                        tt(TZ, glv, SA(20), ALU.subtract, [bkk2, "SALL" + sx], kk)
                        act(SA(12), TZ, AF.Exp, kk, ["SALL" + sx])
                        act(SA(16), glv, AF.Exp, [bkk2], ["SALL" + sx])
                        tt(SA(4), SA(0), SA(8), ALU.mult, ["SALL" + sx], ["SALL" + sx])

                def dn_unit(ch, d, hp):
                    hf = d
                    psl = slice(hf * 64, hf * 64 + 64)
                    sx = "_%d" % hf
                    sy = "_%d%d" % (hf, hp)
                    K_ = lambda *n: [x + sy for x in n]
                    bx, by = [[(PC, PD), (PF, PG)], [(PA, PB), (PE_, PT)]][hf][hp]
                    kx, ky = [[("PC", "PD"), ("PF", "PG")], [("PA", "PB"), ("PE_", "PT")]][hf][hp]
                    M = MSK[hf]
                    v2 = lambda ap: ap.rearrange("p (h c) -> p h c", h=2)
                    sh = [64, 2, 64]
                    cs_ = slice(hp * 128, hp * 128 + 128)
                    SAk = "SALL" + sx
                    SA = lambda o: SALL[psl, ch, o + 2 * hp:o + 2 * hp + 2]
                    ld(QKV3[psl, :].rearrange("p (t c) -> p t c", t=3)[:, :, hp * 128:hp * 128 + 128],
                       QKVD[ch * 64:ch * 64 + 64, :].rearrange("p (t c) -> p t c", t=3)[:, :, hp * 128:hp * 128 + 128],
                       [("QKVD", ch // 2)], ["QKV3" + sy], key=("ld", "QKV3" + sy))
                    Q = QKV3[psl, 0 + hp * 128:128 + hp * 128]
                    K = QKV3[psl, 256 + hp * 128:384 + hp * 128]
                    V = QKV3[psl, 512 + hp * 128:640 + hp * 128]
                    QK3 = ["QKV3" + sy]
                    tt(v2(KS[psl, cs_]), v2(K), bc3(SA(0), sh), ALU.mult, QK3 + [SAk], K_("KS"))
                    tt(RH[psl, 2 * hp:2 * hp + 2, 0:64], v2(V), bc3(SA(0), sh), ALU.mult, QK3 + [SAk], K_("RH"), eng="pool")
                    tt(RH[psl, 2 * hp:2 * hp + 2, 64:128], v2(K), bc3(SA(4), sh), ALU.mult, QK3 + [SAk], K_("RH"))
                    tt(v2(KDEC[psl, cs_]), v2(K), bc3(SA(12), sh), ALU.mult, QK3 + [SAk], K_("KDEC"), eng="pool")
                    tt(v2(QDEC[psl, cs_]), v2(Q), bc3(SA(8), sh), ALU.mult, QK3 + [SAk], K_("QDEC"))
                    I64 = IDB[psl, hf * 64:hf * 64 + 64]
                    bxb = bx[psl, 0:256].bitcast(BF16)
                    c0 = hp * 128
                    for h in range(2):
                        hs = slice(c0 + h * 64, c0 + h * 64 + 64)
                        hq = slice(h * 64, h * 64 + 64)
                        tr(bxb[:, h * 64:h * 64 + 64], KS[psl, hs], I64, K_("KS") + ["IDB"], [kx])
                        tr(bxb[:, 128 + h * 64:192 + h * 64], K[:, hq], I64, QK3 + ["IDB"], [kx])
                        tr(bxb[:, 256 + h * 64:320 + h * 64], Q[:, hq], I64, QK3 + ["IDB"], [kx])
                        tr(bxb[:, 384 + h * 64:448 + h * 64], QDEC[psl, hs], I64, K_("QDEC") + ["IDB"], [kx])
                    TRs = TR[psl, hp * 512:hp * 512 + 512]
                    cp(TRs, bxb, [kx], K_("TR"), eng="act")
                    KST = lambda h: TRs[:, h * 64:h * 64 + 64]
                    KT = lambda h: TRs[:, 128 + h * 64:192 + h * 64]
                    QT = lambda h: TRs[:, 256 + h * 64:320 + h * 64]
                    QDT = lambda h: TRs[:, 384 + h * 64:448 + h * 64]
                    for h in range(2):
                        mm(by[psl, h * 64:h * 64 + 64], KST(h), KST(h), K_("TR"), [ky])
                        mm(by[psl, 128 + h * 64:192 + h * 64], KT(h), QT(h), K_("TR"), [ky])
                    tt(v2(DG[psl, cs_]), M["ID4"], bc3(SA(20), sh), ALU.mult, ["M64", SAk], K_("DG"), eng="pool")
                    act(NDG[psl, cs_], DG[psl, cs_], AF.Identity, K_("DG"), K_("NDG"), scale=-1.0)
                    for h in range(2):
                        hs = slice(c0 + h * 64, c0 + h * 64 + 64)
                        mm(by[psl, 256 + h * 64:320 + h * 64], TRI[psl, 2, :], DG[psl, hs], ["TRI"] + K_("DG"), [ky], start=True, stop=False)
                        mm(by[psl, 256 + h * 64:320 + h * 64], NDG[psl, hs], TRI[psl, 2, :], ["TRI"] + K_("NDG"), [ky], start=False, stop=True)
                    tt(v2(DTT[psl, cs_]), v2(by[psl, 256:384]), M["MINC"][d], ALU.add, [ky, "M64"], K_("DTT"))
                    act(DTT[psl, cs_], DTT[psl, cs_], AF.Exp, K_("DTT"), K_("DTT"))
                    C0_, B0_ = Cb[0], Bb[0]
                    CD, BD = Cb[1], Bb[1]
                    TT_, TN_ = Tb[0], Tb[1]
                    tt(v2(C0_[psl, cs_]), v2(by[psl, 0:128]), M["SUL"][d], ALU.mult, [ky, "M64"], K_("C_a"))
                    tt(v2(B0_[psl, cs_]), v2(by[psl, 0:128]), M["SUL"][1 - d], ALU.mult, [ky, "M64"], K_("B_a"))
                    tt(QKT[psl, cs_], by[psl, 128:256], DTT[psl, cs_], ALU.mult, [ky] + K_("DTT"), K_("QKT"))
                    tt(v2(CD[psl, cs_]), v2(C0_[psl, cs_]), M["MD8"], ALU.mult, K_("C_a") + ["M64"], K_("C_b"), eng="pool")
                    tt(v2(BD[psl, cs_]), v2(B0_[psl, cs_]), M["MD8"], ALU.mult, K_("B_a") + ["M64"], K_("B_b"), eng="pool")
                    tt(v2(TT_[psl, cs_]), v2(CD[psl, cs_]), M["ID4"], ALU.add, K_("C_b") + ["M64"], K_("T_a"), eng="pool")
                    tt(v2(TN_[psl, cs_]), v2(BD[psl, cs_]), M["ID4"], ALU.add, K_("B_b") + ["M64"], K_("T_b"), eng="pool")

                    def grp(dst, dk, off, lt, lk, rt, rk):
                        for h in range(2):
                            hs = slice(c0 + h * 64, c0 + h * 64 + 64)
                            mm(dst[psl, off + h * 64:off + h * 64 + 64], lt[psl, hs], rt[psl, hs], K_(lk, rk), [dk])

                    for lev in range(2):
                        grp(bx, kx, 0, CD, "C_b", BD, "B_b")
                        grp(bx, kx, 128, BD, "B_b", CD, "C_b")
                        cp(BD[psl, cs_], bx[psl, 0:128], [kx], K_("B_b"), eng="act")
                        cp(CD[psl, cs_], bx[psl, 128:256], [kx], K_("C_b"), eng="act")
                        grp(by, ky, 0, BD, "B_b", TT_, "T_a")
                        grp(by, ky, 128, CD, "C_b", TN_, "T_b")
                        tt(TT_[psl, cs_], TT_[psl, cs_], by[psl, 0:128], ALU.add, K_("T_a") + [ky], K_("T_a"))
                        tt(TN_[psl, cs_], TN_[psl, cs_], by[psl, 128:256], ALU.add, K_("T_b") + [ky], K_("T_b"))
                    BOF = VN
                    for li in range(3):
                        last = (li == 2)
                        tt(v2(BOF[psl, cs_]), v2(B0_[psl, cs_]), M["MOFF"][li], ALU.mult, K_("B_a") + ["M64"], K_("VN"), eng="pool")
                        grp(bx, kx, 0, BOF, "VN", TT_, "T_a")
                        cp(XA[psl, cs_], bx[psl, 0:128], [kx], K_("XA"), eng="act")
                        if not last:
                            tt(v2(COF[psl, cs_]), v2(C0_[psl, cs_]), M["MOFF"][li], ALU.mult, K_("C_a") + ["M64"], K_("COF"), eng="pool")
                            grp(bx, kx, 128, COF, "COF", TN_, "T_b")
                            cp(XB[psl, cs_], bx[psl, 128:256], [kx], K_("XB"), eng="act")
                        grp(by, ky, 0, TN_, "T_b", XA, "XA")
                        if not last:
                            grp(by, ky, 128, TT_, "T_a", XB, "XB")
                        tt(TT_[psl, cs_], TT_[psl, cs_], by[psl, 0:128], ALU.add, K_("T_a") + [ky], K_("T_a"))
                        if not last:
                            tt(TN_[psl, cs_], TN_[psl, cs_], by[psl, 128:256], ALU.add, K_("T_b") + [ky], K_("T_b"))
                    tt(MTT[psl, cs_], TT_[psl, cs_], DTT[psl, cs_], ALU.mult, K_("T_a", "DTT"), K_("MTT"))
                    for h in range(2):
                        hs = slice(c0 + h * 64, c0 + h * 64 + 64)
                        mm(by[psl, h * 128:(h + 1) * 128], MTT[psl, hs], RH[psl, 2 * hp + h, :], K_("MTT", "RH"), [ky])
                    byv = by[psl, 0:256].rearrange("p (h c) -> p h c", h=2)
                    tt(v2(UU[psl, cs_]), byv[:, :, 0:64], bc3(SA(0), sh), ALU.mult, [ky, SAk], K_("UU"))
                    tt(v2(WB_[psl, cs_]), byv[:, :, 64:128], bc3(SA(0), sh), ALU.mult, [ky, SAk], K_("WB_"))
                    bxw = bx[psl, 0:64].bitcast(BF16)
                    for h in range(2):
                        hs = slice(c0 + h * 64, c0 + h * 64 + 64)
                        tr(bxw[:, h * 64:h * 64 + 64], WB_[psl, hs], I64, K_("WB_") + ["IDB"], [kx])
                    cp(WT[psl, cs_], bxw, [kx], K_("WT"), eng="act")
                    for h in range(2):
                        hs = slice(c0 + h * 64, c0 + h * 64 + 64)
                        mm(bx[psl, 128 + h * 64:192 + h * 64], WT[psl, hs], SSb[psl, hs], K_("WT", "SSb"), [kx])
                    tt(VN[psl, cs_], UU[psl, cs_], bx[psl, 128:256], ALU.subtract, K_("UU") + [kx], K_("VN"))
                    for h in range(2):
                        hs = slice(c0 + h * 64, c0 + h * 64 + 64)
                        mm(by[psl, h * 64:h * 64 + 64], QDT(h), SSb[psl, hs], K_("TR", "SSb"), [ky], start=True, stop=False)
                        mm(by[psl, h * 64:h * 64 + 64], QKT[psl, hs], VN[psl, hs], K_("QKT", "VN"), [ky], start=False, stop=True)
                    cp(OO[psl, cs_], by[psl, 0:128], [ky], K_("OO"), eng="act")
                    for h in range(2):
                        hs = slice(c0 + h * 64, c0 + h * 64 + 64)
                        mm(bx[psl, h * 64:h * 64 + 64], KDEC[psl, hs], VN[psl, hs], K_("KDEC", "VN"), [kx])
                    tt(v2(STMP[psl, cs_]), v2(SS[psl, cs_]), bc3(SA(16), sh), ALU.mult, K_("SS") + [SAk], K_("STMP"), eng="pool")
                    tt(SS[psl, cs_], STMP[psl, cs_], bx[psl, 0:128], ALU.add, K_("STMP") + [kx], K_("SS"))
                    cp(SSb[psl, cs_], SS[psl, cs_], K_("SS"), K_("SSb"), eng="pool")
                    stq(OFD[d, ch * 64:ch * 64 + 64, cs_], OO[psl, cs_], K_("OO"), [("OFD", d, ch, hp)], ("st", "OO" + sy))

                def init_S(d):
                    psl = slice(d * 64, d * 64 + 64)
                    sx = "_%d" % d
                    if pidx is None:
                        ld(SS[psl, :].rearrange("p (h c) -> p h c", h=4), sd[l, d].rearrange("h k v -> k h v"), [], ["SS" + sx + "0", "SS" + sx + "1"], key=("ld", "SS" + sx))
                    else:
                        P.add("pool", lambda e: e.memset(SS[psl, :], 0.0), reads=[], writes=["SS" + sx + "0", "SS" + sx + "1"])
                    cp(SSb[psl, :], SS[psl, :], ["SS" + sx + "0", "SS" + sx + "1"], ["SSb" + sx + "0", "SSb" + sx + "1"], eng="pool")

                def store_S(d):
                    psl = slice(d * 64, d * 64 + 64)
                    sx = "_%d" % d
                    if pidx is not None:
                        stq(nsd[pidx, l, d].rearrange("h k v -> k h v"), SS[psl, :].rearrange("p (h c) -> p h c", h=4), ["SS" + sx + "0", "SS" + sx + "1"],
                            [("ns", pidx, l, d)], ("st", "SS" + sx))

                if dn_on:
                    init_S(0)
                    init_S(1)
                    dn_scal_all(0)
                    dn_scal_all(1)
                    streams = []
                    for (dd, hh) in ((0, 0), (1, 0), (0, 1), (1, 1)):
                        P.rec = []
                        for i in range(nch):
                            dn_unit(i if dd == 0 else nch - 1 - i, dd, hh)
                        streams.append(P.rec)
                    P.rec = None
                    ulen = len(streams[0]) // nch
                    offs = [0, ulen // 4, ulen // 2, (3 * ulen) // 4]
                    for j in range(max(len(r) for r in streams) + offs[-1]):
                        for r, o in zip(streams, offs):
                            jj = j - o
                            if 0 <= jj < len(r):
                                P.add(*r[jj][:2], reads=r[jj][2], writes=r[jj][3], dma=r[jj][4], sem_key=r[jj][5])
                    store_S(0)
                    store_S(1)
                    DGK = ["DG_00", "DG_01", "DG_10", "DG_11"]
                    NDGK = ["NDG_00", "NDG_01", "NDG_10", "NDG_11"]
                    STK = ["STMP_00", "STMP_01", "STMP_10", "STMP_11"]
                    for t in range(ntile):
                        tsl = slice(t * 128, t * 128 + 128)
                        ld(DG[:], OFD[0, tsl, :], [("OFD", 0, 2 * t + a, b) for a in range(2) for b in range(2)], DGK, key=("ld", "DGc"))
                        ld(NDG[:], OFD[1, tsl, :], [("OFD", 1, 2 * t + a, b) for a in range(2) for b in range(2)], NDGK, key=("ld", "NDGc"))
                        tt(DG[:], DG[:], NDG[:], ALU.add, DGK + NDGK, DGK)
                        tt(SQ[:, 0:256], DG[:], DG[:], ALU.mult, DGK, ["SQ"])
                        P.add("dve", lambda e: e.tensor_reduce(out=RN[:, 0:4], in_=SQ[:, 0:256].rearrange("p (h c) -> p h c", h=4), axis=AX.X, op=ALU.add),
                              reads=["SQ"], writes=["RN"])
                        ts(RN[:, 0:4], RN[:, 0:4], 1.0 / 64.0, EPS, ALU.mult, ALU.add, ["RN"], ["RN"])
                        act(RN[:, 0:4], RN[:, 0:4], AF.Sqrt, ["RN"], ["RN"])
                        recip(RN[:, 8:12], RN[:, 0:4], ["RN"], ["RN"])
                        tt(DG[:].rearrange("p (h c) -> p h c", h=4), DG[:].rearrange("p (h c) -> p h c", h=4), bc3(RN[:, 8:12], [128, 4, 64]),
                           ALU.mult, DGK + ["RN"], DGK)
                        tt(DG[:], DG[:], DNG[:], ALU.mult, DGK + ["DNG"], DGK)
                        proj_tm(PC[:, 0:256], tsl, 768, 256, 128, "PC")
                        act(GA[:], PC[:, 0:256], AF.Silu, ["PC"], STK)
                        tt(DG[:], DG[:], GA[:], ALU.mult, DGK + STK, DGK)
                        stq(MIXD[tsl, 0:256], DG[:], DGK, [("MIXD", t, 0)], ("st", "DGc"))
                P.in_region = False

                load_wg(1040, 768)
                P.in_region = ("sgu" == DBG_REGION)
                K4 = lambda n: [n + "_00", n + "_01", n + "_10", n + "_11"]
                for t in (range(ntile) if "sgu" not in skip else []):
                    tsl = slice(t * 128, t * 128 + 128)
                    o = t % 2
                    uvg, uvk = ((UVG, ["UVG"]), (UVG2, ["UVG2"]))[o]
                    vns, vnk = ((VNS, ["VNS"]), (DG, K4("DG")))[o]
                    sgb, sgk = ((SGB, ["SGB"]), (NDG, K4("NDG")))[o]
                    yb, ybk = ((YB, ["YB"]), (DTT, K4("DTT")))[o]
                    (pa, pak), (pb, pbk) = (((PA, "PA"), (PB, "PB")), ((PC, "PC"), (PD, "PD")))[o]
                    stc, stk = o * 4, "ST%d" % o
                    proj_tm(pa[:], tsl, 0, 512, 128, pak)
                    act(uvg[:], pa[:], AF.Gelu, [pak], uvk)
                    act(JNK[:, 0:256], uvg[:, 256:512], AF.Square, uvk, ["MXT", stk], accum=ST[:, stc:stc + 1])
                    rstd_from_ssq(ST[:, stc:stc + 1], ST[:, stc + 2:stc + 3], 256.0, stk)
                    stt(vns[:], uvg[:, 256:512], ST[:, stc + 2:stc + 3], SGNG[:], ALU.mult, ALU.mult, uvk + [stk, "SGNG"], vnk)
                    for g in range(4):
                        mm(pb[:, g * 64:g * 64 + 64], SWT[:, g, :], vns[:, g * 64:g * 64 + 64], ["SWT"] + vnk, [pbk])
                    proj_tm(pb[:, 256:512], tsl, 512, 256, 128, pbk)
                    act(sgb[:], pb[:, 256:512], AF.Silu, [pbk], sgk)
                    for g in range(4):
                        stt(yb[:, g * 64:g * 64 + 64], pb[:, g * 64:g * 64 + 64], SGBT[:, g:g + 1], uvg[:, g * 64:g * 64 + 64], ALU.add, ALU.mult,
                            [pbk, "SGBT"] + uvk, ybk)
                    tt(yb[:], yb[:], sgb[:], ALU.mult, ybk + sgk, ybk)
                    stq(MIXD[t * 128:t * 128 + 128, 256:512], yb[:], ybk, [("MIXD", t, 1)], ("st", ybk[0]), q="pool")

                P.in_region = False
                load_wg(1808, 512)

                def pool_out(j):
                    o = j % 2
                    PCx, pck = ((PC, "PC"), (PD, "PD"))[o]
                    sgb, sgk = ((SGB, ["SGB"]), (NDG, K4("NDG")))[o]
                    yb, ybk = ((YB, ["YB"]), (DTT, K4("DTT")))[o]
                    typ = 0 if j == 0 else (2 if j == ntile - 1 else 1)
                    for g in range(4):
                        gs = slice(g * 64, g * 64 + 64)
                        terms = []
                        if j > 0:
                            terms.append((3, (j - 1) % 3))
                        terms.append((typ, j % 3))
                        if j < ntile - 1:
                            terms.append((4, (j + 1) % 3))
                        for i, (ty, zi) in enumerate(terms):
                            mm(PCx[:, gs], BAND[:, ty, g, :], ZR[zi][:, gs], ["BAND", "ZR%d" % zi], [pck], start=(i == 0), stop=(i == len(terms) - 1))
                    proj_tm(PCx[:, 256:512], slice(j * 128, j * 128 + 128), 256, 256, 128, pck)
                    act(sgb[:], PCx[:, 256:512], AF.Silu, [pck], sgk)
                    tt(yb[:], PCx[:, 0:256], sgb[:], ALU.mult, [pck] + sgk, ybk)
                    stq(MIXD[j * 128:j * 128 + 128, 512:768], yb[:], ybk, [("MIXD", j, 2)], ("st", ybk[0]), q="pool")

                for t in (range(ntile) if "pool" not in skip else []):
                    tsl = slice(t * 128, t * 128 + 128)
                    xc, xk = ((XCT, "XCT"), (XCT2, "XCT2"))[t % 2]
                    (pa, pak), (pb, pbk) = (((PA, "PA"), (PB, "PB")), ((PE_, "PE_"), (PF, "PF")))[t % 2]
                    for c in range(2):
                        proj_fm(pa[:, c * 128:c * 128 + 128], tsl, c * 128, 128, pak)
                    cp(xc[:].rearrange("p c t -> p (c t)"), pa[:, 0:256], [pak], [xk], eng="act")
                    for c in range(2):
                        mm(pb[:, c * 128:c * 128 + 128], xc[:, c, :], PWB[:, c, :], [xk, "PWB"], [pbk])
                    cp(ZR[t % 3][:], pb[:, 0:256], [pbk], ["ZR%d" % (t % 3)])
                    if t >= 1:
                        pool_out(t - 1)
                if "pool" not in skip:
                    pool_out(ntile - 1)

                load_wg(2320, 512)
                A = T // 64
                ai = 0 if A == 64 else 1
                e2d = cd["c_e2_%d" % A]
                f_on = "fourier" not in skip
                XCTb = [(XCT, "XCT"), (XCT2, "XCT2")]
                ZCSb = [(UVG, "UVG"), (UVG2, "UVG2")]
                for t in (range(ntile) if f_on else []):
                    tsl = slice(t * 128, t * 128 + 128)
                    xc, xk = XCTb[t % 2]
                    zc, zk = ZCSb[t % 2]
                    pa, pak = (PA, "PA") if t % 2 == 0 else (PC, "PC")
                    pb, pbk = (PB, "PB") if t % 2 == 0 else (PD, "PD")
                    for c in range(2):
                        proj_fm(pa[:, c * 128:c * 128 + 128], tsl, c * 128, 128, pak)
                    cp(xc[:].rearrange("p c t -> p (c t)"), pa[:, 0:256], [pak], [xk], eng="act")
                    for cs in range(2):
                        for c in range(2):
                            mm(pb[:, cs * 256 + c * 128:cs * 256 + c * 128 + 128], xc[:, c, :], PCS[:, cs, c, :], [xk, "PCS"], [pbk])
                    cp(zc[:], pb[:], [pbk], [zk])
                    if A != 4:
                        P.add("pool", lambda e, t=t, zc=zc: [e.dma_start(out=ZD[cs, t * 128:t * 128 + 128, :], in_=zc[:, cs * 256:cs * 256 + 256]) for cs in range(2)],
                              reads=[zk], writes=[("ZD", t)], dma=2, sem_key=("st", zk, "pool"))
                if A == 4 and f_on:
                    ld(QKVF[:], cd["c_e256"], [], ["QKVF"])
                    E256 = QKVF[:].rearrange("p (t s k) -> p t s k", t=2, s=2)
                    for kt in range(2):
                        pe, pek = (PE_, "PE_") if kt == 0 else (PF, "PF")
                        pg, pgk = (PG, "PG") if kt == 0 else (PT, "PT")
                        ga, gak = (STMP[:], ["STMP_00", "STMP_01", "STMP_10", "STMP_11"]) if kt == 0 else (SGB[:], ["SGB"])
                        yd, ydk = (YB[:], "YB") if kt == 0 else (VNS[:], "VNS")
                        i_ = 0
                        for tt2 in range(2):
                            zc, zk = ZCSb[tt2]
                            for cs in range(2):
                                mm(pe[:, 0:256], E256[:, tt2, cs, kt * 128:(kt + 1) * 128], zc[:, cs * 256:(cs + 1) * 256], ["QKVF", zk], [pek],
                                   start=(i_ == 0), stop=(i_ == 3))
                                i_ += 1
                        proj_tm(pg[:, 0:256], slice(kt * 128, kt * 128 + 128), 256, 256, 128, pgk)
                        act(ga, pg[:, 0:256], AF.Silu, [pgk], gak)
                        tt(yd, pe[:, 0:256], ga, ALU.mult, [pek] + gak, [ydk])
                        stq(MIXD[kt * 128:(kt + 1) * 128, 768:1024], yd, [ydk], [("MIXD", "f", 2 * kt), ("MIXD", "f", 2 * kt + 1)], ("st", ydk), q="pool")
                ZDall = [("ZD", t) for t in range(ntile)]
                zdv = ZD.rearrange("s (a b) c -> s a b c", b=64)
                ZAb = [(ZA[0:A], "ZA"), (QKVF[0:A, :].rearrange("p (s c) -> p s c", s=2), "QKVF")]
                VVb = [(VV[0:A], "VV"), (SQ[0:A, :].rearrange("p (s c) -> p s c", s=2), "SQ")]
                CA_, SA_, NSA_ = DFTA[0:A, ai, 0, 0:A], DFTA[0:A, ai, 1, 0:A], DFTA[0:A, ai, 2, 0:A]
                for bb in (range(32) if f_on and "f1" not in skip and A != 4 else []):
                    za, zak = ZAb[bb % 2]
                    vv, vvk = VVb[bb % 2]
                    pc, pck = (PE_, "PE_") if bb % 2 == 0 else (PF, "PF")
                    pd, pdk = (PG, "PG") if bb % 2 == 0 else (PT, "PT")
                    P.add("sp", lambda e, bb=bb, A=A, zdv=zdv, za=za: [e.dma_start(out=za[:, cs, :].rearrange("a (b c) -> a b c", b=2), in_=zdv[cs, 0:A, bb * 2:bb * 2 + 2, :]) for cs in range(2)],
                          reads=ZDall, writes=[zak], dma=2, sem_key=("ld", zak))
                    mm(pc[0:A, :], CA_, za[:, 0, :], ["DFTA", zak], [pck], start=True, stop=False)
                    mm(pc[0:A, :], NSA_, za[:, 1, :], ["DFTA", zak], [pck], start=False, stop=True)
                    mm(pd[0:A, :], CA_, za[:, 1, :], ["DFTA", zak], [pdk], start=True, stop=False)
                    mm(pd[0:A, :], SA_, za[:, 0, :], ["DFTA", zak], [pdk], start=False, stop=True)
                    cp(vv[:, 0, :], pc[0:A, :], [pck], [vvk])
                    cp(vv[:, 1, :], pd[0:A, :], [pdk], [vvk], eng="act")
                    P.add("pool", lambda e, bb=bb, A=A, vv=vv: [e.dma_start(out=VD[ri, 0:A, bb * 2:bb * 2 + 2, :], in_=vv[:, ri, :].rearrange("a (b c) -> a b c", b=2)) for ri in range(2)],
                          reads=[vvk], writes=[("VD", bb)], dma=2, sem_key=("st", vvk, "pool"))
                VDall = [("VD", bb) for bb in range(32)]
                mixv = MIXD[0:T, :].rearrange("(q a) c -> a q c", a=A)
                VBb = [(VB, "VB"), (VB2, "VB2")]
                E2b = [(E2, "E2"), (E2_2, "E2_2")]
                GAb = [(STMP[0:64, :], ["STMP_00", "STMP_01"]), (SGB[0:64, :], ["SGB"])]
                YDb = [(YB[0:64, :], "YB"), (VNS[0:64, :], "VNS")]
                for p in (range(A) if f_on and "f2" not in skip and A != 4 else []):
                    vb, vbk = VBb[p % 2]
                    e2, e2k = E2b[p % 2]
                    ga, gak = GAb[p % 2]
                    yd, ydk = YDb[p % 2]
                    pe, pek = (PA, "PA") if p % 2 == 0 else (PB, "PB")
                    pg, pgk = (PC, "PC") if p % 2 == 0 else (PD, "PD")
                    P.add("sp", lambda e, p=p, vb=vb: [e.dma_start(out=vb[:, ri, :], in_=VD[ri, p, :, :]) for ri in range(2)],
                          reads=VDall, writes=[vbk], dma=2, sem_key=("ld", vbk))
                    ld(e2[:], e2d[p], [], [e2k])
                    mm(pe[0:64, 0:256], e2[:, 0, :], vb[:, 0, :], [e2k, vbk], [pek], start=True, stop=False)
                    mm(pe[0:64, 0:256], e2[:, 1, :], vb[:, 1, :], [e2k, vbk], [pek], start=False, stop=True)
                    proj_tm(pg[0:64, 0:256], slice(p, p + A * 63 + 1, A), 256, 256, 64, pgk)
                    act(ga, pg[0:64, 0:256], AF.Silu, [pgk], gak)
                    tt(yd, pe[0:64, 0:256], ga, ALU.mult, [pek] + gak, [ydk])
                    stq(mixv[p, :, 768:1024], yd, [ydk], [("MIXD", "f", p)], ("st", ydk), q="pool")

                mix_keys_f = [("MIXD", "f", p) for p in range(A)]
                P.add("pool", lambda e, wo_v=wo_v: e.dma_start(out=WG[:, :, 0:1024], in_=wo_v), reads=[], writes=["WG"], dma=1, sem_key=("ld", "WG"))
                for t in (range(ntile) if "p3" not in skip else []):
                    g0 = tok0 // 128 + t
                    mx, mxk = ((XT2, "XT2"), (QKVF, "QKVF"))[t % 2]
                    xt, xtk = ((XT, "XT"), (SQ, "SQ"))[t % 2]
                    mt, mtk = ((MXT, "MXT"), (XN[:].rearrange("p (k t) -> p k t", k=8), "XN"))[t % 2]
                    (pa, pak), (pb, pbk) = (((PA, "PA"), (PB, "PB")), ((PE_, "PE_"), (PF, "PF")))[t % 2]
                    pouts = (((PC, "PC"), (PD, "PD")), ((PG, "PG"), (PT, "PT")))[t % 2]
                    ld(mx[:], MIXD[t * 128:(t + 1) * 128, :],
                       [("MIXD", t, 0), ("MIXD", t, 1), ("MIXD", t, 2)] + mix_keys_f, [mxk])
                    for k in range(8):
                        dst, dk = (pa, pak) if k < 4 else (pb, pbk)
                        tr(dst[:, (k % 4) * 128:(k % 4) * 128 + 128], mx[:, k * 128:(k + 1) * 128], IDN[:], [mxk, "IDN"], [dk])
                    cp(mt[:, 0:4, :].rearrange("p k t -> p (k t)"), pa[:], [pak], [mtk])
                    cp(mt[:, 4:8, :].rearrange("p k t -> p (k t)"), pb[:], [pbk], [mtk], eng="act")
                    ld(xt[:], XS[g0 * 128:(g0 + 1) * 128, :], [("XS", g0)], [xtk])
                    for n in range(2):
                        dst, dk = pouts[n]
                        for k in range(8):
                            mm(dst[:], mt[:, k, :], WG[:, k, n * 512:(n + 1) * 512], [mtk, "WG"], [dk], start=(k == 0), stop=(k == 7))
                        tt(mx[:, n * 512:(n + 1) * 512], dst[:], GATE[:, cond, n * 512:(n + 1) * 512], ALU.mult, [dk, "GATE"], [mxk])
                    tt(xt[:], xt[:], mx[:], ALU.add, [xtk, mxk], [xtk], eng="pool")
                    stq(XS[g0 * 128:(g0 + 1) * 128, :], xt[:], [xtk], [("XS", g0)], ("st", xtk), q="pool")

        outs = []
        ld(ABG[:], final_norm_g.partition_broadcast(128), [], ["ABG"])
        fin_tiles = []
        for (tok0, T, cond, pidx) in seqs:
            fin_tiles += list(range(tok0 // 128, (tok0 + T) // 128))
        for g0 in fin_tiles:
            ld(XT[:], XS[g0 * 128:(g0 + 1) * 128, :], [("XS", g0)], ["XT"])
            act(JNK, XT[:], AF.Square, ["XT"], ["MXT", "ST0"], accum=ST[:, 0:1])
            rstd_from_ssq(ST[:, 0:1], ST[:, 2:3], 1024.0, "ST0")
            stt(XT2[:], XT[:], ST[:, 2:3], ABG[:], ALU.mult, ALU.mult, ["XT", "ST0", "ABG"], ["XT2"])
            if g0 < 32:
                dst = ys[g0 * 128:(g0 + 1) * 128, :]
            else:
                dst = yp[(g0 - 32) * 128:(g0 - 31) * 128, :]
            stq(dst, XT2[:], ["XT2"], [("Y", g0)], ("st", "XT2"), q="pool")
            outs.append(("Y", g0))
        for (tok0, T, cond, pidx) in seqs:
            for l in range(nl_run):
                for d in range(2):
                    if pidx is not None:
                        outs.append(("ns", pidx, l, d))
        P.add("sp", None, reads=outs)
        P.build()
    return nc, consts


_CACHE = {}


def kernel(**inputs):
    if "nc" not in _CACHE:
        _CACHE["nc"] = build_program()
    nc, consts = _CACHE["nc"]
    f = lambda a: np.ascontiguousarray(np.asarray(a, dtype=np.float32))
    shared = {k: f(inputs[k]) for k in ["ada_w", "ada_b", "norm_g", "w_in", "conv_qkv", "dn_norm_g", "sgu_norm_g", "sgu_w", "sgu_b",
                                         "pool_w", "pool_scale", "fourier_w", "w_out", "final_norm_g"]}
    shared["a_log"] = f(inputs["a_log"]).reshape(NL, 8)
    shared["dt_bias"] = f(inputs["dt_bias"]).reshape(NL, 8)
    for k, v in consts.items():
        shared[k] = f(v)
    xsm = f(inputs["x_sample"])
    xpr = f(inputs["x_prompt"])
    sdl = f(inputs["state_delta"])
    c = f(inputs["c"])
    cctx = f(inputs["c_ctx"])
    in_maps = []
    for i in range(8):
        m = dict(shared)
        m["xs"] = xsm[i]
        m["xp"] = np.ascontiguousarray(xpr[4 * i:4 * i + 4].reshape(4 * TPR, 1024))
        m["sd"] = sdl[i]
        m["cc"] = np.ascontiguousarray(np.stack([c[i], cctx], axis=0))
        in_maps.append(m)
    res = run_bass_kernel_spmd(nc, in_maps, core_ids=list(range(8)))
    y_sample = np.stack([np.asarray(r["ys"], np.float32) for r in res.results], axis=0)
    y_prompt = np.concatenate([np.asarray(r["yp"], np.float32).reshape(4, TPR, 1024) for r in res.results], axis=0)
    ns = np.concatenate([np.asarray(r["ns"], np.float32) for r in res.results], axis=0)
    return (y_prompt, y_sample, ns)
```

```python
import contextlib
import numpy as np
import concourse.bass as bass
import concourse.mybir as mybir
from concourse.bass_utils import run_bass_kernel_spmd

F32 = mybir.dt.float32
BF16 = mybir.dt.bfloat16
AF = mybir.ActivationFunctionType
ALU = mybir.AluOpType
AX = mybir.AxisListType
ENGS = ("pe", "act", "dve", "pool", "sp")
EPS = 1e-6
NL = 4
TSAMP = 4096
TPR = 256
NTOK = TSAMP + 4 * TPR


PSUM_KEYS = {"PA", "PB", "PC", "PD", "PE_", "PF", "PG", "PT"}


class Prog:
    def __init__(self, nc):
        self.nc = nc
        self.ops = []
        self.last_w = {}
        self.readers = {}

    limit = None
    in_region = False
    region_count = 0

    rec = None

    def add(self, eng, fn, reads=(), writes=(), dma=0, sem_key=None):
        if self.rec is not None:
            self.rec.append((eng, fn, list(reads), list(writes), dma, sem_key))
            return
        if self.in_region and self.limit is not None:
            if self.region_count >= self.limit:
                return
            self.region_count += 1
        i = len(self.ops)
        deps = []
        for r in reads:
            lw = self.last_w.get(r)
            if lw is not None:
                deps.append((lw, "raw"))
            if r in PSUM_KEYS:
                for rd in self.readers.get(r, ()):
                    deps.append((rd, "war"))
        for w in writes:
            lw = self.last_w.get(w)
            if lw is not None:
                deps.append((lw, "waw"))
            for rd in self.readers.get(w, ()):
                deps.append((rd, "war"))
        for r in reads:
            self.readers.setdefault(r, []).append(i)
        for w in writes:
            self.last_w[w] = i
            self.readers[w] = []
        assert (not dma) or sem_key is not None
        self.ops.append(dict(eng=eng, fn=fn, deps=deps, dma=dma, sem_key=sem_key, signal=False))
        return i

    def build(self):
        nc = self.nc
        ops = self.ops
        cnt = {e: 0 for e in ENGS}
        for o in ops:
            o["eidx"] = cnt[o["eng"]]
            cnt[o["eng"]] += 1
        dma_cum = {}
        for o in ops:
            if o["dma"]:
                k = o["sem_key"]
                dma_cum[k] = dma_cum.get(k, 0) + 16 * o["dma"]
                o["dma_val"] = dma_cum[k]
        known = {e: {f: -1 for f in ENGS} for e in ENGS}
        known_dma = {e: {} for e in ENGS}
        for o in ops:
            e = o["eng"]
            w_eng = {}
            w_dma = {}
            for (d, kind) in o["deps"]:
                D = ops[d]
                if D["dma"]:
                    k = D["sem_key"]
                    if known_dma[e].get(k, 0) >= D["dma_val"]:
                        continue
                    w_dma[k] = max(w_dma.get(k, 0), D["dma_val"])
                else:
                    f = D["eng"]
                    if f == e:
                        if e == "pe" or e == "sp":
                            continue
                        pass
                    if known[e][f] >= D["eidx"]:
                        continue
                    if f not in w_eng or ops[w_eng[f]]["eidx"] < D["eidx"]:
                        w_eng[f] = d
            o["w_eng"] = w_eng
            o["w_dma"] = w_dma
            for f, d in w_eng.items():
                ops[d]["signal"] = True
                known[e][f] = ops[d]["eidx"]
            for k, v in w_dma.items():
                known_dma[e][k] = v
        sig = {e: 0 for e in ENGS}
        for o in ops:
            if o["signal"] and not o["dma"]:
                sig[o["eng"]] += 1
                o["sig_val"] = sig[o["eng"]]
        with contextlib.ExitStack() as st:
            esem = {e: st.enter_context(nc.semaphore("s_" + e)) for e in ENGS}
            dsem = {}
            for k in dma_cum:
                dsem[k] = st.enter_context(nc.semaphore("d%d" % len(dsem)))
            block = st.enter_context(nc.Block())
            per = {e: [o for o in ops if o["eng"] == e] for e in ENGS}

            def emit(engh, lst):
                for o in lst:
                    for f, d in o["w_eng"].items():
                        engh.wait_ge(esem[f], ops[d]["sig_val"])
                    for k, v in o["w_dma"].items():
                        engh.wait_ge(dsem[k], v)
                    if o["fn"] is None:
                        continue
                    ins = o["fn"](engh)
                    if o["dma"]:
                        if not isinstance(ins, (list, tuple)):
                            ins = [ins]
                        assert len(ins) == o["dma"], (len(ins), o["dma"])
                        for x in ins:
                            x.then_inc(dsem[o["sem_key"]], 16)
                    elif o["signal"]:
                        ins.then_inc(esem[o["eng"]], 1)

            @block.sync
            def _(eng):
                emit(eng, per["sp"])

            @block.scalar
            def _(eng):
                emit(eng, per["act"])

            @block.vector
            def _(eng):
                emit(eng, per["dve"])

            @block.gpsimd
            def _(eng):
                emit(eng, per["pool"])

            @block.tensor
            def _(eng):
                emit(eng, per["pe"])


def host_consts():
    c = {}
    c["c_ident"] = np.eye(128, dtype=np.float32)
    r = np.arange(64)[:, None]
    cc = np.arange(64)[None, :]
    su = np.where(cc > r, -1.0, 0.0).astype(np.float32)
    sl = np.where(cc < r, -1.0, 0.0).astype(np.float32)
    mf = np.where(cc >= r, 0.0, -30000.0).astype(np.float32)
    mb = np.where(cc <= r, 0.0, -30000.0).astype(np.float32)
    i64 = np.eye(64, dtype=np.float32)
    t4 = lambda m: m
    blk = lambda b: (r // b == cc // b)
    md8 = blk(8).astype(np.float32)
    moff = [(blk(2 * b) & ~blk(b)).astype(np.float32) for b in (8, 16, 32)]
    c["c_m64"] = np.stack([su, sl, mf, mb, i64, md8] + moff, axis=1).astype(np.float32)
    tri = np.stack([(r <= cc), (r >= cc), np.ones((64, 64), bool)], axis=1).astype(np.float32)
    c["c_tri"] = tri
    T = TSAMP
    rows = T // 64
    rr = np.repeat(np.arange(rows, dtype=np.float32), 64)
    col = np.tile(np.arange(64, dtype=np.float32), rows)
    nf = 256
    freqs = np.power(np.float32(10000.0), -np.arange(nf, dtype=np.float32) / np.float32(nf)).astype(np.float32)
    ar = rr[:, None] * freqs[None]
    ac = col[:, None] * freqs[None]
    c["c_pos"] = np.concatenate([np.sin(ar), np.cos(ar), np.sin(ac), np.cos(ac)], axis=-1).astype(np.float32)
    band = np.zeros((128, 5, 4, 128), np.float32)
    for gi, w in enumerate((2, 4, 8, 16)):
        Tn = 384
        Fm = np.zeros((Tn, Tn), np.float64)
        for t in range(Tn):
            lo = min(max(t - w // 2, 0), Tn)
            hi = min(max(t + w - w // 2, 0), Tn)
            Fm[t, lo:hi] = 1.0 / (hi - lo)
        Fm -= np.eye(Tn)
        band[:, 0, gi, :] = Fm[0:128, 0:128].T
        band[:, 1, gi, :] = Fm[128:256, 128:256].T
        band[:, 2, gi, :] = Fm[256:384, 256:384].T
        band[:, 3, gi, :] = Fm[128:256, 0:128].T
        band[:, 4, gi, :] = Fm[128:256, 256:384].T
    c["c_band"] = band
    k64 = np.arange(64)
    ang = 2 * np.pi * np.outer(k64, k64) / 64.0
    C64 = np.cos(ang)
    S64 = np.sin(ang)
    pad = np.zeros((64, 2, 2, 128), np.float32)
    for half in range(2):
        pad[:, half, 0, half * 64:(half + 1) * 64] = C64
        pad[:, half, 1, half * 64:(half + 1) * 64] = S64
    c["c_dftpad"] = pad
    dA = np.zeros((64, 2, 3, 64), np.float32)
    for ai, A in enumerate((64, 4)):
        a = np.arange(A)
        an = 2 * np.pi * np.outer(a, a) / A
        dA[:A, ai, 0, :A] = np.cos(an)
        dA[:A, ai, 1, :A] = np.sin(an)
        dA[:A, ai, 2, :A] = -np.sin(an)
        Tt = 64 * A
        p = np.arange(A)[:, None, None]
        b = np.arange(64)[None, :, None]
        q = np.arange(64)[None, None, :]
        be = 2 * np.pi * ((b * (p + A * q)) % Tt) / Tt
        nrm = 1.0 / np.sqrt(64.0 * Tt)
        e2 = np.stack([np.cos(be) * nrm, -np.sin(be) * nrm], axis=2).astype(np.float32)
        c["c_e2_%d" % A] = e2
    c["c_dfta"] = dA
    tt_ = np.arange(256)
    th = 2 * np.pi * np.outer(tt_, tt_) / 256.0
    nrm = 1.0 / np.sqrt(64.0 * 256.0)
    e256 = np.stack([np.cos(th) * nrm, -np.sin(th) * nrm], axis=1)
    e256 = e256.reshape(2, 128, 2, 256).transpose(1, 0, 2, 3)
    c["c_e256"] = np.ascontiguousarray(e256.reshape(128, 1024)).astype(np.float32)
    return c


import os
DBG_REGION = os.environ.get("DBG_REGION", "")
DBG_LIMIT = os.environ.get("DBG_LIMIT")


def build_program(nl_run=NL, seq_sel=(0, 1, 2, 3, 4), skip=()):
    nc = bass.Bass("TRN2", target_bir_lowering=False)
    P = Prog(nc)
    P.limit = int(DBG_LIMIT) if DBG_LIMIT else None
    consts = host_consts()

    def din(name, shape):
        return nc.dram_tensor(name, list(shape), F32, kind="ExternalInput").ap()

    xs = din("xs", [TSAMP, 1024])
    xp = din("xp", [4 * TPR, 1024])
    sd = din("sd", [NL, 2, 4, 64, 64])
    ccd = din("cc", [2, 1024])
    ada_w = din("ada_w", [NL, 1024, 3072])
    ada_b = din("ada_b", [NL, 3072])
    norm_g = din("norm_g", [NL, 1024])
    w_in = din("w_in", [NL, 1024, 2832])
    conv_qkv = din("conv_qkv", [NL, 4, 768])
    a_log = din("a_log", [NL, 8])
    dt_bias = din("dt_bias", [NL, 8])
    dn_norm_g = din("dn_norm_g", [NL, 64])
    sgu_norm_g = din("sgu_norm_g", [NL, 256])
    sgu_w = din("sgu_w", [NL, 4, 128, 128])
    sgu_b = din("sgu_b", [NL, 4, 128])
    pool_w = din("pool_w", [NL, 4, 64, 64])
    pool_scale = din("pool_scale", [NL, 256])
    fourier_w = din("fourier_w", [NL, 4, 64, 64])
    w_out = din("w_out", [NL, 1024, 1024])
    final_norm_g = din("final_norm_g", [1024])
    cd = {k: din(k, v.shape) for k, v in consts.items()}

    ys = nc.dram_tensor("ys", [TSAMP, 1024], F32, kind="ExternalOutput").ap()
    yp = nc.dram_tensor("yp", [4 * TPR, 1024], F32, kind="ExternalOutput").ap()
    nsd = nc.dram_tensor("ns", [4, NL, 2, 4, 64, 64], F32, kind="ExternalOutput").ap()

    XS = nc.dram_tensor("XS", [NTOK, 1024], F32, kind="Internal").ap()
    MIXD = nc.dram_tensor("MIXD", [TSAMP, 1024], F32, kind="Internal").ap()
    QKVD = nc.dram_tensor("QKVD", [TSAMP, 768], BF16, kind="Internal").ap()
    OFD = nc.dram_tensor("OFD", [2, TSAMP, 256], F32, kind="Internal").ap()
    ZD = nc.dram_tensor("ZD", [2, TSAMP, 256], F32, kind="Internal").ap()
    VD = nc.dram_tensor("VD", [2, 64, 64, 256], F32, kind="Internal").ap()

    st = contextlib.ExitStack()
    with st:
        def sb(name, shape, dt=F32):
            return st.enter_context(nc.sbuf_tensor(name, list(shape), dt))

        def ps(name, shape, dt=F32):
            return st.enter_context(nc.psum_tensor(name, list(shape), dt))

        HT = sb("HT", [128, 8, TSAMP], BF16)
        WG = sb("WG", [128, 8, 1040], BF16)
        AW = sb("AW", [128, 8, 256], BF16)
        GATE = sb("GATE", [128, 2, 1024])
        XT = sb("XT", [128, 1024])
        XT2 = sb("XT2", [128, 1024])
        XN = sb("XN", [128, 1024], BF16)
        MXT = sb("MXT", [128, 8, 128], BF16)
        JNK = MXT[:].rearrange("p k t -> p (k t)")
        IDN = sb("IDN", [128, 128])
        IDB = sb("IDB", [128, 128], BF16)
        M64 = sb("M64", [128, 9, 64])
        TRI = sb("TRI", [128, 3, 64])
        BAND = sb("BAND", [128, 5, 4, 128])
        DFTPAD = sb("DFTPAD", [64, 2, 2, 128])
        DFTA = sb("DFTA", [64, 2, 3, 64])
        SCB = sb("SCB", [128, 8, 2, 128], BF16)
        TMPL = sb("TMPL", [128, 128])
        CCT = sb("CCT", [128, 16])
        SCF = sb("SCF", [128, 16])
        SCb = sb("SCb", [128, 8, 2], BF16)
        NG = sb("NG", [128, 8])
        AB = sb("AB", [128, 24])
        ABG = sb("ABG", [128, 1024])
        MODT = sb("MODT", [128, 24, 2])
        AM = sb("AM", [128, 8, 2])
        CW = sb("CW", [128, 24])
        ALB = sb("ALB", [128, 8])
        DTB = sb("DTB", [128, 8])
        NEGA = sb("NEGA", [128, 8])
        DNG = sb("DNG", [128, 256])
        SGNG = sb("SGNG", [128, 256])
        SWT = sb("SWT", [128, 4, 128])
        SGBT = sb("SGBT", [128, 4])
        PWB = sb("PWB", [128, 2, 128])
        PSC = sb("PSC", [128, 256])
        FW = sb("FW", [64, 4, 64])
        PCS = sb("PCS", [128, 2, 2, 128])
        ST = sb("ST0", [128, 8])
        TL = 128
        RAW = sb("RAW", [128, 6, TL + 3])
        CS = sb("CS", [128, 6, TL])
        QKVF = sb("QKVF", [128, 1024])
        QKV3 = sb("QKV3", [128, 768], BF16)
        SALL = sb("SALL", [128, 64, 24])
        TB_ = sb("TB_", [128, 32, 4])
        TZ_ = sb("TZ_", [128, 32, 4])
        TG_ = sb("TG_", [128, 32, 4])
        KS = sb("KS", [128, 256], BF16)
        RH = sb("RH", [128, 4, 128], BF16)
        KDEC = sb("KDEC", [128, 256], BF16)
        QDEC = sb("QDEC", [128, 256], BF16)
        TR = sb("TR", [128, 1024], BF16)
        SQ = sb("SQ", [128, 1024])
        XA = sb("XA", [128, 256], BF16)
        XB = sb("XB", [128, 256], BF16)
        COF = sb("COF", [128, 256], BF16)
        SSb = sb("SSb", [128, 256], BF16)
        DG = sb("DG", [128, 256])
        NDG = sb("NDG", [128, 256])
        DTT = sb("DTT", [128, 256])
        Cb = [sb("C_a", [128, 256], BF16), sb("C_b", [128, 256], BF16)]
        Bb = [sb("B_a", [128, 256], BF16), sb("B_b", [128, 256], BF16)]
        Tb = [sb("T_a", [128, 256], BF16), sb("T_b", [128, 256], BF16)]
        QKT = sb("QKT", [128, 256], BF16)
        MTT = sb("MTT", [128, 256], BF16)
        UU = sb("UU", [128, 256])
        WB_ = sb("WB_", [128, 256], BF16)
        WT = sb("WT", [128, 256], BF16)
        VN = sb("VN", [128, 256], BF16)
        OO = sb("OO", [128, 256])
        SS = sb("SS", [128, 256])
        STMP = sb("STMP", [128, 256])
        GA = STMP
        RN = sb("RN", [128, 16])
        UVG = sb("UVG", [128, 512])
        VNS = sb("VNS", [128, 256])
        SGB = sb("SGB", [128, 256])
        YB = sb("YB", [128, 256])
        XCT = sb("XCT", [128, 2, 128])
        ZR = [sb("ZR%d" % i, [128, 256]) for i in range(3)]
        ZCS = UVG
        ZA = sb("ZA", [64, 2, 512])
        VV = sb("VV", [64, 2, 512])
        VB = sb("VB", [64, 2, 256])
        E2 = sb("E2", [64, 2, 64])
        VB2 = sb("VB2", [64, 2, 256])
        E2_2 = sb("E2_2", [64, 2, 64])
        XCT2 = sb("XCT2", [128, 2, 128])
        UVG2 = sb("UVG2", [128, 512])
        YD = YB[0:64, :]
        if os.environ.get("EXTRA_SB"):
            sb("EXTRA", [128, int(os.environ["EXTRA_SB"]) // 4])
        PA = ps("PA", [128, 512])
        PB = ps("PB", [128, 512])
        PC = ps("PC", [128, 512])
        PD = ps("PD", [128, 512])
        PE_ = ps("PE_", [128, 512])
        PF = ps("PF", [128, 512])
        PG = ps("PG", [128, 512])
        PT = ps("PT", [128, 512])
        PTb = PT[:].bitcast(BF16)

        def ld(out, in_, r, w, q="sp", key=None, n=1, nc_ok=False):
            kw = dict(allow_slow_non_contiguous=True) if nc_ok else {}
            P.add(q, lambda e: e.dma_start(out=out, in_=in_, **kw), reads=r, writes=w, dma=1, sem_key=key or ("ld", w[0]))

        def stq(out, in_, r, w, key, q="sp"):
            P.add(q, lambda e: e.dma_start(out=out, in_=in_), reads=r, writes=w, dma=1, sem_key=(key, q))

        def mm(out, lhsT, rhs, r, w, start=True, stop=True):
            P.add("pe", lambda e: e.matmul(out, lhsT=lhsT, rhs=rhs, start=start, stop=stop), reads=r, writes=w)

        def tr(out, in_, ident, r, w):
            P.add("pe", lambda e: e.transpose(out=out, in_=in_, identity=ident), reads=r, writes=w)

        def act(out, in_, func, r, w, bias=None, scale=None, accum=None):
            kw = {}
            if bias is not None:
                kw["bias"] = bias
            if scale is not None:
                kw["scale"] = scale
            if accum is not None:
                kw["accum_out"] = accum
            P.add("act", lambda e: e.activation(out=out, in_=in_, func=func, **kw), reads=r, writes=w)

        def tt(out, in0, in1, op, r, w, eng="dve"):
            P.add(eng, lambda e: e.tensor_tensor(out=out, in0=in0, in1=in1, op=op), reads=r, writes=w)

        def ts(out, in0, s1, s2, op0, op1, r, w, eng="dve"):
            if op1 is None:
                P.add(eng, lambda e: e.tensor_scalar(out=out, in0=in0, scalar1=s1, scalar2=None, op0=op0), reads=r, writes=w)
            else:
                P.add(eng, lambda e: e.tensor_scalar(out=out, in0=in0, scalar1=s1, scalar2=s2, op0=op0, op1=op1), reads=r, writes=w)

        def stt(out, in0, scalar, in1, op0, op1, r, w):
            P.add("dve", lambda e: e.scalar_tensor_tensor(out=out, in0=in0, scalar=scalar, in1=in1, op0=op0, op1=op1), reads=r, writes=w)

        def cp(out, in_, r, w, eng="dve"):
            if eng == "act":
                P.add(eng, lambda e: e.activation(out=out, in_=in_, func=AF.Identity), reads=r, writes=w)
            else:
                P.add(eng, lambda e: e.tensor_copy(out=out, in_=in_), reads=r, writes=w)

        def recip(out, in_, r, w):
            P.add("dve", lambda e: e.reciprocal(out=out, in_=in_), reads=r, writes=w)

        def rstd_from_ssq(ssq_ap, out_ap, n, key):
            ts(ssq_ap, ssq_ap, 1.0 / n, EPS, ALU.mult, ALU.add, [key], [key])
            act(ssq_ap, ssq_ap, AF.Sqrt, [key], [key])
            recip(out_ap, ssq_ap, [key], [key])

        def load_T(dst_ap, src_ap, n, dkey):
            ld(TMPL[0:n, :], src_ap, [], ["TMPL"])
            tr(PG[:, 0:n], TMPL[0:n, :], IDN[0:n, 0:n], ["TMPL", "IDN"], ["PG"])
            cp(dst_ap, PG[:, 0:n], ["PG"], [dkey])

        def bc3(ap, shape):
            return ap.unsqueeze(2).to_broadcast(shape)

        ld(IDN[:], cd["c_ident"], [], ["IDN"])
        cp(IDB[:], IDN[:], ["IDN"], ["IDB"])
        for hf in range(2):
            ld(M64[hf * 64:hf * 64 + 64], cd["c_m64"], [], ["M64"])
            ld(TRI[hf * 64:hf * 64 + 64], cd["c_tri"], [], ["TRI"])
        ld(BAND[:], cd["c_band"], [], ["BAND"])
        ld(DFTPAD[:], cd["c_dftpad"], [], ["DFTPAD"])
        ld(DFTA[:], cd["c_dfta"], [], ["DFTA"])
        def masks(hf):
            psl = slice(hf * 64, hf * 64 + 64)
            bh = lambda i: M64[psl, i, :].unsqueeze(1).to_broadcast([64, 2, 64])
            return dict(SUL=(bh(0), bh(1)), MINC=(bh(2), bh(3)), ID4=bh(4), MD8=bh(5), MOFF=[bh(6), bh(7), bh(8)])
        MSK = [masks(0), masks(1)]

        load_T(CCT[:], ccd.rearrange("c (k p) -> (c k) p", p=128), 16, "CCT")
        act(SCF[:], CCT[:], AF.Silu, ["CCT"], ["SCF"])
        cp(SCb[:].rearrange("p k c -> p c k"), SCF[:].rearrange("p (c k) -> p c k", c=2), ["SCF"], ["SCb"])
        cp(SCB[:], SCb[:].unsqueeze(3).to_broadcast([128, 8, 2, 128]), ["SCb"], ["SCB"])

        for t in (range(TSAMP // 128) if 0 in seq_sel else []):
            ld(XT[:], xs[t * 128:(t + 1) * 128, :], [], ["XT"])
            ld(XT2[:], cd["c_pos"][t * 128:(t + 1) * 128, :], [], ["XT2"])
            tt(XT[:], XT[:], XT2[:], ALU.add, ["XT", "XT2"], ["XT"])
            stq(XS[t * 128:(t + 1) * 128, :], XT[:], ["XT"], [("XS", t)], ("st", "XT"))
        for t in range(8):
            if (1 + t // 2) in seq_sel:
                ld(XT[:], xp[t * 128:(t + 1) * 128, :], [], ["XT"])
                stq(XS[TSAMP + t * 128:TSAMP + (t + 1) * 128, :], XT[:], ["XT"], [("XS", 32 + t)], ("st", "XT"))

        seqs = [(0, TSAMP, 0, None)] + [(TSAMP + i * TPR, TPR, 1, i) for i in range(4)]
        seqs = [seqs[i] for i in seq_sel]

        for l in range(nl_run):
            load_T(NG[:], norm_g[l].rearrange("(k p) -> k p", p=128), 8, "NG")
            load_T(AB[:], ada_b[l].rearrange("(k p) -> k p", p=128), 24, "AB")
            load_T(CW[:], conv_qkv[l].rearrange("j (c p) -> (j c) p", p=128), 24, "CW")
            ld(ABG[:], ada_b[l, 2048:3072].partition_broadcast(128), [], ["ABG"])
            ld(ALB[:], a_log[l].partition_broadcast(128), [], ["ALB"])
            ld(DTB[:], dt_bias[l].partition_broadcast(128), [], ["DTB"])
            act(NEGA[:], ALB[:], AF.Exp, ["ALB"], ["NEGA"])
            ts(NEGA[:], NEGA[:], -1.0, None, ALU.mult, None, ["NEGA"], ["NEGA"])
            for h in range(4):
                ld(DNG[:, h * 64:(h + 1) * 64], dn_norm_g[l].partition_broadcast(128), [], ["DNG"], key=("ld", "DNG"))
            ld(SGNG[:], sgu_norm_g[l].partition_broadcast(128), [], ["SGNG"])
            ld(PSC[:], pool_scale[l].partition_broadcast(128), [], ["PSC"])
            load_T(SGBT[:], sgu_b[l], 4, "SGBT")
            for g in range(4):
                ld(TMPL[:], sgu_w[l, g], [], ["TMPL"])
                tr(PG[:, 0:128], TMPL[:], IDN[:], ["TMPL", "IDN"], ["PG"])
                cp(SWT[:, g, :], PG[:, 0:128], ["PG"], ["SWT"])
            P.add("pool", lambda e: e.memset(PWB[:], 0.0), reads=[], writes=["PWB"])
            for g in range(4):
                hb = (g % 2) * 64
                ld(PWB[hb:hb + 64, g // 2, hb:hb + 64], pool_w[l, g], [], ["PWB"], key=("ld", "PWB"))
            for c in range(2):
                tt(PWB[:, c, :], PWB[:, c, :], PSC[:, c * 128:(c + 1) * 128], ALU.mult, ["PWB", "PSC"], ["PWB"])
            ld(FW[:], fourier_w[l].rearrange("g c d -> c g d"), [], ["FW"])
            for g in range(4):
                for cs in range(2):
                    mm(PG[:, 0:64], DFTPAD[:, g % 2, cs, :], FW[:, g, :], ["DFTPAD", "FW"], ["PG"])
                    cp(PCS[:, cs, g // 2, (g % 2) * 64:(g % 2) * 64 + 64], PG[:, 0:64], ["PG"], ["PCS"])
            wo_v = w_out[l].rearrange("(k p) n -> p k n", p=128)
            aw_v = ada_w[l].rearrange("(k p) n -> p k n", p=128)
            for n in range(12):
                P.add("pool", lambda e, n=n, aw_v=aw_v: e.dma_start(out=AW[:], in_=aw_v[:, :, n * 256:(n + 1) * 256]),
                      reads=[], writes=["AW"], dma=1, sem_key=("ld", "AW"))
                for c in range(2):
                    for k in range(8):
                        mm(PG[:, 0:2], AW[:, k, c * 128:(c + 1) * 128], SCb[:, k, :], ["AW", "SCb"], ["PG"], start=(k == 0), stop=(k == 7))
                    ts(MODT[:, n * 2 + c, :], PG[:, 0:2], AB[:, n * 2 + c:n * 2 + c + 1], None, ALU.add, None, ["PG", "AB"], ["MODT"])
                if n >= 8:
                    for cond in range(2):
                        for k in range(8):
                            mm(PA[:, 0:256], SCB[:, k, cond, :], AW[:, k, :], ["SCB", "AW"], ["PA"], start=(k == 0), stop=(k == 7))
                        tt(GATE[:, cond, (n - 8) * 256:(n - 7) * 256], PA[:, 0:256], ABG[:, (n - 8) * 256:(n - 7) * 256], ALU.add,
                           ["PA", "ABG"], ["GATE"])
            ts(AM[:], MODT[:, 8:16, :], 1.0, None, ALU.add, None, ["MODT"], ["AM"])
            tt(AM[:], AM[:], bc3(NG[:], [128, 8, 2]), ALU.mult, ["AM", "NG"], ["AM"])

            win_v = w_in[l].rearrange("(k p) n -> p k n", p=128)

            def load_wg(c0, ncol):
                P.add("pool", lambda e, wv=win_v: e.dma_start(out=WG[:, :, 0:ncol], in_=wv[:, :, c0:c0 + ncol]),
                      reads=[], writes=["WG"], dma=1, sem_key=("ld", "WG"))

            for (tok0, T, cond, pidx) in seqs:
                ntile = T // 128
                CSj = CS[:].rearrange("p a b -> p (a b)").bitcast(BF16)[:, 0:1024]
                for t in range(ntile):
                    g0 = tok0 // 128 + t
                    xt, xtk = ((XT, "XT"), (XT2, "XT2"))[t % 2]
                    xn, xnk = ((XN[:], "XN"), (MXT[:].rearrange("p k t -> p (k t)"), "MXT"))[t % 2]
                    ptb, ptk = ((PTb, "PT"), (PG[:].bitcast(BF16), "PG"))[t % 2]
                    stc = (t % 2) * 4
                    ld(xt[:], XS[g0 * 128:(g0 + 1) * 128, :], [("XS", g0)], [xtk])
                    act(CSj, xt[:], AF.Square, [xtk], ["CS%d" % c_ for c_ in range(6)] + ["ST%d" % (t % 2)], accum=ST[:, stc:stc + 1])
                    rstd_from_ssq(ST[:, stc:stc + 1], ST[:, stc + 2:stc + 3], 1024.0, "ST%d" % (t % 2))
                    ts(xn, xt[:], ST[:, stc + 2:stc + 3], None, ALU.mult, None, [xtk, "ST%d" % (t % 2)], [xnk])
                    for k in range(8):
                        tr(ptb[:, k * 128:(k + 1) * 128], xn[:, k * 128:(k + 1) * 128], IDB[:], [xnk, "IDB"], [ptk])
                    for k in range(8):
                        act(HT[:, k, t * 128:(t + 1) * 128], ptb[:, k * 128:(k + 1) * 128], AF.Identity, [ptk, "AM", "MODT"],
                            [("HT", t)], bias=MODT[:, k, cond:cond + 1], scale=AM[:, k, cond:cond + 1])
                HTall = [("HT", t) for t in range(ntile)]

                def proj_tm(out_ps, tok_sl, c0, ncol, M, okey):
                    for k in range(8):
                        mm(out_ps, HT[:, k, tok_sl], WG[:, k, c0:c0 + ncol], HTall + ["WG"], [okey], start=(k == 0), stop=(k == 7))

                def proj_fm(out_ps, tok_sl, c0, ncol, okey, start=True):
                    for k in range(8):
                        mm(out_ps, WG[:, k, c0:c0 + ncol], HT[:, k, tok_sl], HTall + ["WG"], [okey], start=(k == 0), stop=(k == 7))

                load_wg(0, 1040)
                nch = T // 64
                P.in_region = ("dn" == DBG_REGION)
                dn_on = "dn" not in skip
                for tl in (range(T // TL) if dn_on else []):
                    s0 = tl * TL
                    lo = s0 - 2 if s0 > 0 else 0
                    hi = s0 + TL + 1 if s0 + TL < T else T
                    c_lo = lo - (s0 - 2)
                    ncol = hi - lo
                    for c in range(6):
                        pm, pmk = ((PA, "PA"), (PB, "PB"), (PE_, "PE_"), (PF, "PF"), (PG, "PG"), (PT, "PT"))[c]
                        rk, ck = "RAW%d" % c, "CS%d" % c
                        proj_fm(pm[:, 0:ncol], slice(lo, hi), c * 128, 128, pmk)
                        if c_lo > 0:
                            P.add("pool", lambda e, c=c: e.memset(RAW[:, c, 0:2], 0.0), reads=[], writes=[rk])
                        if hi == T:
                            P.add("pool", lambda e, c=c: e.memset(RAW[:, c, 2 + TL:3 + TL], 0.0), reads=[], writes=[rk])
                        cp(RAW[:, c, c_lo:c_lo + ncol], pm[:, 0:ncol], [pmk], [rk], eng="act")
                        ts(CS[:, c, :], RAW[:, c, 0:TL], CW[:, 0 * 6 + c:0 * 6 + c + 1], None, ALU.mult, None, [rk, "CW"], [ck])
                        for j in range(1, 4):
                            stt(CS[:, c, :], RAW[:, c, j:j + TL], CW[:, j * 6 + c:j * 6 + c + 1], CS[:, c, :], ALU.mult, ALU.add,
                                [rk, "CW", ck], [ck])
                        act(CS[:, c, :], CS[:, c, :], AF.Silu, [ck], [ck])
                    for c in range(6):
                        dst = (PC if c < 4 else PD)
                        off = (c % 4) * 128
                        tr(dst[:, off:off + 128], CS[:, c, :], IDN[:], ["CS%d" % c, "IDN"], ["PC" if c < 4 else "PD"])
                    cp(QKVF[:, 0:512], PC[:, :], ["PC"], ["QKVF"])
                    cp(QKVF[:, 512:768], PD[:, 0:256], ["PD"], ["QKVF"], eng="act")
                    tt(SQ[:, 0:512], QKVF[:, 0:512], QKVF[:, 0:512], ALU.mult, ["QKVF"], ["SQ"])
                    P.add("dve", lambda e: e.tensor_reduce(out=RN[:, 0:8], in_=SQ[:, 0:512].rearrange("p (h c) -> p h c", h=8), axis=AX.X, op=ALU.add),
                          reads=["SQ"], writes=["RN"])
                    ts(RN[:, 0:8], RN[:, 0:8], EPS, None, ALU.add, None, ["RN"], ["RN"])
                    act(RN[:, 0:8], RN[:, 0:8], AF.Sqrt, ["RN"], ["RN"])
                    recip(RN[:, 8:16], RN[:, 0:8], ["RN"], ["RN"])
                    ts(RN[:, 8:12], RN[:, 8:12], 0.125, None, ALU.mult, None, ["RN"], ["RN"])
                    tt(QKV3[:, 0:512].rearrange("p (h c) -> p h c", h=8), QKVF[:, 0:512].rearrange("p (h c) -> p h c", h=8),
                       bc3(RN[:, 8:16], [128, 8, 64]), ALU.mult, ["QKVF", "RN"], ["QKV3_00", "QKV3_01", "QKV3_10", "QKV3_11"])
                    cp(QKV3[:, 512:768], QKVF[:, 512:768], ["QKVF"], ["QKV3_00", "QKV3_01", "QKV3_10", "QKV3_11"], eng="pool")
                    stq(QKVD[s0:s0 + 128, :], QKV3[:], ["QKV3_00", "QKV3_01", "QKV3_10", "QKV3_11"], [("QKVD", tl)], ("st", "QKV3"))

                def dn_scal_all(d):
                    hf = d
                    psl = slice(hf * 64, hf * 64 + 64)
                    sx = "_%d" % hf
                    bk, bkk = (PA, "PA") if hf == 0 else (PB, "PB")
                    bk2, bkk2 = (PC, "PC") if hf == 0 else (PD, "PD")
                    for g0 in range(0, nch, 32):
                        n = min(32, nch - g0)
                        for j in range(n):
                            ch = g0 + j
                            proj_tm(bk[psl, j * 16:(j + 1) * 16], slice(ch * 64, ch * 64 + 64), 1024, 16, 64, bkk)
                        pv = bk[psl, 0:n * 16].rearrange("p (n c) -> p n c", c=16)
                        TB, TZ, TG = TB_[psl, 0:n, :], TZ_[psl, 0:n, :], TG_[psl, 0:n, :]
                        SA = lambda o: SALL[psl, g0:g0 + n, o:o + 4]
                        kk = ["TB_" + sx]
                        act(TB, pv[:, :, d * 4:d * 4 + 4], AF.Sigmoid, [bkk], kk)
                        act(SA(0), TB, AF.Sqrt, kk, ["SALL" + sx])
                        tt(TZ, pv[:, :, 8 + d * 4:12 + d * 4], DTB[psl, d * 4:d * 4 + 4].unsqueeze(1).to_broadcast([64, n, 4]), ALU.add, [bkk, "DTB"], kk)
                        act(TZ, TZ, AF.Exp, kk, kk)
                        act(TZ, TZ, AF.Ln, kk, kk, bias=1.0)
                        tt(TG, TZ, NEGA[psl, d * 4:d * 4 + 4].unsqueeze(1).to_broadcast([64, n, 4]), ALU.mult, kk + ["NEGA"], kk)
                        gflat = TG_[psl, 0:n, :].rearrange("p n c -> p (n c)")
                        mm(bk2[psl, 0:n * 4], TRI[psl, d, :], gflat, ["TRI"] + kk, [bkk2])
                        mm(bk2[psl, 128:128 + n * 4], TRI[psl, 2, :], gflat, ["TRI"] + kk, [bkk2])
                        gcv = bk2[psl, 0:n * 4].rearrange("p (n c) -> p n c", c=4)
                        glv = bk2[psl, 128:128 + n * 4].rearrange("p (n c) -> p n c", c=4)
                        cp(SA(20), gcv, [bkk2], ["SALL" + sx])
                        act(SA(8), gcv, AF.Exp, [bkk2], ["SALL" + sx])
                        tt(TZ, glv, SA(20), ALU.subtract, [bkk2, "SALL" + sx], kk)
                        act(SA(12), TZ, AF.Exp, kk, ["SALL" + sx])
                        act(SA(16), glv, AF.Exp, [bkk2], ["SALL" + sx])
                        tt(SA(4), SA(0), SA(8), ALU.mult, ["SALL" + sx], ["SALL" + sx])

                def dn_unit(ch, d, hp):
                    hf = d
                    psl = slice(hf * 64, hf * 64 + 64)
                    sx = "_%d" % hf
                    sy = "_%d%d" % (hf, hp)
                    K_ = lambda *n: [x + sy for x in n]
                    bx, by = [[(PC, PD), (PF, PG)], [(PA, PB), (PE_, PT)]][hf][hp]
                    kx, ky = [[("PC", "PD"), ("PF", "PG")], [("PA", "PB"), ("PE_", "PT")]][hf][hp]
                    M = MSK[hf]
                    v2 = lambda ap: ap.rearrange("p (h c) -> p h c", h=2)
                    sh = [64, 2, 64]
                    cs_ = slice(hp * 128, hp * 128 + 128)
                    SAk = "SALL" + sx
                    SA = lambda o: SALL[psl, ch, o + 2 * hp:o + 2 * hp + 2]
                    ld(QKV3[psl, :].rearrange("p (t c) -> p t c", t=3)[:, :, hp * 128:hp * 128 + 128],
                       QKVD[ch * 64:ch * 64 + 64, :].rearrange("p (t c) -> p t c", t=3)[:, :, hp * 128:hp * 128 + 128],
                       [("QKVD", ch // 2)], ["QKV3" + sy], key=("ld", "QKV3" + sy))
                    Q = QKV3[psl, 0 + hp * 128:128 + hp * 128]
                    K = QKV3[psl, 256 + hp * 128:384 + hp * 128]
                    V = QKV3[psl, 512 + hp * 128:640 + hp * 128]
                    QK3 = ["QKV3" + sy]
                    tt(v2(KS[psl, cs_]), v2(K), bc3(SA(0), sh), ALU.mult, QK3 + [SAk], K_("KS"))
                    tt(RH[psl, 2 * hp:2 * hp + 2, 0:64], v2(V), bc3(SA(0), sh), ALU.mult, QK3 + [SAk], K_("RH"), eng="pool")
                    tt(RH[psl, 2 * hp:2 * hp + 2, 64:128], v2(K), bc3(SA(4), sh), ALU.mult, QK3 + [SAk], K_("RH"))
                    tt(v2(KDEC[psl, cs_]), v2(K), bc3(SA(12), sh), ALU.mult, QK3 + [SAk], K_("KDEC"), eng="pool")
                    tt(v2(QDEC[psl, cs_]), v2(Q), bc3(SA(8), sh), ALU.mult, QK3 + [SAk], K_("QDEC"))
                    I64 = IDB[psl, hf * 64:hf * 64 + 64]
                    bxb = bx[psl, 0:256].bitcast(BF16)
                    c0 = hp * 128
                    for h in range(2):
                        hs = slice(c0 + h * 64, c0 + h * 64 + 64)
                        hq = slice(h * 64, h * 64 + 64)
                        tr(bxb[:, h * 64:h * 64 + 64], KS[psl, hs], I64, K_("KS") + ["IDB"], [kx])
                        tr(bxb[:, 128 + h * 64:192 + h * 64], K[:, hq], I64, QK3 + ["IDB"], [kx])
                        tr(bxb[:, 256 + h * 64:320 + h * 64], Q[:, hq], I64, QK3 + ["IDB"], [kx])
                        tr(bxb[:, 384 + h * 64:448 + h * 64], QDEC[psl, hs], I64, K_("QDEC") + ["IDB"], [kx])
                    TRs = TR[psl, hp * 512:hp * 512 + 512]
                    cp(TRs, bxb, [kx], K_("TR"), eng="act")
                    KST = lambda h: TRs[:, h * 64:h * 64 + 64]
                    KT = lambda h: TRs[:, 128 + h * 64:192 + h * 64]
                    QT = lambda h: TRs[:, 256 + h * 64:320 + h * 64]
                    QDT = lambda h: TRs[:, 384 + h * 64:448 + h * 64]
                    for h in range(2):
                        mm(by[psl, h * 64:h * 64 + 64], KST(h), KST(h), K_("TR"), [ky])
                        mm(by[psl, 128 + h * 64:192 + h * 64], KT(h), QT(h), K_("TR"), [ky])
                    tt(v2(DG[psl, cs_]), M["ID4"], bc3(SA(20), sh), ALU.mult, ["M64", SAk], K_("DG"), eng="pool")
                    act(NDG[psl, cs_], DG[psl, cs_], AF.Identity, K_("DG"), K_("NDG"), scale=-1.0)
                    for h in range(2):
                        hs = slice(c0 + h * 64, c0 + h * 64 + 64)
                        mm(by[psl, 256 + h * 64:320 + h * 64], TRI[psl, 2, :], DG[psl, hs], ["TRI"] + K_("DG"), [ky], start=True, stop=False)
                        mm(by[psl, 256 + h * 64:320 + h * 64], NDG[psl, hs], TRI[psl, 2, :], ["TRI"] + K_("NDG"), [ky], start=False, stop=True)
                    tt(v2(DTT[psl, cs_]), v2(by[psl, 256:384]), M["MINC"][d], ALU.add, [ky, "M64"], K_("DTT"))
                    act(DTT[psl, cs_], DTT[psl, cs_], AF.Exp, K_("DTT"), K_("DTT"))
                    C0_, B0_ = Cb[0], Bb[0]
                    CD, BD = Cb[1], Bb[1]
                    TT_, TN_ = Tb[0], Tb[1]
                    tt(v2(C0_[psl, cs_]), v2(by[psl, 0:128]), M["SUL"][d], ALU.mult, [ky, "M64"], K_("C_a"))
                    tt(v2(B0_[psl, cs_]), v2(by[psl, 0:128]), M["SUL"][1 - d], ALU.mult, [ky, "M64"], K_("B_a"))
                    tt(QKT[psl, cs_], by[psl, 128:256], DTT[psl, cs_], ALU.mult, [ky] + K_("DTT"), K_("QKT"))
                    tt(v2(CD[psl, cs_]), v2(C0_[psl, cs_]), M["MD8"], ALU.mult, K_("C_a") + ["M64"], K_("C_b"), eng="pool")
                    tt(v2(BD[psl, cs_]), v2(B0_[psl, cs_]), M["MD8"], ALU.mult, K_("B_a") + ["M64"], K_("B_b"), eng="pool")
                    tt(v2(TT_[psl, cs_]), v2(CD[psl, cs_]), M["ID4"], ALU.add, K_("C_b") + ["M64"], K_("T_a"), eng="pool")
                    tt(v2(TN_[psl, cs_]), v2(BD[psl, cs_]), M["ID4"], ALU.add, K_("B_b") + ["M64"], K_("T_b"), eng="pool")

                    def grp(dst, dk, off, lt, lk, rt, rk):
                        for h in range(2):
                            hs = slice(c0 + h * 64, c0 + h * 64 + 64)
                            mm(dst[psl, off + h * 64:off + h * 64 + 64], lt[psl, hs], rt[psl, hs], K_(lk, rk), [dk])

                    for lev in range(2):
                        grp(bx, kx, 0, CD, "C_b", BD, "B_b")
                        grp(bx, kx, 128, BD, "B_b", CD, "C_b")
                        cp(BD[psl, cs_], bx[psl, 0:128], [kx], K_("B_b"), eng="act")
                        cp(CD[psl, cs_], bx[psl, 128:256], [kx], K_("C_b"), eng="act")
                        grp(by, ky, 0, BD, "B_b", TT_, "T_a")
                        grp(by, ky, 128, CD, "C_b", TN_, "T_b")
                        tt(TT_[psl, cs_], TT_[psl, cs_], by[psl, 0:128], ALU.add, K_("T_a") + [ky], K_("T_a"))
                        tt(TN_[psl, cs_], TN_[psl, cs_], by[psl, 128:256], ALU.add, K_("T_b") + [ky], K_("T_b"))
                    BOF = VN
                    for li in range(3):
                        last = (li == 2)
                        tt(v2(BOF[psl, cs_]), v2(B0_[psl, cs_]), M["MOFF"][li], ALU.mult, K_("B_a") + ["M64"], K_("VN"), eng="pool")
                        grp(bx, kx, 0, BOF, "VN", TT_, "T_a")
                        cp(XA[psl, cs_], bx[psl, 0:128], [kx], K_("XA"), eng="act")
                        if not last:
                            tt(v2(COF[psl, cs_]), v2(C0_[psl, cs_]), M["MOFF"][li], ALU.mult, K_("C_a") + ["M64"], K_("COF"), eng="pool")
                            grp(bx, kx, 128, COF, "COF", TN_, "T_b")
                            cp(XB[psl, cs_], bx[psl, 128:256], [kx], K_("XB"), eng="act")
                        grp(by, ky, 0, TN_, "T_b", XA, "XA")
                        if not last:
                            grp(by, ky, 128, TT_, "T_a", XB, "XB")
                        tt(TT_[psl, cs_], TT_[psl, cs_], by[psl, 0:128], ALU.add, K_("T_a") + [ky], K_("T_a"))
                        if not last:
                            tt(TN_[psl, cs_], TN_[psl, cs_], by[psl, 128:256], ALU.add, K_("T_b") + [ky], K_("T_b"))
                    tt(MTT[psl, cs_], TT_[psl, cs_], DTT[psl, cs_], ALU.mult, K_("T_a", "DTT"), K_("MTT"))
                    for h in range(2):
                        hs = slice(c0 + h * 64, c0 + h * 64 + 64)
                        mm(by[psl, h * 128:(h + 1) * 128], MTT[psl, hs], RH[psl, 2 * hp + h, :], K_("MTT", "RH"), [ky])
                    byv = by[psl, 0:256].rearrange("p (h c) -> p h c", h=2)
                    tt(v2(UU[psl, cs_]), byv[:, :, 0:64], bc3(SA(0), sh), ALU.mult, [ky, SAk], K_("UU"))
                    tt(v2(WB_[psl, cs_]), byv[:, :, 64:128], bc3(SA(0), sh), ALU.mult, [ky, SAk], K_("WB_"))
                    bxw = bx[psl, 0:64].bitcast(BF16)
                    for h in range(2):
                        hs = slice(c0 + h * 64, c0 + h * 64 + 64)
                        tr(bxw[:, h * 64:h * 64 + 64], WB_[psl, hs], I64, K_("WB_") + ["IDB"], [kx])
                    cp(WT[psl, cs_], bxw, [kx], K_("WT"), eng="act")
                    for h in range(2):
                        hs = slice(c0 + h * 64, c0 + h * 64 + 64)
                        mm(bx[psl, 128 + h * 64:192 + h * 64], WT[psl, hs], SSb[psl, hs], K_("WT", "SSb"), [kx])
                    tt(VN[psl, cs_], UU[psl, cs_], bx[psl, 128:256], ALU.subtract, K_("UU") + [kx], K_("VN"))
                    for h in range(2):
                        hs = slice(c0 + h * 64, c0 + h * 64 + 64)
                        mm(by[psl, h * 64:h * 64 + 64], QDT(h), SSb[psl, hs], K_("TR", "SSb"), [ky], start=True, stop=False)
                        mm(by[psl, h * 64:h * 64 + 64], QKT[psl, hs], VN[psl, hs], K_("QKT", "VN"), [ky], start=False, stop=True)
                    cp(OO[psl, cs_], by[psl, 0:128], [ky], K_("OO"), eng="act")
                    for h in range(2):
                        hs = slice(c0 + h * 64, c0 + h * 64 + 64)
                        mm(bx[psl, h * 64:h * 64 + 64], KDEC[psl, hs], VN[psl, hs], K_("KDEC", "VN"), [kx])
                    tt(v2(STMP[psl, cs_]), v2(SS[psl, cs_]), bc3(SA(16), sh), ALU.mult, K_("SS") + [SAk], K_("STMP"), eng="pool")
                    tt(SS[psl, cs_], STMP[psl, cs_], bx[psl, 0:128], ALU.add, K_("STMP") + [kx], K_("SS"))
                    cp(SSb[psl, cs_], SS[psl, cs_], K_("SS"), K_("SSb"), eng="pool")
                    stq(OFD[d, ch * 64:ch * 64 + 64, cs_], OO[psl, cs_], K_("OO"), [("OFD", d, ch, hp)], ("st", "OO" + sy))

                def init_S(d):
                    psl = slice(d * 64, d * 64 + 64)
                    sx = "_%d" % d
                    if pidx is None:
                        ld(SS[psl, :].rearrange("p (h c) -> p h c", h=4), sd[l, d].rearrange("h k v -> k h v"), [], ["SS" + sx + "0", "SS" + sx + "1"], key=("ld", "SS" + sx))
                    else:
                        P.add("pool", lambda e: e.memset(SS[psl, :], 0.0), reads=[], writes=["SS" + sx + "0", "SS" + sx + "1"])
                    cp(SSb[psl, :], SS[psl, :], ["SS" + sx + "0", "SS" + sx + "1"], ["SSb" + sx + "0", "SSb" + sx + "1"], eng="pool")

                def store_S(d):
                    psl = slice(d * 64, d * 64 + 64)
                    sx = "_%d" % d
                    if pidx is not None:
                        stq(nsd[pidx, l, d].rearrange("h k v -> k h v"), SS[psl, :].rearrange("p (h c) -> p h c", h=4), ["SS" + sx + "0", "SS" + sx + "1"],
                            [("ns", pidx, l, d)], ("st", "SS" + sx))

                if dn_on:
                    init_S(0)
                    init_S(1)
                    dn_scal_all(0)
                    dn_scal_all(1)
                    streams = []
                    for (dd, hh) in ((0, 0), (1, 0), (0, 1), (1, 1)):
                        P.rec = []
                        for i in range(nch):
                            dn_unit(i if dd == 0 else nch - 1 - i, dd, hh)
                        streams.append(P.rec)
                    P.rec = None
                    ulen = len(streams[0]) // nch
                    _sd = int(os.environ.get("DN_STAG", "6"))
                    offs = [0, ulen // _sd, (2 * ulen) // _sd, (3 * ulen) // _sd] if _sd > 0 else [0, 0, 0, 0]
                    for j in range(max(len(r) for r in streams) + offs[-1]):
                        for r, o in zip(streams, offs):
                            jj = j - o
                            if 0 <= jj < len(r):
                                P.add(*r[jj][:2], reads=r[jj][2], writes=r[jj][3], dma=r[jj][4], sem_key=r[jj][5])
                    store_S(0)
                    store_S(1)
                    DGK = ["DG_00", "DG_01", "DG_10", "DG_11"]
                    NDGK = ["NDG_00", "NDG_01", "NDG_10", "NDG_11"]
                    STK = ["STMP_00", "STMP_01", "STMP_10", "STMP_11"]
                    for t in range(ntile):
                        tsl = slice(t * 128, t * 128 + 128)
                        ld(DG[:], OFD[0, tsl, :], [("OFD", 0, 2 * t + a, b) for a in range(2) for b in range(2)], DGK, key=("ld", "DGc"))
                        ld(NDG[:], OFD[1, tsl, :], [("OFD", 1, 2 * t + a, b) for a in range(2) for b in range(2)], NDGK, key=("ld", "NDGc"))
                        tt(DG[:], DG[:], NDG[:], ALU.add, DGK + NDGK, DGK)
                        tt(SQ[:, 0:256], DG[:], DG[:], ALU.mult, DGK, ["SQ"])
                        P.add("dve", lambda e: e.tensor_reduce(out=RN[:, 0:4], in_=SQ[:, 0:256].rearrange("p (h c) -> p h c", h=4), axis=AX.X, op=ALU.add),
                              reads=["SQ"], writes=["RN"])
                        ts(RN[:, 0:4], RN[:, 0:4], 1.0 / 64.0, EPS, ALU.mult, ALU.add, ["RN"], ["RN"])
                        act(RN[:, 0:4], RN[:, 0:4], AF.Sqrt, ["RN"], ["RN"])
                        recip(RN[:, 8:12], RN[:, 0:4], ["RN"], ["RN"])
                        tt(DG[:].rearrange("p (h c) -> p h c", h=4), DG[:].rearrange("p (h c) -> p h c", h=4), bc3(RN[:, 8:12], [128, 4, 64]),
                           ALU.mult, DGK + ["RN"], DGK)
                        tt(DG[:], DG[:], DNG[:], ALU.mult, DGK + ["DNG"], DGK)
                        proj_tm(PC[:, 0:256], tsl, 768, 256, 128, "PC")
                        act(GA[:], PC[:, 0:256], AF.Silu, ["PC"], STK)
                        tt(DG[:], DG[:], GA[:], ALU.mult, DGK + STK, DGK)
                        stq(MIXD[tsl, 0:256], DG[:], DGK, [("MIXD", t, 0)], ("st", "DGc"))
                P.in_region = False

                load_wg(1040, 768)
                P.in_region = ("sgu" == DBG_REGION)
                K4 = lambda n: [n + "_00", n + "_01", n + "_10", n + "_11"]
                for t in (range(ntile) if "sgu" not in skip else []):
                    tsl = slice(t * 128, t * 128 + 128)
                    o = t % 2
                    uvg, uvk = ((UVG, ["UVG"]), (UVG2, ["UVG2"]))[o]
                    vns, vnk = ((VNS, ["VNS"]), (DG, K4("DG")))[o]
                    sgb, sgk = ((SGB, ["SGB"]), (NDG, K4("NDG")))[o]
                    yb, ybk = ((YB, ["YB"]), (DTT, K4("DTT")))[o]
                    (pa, pak), (pb, pbk) = (((PA, "PA"), (PB, "PB")), ((PC, "PC"), (PD, "PD")))[o]
                    stc, stk = o * 4, "ST%d" % o
                    proj_tm(pa[:], tsl, 0, 512, 128, pak)
                    act(uvg[:], pa[:], AF.Gelu, [pak], uvk)
                    act(JNK[:, 0:256], uvg[:, 256:512], AF.Square, uvk, ["MXT", stk], accum=ST[:, stc:stc + 1])
                    rstd_from_ssq(ST[:, stc:stc + 1], ST[:, stc + 2:stc + 3], 256.0, stk)
                    stt(vns[:], uvg[:, 256:512], ST[:, stc + 2:stc + 3], SGNG[:], ALU.mult, ALU.mult, uvk + [stk, "SGNG"], vnk)
                    for g in range(4):
                        mm(pb[:, g * 64:g * 64 + 64], SWT[:, g, :], vns[:, g * 64:g * 64 + 64], ["SWT"] + vnk, [pbk])
                    proj_tm(pb[:, 256:512], tsl, 512, 256, 128, pbk)
                    act(sgb[:], pb[:, 256:512], AF.Silu, [pbk], sgk)
                    for g in range(4):
                        stt(yb[:, g * 64:g * 64 + 64], pb[:, g * 64:g * 64 + 64], SGBT[:, g:g + 1], uvg[:, g * 64:g * 64 + 64], ALU.add, ALU.mult,
                            [pbk, "SGBT"] + uvk, ybk)
                    tt(yb[:], yb[:], sgb[:], ALU.mult, ybk + sgk, ybk)
                    stq(MIXD[t * 128:t * 128 + 128, 256:512], yb[:], ybk, [("MIXD", t, 1)], ("st", ybk[0]), q="pool")

                P.in_region = False
                load_wg(1808, 512)

                def pool_out(j):
                    o = j % 2
                    PCx, pck = ((PC, "PC"), (PD, "PD"))[o]
                    sgb, sgk = ((SGB, ["SGB"]), (NDG, K4("NDG")))[o]
                    yb, ybk = ((YB, ["YB"]), (DTT, K4("DTT")))[o]
                    typ = 0 if j == 0 else (2 if j == ntile - 1 else 1)
                    for g in range(4):
                        gs = slice(g * 64, g * 64 + 64)
                        terms = []
                        if j > 0:
                            terms.append((3, (j - 1) % 3))
                        terms.append((typ, j % 3))
                        if j < ntile - 1:
                            terms.append((4, (j + 1) % 3))
                        for i, (ty, zi) in enumerate(terms):
                            mm(PCx[:, gs], BAND[:, ty, g, :], ZR[zi][:, gs], ["BAND", "ZR%d" % zi], [pck], start=(i == 0), stop=(i == len(terms) - 1))
                    proj_tm(PCx[:, 256:512], slice(j * 128, j * 128 + 128), 256, 256, 128, pck)
                    act(sgb[:], PCx[:, 256:512], AF.Silu, [pck], sgk)
                    tt(yb[:], PCx[:, 0:256], sgb[:], ALU.mult, [pck] + sgk, ybk)
                    stq(MIXD[j * 128:j * 128 + 128, 512:768], yb[:], ybk, [("MIXD", j, 2)], ("st", ybk[0]), q="pool")

                for t in (range(ntile) if "pool" not in skip else []):
                    tsl = slice(t * 128, t * 128 + 128)
                    xc, xk = ((XCT, "XCT"), (XCT2, "XCT2"))[t % 2]
                    (pa, pak), (pb, pbk) = (((PA, "PA"), (PB, "PB")), ((PE_, "PE_"), (PF, "PF")))[t % 2]
                    for c in range(2):
                        proj_fm(pa[:, c * 128:c * 128 + 128], tsl, c * 128, 128, pak)
                    cp(xc[:].rearrange("p c t -> p (c t)"), pa[:, 0:256], [pak], [xk], eng="act")
                    for c in range(2):
                        mm(pb[:, c * 128:c * 128 + 128], xc[:, c, :], PWB[:, c, :], [xk, "PWB"], [pbk])
                    cp(ZR[t % 3][:], pb[:, 0:256], [pbk], ["ZR%d" % (t % 3)])
                    if t >= 1:
                        pool_out(t - 1)
                if "pool" not in skip:
                    pool_out(ntile - 1)

                load_wg(2320, 512)
                A = T // 64
                ai = 0 if A == 64 else 1
                e2d = cd["c_e2_%d" % A]
                f_on = "fourier" not in skip
                XCTb = [(XCT, "XCT"), (XCT2, "XCT2")]
                ZCSb = [(UVG, "UVG"), (UVG2, "UVG2")]
                for t in (range(ntile) if f_on else []):
                    tsl = slice(t * 128, t * 128 + 128)
                    xc, xk = XCTb[t % 2]
                    zc, zk = ZCSb[t % 2]
                    pa, pak = (PA, "PA") if t % 2 == 0 else (PC, "PC")
                    pb, pbk = (PB, "PB") if t % 2 == 0 else (PD, "PD")
                    for c in range(2):
                        proj_fm(pa[:, c * 128:c * 128 + 128], tsl, c * 128, 128, pak)
                    cp(xc[:].rearrange("p c t -> p (c t)"), pa[:, 0:256], [pak], [xk], eng="act")
                    for cs in range(2):
                        for c in range(2):
                            mm(pb[:, cs * 256 + c * 128:cs * 256 + c * 128 + 128], xc[:, c, :], PCS[:, cs, c, :], [xk, "PCS"], [pbk])
                    cp(zc[:], pb[:], [pbk], [zk])
                    if A != 4:
                        P.add("pool", lambda e, t=t, zc=zc: [e.dma_start(out=ZD[cs, t * 128:t * 128 + 128, :], in_=zc[:, cs * 256:cs * 256 + 256]) for cs in range(2)],
                              reads=[zk], writes=[("ZD", t)], dma=2, sem_key=("st", zk, "pool"))
                if A == 4 and f_on:
                    ld(QKVF[:], cd["c_e256"], [], ["QKVF"])
                    E256 = QKVF[:].rearrange("p (t s k) -> p t s k", t=2, s=2)
                    for kt in range(2):
                        pe, pek = (PE_, "PE_") if kt == 0 else (PF, "PF")
                        pg, pgk = (PG, "PG") if kt == 0 else (PT, "PT")
                        ga, gak = (STMP[:], ["STMP_00", "STMP_01", "STMP_10", "STMP_11"]) if kt == 0 else (SGB[:], ["SGB"])
                        yd, ydk = (YB[:], "YB") if kt == 0 else (VNS[:], "VNS")
                        i_ = 0
                        for tt2 in range(2):
                            zc, zk = ZCSb[tt2]
                            for cs in range(2):
                                mm(pe[:, 0:256], E256[:, tt2, cs, kt * 128:(kt + 1) * 128], zc[:, cs * 256:(cs + 1) * 256], ["QKVF", zk], [pek],
                                   start=(i_ == 0), stop=(i_ == 3))
                                i_ += 1
                        proj_tm(pg[:, 0:256], slice(kt * 128, kt * 128 + 128), 256, 256, 128, pgk)
                        act(ga, pg[:, 0:256], AF.Silu, [pgk], gak)
                        tt(yd, pe[:, 0:256], ga, ALU.mult, [pek] + gak, [ydk])
                        stq(MIXD[kt * 128:(kt + 1) * 128, 768:1024], yd, [ydk], [("MIXD", "f", 2 * kt), ("MIXD", "f", 2 * kt + 1)], ("st", ydk), q="pool")
                ZDall = [("ZD", t) for t in range(ntile)]
                zdv = ZD.rearrange("s (a b) c -> s a b c", b=64)
                ZAb = [(ZA[0:A], "ZA"), (QKVF[0:A, :].rearrange("p (s c) -> p s c", s=2), "QKVF")]
                VVb = [(VV[0:A], "VV"), (SQ[0:A, :].rearrange("p (s c) -> p s c", s=2), "SQ")]
                CA_, SA_, NSA_ = DFTA[0:A, ai, 0, 0:A], DFTA[0:A, ai, 1, 0:A], DFTA[0:A, ai, 2, 0:A]
                for bb in (range(32) if f_on and "f1" not in skip and A != 4 else []):
                    za, zak = ZAb[bb % 2]
                    vv, vvk = VVb[bb % 2]
                    pc, pck = (PE_, "PE_") if bb % 2 == 0 else (PF, "PF")
                    pd, pdk = (PG, "PG") if bb % 2 == 0 else (PT, "PT")
                    P.add("sp", lambda e, bb=bb, A=A, zdv=zdv, za=za: [e.dma_start(out=za[:, cs, :].rearrange("a (b c) -> a b c", b=2), in_=zdv[cs, 0:A, bb * 2:bb * 2 + 2, :]) for cs in range(2)],
                          reads=ZDall, writes=[zak], dma=2, sem_key=("ld", zak))
                    mm(pc[0:A, :], CA_, za[:, 0, :], ["DFTA", zak], [pck], start=True, stop=False)
                    mm(pc[0:A, :], NSA_, za[:, 1, :], ["DFTA", zak], [pck], start=False, stop=True)
                    mm(pd[0:A, :], CA_, za[:, 1, :], ["DFTA", zak], [pdk], start=True, stop=False)
                    mm(pd[0:A, :], SA_, za[:, 0, :], ["DFTA", zak], [pdk], start=False, stop=True)
                    cp(vv[:, 0, :], pc[0:A, :], [pck], [vvk])
                    cp(vv[:, 1, :], pd[0:A, :], [pdk], [vvk], eng="act")
                    P.add("pool", lambda e, bb=bb, A=A, vv=vv: [e.dma_start(out=VD[ri, 0:A, bb * 2:bb * 2 + 2, :], in_=vv[:, ri, :].rearrange("a (b c) -> a b c", b=2)) for ri in range(2)],
                          reads=[vvk], writes=[("VD", bb)], dma=2, sem_key=("st", vvk, "pool"))
                VDall = [("VD", bb) for bb in range(32)]
                mixv = MIXD[0:T, :].rearrange("(q a) c -> a q c", a=A)
                VBb = [(VB, "VB"), (VB2, "VB2")]
                E2b = [(E2, "E2"), (E2_2, "E2_2")]
                GAb = [(STMP[0:64, :], ["STMP_00", "STMP_01"]), (SGB[0:64, :], ["SGB"])]
                YDb = [(YB[0:64, :], "YB"), (VNS[0:64, :], "VNS")]
                for p in (range(A) if f_on and "f2" not in skip and A != 4 else []):
                    vb, vbk = VBb[p % 2]
                    e2, e2k = E2b[p % 2]
                    ga, gak = GAb[p % 2]
                    yd, ydk = YDb[p % 2]
                    pe, pek = (PA, "PA") if p % 2 == 0 else (PB, "PB")
                    pg, pgk = (PC, "PC") if p % 2 == 0 else (PD, "PD")
                    P.add("sp", lambda e, p=p, vb=vb: [e.dma_start(out=vb[:, ri, :], in_=VD[ri, p, :, :]) for ri in range(2)],
                          reads=VDall, writes=[vbk], dma=2, sem_key=("ld", vbk))
                    ld(e2[:], e2d[p], [], [e2k])
                    mm(pe[0:64, 0:256], e2[:, 0, :], vb[:, 0, :], [e2k, vbk], [pek], start=True, stop=False)
                    mm(pe[0:64, 0:256], e2[:, 1, :], vb[:, 1, :], [e2k, vbk], [pek], start=False, stop=True)
                    proj_tm(pg[0:64, 0:256], slice(p, p + A * 63 + 1, A), 256, 256, 64, pgk)
                    act(ga, pg[0:64, 0:256], AF.Silu, [pgk], gak)
                    tt(yd, pe[0:64, 0:256], ga, ALU.mult, [pek] + gak, [ydk])
                    stq(mixv[p, :, 768:1024], yd, [ydk], [("MIXD", "f", p)], ("st", ydk), q="pool")

                mix_keys_f = [("MIXD", "f", p) for p in range(A)]
                P.add("pool", lambda e, wo_v=wo_v: e.dma_start(out=WG[:, :, 0:1024], in_=wo_v), reads=[], writes=["WG"], dma=1, sem_key=("ld", "WG"))
                for t in (range(ntile) if "p3" not in skip else []):
                    g0 = tok0 // 128 + t
                    mx, mxk = ((XT2, "XT2"), (QKVF, "QKVF"))[t % 2]
                    xt, xtk = ((XT, "XT"), (SQ, "SQ"))[t % 2]
                    mt, mtk = ((MXT, "MXT"), (XN[:].rearrange("p (k t) -> p k t", k=8), "XN"))[t % 2]
                    (pa, pak), (pb, pbk) = (((PA, "PA"), (PB, "PB")), ((PE_, "PE_"), (PF, "PF")))[t % 2]
                    pouts = (((PC, "PC"), (PD, "PD")), ((PG, "PG"), (PT, "PT")))[t % 2]
                    ld(mx[:], MIXD[t * 128:(t + 1) * 128, :],
                       [("MIXD", t, 0), ("MIXD", t, 1), ("MIXD", t, 2)] + mix_keys_f, [mxk])
                    for k in range(8):
                        dst, dk = (pa, pak) if k < 4 else (pb, pbk)
                        tr(dst[:, (k % 4) * 128:(k % 4) * 128 + 128], mx[:, k * 128:(k + 1) * 128], IDN[:], [mxk, "IDN"], [dk])
                    cp(mt[:, 0:4, :].rearrange("p k t -> p (k t)"), pa[:], [pak], [mtk])
                    cp(mt[:, 4:8, :].rearrange("p k t -> p (k t)"), pb[:], [pbk], [mtk], eng="act")
                    ld(xt[:], XS[g0 * 128:(g0 + 1) * 128, :], [("XS", g0)], [xtk])
                    for n in range(2):
                        dst, dk = pouts[n]
                        for k in range(8):
                            mm(dst[:], mt[:, k, :], WG[:, k, n * 512:(n + 1) * 512], [mtk, "WG"], [dk], start=(k == 0), stop=(k == 7))
                        tt(mx[:, n * 512:(n + 1) * 512], dst[:], GATE[:, cond, n * 512:(n + 1) * 512], ALU.mult, [dk, "GATE"], [mxk])
                    tt(xt[:], xt[:], mx[:], ALU.add, [xtk, mxk], [xtk], eng="pool")
                    stq(XS[g0 * 128:(g0 + 1) * 128, :], xt[:], [xtk], [("XS", g0)], ("st", xtk), q="pool")

        outs = []
        ld(ABG[:], final_norm_g.partition_broadcast(128), [], ["ABG"])
        fin_tiles = []
        for (tok0, T, cond, pidx) in seqs:
            fin_tiles += list(range(tok0 // 128, (tok0 + T) // 128))
        for g0 in fin_tiles:
            ld(XT[:], XS[g0 * 128:(g0 + 1) * 128, :], [("XS", g0)], ["XT"])
            act(JNK, XT[:], AF.Square, ["XT"], ["MXT", "ST0"], accum=ST[:, 0:1])
            rstd_from_ssq(ST[:, 0:1], ST[:, 2:3], 1024.0, "ST0")
            stt(XT2[:], XT[:], ST[:, 2:3], ABG[:], ALU.mult, ALU.mult, ["XT", "ST0", "ABG"], ["XT2"])
            if g0 < 32:
                dst = ys[g0 * 128:(g0 + 1) * 128, :]
            else:
                dst = yp[(g0 - 32) * 128:(g0 - 31) * 128, :]
            stq(dst, XT2[:], ["XT2"], [("Y", g0)], ("st", "XT2"), q="pool")
            outs.append(("Y", g0))
        for (tok0, T, cond, pidx) in seqs:
            for l in range(nl_run):
                for d in range(2):
                    if pidx is not None:
                        outs.append(("ns", pidx, l, d))
        P.add("sp", None, reads=outs)
        P.build()
    return nc, consts


_CACHE = {}


def kernel(**inputs):
    if "nc" not in _CACHE:
        _CACHE["nc"] = build_program()
    nc, consts = _CACHE["nc"]
    f = lambda a: np.ascontiguousarray(np.asarray(a, dtype=np.float32))
    shared = {k: f(inputs[k]) for k in ["ada_w", "ada_b", "norm_g", "w_in", "conv_qkv", "dn_norm_g", "sgu_norm_g", "sgu_w", "sgu_b",
                                         "pool_w", "pool_scale", "fourier_w", "w_out", "final_norm_g"]}
    shared["a_log"] = f(inputs["a_log"]).reshape(NL, 8)
    shared["dt_bias"] = f(inputs["dt_bias"]).reshape(NL, 8)
    for k, v in consts.items():
        shared[k] = f(v)
    xsm = f(inputs["x_sample"])
    xpr = f(inputs["x_prompt"])
    sdl = f(inputs["state_delta"])
    c = f(inputs["c"])
    cctx = f(inputs["c_ctx"])
    in_maps = []
    for i in range(8):
        m = dict(shared)
        m["xs"] = xsm[i]
        m["xp"] = np.ascontiguousarray(xpr[4 * i:4 * i + 4].reshape(4 * TPR, 1024))
        m["sd"] = sdl[i]
        m["cc"] = np.ascontiguousarray(np.stack([c[i], cctx], axis=0))
        in_maps.append(m)
    res = run_bass_kernel_spmd(nc, in_maps, core_ids=list(range(8)))
    y_sample = np.stack([np.asarray(r["ys"], np.float32) for r in res.results], axis=0)
    y_prompt = np.concatenate([np.asarray(r["yp"], np.float32).reshape(4, TPR, 1024) for r in res.results], axis=0)
    ns = np.concatenate([np.asarray(r["ns"], np.float32) for r in res.results], axis=0)
    return (y_prompt, y_sample, ns)
```
